# Optimizing a Trainium2 kernel written in Bass

```python
import math
import jax, jax.numpy as jnp
from jax import lax
import numpy as np

D_MODEL = 1024
BATCH = 32
SEQ = 256
DEPTH = 4
DEC_BATCH = 8
DEC_SEQ = 4096
PAST_LEN = 256

GRID_W = 64
N_EVEN = (DEPTH + 1) // 2
N_ODD = DEPTH // 2
A_WIDTH = D_MODEL // 2
A_GROUPS = 4
A_CH = A_WIDTH // A_GROUPS
CHUNK = 128
B_WIDTH = D_MODEL // 2
HYENA_ORDER = 2
FILTER_BANDS = 16
FILTER_EMB = 2 * FILTER_BANDS + 1
FILTER_HIDDEN = 64
IN_EVEN = 2 * A_WIDTH + (HYENA_ORDER + 1) * B_WIDTH
MIX_EVEN = A_WIDTH + B_WIDTH
MLA_HEADS = 8
Q_RANK = D_MODEL // 2
KV_RANK = D_MODEL // 4
NOPE_DIM = 128
ROPE_DIM = 64
V_DIM = 128
QK_DIM = NOPE_DIM + ROPE_DIM
ROPE_BASE = 10000.0
Q_BLOCK = 128
D_FF = ((8 * D_MODEL // 3 + 127) // 128) * 128
EPS = 1e-6

kernel_name = 'hybrid_diffusion_chunkmlp_hyena_mla_step'


def rmsnorm(x, g):
    xf = x.astype(jnp.float32)
    y = xf * lax.rsqrt(jnp.mean(xf * xf, axis=-1, keepdims=True) + EPS)
    return (y * g.astype(jnp.float32)).astype(x.dtype)


def dwconv3(x, w, b):
    xp = jnp.pad(x, ((0, 0), (1, 1), (0, 0)))
    return xp[:, :-2] * w[0] + xp[:, 1:-1] * w[1] + xp[:, 2:] * w[2] + b


def adaln(cond, w, b):
    m = jax.nn.silu(cond) @ w + b
    return jnp.split(m[:, None, :], 6, axis=-1)


def axial_rope(x, seq_len):
    rows = seq_len // GRID_W
    row = jnp.repeat(jnp.arange(rows), GRID_W)
    col = jnp.tile(jnp.arange(GRID_W), rows)
    half = ROPE_DIM // 2
    inv = 1.0 / (ROPE_BASE ** (jnp.arange(0, half, 2, dtype=jnp.float32) / half))

    def rot(xa, pos):
        ang = pos.astype(jnp.float32)[:, None] * inv[None]
        cos = jnp.cos(ang)[None, :, None, :]
        sin = jnp.sin(ang)[None, :, None, :]
        x1, x2 = jnp.split(xa.astype(jnp.float32), 2, axis=-1)
        return jnp.concatenate([x1 * cos - x2 * sin, x2 * cos + x1 * sin], axis=-1)

    xr, xc = jnp.split(x, 2, axis=-1)
    return jnp.concatenate([rot(xr, row), rot(xc, col)], axis=-1).astype(x.dtype)


def rope_heads(x, seq_len):
    return jnp.concatenate([x[..., :NOPE_DIM], axial_rope(x[..., NOPE_DIM:], seq_len)], axis=-1)


def hyena_filters(L, w1, b1, w2, b2, w3, freq, decay):
    f32 = jnp.float32
    t = jnp.arange(L, dtype=f32)
    tn = t / L
    bands = jnp.arange(1, FILTER_BANDS + 1, dtype=f32)
    ang = (2.0 * math.pi) * tn[:, None] * bands[None]
    z = jnp.concatenate([tn[:, None], jnp.sin(ang), jnp.cos(ang)], axis=-1)
    fr = freq.astype(f32)
    h = jnp.sin(fr * (z @ w1.astype(f32) + b1.astype(f32)))
    h = jnp.sin(fr * (h @ w2.astype(f32) + b2.astype(f32)))
    h = (h @ w3.astype(f32)).reshape(L, HYENA_ORDER, B_WIDTH)
    dist = jnp.abs(t - L // 2) / L
    h = h * jnp.exp(-jnp.abs(decay.astype(f32))[None] * dist[:, None, None])
    return h / (jnp.sum(jnp.abs(h), axis=0, keepdims=True) + EPS)


def long_conv(u, h, d):
    L = u.shape[1]
    n = 2 * L
    y = jnp.fft.irfft(jnp.fft.rfft(u, n=n, axis=1) * jnp.fft.rfft(h, n=n, axis=0)[None], n=n, axis=1)
    return y[:, L // 2: L // 2 + L] + u * d.astype(jnp.float32)


def even_mixer(h, P, i):
    B, L, _ = h.shape
    p = h @ P['mix_w_in'][i]
    u, v = jnp.split(jax.nn.gelu(p[..., :2 * A_WIDTH]), 2, axis=-1)
    v = v.reshape(B, L // CHUNK, CHUNK, A_GROUPS, A_CH)
    s = jnp.einsum('gpq,bnqgc->bnpgc', P['sgu_w'][i], v) + P['sgu_b'][i].T[None, None, :, :, None]
    a_out = u * s.reshape(B, L, A_WIDTH)
    xb = dwconv3(p[..., 2 * A_WIDTH:], P['hy_conv_w'][i], P['hy_conv_b'][i]).astype(jnp.float32)
    vb, x1, x2 = jnp.split(xb, 3, axis=-1)
    filt = hyena_filters(L, P['hy_f_w1'][i], P['hy_f_b1'][i], P['hy_f_w2'][i], P['hy_f_b2'][i],
                         P['hy_f_w3'][i], P['hy_f_freq'][i], P['hy_decay'][i])
    d = P['hy_d'][i]
    z = x1 * long_conv(vb, filt[:, 0], d[0])
    z = x2 * long_conv(z, filt[:, 1], d[1])
    return jnp.concatenate([a_out, z.astype(h.dtype)], axis=-1) @ P['mix_w_out'][i]


def mla_queries(h, P, j):
    B, L, _ = h.shape
    q = (rmsnorm(h @ P['mla_w_dq'][j], P['mla_q_norm'][j]) @ P['mla_w_uq'][j]).reshape(B, L, MLA_HEADS, QK_DIM)
    return rmsnorm(q, P['mla_q_head_norm'][j])


def mla_compress(h, P, j):
    dkv = h @ P['mla_w_dkv'][j]
    return rmsnorm(dkv[..., :KV_RANK], P['mla_kv_norm'][j]), dkv[..., KV_RANK:]


def mla_expand(ckv, krope, P, j):
    B, L, _ = ckv.shape
    kv = (ckv @ P['mla_w_ukv'][j]).reshape(B, L, MLA_HEADS, NOPE_DIM + V_DIM)
    k = jnp.concatenate([kv[..., :NOPE_DIM],
                         jnp.broadcast_to(krope[:, :, None, :], (B, L, MLA_HEADS, ROPE_DIM))], axis=-1)
    return rmsnorm(k, P['mla_k_head_norm'][j]), kv[..., NOPE_DIM:]


def block_attention(q, k, v):
    B, Lq, H, Dk = q.shape
    nb = Lq // Q_BLOCK
    scale = 1.0 / math.sqrt(Dk)
    qb = q.reshape(B, nb, Q_BLOCK, H, Dk).transpose(1, 0, 2, 3, 4)

    def one(qblk):
        s = jnp.einsum('bqhd,bkhd->bhqk', qblk, k).astype(jnp.float32) * scale
        pr = jax.nn.softmax(s, axis=-1)
        return jnp.einsum('bhqk,bkhd->bqhd', pr.astype(v.dtype), v)

    out = lax.map(one, qb)
    return out.transpose(1, 0, 2, 3, 4).reshape(B, Lq, H * v.shape[-1])


def conv_ffn(h, P, l):
    up = dwconv3(h @ P['ffn_w_up'][l], P['ffn_conv_w'][l], P['ffn_conv_b'][l])
    g, u = jnp.split(up, 2, axis=-1)
    return (jax.nn.silu(g) * u) @ P['ffn_w_down'][l]


def trunk(x, cond, P, ctx_cache):
    B, L, _ = x.shape
    latent = ctx_cache is not None
    new_ckv = []
    new_kr = []
    for l in range(DEPTH):
        sh1, sc1, g1, sh2, sc2, g2 = adaln(cond, P['ada_w'][l], P['ada_b'][l])
        h = rmsnorm(x, P['norm_g'][l, 0]) * (1.0 + sc1) + sh1
        if l % 2 == 0:
            out = even_mixer(h, P, l // 2)
        else:
            j = l // 2
            q = mla_queries(h, P, j)
            ckv, kr = mla_compress(h, P, j)
            k, v = mla_expand(ckv, kr, P, j)
            if latent:
                q = rope_heads(q, L)
                k = rope_heads(k, L)
                kc, vc = mla_expand(ctx_cache[0][:, j], ctx_cache[1][:, j], P, j)
                k = jnp.concatenate([k, kc], axis=1)
                v = jnp.concatenate([v, vc], axis=1)
            else:
                new_ckv.append(ckv)
                new_kr.append(kr)
            out = block_attention(q, k, v) @ P['mla_w_o'][j]
        x = x + g1 * out
        h = rmsnorm(x, P['norm_g'][l, 1]) * (1.0 + sc2) + sh2
        x = x + g2 * conv_ffn(h, P, l)
    return x, new_ckv, new_kr


def setup_inputs(seed: int = 0) -> dict:
    key = jax.random.key(seed)
    ks = iter(jax.random.split(key, 64))

    def nrm(shape, scale):
        return jax.random.normal(next(ks), shape, jnp.float32) * scale

    def gain(shape):
        return 1.0 + nrm(shape, 0.05)

    D = D_MODEL
    return {
        'x_prompt': nrm((BATCH, SEQ, D), 1.0),
        'x_sample': nrm((DEC_BATCH, DEC_SEQ, D), 1.0),
        'cache_ckv': nrm((DEC_BATCH, N_ODD, PAST_LEN, KV_RANK), 1.0),
        'cache_krope': nrm((DEC_BATCH, N_ODD, PAST_LEN, ROPE_DIM), 1.0),
        'c': nrm((DEC_BATCH, D), 1.0),
        'c_ctx': nrm((D,), 1.0),
        'ada_w': nrm((DEPTH, D, 6 * D), 0.5 * D ** -0.5),
        'ada_b': nrm((DEPTH, 6 * D), 0.02),
        'norm_g': gain((DEPTH, 2, D)),
        'mix_w_in': nrm((N_EVEN, D, IN_EVEN), D ** -0.5),
        'sgu_w': nrm((N_EVEN, A_GROUPS, CHUNK, CHUNK), CHUNK ** -0.5),
        'sgu_b': nrm((N_EVEN, A_GROUPS, CHUNK), 0.02),
        'hy_conv_w': nrm((N_EVEN, 3, (HYENA_ORDER + 1) * B_WIDTH), 3 ** -0.5),
        'hy_conv_b': nrm((N_EVEN, (HYENA_ORDER + 1) * B_WIDTH), 0.02),
        'hy_f_w1': nrm((N_EVEN, FILTER_EMB, FILTER_HIDDEN), FILTER_EMB ** -0.5),
        'hy_f_b1': nrm((N_EVEN, FILTER_HIDDEN), 0.02),
        'hy_f_w2': nrm((N_EVEN, FILTER_HIDDEN, FILTER_HIDDEN), FILTER_HIDDEN ** -0.5),
        'hy_f_b2': nrm((N_EVEN, FILTER_HIDDEN), 0.02),
        'hy_f_w3': nrm((N_EVEN, FILTER_HIDDEN, HYENA_ORDER * B_WIDTH), FILTER_HIDDEN ** -0.5),
        'hy_f_freq': 1.0 + nrm((N_EVEN, FILTER_HIDDEN), 0.1),
        'hy_decay': jax.random.uniform(next(ks), (N_EVEN, HYENA_ORDER, B_WIDTH), jnp.float32, 3.0, 15.0),
        'hy_d': nrm((N_EVEN, HYENA_ORDER, B_WIDTH), 0.1),
        'mix_w_out': nrm((N_EVEN, MIX_EVEN, D), MIX_EVEN ** -0.5),
        'mla_w_dq': nrm((N_ODD, D, Q_RANK), D ** -0.5),
        'mla_q_norm': gain((N_ODD, Q_RANK)),
        'mla_w_uq': nrm((N_ODD, Q_RANK, MLA_HEADS * QK_DIM), Q_RANK ** -0.5),
        'mla_w_dkv': nrm((N_ODD, D, KV_RANK + ROPE_DIM), D ** -0.5),
        'mla_kv_norm': gain((N_ODD, KV_RANK)),
        'mla_w_ukv': nrm((N_ODD, KV_RANK, MLA_HEADS * (NOPE_DIM + V_DIM)), KV_RANK ** -0.5),
        'mla_q_head_norm': gain((N_ODD, QK_DIM)),
        'mla_k_head_norm': gain((N_ODD, QK_DIM)),
        'mla_w_o': nrm((N_ODD, MLA_HEADS * V_DIM, D), (MLA_HEADS * V_DIM) ** -0.5),
        'ffn_w_up': nrm((DEPTH, D, 2 * D_FF), D ** -0.5),
        'ffn_conv_w': nrm((DEPTH, 3, 2 * D_FF), 3 ** -0.5),
        'ffn_conv_b': nrm((DEPTH, 2 * D_FF), 0.02),
        'ffn_w_down': nrm((DEPTH, D_FF, D), D_FF ** -0.5),
    }


def reference(x_prompt, x_sample, cache_ckv, cache_krope, c, c_ctx,
              ada_w, ada_b, norm_g,
              mix_w_in, sgu_w, sgu_b, hy_conv_w, hy_conv_b,
              hy_f_w1, hy_f_b1, hy_f_w2, hy_f_b2, hy_f_w3, hy_f_freq, hy_decay, hy_d, mix_w_out,
              mla_w_dq, mla_q_norm, mla_w_uq, mla_w_dkv, mla_kv_norm, mla_w_ukv,
              mla_q_head_norm, mla_k_head_norm, mla_w_o,
              ffn_w_up, ffn_conv_w, ffn_conv_b, ffn_w_down):
    P = {
        'ada_w': ada_w, 'ada_b': ada_b, 'norm_g': norm_g,
        'mix_w_in': mix_w_in, 'sgu_w': sgu_w, 'sgu_b': sgu_b,
        'hy_conv_w': hy_conv_w, 'hy_conv_b': hy_conv_b,
        'hy_f_w1': hy_f_w1, 'hy_f_b1': hy_f_b1, 'hy_f_w2': hy_f_w2, 'hy_f_b2': hy_f_b2,
        'hy_f_w3': hy_f_w3, 'hy_f_freq': hy_f_freq, 'hy_decay': hy_decay, 'hy_d': hy_d,
        'mix_w_out': mix_w_out,
        'mla_w_dq': mla_w_dq, 'mla_q_norm': mla_q_norm, 'mla_w_uq': mla_w_uq,
        'mla_w_dkv': mla_w_dkv, 'mla_kv_norm': mla_kv_norm, 'mla_w_ukv': mla_w_ukv,
        'mla_q_head_norm': mla_q_head_norm, 'mla_k_head_norm': mla_k_head_norm, 'mla_w_o': mla_w_o,
        'ffn_w_up': ffn_w_up, 'ffn_conv_w': ffn_conv_w, 'ffn_conv_b': ffn_conv_b, 'ffn_w_down': ffn_w_down,
    }
    y_prompt, ckv_list, kr_list = trunk(x_prompt, c_ctx[None, :], P, None)
    new_cache_ckv = jnp.stack(ckv_list, axis=1)
    new_cache_krope = jnp.stack(kr_list, axis=1)
    y_sample, _, _ = trunk(x_sample, c, P, (cache_ckv, cache_krope))
    return (y_prompt, y_sample, new_cache_ckv, new_cache_krope)
```

```python
import contextlib
import math
import numpy as np
import ml_dtypes
import concourse.bass as bass
import concourse.mybir as mybir
from concourse.bass_utils import run_bass_kernel_spmd

F32 = mybir.dt.float32
BF16 = mybir.dt.bfloat16
U8 = mybir.dt.uint8
AF = mybir.ActivationFunctionType
ALU = mybir.AluOpType

COMPUTE = ("pe", "act", "dve", "pool")
STREAMS = ("pe", "act", "dve", "pool", "sp")


class Prog:
    def __init__(self, nc, strict=True):
        self.nc = nc
        self.strict = strict
        self.ops = []
        self.res = {}
        self.dma_cnt = {}
        self.last_real = {s: None for s in STREAMS}

    def _dep_entry(self, a):
        A = self.ops[a]
        if A["dma"] is not None:
            return ("d", A["dma"], 16 * self.dma_cnt[A["dma"]])
        return ("c", a)

    def op(self, eng, fn, r=(), w=(), dma=None, x=()):
        idx = len(self.ops)
        deps = set()
        for name in x:
            st = self.res.setdefault(name, [None, []])
            if st[0] is not None:
                deps.add(st[0])
            for rd in st[1]:
                if self.ops[rd]["eng"] != eng:
                    deps.add(rd)
        for name in r:
            st = self.res.setdefault(name, [None, []])
            if st[0] is not None:
                deps.add(st[0])
        for name in w:
            st = self.res.setdefault(name, [None, []])
            if st[0] is not None:
                deps.add(st[0])
            for rd in st[1]:
                deps.add(rd)
        if dma is not None:
            self.dma_cnt[dma] = self.dma_cnt.get(dma, 0)
        dep_entries = []
        for a in sorted(deps):
            A = self.ops[a]
            if A["dma"] is None and A["eng"] == eng and dma is None:
                if eng == "pe" or not self.strict:
                    continue
            dep_entries.append(self._dep_entry(a))
        if dma is not None:
            self.dma_cnt[dma] += 1
        self.ops.append(dict(eng=eng, fn=fn, deps=dep_entries, dma=dma, sig=False, val=None))
        for name in list(r) + list(x):
            self.res[name][1].append(idx)
        for name in w:
            self.res[name] = [idx, []]
        if fn is not None and dma is None:
            self.last_real[eng] = idx
        return idx

    def barrier(self):
        ents = []
        for s in COMPUTE:
            a = self.last_real.get(s)
            if a is not None:
                ents.append(("c", a))
        for key, cnt in self.dma_cnt.items():
            if cnt:
                ents.append(("d", key, 16 * cnt))
        for s in STREAMS:
            self.ops.append(dict(eng=s, fn=None, deps=list(ents), dma=None, sig=False, val=None))
        self.res = {}

    def emit(self):
        nc = self.nc
        ops = self.ops
        for o in ops:
            for d in o["deps"]:
                if d[0] == "c":
                    ops[d[1]]["sig"] = True
        cnt = {s: 0 for s in COMPUTE}
        for o in ops:
            if o["dma"] is None and o["sig"]:
                cnt[o["eng"]] += 1
                o["val"] = cnt[o["eng"]]
        dma_keys = list(self.dma_cnt.keys())
        with contextlib.ExitStack() as es:
            esem = {s: es.enter_context(nc.semaphore("sem_" + s)) for s in COMPUTE}
            dsem = {k: es.enter_context(nc.semaphore("dsem_%d" % i)) for i, k in enumerate(dma_keys)}
            block = es.enter_context(nc.Block())
            self.n_inst = {s: 0 for s in STREAMS}

            def run_stream(s, engine):
                waited = {}
                for o in ops:
                    if o["eng"] != s:
                        continue
                    for d in o["deps"]:
                        if d[0] == "c":
                            A = ops[d[1]]
                            sem, val = esem[A["eng"]], A["val"]
                            key = ("c", A["eng"])
                        else:
                            sem, val = dsem[d[1]], d[2]
                            key = ("d", d[1])
                        if waited.get(key, 0) >= val:
                            continue
                        waited[key] = val
                        engine.wait_ge(sem, val)
                        self.n_inst[s] += 1
                    if o["fn"] is None:
                        continue
                    ins = o["fn"](engine)
                    self.n_inst[s] += 1
                    if o["dma"] is not None:
                        ins.then_inc(dsem[o["dma"]], 16)
                    elif o["sig"]:
                        ins.then_inc(esem[s], 1)
                if s == "sp":
                    for k, c in self.dma_cnt.items():
                        if c and waited.get(("d", k), 0) < 16 * c:
                            engine.wait_ge(dsem[k], 16 * c)

            @block.tensor
            def _(e):
                run_stream("pe", e)

            @block.scalar
            def _(e):
                run_stream("act", e)

            @block.vector
            def _(e):
                run_stream("dve", e)

            @block.gpsimd
            def _(e):
                run_stream("pool", e)

            @block.sync
            def _(e):
                run_stream("sp", e)


class Arena:
    def __init__(self, nc, es, nbytes):
        self.t = es.enter_context(nc.sbuf_tensor("arena", [128, nbytes], U8))
        self.n = nbytes
        self.off = 0

    def alloc(self, shape_free, dtype, parts=128):
        size = {F32: 4, BF16: 2, U8: 1}[dtype]
        n = int(np.prod(shape_free)) * size
        off = (self.off + 63) // 64 * 64
        assert off + n <= self.n, ("SBUF arena overflow", off, n, self.n)
        self.off = off + n
        ap = self.t[0:parts, off:off + n].bitcast(dtype)
        if len(shape_free) == 1:
            return ap
        names = " ".join("d%d" % i for i in range(len(shape_free)))
        kw = {"d%d" % i: int(s) for i, s in enumerate(shape_free)}
        return ap.rearrange("p (%s) -> p %s" % (names, names), **kw)

    def mark(self):
        return self.off

    def release(self, m):
        self.off = m


D = 1024
DEPTH = 4
T = 5120
NT = 10
TS = 4096
LP = 256
SEGS = [(0, 4096)] + [(4096 + 256 * i, 4096 + 256 * (i + 1)) for i in range(4)]
DFF = 2816
EPS = 1e-6
HEADS = 8
NKEY = T + 256

class PV:
    def __init__(self):
        self.cols = {}
        self.n = 0
        self.data = []

    def add(self, name, vec):
        v = np.asarray(vec, np.float32).reshape(-1)
        if v.size % 128:
            v = np.concatenate([v, np.zeros(128 - v.size % 128, np.float32)])
        c = v.size // 128
        self.cols[name] = (self.n, c)
        self.data.append(v.reshape(c, 128).T)
        self.n += c

    def array(self):
        return np.ascontiguousarray(np.concatenate(self.data, axis=1))


def pv_layout(inp=None):
    pv = PV()

    def g(name, shape):
        return inp[name] if inp is not None else np.zeros(shape, np.float32)
    ng = g("norm_g", (4, 2, 1024)); ab = g("ada_b", (4, 6144))
    fw = g("ffn_conv_w", (4, 3, 5632)); fb = g("ffn_conv_b", (4, 5632))
    hw = g("hy_conv_w", (2, 3, 1536)); hb = g("hy_conv_b", (2, 1536)); hd = g("hy_d", (2, 2, 512))
    qn = g("mla_q_norm", (2, 512)); kn = g("mla_kv_norm", (2, 256))
    qh = g("mla_q_head_norm", (2, 192)); kh = g("mla_k_head_norm", (2, 192))
    b1 = g("hy_f_b1", (2, 64)); b2 = g("hy_f_b2", (2, 64)); fr = g("hy_f_freq", (2, 64))
    for l in range(4):
        for s in range(2):
            pv.add("ng%d%d" % (l, s), ng[l, s])
        pv.add("ab%d" % l, ab[l])
        for k in range(3):
            pv.add("fw%d%d" % (l, k), fw[l, k])
        pv.add("fb%d" % l, fb[l])
    for i in range(2):
        for k in range(3):
            pv.add("hw%d%d" % (i, k), hw[i, k])
        pv.add("hb%d" % i, hb[i])
        for o in range(2):
            pv.add("hd%d%d" % (i, o), hd[i, o])
        pv.add("qn%d" % i, qn[i]); pv.add("kn%d" % i, kn[i])
        pv.add("qhn%d" % i, qh[i, :128]); pv.add("qhr%d" % i, qh[i, 128:])
        pv.add("khn%d" % i, kh[i, :128]); pv.add("khr%d" % i, kh[i, 128:])
        pv.add("b1%d" % i, b1[i]); pv.add("b2%d" % i, b2[i]); pv.add("fr%d" % i, fr[i])
    return pv


_CONST = {}


def host_consts():
    if _CONST:
        return _CONST
    bf = ml_dtypes.bfloat16
    c = {}
    c["ident"] = np.eye(128, dtype=np.float32)
    t = np.arange(4096)
    row, col = t // 64, t % 64
    inv = 1.0 / (10000.0 ** (np.arange(0, 32, 2, dtype=np.float32) / 32))
    ang = np.zeros((64, 4096), np.float32)
    for p in range(64):
        pos = row if p < 32 else col
        ang[p] = pos.astype(np.float32) * inv[p % 16]
    c["ropec"] = np.cos(ang).astype(np.float32)
    c["ropes"] = np.sin(ang).astype(np.float32)
    RT = np.zeros((64, 64), np.float32)
    for base in (0, 32):
        for i in range(16):
            RT[base + 16 + i, base + i] = -1.0
            RT[base + i, base + 16 + i] = 1.0
    c["rotT"] = RT
    for L in (4096, 256):
        N = 2 * L
        tt = np.arange(L, dtype=np.float32)
        tn = tt / L
        bands = np.arange(1, 17, dtype=np.float32)
        angf = (2.0 * math.pi) * tn[:, None] * bands[None]
        z = np.concatenate([tn[:, None], np.sin(angf), np.cos(angf)], axis=-1).astype(np.float32)
        c["zf%d" % L] = np.ascontiguousarray(z.T)
        dist = np.abs(tt - L // 2) / L
        c["nd%d" % L] = np.ascontiguousarray((-dist).astype(np.float32).reshape(L // 128, 128).T)
        n = np.arange(L, dtype=np.int64)
        k = np.arange(L, dtype=np.int64)
        nch = L // 128
        m = ((2 * k[None, :] + 1) * n[:, None]) % (2 * N)
        th = (math.pi / N) * m.astype(np.float64)
        Fc = np.cos(th).astype(np.float32); Fs = np.sin(th).astype(np.float32)
        F = np.stack([Fc, Fs], 0).reshape(2, nch, 128, nch, 128)
        c["F%d" % L] = np.ascontiguousarray(F.transpose(3, 2, 1, 0, 4)).astype(bf)
        m = ((2 * k[:, None] + 1) * (n[None, :] + L // 2)) % (2 * N)
        th = (math.pi / N) * m.astype(np.float64)
        Gc = (np.cos(th) * (2.0 / N)).astype(np.float32); Gs = (np.sin(th) * (2.0 / N)).astype(np.float32)
        tw = min(512, L)
        kgn = max(1, nch // 8); kl = nch // kgn
        G = np.stack([Gc, Gs], 0).reshape(2, kgn, kl, 128, L // tw, tw)
        c["G%d" % L] = np.ascontiguousarray(G.transpose(4, 1, 3, 2, 0, 5)).astype(bf)
        if L == 4096:
            H = L // 2
            Fh = np.stack([Fc[:, :H], Fs[:, :H]], 0)
            Fh = Fh.reshape(2, 16, 128, 2, 16, 128)
            c["FE"] = np.ascontiguousarray(Fh.transpose(4, 2, 3, 1, 0, 5)).astype(bf)
            Gh = np.stack([Gc[:H], Gs[:H]], 0)
            Gh = Gh.reshape(2, 2, 8, 128, 4, 512, 2)
            c["GE"] = np.ascontiguousarray(Gh.transpose(4, 6, 1, 3, 2, 0, 5)).astype(bf)
            ndv = (-dist).astype(np.float32).reshape(16, 128, 2)
            c["ndE"] = np.ascontiguousarray(ndv.transpose(1, 2, 0).reshape(128, 32))
            del c["F4096"], c["G4096"], c["nd4096"]
    _CONST.update(c)
    return _CONST


WEIGHTS = ["ada_w", "mix_w_in", "hy_f_w1", "hy_f_w2", "hy_f_w3", "mix_w_out", "mla_w_dq", "mla_w_uq",
           "mla_w_dkv", "mla_w_ukv", "mla_w_o", "ffn_w_up", "ffn_w_down"]
WSHAPE = {"ada_w": (4, 1024, 6144), "mix_w_in": (2, 1024, 2560), "hy_f_w1": (2, 33, 64), "hy_f_w2": (2, 64, 64),
          "hy_f_w3": (2, 64, 1024), "mix_w_out": (2, 1024, 1024), "mla_w_dq": (2, 1024, 512),
          "mla_w_uq": (2, 512, 1536), "mla_w_dkv": (2, 1024, 320), "mla_w_ukv": (2, 256, 2048),
          "mla_w_o": (2, 1024, 1024), "ffn_w_up": (4, 1024, 5632), "ffn_w_down": (4, 2816, 1024)}


def build(dbg=(), depth=DEPTH):
    nc = bass.Bass("TRN2", target_bir_lowering=False)
    pvl = pv_layout()
    C = host_consts()

    def din(name, shape, dt=F32):
        return nc.dram_tensor(name, list(shape), dt, kind="ExternalInput").ap()

    def dscr(name, shape, dt):
        kind = "ExternalOutput" if name in dbg else "Internal"
        return nc.dram_tensor(name, list(shape), dt, kind=kind).ap()

    xs_d = din("xs", (4096, 1024)); xp_d = din("xp", (1024, 1024))
    cckv_d = din("cckv", (2, 256, 256)); ckr_d = din("ckr", (2, 256, 64))
    cond_d = din("condT", (128, 8, 2)); pv_d = din("pv", (128, pvl.n))
    sguT_d = din("sguT", (2, 128, 4, 128)); sgub_d = din("sgub", (2, 1, 512)); dec_d = din("decbc", (2, 128, 1024))
    W = {k: din(k, WSHAPE[k]) for k in WEIGHTS}
    cd = {k: din("c_" + k, v.shape, BF16 if v.dtype != np.float32 else F32) for k, v in C.items()}
    ys_d = nc.dram_tensor("ys", [4096, 1024], F32, kind="ExternalOutput").ap()
    yp_d = nc.dram_tensor("yp", [1024, 1024], F32, kind="ExternalOutput").ap()
    nckv_d = nc.dram_tensor("nckv", [4, 2, 256, 256], F32, kind="ExternalOutput").ap()
    nkr_d = nc.dram_tensor("nkr", [4, 2, 256, 64], F32, kind="ExternalOutput").ap()
    xT_d = dscr("xT", (1024, T), F32)
    act_d = dscr("actT", (DFF, T), BF16)
    pT_d = dscr("pT", (2048, T), BF16)
    vbtm_d = dscr("vbtm", (T, 512), BF16)
    aT_d = dscr("aT", (512, T), BF16)
    z1T_d = dscr("z1T", (512, T), BF16)
    zT_d = dscr("zT", (512, T), BF16)
    hf_d = {4096: dscr("hf4096", (2048, 4, 1024), F32), 256: dscr("hf256", (256, 2, 1024), F32)}
    oT_d = dscr("oT", (1024, T), BF16)
    qnT_d = dscr("qnT", (512, T), BF16)

    es = contextlib.ExitStack()
    with es:
        A = Arena(nc, es, 190 * 1024)
        psl = [es.enter_context(nc.psum_tensor("ps%d" % i, [128, 512], F32)) for i in range(8)]
        ps = [p_[:, :] for p_ in psl]
        P = Prog(nc)
        uid = [0]

        def U(s):
            uid[0] += 1
            return "%s#%d" % (s, uid[0])

        pvt = A.alloc([pvl.n], F32)
        ident = A.alloc([128], F32)
        ones = A.alloc([128], F32)
        onesb = A.alloc([128], BF16)
        epsT = A.alloc([1], F32)
        condT = A.alloc([8, 2], F32)
        scond = A.alloc([8, 2], BF16)
        mods = A.alloc([DEPTH, 48, 2], F32)
        gm = A.alloc([DEPTH, 2, 8, 2], F32)
        rotT = A.alloc([64], F32)
        P.op("sp", lambda e: e.dma_start(out=pvt, in_=pv_d), w=["pvt"], dma="c0")
        P.op("sp", lambda e: e.dma_start(out=ident, in_=cd["ident"]), w=["ident"], dma="c0")
        P.op("sp", lambda e: e.dma_start(out=condT, in_=cond_d), w=["condT"], dma="c0")
        P.op("sp", lambda e: e.dma_start(out=rotT[0:64, :], in_=cd["rotT"]), w=["rotT"], dma="c0")
        P.op("dve", lambda e: e.memset(ones, 1.0), w=["ones"])
        P.op("dve", lambda e: e.memset(onesb, 1.0), w=["onesb"])
        P.op("dve", lambda e: e.memset(epsT, EPS), w=["eps"])
        P.op("act", lambda e: e.activation(out=scond, in_=condT, func=AF.Silu), r=["condT"], w=["scond"])
        P.barrier()
        base_mark = A.mark()

        def pvc(name, j=0, parts=128):
            o, c = pvl.cols[name]
            return pvt[0:parts, o + j:o + j + 1]

        def phase_end():
            P.barrier()
            A.release(base_mark)

        def load_w(dst, src, key):
            P.op("pool", lambda e: e.dma_start(out=dst, in_=src), w=[key], dma="w:" + key)

        def rstd_from(psb, n, out_sb, psname):
            P.op("act", lambda e: e.activation(out=out_sb, in_=psb, func=AF.Sqrt, scale=1.0 / n, bias=epsT[0:out_sb.shape[0], 0:1]),
                 x=[psname], r=["eps"], w=[U("rs")])
            nm = P.ops
            P.op("dve", lambda e: e.reciprocal(out=out_sb, in_=out_sb), r=[], w=[])

        adab = A.alloc([2, 8, 1536], BF16)
        for l in range(depth):
            for pc in range(4):
                slot = (l * 4 + pc) % 2
                key = "adaw%d" % slot
                load_w(adab[:, slot], W["ada_w"][l].rearrange("(c p) n -> p c n", p=128)[:, :, pc * 1536:(pc + 1) * 1536], key)

                def mm(e, slot=slot, pc=pc):
                    last = None
                    for f in range(12):
                        for kc in range(8):
                            last = e.matmul(ps[0][:, (pc * 12 + f) * 2:(pc * 12 + f) * 2 + 2], lhsT=adab[:, slot, kc, f * 128:(f + 1) * 128],
                                            rhs=scond[:, kc, :], start=(kc == 0), stop=(kc == 7))
                    return last
                P.op("pe", mm, r=[key, "scond"], w=["ps0"])
            ao, _ = pvl.cols["ab%d" % l]
            for ci in range(2):
                P.op("dve", lambda e, l=l, ci=ci, ao=ao: e.tensor_tensor(out=mods[:, l, :, ci], in0=ps[0][:, ci:96:2], in1=pvt[:, ao:ao + 48], op=ALU.add),
                     x=["ps0"], r=["pvt"], w=["mods"])
            for s in range(2):
                go, _ = pvl.cols["ng%d%d" % (l, s)]
                for ci in range(2):
                    P.op("dve", lambda e, l=l, s=s, ci=ci, go=go: e.scalar_tensor_tensor(
                        out=gm[:, l, s, :, ci], in0=mods[:, l, (3 * s + 1) * 8:(3 * s + 2) * 8, ci], scalar=1.0,
                        in1=pvt[:, go:go + 8], op0=ALU.add, op1=ALU.mult), r=["mods", "pvt"], w=["gm"])
        phase_end()

        def mod_sh(l, s, dc, ci):
            return mods[:, l, (3 * s) * 8 + dc, ci:ci + 1]

        def mod_gate(l, s, dc, ci):
            return mods[:, l, (3 * s + 2) * 8 + dc, ci:ci + 1]

        def ci_of_tile(ti):
            return 0 if ti < 8 else 1

        def xT_tile_ap(ti):
            return xT_d.rearrange("(c p) t -> p c t", p=128)[:, :, ti * 512:(ti + 1) * 512]

        xin = [A.alloc([4, 1024], F32) for _ in range(2)]
        xTt = [A.alloc([8, 512], F32) for _ in range(2)]

        def x_rows(ti):
            if ti < 8:
                return xs_d[ti * 512:(ti + 1) * 512, :]
            return xp_d[(ti - 8) * 512:(ti - 7) * 512, :]

        def y_rows(ti):
            if ti < 8:
                return ys_d[ti * 512:(ti + 1) * 512, :]
            return yp_d[(ti - 8) * 512:(ti - 7) * 512, :]

        def ld_x(ti):
            P.op("sp", lambda e: e.dma_start(out=xin[ti % 2], in_=x_rows(ti).rearrange("(c p) d -> p c d", p=128)),
                 w=["xin%d" % (ti % 2)], dma="xin%d" % (ti % 2))
        ld_x(0)
        for ti in range(NT):
            if ti + 1 < NT:
                ld_x(ti + 1)
            s = ti % 2
            for dc in range(8):
                b = dc % 2

                def tr(e, s=s, dc=dc, b=b):
                    last = None
                    for tc in range(4):
                        last = e.transpose(ps[b][:, tc * 128:(tc + 1) * 128], xin[s][:, tc, dc * 128:(dc + 1) * 128], ident)
                    return last
                P.op("pe", tr, r=["xin%d" % s, "ident"], w=["ps%d" % b])
                eng = "act" if dc % 2 == 0 else "dve"
                if eng == "act":
                    P.op("act", lambda e, s=s, dc=dc, b=b: e.activation(out=xTt[s][:, dc, :], in_=ps[b], func=AF.Copy), x=["ps%d" % b], w=["xTt%d" % s])
                else:
                    P.op("dve", lambda e, s=s, dc=dc, b=b: e.tensor_copy(out=xTt[s][:, dc, :], in_=ps[b]), x=["ps%d" % b], w=["xTt%d" % s])
            P.op("sp", lambda e, s=s, ti=ti: e.dma_start(out=xT_tile_ap(ti), in_=xTt[s]), r=["xTt%d" % s], w=["d:xT"], dma="st:xTt%d" % s)
        phase_end()

        def prologue(l, s, hT):
            xt = [A.alloc([8, 512], F32) for _ in range(2)]
            sqs = [A.alloc([8, 512], BF16) for _ in range(2)]
            rss = [A.alloc([512], F32) for _ in range(2)]
            tmp = [A.alloc([512], F32) for _ in range(2)]

            def ld(ti):
                P.op("sp", lambda e: e.dma_start(out=xt[ti % 2], in_=xT_tile_ap(ti)), r=["d:xT"], w=["pxt%d" % (ti % 2)], dma="pxt%d" % (ti % 2))
            ld(0)
            for ti in range(NT):
                if ti + 1 < NT:
                    ld(ti + 1)
                b = ti % 2
                ci = ci_of_tile(ti)
                sq, rs = sqs[b], rss[b]
                pbk = 2 + b
                P.op("act", lambda e, b=b, sq=sq: e.activation(out=sq, in_=xt[b], func=AF.Square), r=["pxt%d" % b], w=["psq%d" % b])

                def mm(e, sq=sq, pbk=pbk):
                    last = None
                    for dc in range(8):
                        last = e.matmul(ps[pbk], lhsT=onesb, rhs=sq[:, dc, :], start=(dc == 0), stop=(dc == 7))
                    return last
                P.op("pe", mm, r=["psq%d" % b, "onesb"], w=["ps%d" % pbk])
                P.op("act", lambda e, rs=rs, pbk=pbk: e.activation(out=rs, in_=ps[pbk], func=AF.Sqrt, scale=1.0 / D, bias=epsT[:, 0:1]), x=["ps%d" % pbk], r=["eps"], w=["prs%d" % b])
                P.op("dve", lambda e, rs=rs: e.reciprocal(out=rs, in_=rs), r=["prs%d" % b], w=["prs%d" % b])
                for dc in range(8):
                    tb = dc % 2
                    P.op("dve", lambda e, b=b, dc=dc, tb=tb, rs=rs: e.tensor_tensor(out=tmp[tb], in0=xt[b][:, dc, :], in1=rs, op=ALU.mult),
                         r=["pxt%d" % b, "prs%d" % b], w=["ptmp%d" % tb])
                    P.op("act", lambda e, dc=dc, tb=tb, ti=ti, ci=ci: e.activation(
                        out=hT[:, dc, ti * 512:(ti + 1) * 512], in_=tmp[tb], func=AF.Identity,
                        scale=gm[:, l, s, dc, ci:ci + 1], bias=mod_sh(l, s, dc, ci)), r=["ptmp%d" % tb, "gm", "mods"], w=["hT"])

        def out_proj(l, s, wsrc, nk, in_tiles_fn):
            wt = A.alloc([nk, 1024], BF16)
            load_w(wt, wsrc.rearrange("(c p) n -> p c n", p=128), "opw")
            it = [A.alloc([nk, 512], BF16) for _ in range(2)]
            xt = [A.alloc([8, 512], F32) for _ in range(2)]

            def ld(ti):
                b = ti % 2
                in_tiles_fn(ti, it[b], "opin%d" % b)
                P.op("sp", lambda e: e.dma_start(out=xt[b], in_=xT_tile_ap(ti)), r=["d:xT"], w=["opx%d" % b], dma="opx%d" % b)
            ld(0)
            for ti in range(NT):
                if ti + 1 < NT:
                    ld(ti + 1)
                b = ti % 2
                ci = ci_of_tile(ti)
                for dc in range(8):
                    pb = 4 + dc % 4

                    def mm(e, b=b, dc=dc, pb=pb):
                        last = None
                        for kc in range(nk):
                            last = e.matmul(ps[pb], lhsT=wt[:, kc, dc * 128:(dc + 1) * 128], rhs=it[b][:, kc, :], start=(kc == 0), stop=(kc == nk - 1))
                        return last
                    P.op("pe", mm, r=["opw", "opin%d" % b], w=["ps%d" % pb])
                    P.op("dve", lambda e, b=b, dc=dc, pb=pb, ci=ci: e.scalar_tensor_tensor(
                        out=xt[b][:, dc, :], in0=ps[pb], scalar=mod_gate(l, s, dc, ci), in1=xt[b][:, dc, :], op0=ALU.mult, op1=ALU.add),
                        x=["ps%d" % pb], r=["mods"], w=["opx%d" % b])
                P.op("sp", lambda e, b=b, ti=ti: e.dma_start(out=xT_tile_ap(ti), in_=xt[b]), r=["opx%d" % b], w=["d:xT"], dma="st:opx%d" % b)

        def dwconv(raw, co, wname, bname, j, segs, rawname="raw"):
            P.op("dve", lambda e: e.tensor_scalar(out=co, in0=raw, scalar1=pvc(wname + "1", j), scalar2=pvc(bname, j), op0=ALU.mult, op1=ALU.add),
                 r=[rawname, "pvt"], w=["co"])
            for (s0, s1) in segs:
                P.op("dve", lambda e, s0=s0, s1=s1: e.scalar_tensor_tensor(out=co[:, s0 + 1:s1], in0=raw[:, s0:s1 - 1], scalar=pvc(wname + "0", j),
                                                                           in1=co[:, s0 + 1:s1], op0=ALU.mult, op1=ALU.add), r=[rawname, "pvt"], w=["co"])
                P.op("dve", lambda e, s0=s0, s1=s1: e.scalar_tensor_tensor(out=co[:, s0:s1 - 1], in0=raw[:, s0 + 1:s1], scalar=pvc(wname + "2", j),
                                                                           in1=co[:, s0:s1 - 1], op0=ALU.mult, op1=ALU.add), r=[rawname, "pvt"], w=["co"])

        def gemm_rows(wt, hT, raw, wkey, rawname="raw"):
            for ti in range(NT):
                pb = ti % 4

                def mm(e, ti=ti, pb=pb):
                    last = None
                    for kc in range(8):
                        last = e.matmul(ps[pb], lhsT=wt[:, kc, :], rhs=hT[:, kc, ti * 512:(ti + 1) * 512], start=(kc == 0), stop=(kc == 7))
                    return last
                P.op("pe", mm, r=[wkey, "hT"], w=["ps%d" % pb])
                P.op("act", lambda e, ti=ti, pb=pb: e.activation(out=raw[:, ti * 512:(ti + 1) * 512], in_=ps[pb], func=AF.Copy), x=["ps%d" % pb], w=[rawname])

        def ffn(l):
            hT = A.alloc([8, T], BF16)
            m0 = A.mark()
            prologue(l, 1, hT)
            P.barrier()
            A.release(m0)
            wt = [A.alloc([8, 128], BF16) for _ in range(4)]
            raws = [A.alloc([T], F32) for _ in range(2)]
            co = A.alloc([T], F32)
            sg = A.alloc([T], BF16)
            ab = [A.alloc([T], BF16) for _ in range(2)]
            wup = W["ffn_w_up"][l].rearrange("(c p) n -> p c n", p=128)
            NJ = DFF // 128

            def ldw(j):
                for h in range(2):
                    k = (j % 2) * 2 + h
                    load_w(wt[k], wup[:, :, h * DFF + j * 128:h * DFF + (j + 1) * 128], "fw%d" % k)

            def gemm_n(n):
                j, h = n // 2, n % 2
                if h == 0 and j + 1 < NJ:
                    ldw(j + 1)
                k = (j % 2) * 2 + h
                gemm_rows(wt[k], hT, raws[n % 2], "fw%d" % k, "raw%d" % (n % 2))

            def post_n(n):
                j, h = n // 2, n % 2
                dwconv(raws[n % 2], co, "fw%d" % l, "fb%d" % l, h * 22 + j, SEGS, "raw%d" % (n % 2))
                if h == 0:
                    P.op("act", lambda e: e.activation(out=sg, in_=co, func=AF.Silu), r=["co"], w=["sg"])
                else:
                    a = ab[j % 2]
                    P.op("dve", lambda e, a=a: e.tensor_tensor(out=a, in0=co, in1=sg, op=ALU.mult), r=["co", "sg"], w=["ab%d" % (j % 2)])
                    P.op("sp", lambda e, a=a, j=j: e.dma_start(out=act_d[j * 128:(j + 1) * 128, :], in_=a), r=["ab%d" % (j % 2)], w=["d:act"],
                         dma="st:ab%d" % (j % 2))
            ldw(0)
            gemm_n(0)
            for n in range(2 * NJ):
                if n + 1 < 2 * NJ:
                    gemm_n(n + 1)
                post_n(n)
            phase_end()

            def in_tiles(ti, dst, key):
                P.op("sp", lambda e: e.dma_start(out=dst, in_=act_d.rearrange("(c p) t -> p c t", p=128)[:, :, ti * 512:(ti + 1) * 512]),
                     r=["d:act"], w=[key], dma=key)
            out_proj(l, 1, W["ffn_w_down"][l], 22, in_tiles)
            phase_end()

        def gelu_from(src_ap, src_res, src_x, out_ap, out_res, n, graw, gt, tag):
            raw_, t_ = graw[:, 0:n], gt[:, 0:n]
            kr_, kt_ = "glraw" + tag, "glt" + tag
            P.op("act", lambda e: e.activation(out=raw_, in_=src_ap, func=AF.Copy), x=src_x, r=src_res, w=[kr_])
            P.op("dve", lambda e: e.tensor_tensor(out=t_, in0=raw_, in1=raw_, op=ALU.mult), r=[kr_], w=[kt_])
            P.op("dve", lambda e: e.tensor_scalar(out=t_, in0=t_, scalar1=0.044715, scalar2=1.0, op0=ALU.mult, op1=ALU.add), r=[kt_], w=[kt_])
            P.op("dve", lambda e: e.tensor_tensor(out=t_, in0=t_, in1=raw_, op=ALU.mult), r=[kt_, kr_], w=[kt_])
            P.op("act", lambda e: e.activation(out=t_, in_=t_, func=AF.Sigmoid, scale=2.0 * math.sqrt(2.0 / math.pi)), r=[kt_], w=[kt_])
            P.op("dve", lambda e: e.tensor_tensor(out=out_ap, in0=t_, in1=raw_, op=ALU.mult), r=[kt_, kr_], w=out_res)


        def even_mixer(l):
            i = l // 2
            hT = A.alloc([8, T], BF16)
            m0 = A.mark()
            prologue(l, 0, hT)
            P.barrier()
            A.release(m0)
            wt = [A.alloc([8, 128], BF16) for _ in range(2)]
            raws = [A.alloc([T], F32) for _ in range(2)]
            co = A.alloc([T], F32)
            ob = [A.alloc([T], BF16) for _ in range(2)]
            vtm = [A.alloc([4, 128], BF16) for _ in range(2)]
            win = W["mix_w_in"][i].rearrange("(c p) n -> p c n", p=128)
            cols = [c * 128 for c in range(4)] + [1024 + c * 128 for c in range(12)]

            def gemm_q(q):
                load_w(wt[q % 2], win[:, :, cols[q]:cols[q] + 128], "mw%d" % (q % 2))
                gemm_rows(wt[q % 2], hT, raws[q % 2], "mw%d" % (q % 2), "raw%d" % (q % 2))

            def post_q(q):
                raw = raws[q % 2]
                rn = "raw%d" % (q % 2)
                o_ = ob[q % 2]
                okey = "ob%d" % (q % 2)
                if q < 4:
                    P.op("dve", lambda e: e.tensor_tensor(out=co, in0=raw, in1=raw, op=ALU.mult), r=[rn], w=["co"])
                    P.op("dve", lambda e: e.tensor_scalar(out=co, in0=co, scalar1=0.044715, scalar2=1.0, op0=ALU.mult, op1=ALU.add), r=["co"], w=["co"])
                    P.op("dve", lambda e: e.tensor_tensor(out=co, in0=co, in1=raw, op=ALU.mult), r=["co", rn], w=["co"])
                    P.op("act", lambda e: e.activation(out=co, in_=co, func=AF.Sigmoid, scale=2.0 * math.sqrt(2.0 / math.pi)), r=["co"], w=["co"])
                    P.op("dve", lambda e: e.tensor_tensor(out=o_, in0=co, in1=raw, op=ALU.mult), r=["co", rn], w=[okey])
                else:
                    dwconv(raw, co, "hw%d" % i, "hb%d" % i, q - 4, SEGS, rn)
                    P.op("act", lambda e: e.activation(out=o_, in_=co, func=AF.Copy), r=["co"], w=[okey])
                    if q < 8:
                        for tc in range(T // 128):
                            pb = 4 + tc % 2
                            P.op("pe", lambda e, tc=tc, pb=pb: e.transpose(ps[pb][:, 0:128], co[:, tc * 128:(tc + 1) * 128], ident), r=["co", "ident"], w=["ps%d" % pb])
                            vb_ = vtm[tc % 2]
                            P.op("dve", lambda e, pb=pb, vb_=vb_: e.tensor_copy(out=vb_[:, 0, :], in_=ps[pb][:, 0:128]), x=["ps%d" % pb], w=["vtm%d" % (tc % 2)])
                            P.op("sp", lambda e, tc=tc, vb_=vb_: e.dma_start(out=vbtm_d[tc * 128:(tc + 1) * 128, (q - 4) * 128:(q - 3) * 128], in_=vb_[:, 0, :]),
                                 r=["vtm%d" % (tc % 2)], w=["d:vbtm"], dma="st:vtm%d" % (tc % 2))
                P.op("sp", lambda e: e.dma_start(out=pT_d[q * 128:(q + 1) * 128, :], in_=o_), r=[okey], w=["d:pT"], dma="st:" + okey)
            gemm_q(0)
            for q in range(16):
                if q + 1 < 16:
                    gemm_q(q + 1)
                post_q(q)
            P.barrier()
            A.release(m0)
            wv = A.alloc([8, 512], BF16)
            load_w(wv, win[:, :, 512:1024], "wv")
            sgw = A.alloc([4, 128], BF16)
            load_w(sgw, sguT_d[i], "sgw")
            sgb = A.alloc([512], BF16, parts=1)
            load_w(sgb, sgub_d[i], "sgb")
            glr = [A.alloc([512], F32) for _ in range(2)]
            glt = [A.alloc([512], F32) for _ in range(2)]
            vbs = [A.alloc([512], BF16) for _ in range(2)]
            ut = [A.alloc([4, 512], BF16) for _ in range(2)]
            at = [A.alloc([4, 512], BF16) for _ in range(2)]

            def ldu(ti):
                P.op("sp", lambda e: e.dma_start(out=ut[ti % 2], in_=pT_d[0:512, :].rearrange("(c p) t -> p c t", p=128)[:, :, ti * 512:(ti + 1) * 512]),
                     r=["d:pT"], w=["ut%d" % (ti % 2)], dma="ut%d" % (ti % 2))
            ldu(0)

            def sgu_mm(n):
                t0 = n * 128
                pa = 2 * (n % 2)

                def mm(e):
                    last = None
                    for kc in range(8):
                        last = e.matmul(ps[pa], lhsT=hT[:, kc, t0:t0 + 128], rhs=wv[:, kc, :], start=(kc == 0), stop=(kc == 7))
                    return last
                P.op("pe", mm, r=["hT", "wv"], w=["ps%d" % pa])

            def sgu_post(n):
                ti, tc = divmod(n, 4)
                b = ti % 2
                pp = n % 2
                pa, pbb = 2 * pp, 2 * pp + 1
                vb16 = vbs[pp]
                vk = "vb16_%d" % pp
                gelu_from(ps[pa], [], ["ps%d" % pa], vb16, [vk], 512, glr[pp], glt[pp], str(pp))

                def mm2(e):
                    last = None
                    for g in range(4):
                        e.matmul(ps[pbb][:, g * 128:(g + 1) * 128], lhsT=vb16[:, g * 128:(g + 1) * 128], rhs=sgw[:, g, :], start=True, stop=False)
                        last = e.matmul(ps[pbb][:, g * 128:(g + 1) * 128], lhsT=onesb[0:1, :], rhs=sgb[0:1, g * 128:(g + 1) * 128], start=False, stop=True)
                    return last
                P.op("pe", mm2, r=[vk, "sgw", "sgb", "onesb"], w=["ps%d" % pbb])
                P.op("dve", lambda e: e.tensor_tensor(out=at[b][:, :, tc * 128:(tc + 1) * 128], in0=ut[b][:, :, tc * 128:(tc + 1) * 128],
                                                      in1=ps[pbb].rearrange("p (g q) -> p g q", g=4), op=ALU.mult),
                     x=["ps%d" % pbb], r=["ut%d" % b], w=["at%d" % b])
                if tc == 3:
                    P.op("sp", lambda e: e.dma_start(out=aT_d.rearrange("(c p) t -> p c t", p=128)[:, :, ti * 512:(ti + 1) * 512], in_=at[b]),
                         r=["at%d" % b], w=["d:aT"], dma="st:at%d" % b)
            NCHK = T // 128
            sgu_mm(0)
            for n in range(NCHK):
                if n % 4 == 0 and n // 4 + 1 < NT:
                    ldu(n // 4 + 1)
                if n + 1 < NCHK:
                    sgu_mm(n + 1)
                sgu_post(n)
            phase_end()
            for L in (4096, 256):
                hyena_filters(i, L)
                phase_end()
            hyena_conv_eo(i, 0)
            phase_end()
            for sq_ in range(4):
                hyena_conv(i, 256, 4096 + 256 * sq_, tg="_%d" % (sq_ % 2))
            phase_end()

            def in_tiles(ti, dst, key):
                P.op("sp", lambda e: e.dma_start(out=dst[:, 0:4, :], in_=aT_d.rearrange("(c p) t -> p c t", p=128)[:, :, ti * 512:(ti + 1) * 512]),
                     r=["d:aT"], w=[key], dma=key)
                P.op("sp", lambda e: e.dma_start(out=dst[:, 4:8, :], in_=zT_d.rearrange("(c p) t -> p c t", p=128)[:, :, ti * 512:(ti + 1) * 512]),
                     r=["d:zT"], w=[key], dma=key)
            out_proj(l, 0, W["mix_w_out"][i], 8, in_tiles)
            phase_end()

        def sin_rr(arg, out_ap, n, res_in, res_out):
            a_, b_ = sr_a[0:64, 0:n], sr_b[0:64, 0:n]
            P.op("act", lambda e: e.activation(out=a_, in_=arg, func=AF.Sin, scale=0.5), r=res_in, w=["sra"])
            P.op("act", lambda e: e.activation(out=b_, in_=arg, func=AF.Sin, scale=0.25), r=res_in, w=["srb"])
            P.op("dve", lambda e: e.tensor_tensor(out=b_, in0=b_, in1=b_, op=ALU.mult), r=["srb"], w=["srb"])
            P.op("dve", lambda e: e.tensor_scalar(out=b_, in0=b_, scalar1=-4.0, scalar2=2.0, op0=ALU.mult, op1=ALU.add), r=["srb"], w=["srb"])
            P.op("dve", lambda e: e.tensor_tensor(out=out_ap, in0=a_, in1=b_, op=ALU.mult), r=["sra", "srb"], w=res_out)

        sr_a = sr_b = None

        def hyena_filters(i, L):
            nonlocal sr_a, sr_b
            nch = L // 128
            h2 = A.alloc([L], F32)
            hf_tm = A.alloc([nch, 1024], BF16)
            w3 = A.alloc([1024], F32)
            dec = A.alloc([1024], F32)
            nd = A.alloc([nch], F32)
            m1 = A.mark()
            zf = A.alloc([L], F32)
            w1 = A.alloc([64], F32)
            w2 = A.alloc([64], F32)
            frb = A.alloc([2], F32)
            h1 = A.alloc([L], F32)
            arg = A.alloc([512], F32)
            sr_a = A.alloc([512], F32)
            sr_b = A.alloc([512], F32)
            P.op("sp", lambda e: e.dma_start(out=zf[0:33, :], in_=cd["zf%d" % L]), w=["zf"], dma="hfl")
            P.op("sp", lambda e: e.dma_start(out=w1[0:33, :], in_=W["hy_f_w1"][i]), w=["w1"], dma="hfl")
            P.op("sp", lambda e: e.dma_start(out=w2[0:64, :], in_=W["hy_f_w2"][i]), w=["w2"], dma="hfl")
            P.op("sp", lambda e: e.dma_start(out=w3[0:64, :], in_=W["hy_f_w3"][i]), w=["w3"], dma="hfl")
            P.op("sp", lambda e: e.dma_start(out=dec, in_=dec_d[i]), w=["dec"], dma="hfl")
            P.op("sp", lambda e: e.dma_start(out=nd, in_=cd["ndE" if L == 4096 else "nd%d" % L]), w=["nd"], dma="hfl")
            P.op("act", lambda e: e.activation(out=dec, in_=dec, func=AF.Abs), r=["dec"], w=["dec"])
            for q, bn in enumerate(("b1", "b2")):
                P.op("dve", lambda e, q=q, bn=bn: e.tensor_tensor(out=frb[0:64, q:q + 1], in0=pvc("fr%d" % i, 0, 64), in1=pvc("%s%d" % (bn, i), 0, 64), op=ALU.mult),
                     r=["pvt"], w=["frb"])
            tw = min(512, L)
            for (wmat, kk, src, dst, q) in ((w1, 33, zf, h1, 0), (w2, 64, h1, h2, 1)):
                for tt in range(L // tw):
                    P.op("pe", lambda e, wmat=wmat, kk=kk, src=src, tt=tt: e.matmul(ps[0][0:64, 0:tw], lhsT=wmat[0:kk, :], rhs=src[0:kk, tt * tw:(tt + 1) * tw], start=True, stop=True),
                         r=["w1", "w2", "zf", "h1"], w=["ps0"])
                    P.op("dve", lambda e, q=q: e.tensor_scalar(out=arg[0:64, 0:tw], in0=ps[0][0:64, 0:tw], scalar1=pvc("fr%d" % i, 0, 64), scalar2=frb[0:64, q:q + 1],
                                                               op0=ALU.mult, op1=ALU.add), x=["ps0"], r=["pvt", "frb"], w=["arg"])
                    sin_rr(arg[0:64, 0:tw], dst[0:64, tt * tw:(tt + 1) * tw], tw, ["arg"], ["h1" if q == 0 else "h2"])
            P.barrier()
            A.release(m1)
            win_ = A.alloc([1024], F32)
            hwf = A.alloc([1024], F32)
            hab = A.alloc([1024], F32)
            rec = A.alloc([1024], F32)
            EO = (L == 4096)
            for sc in range(nch):
                if EO:
                    par_, mc_ = divmod(sc, 16)
                    h2s = h2[0:64, 256 * mc_ + par_:256 * mc_ + par_ + 255:2]
                else:
                    h2s = h2[0:64, sc * 128:(sc + 1) * 128]

                def mm(e, h2s=h2s):
                    e.matmul(ps[1], lhsT=h2s, rhs=w3[0:64, 0:512], start=True, stop=True)
                    return e.matmul(ps[2], lhsT=h2s, rhs=w3[0:64, 512:1024], start=True, stop=True)
                P.op("pe", mm, r=["h2", "w3"], w=["ps1", "ps2"])
                P.op("act", lambda e, sc=sc: e.activation(out=win_, in_=dec, func=AF.Exp, scale=nd[:, sc:sc + 1]), r=["dec", "nd"], w=["win"])
                for hh in range(2):
                    P.op("dve", lambda e, hh=hh: e.tensor_tensor(out=hwf[:, hh * 512:(hh + 1) * 512], in0=ps[1 + hh], in1=win_[:, hh * 512:(hh + 1) * 512], op=ALU.mult),
                         x=["ps%d" % (1 + hh)], r=["win"], w=["hwf"])
                P.op("act", lambda e: e.activation(out=hab, in_=hwf, func=AF.Abs), r=["hwf"], w=["hab"])
                P.op("dve", lambda e, sc=sc: e.tensor_copy(out=hf_tm[:, sc, :], in_=hwf), r=["hwf"], w=["hftm"])

                def mm3(e, sc=sc):
                    e.matmul(ps[3], lhsT=ones, rhs=hab[:, 0:512], start=(sc == 0), stop=(sc == nch - 1))
                    return e.matmul(ps[4], lhsT=ones, rhs=hab[:, 512:1024], start=(sc == 0), stop=(sc == nch - 1))
                P.op("pe", mm3, r=["hab", "ones"], w=["ps3", "ps4"])
            for hh in range(2):
                P.op("dve", lambda e, hh=hh: e.tensor_scalar(out=rec[:, hh * 512:(hh + 1) * 512], in0=ps[3 + hh], scalar1=EPS, scalar2=None, op0=ALU.add),
                     x=["ps%d" % (3 + hh)], w=["rec"])
            P.op("dve", lambda e: e.reciprocal(out=rec, in_=rec), r=["rec"], w=["rec"])
            if EO:
                ft = [A.alloc([2, 16, 2, 128], BF16) for _ in range(2)]
                hfo = [A.alloc([4, 1024], F32) for _ in range(2)]
                osb = [A.alloc([512], F32) for _ in range(2)]
                tf = A.alloc([512], F32)

                def ldfe(kc):
                    P.op("sp", lambda e: e.dma_start(out=ft[kc % 2], in_=cd["FE"][kc]), w=["ft%d" % (kc % 2)], dma="ft%d" % (kc % 2))
                ldfe(0)
                for kc in range(16):
                    if kc + 1 < 16:
                        ldfe(kc + 1)
                    b = kc % 2
                    for hh in range(2):
                        hs = slice(hh * 512, (hh + 1) * 512)

                        def mm(e, b=b, hs=hs):
                            last = None
                            for bank, (par, cs) in enumerate(((0, 0), (1, 0), (0, 1), (1, 1))):
                                for mc in range(16):
                                    last = e.matmul(ps[bank], lhsT=ft[b][:, par, mc, cs, :], rhs=hf_tm[:, par * 16 + mc, hs], start=(mc == 0), stop=(mc == 15))
                            return last
                        P.op("pe", mm, r=["ft%d" % b, "hftm"], w=["ps0", "ps1", "ps2", "ps3"])
                        P.op("act", lambda e: e.activation(out=osb[0], in_=ps[1], func=AF.Copy), x=["ps1"], w=["osb0"])
                        P.op("act", lambda e: e.activation(out=osb[1], in_=ps[3], func=AF.Copy), x=["ps3"], w=["osb1"])
                        for slot, (pe_, ob_, op_, rev) in enumerate(((0, 0, ALU.add, False), (2, 1, ALU.add, False), (0, 0, ALU.subtract, False), (2, 1, ALU.subtract, True))):
                            if rev:
                                P.op("dve", lambda e, pe_=pe_, ob_=ob_: e.tensor_tensor(out=tf, in0=osb[ob_], in1=ps[pe_], op=ALU.subtract), x=["ps%d" % pe_], r=["osb%d" % ob_], w=["tf"])
                            else:
                                P.op("dve", lambda e, pe_=pe_, ob_=ob_, op_=op_: e.tensor_tensor(out=tf, in0=ps[pe_], in1=osb[ob_], op=op_), x=["ps%d" % pe_], r=["osb%d" % ob_], w=["tf"])
                            P.op("dve", lambda e, b=b, slot=slot, hs=hs: e.tensor_tensor(out=hfo[b][:, slot, hs], in0=tf, in1=rec[:, hs], op=ALU.mult), r=["tf", "rec"], w=["hfo%d" % b])
                    P.op("sp", lambda e, b=b, kc=kc: e.dma_start(out=hf_d[L][kc * 128:(kc + 1) * 128], in_=hfo[b]), r=["hfo%d" % b], w=["d:hf"], dma="st:hfo%d" % b)
                return
            ft = [A.alloc([nch, 2, 128], BF16) for _ in range(2)]
            hfo = [A.alloc([2, 1024], F32) for _ in range(2)]

            def ldf(kc):
                P.op("sp", lambda e: e.dma_start(out=ft[kc % 2], in_=cd["F%d" % L][kc]), w=["ft%d" % (kc % 2)], dma="ft%d" % (kc % 2))
            ldf(0)
            for kc in range(nch):
                if kc + 1 < nch:
                    ldf(kc + 1)
                b = kc % 2
                for cs in range(2):
                    for hh in range(2):
                        pb = 5 + (cs * 2 + hh) % 3

                        def mm(e, b=b, cs=cs, hh=hh, pb=pb):
                            last = None
                            for sc in range(nch):
                                last = e.matmul(ps[pb], lhsT=ft[b][:, sc, cs, :], rhs=hf_tm[:, sc, hh * 512:(hh + 1) * 512], start=(sc == 0), stop=(sc == nch - 1))
                            return last
                        P.op("pe", mm, r=["ft%d" % b, "hftm"], w=["ps%d" % pb])
                        P.op("dve", lambda e, b=b, cs=cs, hh=hh, pb=pb: e.tensor_tensor(out=hfo[b][:, cs, hh * 512:(hh + 1) * 512], in0=ps[pb],
                                                                                      in1=rec[:, hh * 512:(hh + 1) * 512], op=ALU.mult),
                             x=["ps%d" % pb], r=["rec"], w=["hfo%d" % b])
                P.op("sp", lambda e, b=b, kc=kc: e.dma_start(out=hf_d[L][kc * 128:(kc + 1) * 128], in_=hfo[b]), r=["hfo%d" % b], w=["d:hf"], dma="st:hfo%d" % b)

        def hyena_conv_eo(i, s0):
            L = 4096
            intm = A.alloc([2, 16, 512], BF16)
            Ypq = A.alloc([16, 4, 512], BF16)
            fg = [A.alloc([8192], BF16) for _ in range(2)]
            ftv = [f.rearrange("p (a m c k) -> p a m c k", a=2, m=16, c=2, k=128) for f in fg]
            gtv = [f.rearrange("p (l c t) -> p l c t", l=8, c=2, t=512) for f in fg]
            hft = [A.alloc([4, 512], F32) for _ in range(2)]
            TA, TB, T1, T2, T3, T4, T5, T6 = [A.alloc([512], F32) for _ in range(8)]
            dti = [A.alloc([1024], BF16) for _ in range(2)]
            xti = [A.alloc([1024], BF16) for _ in range(2)]
            zo = A.alloc([4, 1024], BF16)
            zf32 = A.alloc([512], F32)
            vsrc = vbtm_d[s0:s0 + L, :].rearrange("(mc p two) n -> two p mc n", p=128, two=2)
            for par in range(2):
                P.op("sp", lambda e, par=par: e.dma_start(out=intm[:, par], in_=vsrc[par]), r=["d:vbtm"], w=["intm"], dma="intm")

            def tt_(out, a, b, op, xa=(), xb=(), ra=(), rb=(), wname=None):
                P.op("dve", lambda e: e.tensor_tensor(out=out, in0=a, in1=b, op=op), x=list(xa) + list(xb), r=list(ra) + list(rb), w=[wname])

            def run_order(o):
                dsrc = pT_d[512:1024, :] if o == 0 else z1T_d
                xsrc = pT_d[1024 + 512 * o:1536 + 512 * o, :]
                zdst = z1T_d if o == 0 else zT_d

                def ldf(kc):
                    b = kc % 2
                    P.op("sp", lambda e: e.dma_start(out=fg[b], in_=cd["FE"][kc].rearrange("p a m c k -> p (a m c k)")), w=["fg%d" % b], dma="fg%d" % b)
                    P.op("sp", lambda e: e.dma_start(out=hft[b], in_=hf_d[L][kc * 128:(kc + 1) * 128, :, o * 512:(o + 1) * 512]), r=["d:hf"], w=["hft%d" % b], dma="hft%d" % b)
                ldf(0)
                for kc in range(16):
                    if kc + 1 < 16:
                        ldf(kc + 1)
                    b = kc % 2

                    def mm(e, b=b):
                        last = None
                        for bank, (par, cs) in enumerate(((0, 0), (1, 0), (0, 1), (1, 1))):
                            for mc in range(16):
                                last = e.matmul(ps[bank], lhsT=ftv[b][:, par, mc, cs, :], rhs=intm[:, par, mc, :], start=(mc == 0), stop=(mc == 15))
                        return last
                    P.op("pe", mm, r=["fg%d" % b, "intm"], w=["ps0", "ps1", "ps2", "ps3"])
                    hk = "hft%d" % b
                    Hc, Hs, Hc2, Hs2 = (hft[b][:, q_, :] for q_ in range(4))
                    P.op("act", lambda e: e.activation(out=TA, in_=ps[1], func=AF.Copy), x=["ps1"], w=["TA"])
                    P.op("act", lambda e: e.activation(out=TB, in_=ps[3], func=AF.Copy), x=["ps3"], w=["TB"])
                    tt_(T1, ps[0], TA, ALU.add, xa=["ps0"], rb=["TA"], wname="T1")
                    tt_(T2, ps[0], TA, ALU.subtract, xa=["ps0"], rb=["TA"], wname="T2")
                    tt_(T3, ps[2], TB, ALU.add, xa=["ps2"], rb=["TB"], wname="T3")
                    tt_(T4, TB, ps[2], ALU.subtract, xb=["ps2"], ra=["TB"], wname="T4")
                    tt_(TA, T1, Hc, ALU.mult, ra=["T1"], rb=[hk], wname="TA")
                    tt_(TB, T3, Hs, ALU.mult, ra=["T3"], rb=[hk], wname="TB")
                    tt_(T5, TA, TB, ALU.subtract, ra=["TA"], rb=["TB"], wname="T5")
                    tt_(TA, T1, Hs, ALU.mult, ra=["T1"], rb=[hk], wname="TA")
                    tt_(TB, T3, Hc, ALU.mult, ra=["T3"], rb=[hk], wname="TB")
                    tt_(T6, TA, TB, ALU.add, ra=["TA"], rb=["TB"], wname="T6")
                    tt_(TA, T2, Hc2, ALU.mult, ra=["T2"], rb=[hk], wname="TA")
                    tt_(TB, T4, Hs2, ALU.mult, ra=["T4"], rb=[hk], wname="TB")
                    tt_(T1, TA, TB, ALU.subtract, ra=["TA"], rb=["TB"], wname="T1")
                    tt_(TA, T2, Hs2, ALU.mult, ra=["T2"], rb=[hk], wname="TA")
                    tt_(TB, T4, Hc2, ALU.mult, ra=["T4"], rb=[hk], wname="TB")
                    tt_(T3, TA, TB, ALU.add, ra=["TA"], rb=["TB"], wname="T3")
                    tt_(Ypq[:, kc, 0, :], T5, T1, ALU.add, ra=["T5"], rb=["T1"], wname="Y")
                    tt_(Ypq[:, kc, 1, :], T6, T3, ALU.subtract, ra=["T6"], rb=["T3"], wname="Y")
                    tt_(Ypq[:, kc, 2, :], T5, T1, ALU.subtract, ra=["T5"], rb=["T1"], wname="Y")
                    tt_(Ypq[:, kc, 3, :], T6, T3, ALU.add, ra=["T6"], rb=["T3"], wname="Y")
                seq = [(tt, par, kg) for tt in range(4) for par in range(2) for kg in range(2)]

                def ldg(q):
                    tt, par, kg = seq[q]
                    P.op("sp", lambda e: e.dma_start(out=fg[q % 2], in_=cd["GE"][tt, par, kg].rearrange("p l c t -> p (l c t)")), w=["fg%d" % (q % 2)], dma="fg%d" % (q % 2))
                ldg(0)
                for q, (tt, par, kg) in enumerate(seq):
                    if q + 1 < len(seq):
                        ldg(q + 1)
                    b = q % 2

                    def mm(e, b=b, kg=kg, par=par):
                        last = None
                        for klc in range(8):
                            kc = kg * 8 + klc
                            for cs in range(2):
                                for cch in range(4):
                                    last = e.matmul(ps[4 + cch], lhsT=Ypq[:, kc, 2 * par + cs, cch * 128:(cch + 1) * 128], rhs=gtv[b][:, klc, cs, :],
                                                    start=(kc == 0 and cs == 0), stop=(kc == 15 and cs == 1))
                        return last
                    P.op("pe", mm, r=["Y", "fg%d" % b], w=["ps4", "ps5", "ps6", "ps7"])
                    if kg == 1:
                        t0 = s0 + tt * 1024
                        for cch in range(4):
                            bb = cch % 2
                            P.op("sp", lambda e, bb=bb, cch=cch, t0=t0: e.dma_start(out=dti[bb], in_=dsrc[cch * 128:(cch + 1) * 128, t0:t0 + 1024]),
                                 r=["d:pT", "d:z1T"], w=["dti%d" % bb], dma="dti%d" % bb)
                            P.op("sp", lambda e, bb=bb, cch=cch, t0=t0: e.dma_start(out=xti[bb], in_=xsrc[cch * 128:(cch + 1) * 128, t0:t0 + 1024]),
                                 r=["d:pT"], w=["xti%d" % bb], dma="xti%d" % bb)
                            P.op("dve", lambda e, bb=bb, cch=cch, par=par: e.scalar_tensor_tensor(out=zf32, in0=dti[bb][:, par:1024:2], scalar=pvc("hd%d%d" % (i, o), cch), in1=ps[4 + cch],
                                                                                               op0=ALU.mult, op1=ALU.add), x=["ps%d" % (4 + cch)], r=["dti%d" % bb, "pvt"], w=["zf32"])
                            P.op("dve", lambda e, bb=bb, par=par: e.tensor_tensor(out=zf32, in0=zf32, in1=xti[bb][:, par:1024:2], op=ALU.mult), r=["zf32", "xti%d" % bb], w=["zf32"])
                            P.op("act", lambda e, cch=cch, par=par: e.activation(out=zo[:, cch, par:1024:2], in_=zf32, func=AF.Copy), r=["zf32"], w=["zo%d" % cch])
                            if par == 1:
                                P.op("sp", lambda e, cch=cch, t0=t0: e.dma_start(out=zdst[cch * 128:(cch + 1) * 128, t0:t0 + 1024], in_=zo[:, cch, :]),
                                     r=["zo%d" % cch], w=["d:z1T" if o == 0 else "d:zT"], dma="st:zo%d" % cch)
                            if o == 0:
                                for blk in range(4):
                                    P.op("pe", lambda e, blk=blk: e.transpose(ps[0][:, blk * 128:(blk + 1) * 128], zf32[:, blk * 128:(blk + 1) * 128], ident), r=["zf32", "ident"], w=["ps0"])
                                for blk in range(4):
                                    P.op("act", lambda e, blk=blk, cch=cch, tt=tt, par=par: e.activation(out=intm[:, par, tt * 4 + blk, cch * 128:(cch + 1) * 128],
                                                                                                     in_=ps[0][:, blk * 128:(blk + 1) * 128], func=AF.Copy), x=["ps0"], w=["intm"])
            run_order(0)
            run_order(1)


        def hyena_conv(i, L, s0, tg=""):
            keep = ("ps", "d:pT", "d:hf", "d:vbtm", "pvt", "ident")

            def rn(n):
                return n if n.startswith(keep) else n + tg

            def op(eng, fn, r=(), w=(), dma=None, x=()):
                return P.op(eng, fn, r=[rn(n) for n in r], w=[rn(n) for n in w], dma=(None if dma is None else dma + tg), x=list(x))
            nch = L // 128
            tw = min(512, L)
            ntt = L // tw
            kgn = max(1, nch // 8)
            kl = nch // kgn
            intm = A.alloc([nch, 512], BF16)
            Y = A.alloc([nch, 2, 512], BF16)
            ft = [A.alloc([nch, 2, 128], BF16) for _ in range(2)]
            gt = [A.alloc([kl, 2, tw], BF16) for _ in range(2)]
            hft = [A.alloc([2, 512], F32) for _ in range(2)]
            t1 = A.alloc([512], F32)
            t2 = A.alloc([512], F32)
            dti = [A.alloc([tw], BF16) for _ in range(2)]
            xti = [A.alloc([tw], BF16) for _ in range(2)]
            zf32 = A.alloc([tw], F32)
            zo = [A.alloc([tw], BF16) for _ in range(2)]
            op("sp", lambda e: e.dma_start(out=intm, in_=vbtm_d[s0:s0 + L, :].rearrange("(c p) n -> p c n", p=128)), r=["d:vbtm"], w=["intm"], dma="intm")
            def run_order(o):
                dsrc = pT_d[512:1024, :] if o == 0 else z1T_d
                xsrc = pT_d[1024 + 512 * o:1536 + 512 * o, :]
                zdst = z1T_d if o == 0 else zT_d

                def ldf(kc):
                    b = kc % 2
                    op("sp", lambda e: e.dma_start(out=ft[b], in_=cd["F%d" % L][kc]), w=["cft%d" % b], dma="cft%d" % b)
                    op("sp", lambda e: e.dma_start(out=hft[b], in_=hf_d[L][kc * 128:(kc + 1) * 128, :, o * 512:(o + 1) * 512]), r=["d:hf"], w=["hft%d" % b], dma="hft%d" % b)
                ldf(0)
                for kc in range(nch):
                    if kc + 1 < nch:
                        ldf(kc + 1)
                    b = kc % 2

                    def mm(e, b=b):
                        last = None
                        for cs in range(2):
                            for sc in range(nch):
                                last = e.matmul(ps[cs], lhsT=ft[b][:, sc, cs, :], rhs=intm[:, sc, :], start=(sc == 0), stop=(sc == nch - 1))
                        return last
                    op("pe", mm, r=["cft%d" % b, "intm"], w=["ps0", "ps1"])
                    Hc, Hs = hft[b][:, 0, :], hft[b][:, 1, :]
                    hk = "hft%d" % b
                    op("dve", lambda e, Hc=Hc: e.tensor_tensor(out=t1, in0=ps[0], in1=Hc, op=ALU.mult), x=["ps0"], r=[hk], w=["t1"])
                    op("dve", lambda e, Hs=Hs: e.tensor_tensor(out=t2, in0=ps[1], in1=Hs, op=ALU.mult), x=["ps1"], r=[hk], w=["t2"])
                    op("dve", lambda e, kc=kc: e.tensor_tensor(out=Y[:, kc, 0, :], in0=t1, in1=t2, op=ALU.subtract), r=["t1", "t2"], w=["Y"])
                    op("dve", lambda e, Hs=Hs: e.tensor_tensor(out=t1, in0=ps[0], in1=Hs, op=ALU.mult), x=["ps0"], r=[hk], w=["t1"])
                    op("dve", lambda e, Hc=Hc: e.tensor_tensor(out=t2, in0=ps[1], in1=Hc, op=ALU.mult), x=["ps1"], r=[hk], w=["t2"])
                    op("dve", lambda e, kc=kc: e.tensor_tensor(out=Y[:, kc, 1, :], in0=t1, in1=t2, op=ALU.add), r=["t1", "t2"], w=["Y"])
                seq = [(tt, kg) for tt in range(ntt) for kg in range(kgn)]

                def ldg(q):
                    tt, kg = seq[q]
                    op("sp", lambda e: e.dma_start(out=gt[q % 2], in_=cd["G%d" % L][tt, kg]), w=["gt%d" % (q % 2)], dma="gt%d" % (q % 2))
                ldg(0)
                for q, (tt, kg) in enumerate(seq):
                    if q + 1 < len(seq):
                        ldg(q + 1)
                    b = q % 2

                    def mm(e, b=b, kg=kg):
                        last = None
                        for klc in range(kl):
                            kc = kg * kl + klc
                            for cs in range(2):
                                for cch in range(4):
                                    last = e.matmul(ps[2 + cch][:, 0:tw], lhsT=Y[:, kc, cs, cch * 128:(cch + 1) * 128], rhs=gt[b][:, klc, cs, :],
                                                    start=(kc == 0 and cs == 0), stop=(kc == nch - 1 and cs == 1))
                        return last
                    op("pe", mm, r=["Y", "gt%d" % b], w=["ps2", "ps3", "ps4", "ps5"])
                    if kg == kgn - 1:
                        t0 = s0 + tt * tw
                        for cch in range(4):
                            bb = cch % 2
                            op("sp", lambda e, bb=bb, cch=cch, t0=t0: e.dma_start(out=dti[bb], in_=dsrc[cch * 128:(cch + 1) * 128, t0:t0 + tw]),
                                 r=["d:pT", "d:z1T"], w=["dti%d" % bb], dma="dti%d" % bb)
                            op("sp", lambda e, bb=bb, cch=cch, t0=t0: e.dma_start(out=xti[bb], in_=xsrc[cch * 128:(cch + 1) * 128, t0:t0 + tw]),
                                 r=["d:pT"], w=["xti%d" % bb], dma="xti%d" % bb)
                            op("dve", lambda e, bb=bb, cch=cch: e.scalar_tensor_tensor(out=zf32, in0=dti[bb], scalar=pvc("hd%d%d" % (i, o), cch), in1=ps[2 + cch][:, 0:tw],
                                                                                        op0=ALU.mult, op1=ALU.add), x=["ps%d" % (2 + cch)], r=["dti%d" % bb, "pvt"], w=["zf32"])
                            op("dve", lambda e, bb=bb: e.tensor_tensor(out=zf32, in0=zf32, in1=xti[bb], op=ALU.mult), r=["zf32", "xti%d" % bb], w=["zf32"])
                            op("act", lambda e, bb=bb: e.activation(out=zo[bb], in_=zf32, func=AF.Copy), r=["zf32"], w=["zo%d" % bb])
                            op("sp", lambda e, bb=bb, cch=cch, t0=t0: e.dma_start(out=zdst[cch * 128:(cch + 1) * 128, t0:t0 + tw], in_=zo[bb]),
                                 r=["zo%d" % bb], w=["d:z1T" if o == 0 else "d:zT"], dma="st:zo%d" % bb)
                            if o == 0:
                                for tc in range(tw // 128):
                                    op("pe", lambda e, tc=tc: e.transpose(ps[6][:, tc * 128:(tc + 1) * 128], zf32[:, tc * 128:(tc + 1) * 128], ident), r=["zf32", "ident"], w=["ps6"])
                                for tc in range(tw // 128):
                                    op("act", lambda e, tc=tc, cch=cch, tt=tt: e.activation(out=intm[:, tt * (tw // 128) + tc, cch * 128:(cch + 1) * 128],
                                                                                           in_=ps[6][:, tc * 128:(tc + 1) * 128], func=AF.Copy), x=["ps6"], w=["intm"])
            run_order(0)
            run_order(1)

        def mla(l):
            j = l // 2
            ckvT = A.alloc([2, NKEY], BF16)
            krT = A.alloc([NKEY], F32)
            m_persist = A.mark()
            hT = A.alloc([8, T], BF16)
            m0 = A.mark()
            prologue(l, 0, hT)
            P.barrier()
            A.release(m0)
            wdq = A.alloc([8, 512], BF16)
            load_w(wdq, W["mla_w_dq"][j].rearrange("(c p) n -> p c n", p=128), "wdq")
            wdkv = A.alloc([8, 320], BF16)
            load_w(wdkv, W["mla_w_dkv"][j].rearrange("(c p) n -> p c n", p=128), "wdkv")
            rawq = A.alloc([4, 512], F32)
            qnb = [A.alloc([4, 512], BF16) for _ in range(2)]
            sq = A.alloc([4, 512], BF16)
            rs = A.alloc([512], F32)
            ckf = A.alloc([2, 512], F32)
            tok = [A.alloc([256], F32) for _ in range(2)]
            tokr = [A.alloc([64], F32) for _ in range(2)]
            cin = A.alloc([2, 256], F32)
            cinr = A.alloc([2, 64], F32)
            def mm_tile(ti):
                sl = slice(ti * 512, (ti + 1) * 512)
                for oc in range(4):
                    def mm(e, oc=oc):
                        last = None
                        for kc in range(8):
                            last = e.matmul(ps[oc], lhsT=wdq[:, kc, oc * 128:(oc + 1) * 128], rhs=hT[:, kc, sl], start=(kc == 0), stop=(kc == 7))
                        return last
                    P.op("pe", mm, r=["wdq", "hT"], w=["ps%d" % oc])
                for oc in range(3):
                    m_ = 128 if oc < 2 else 64

                    def mm(e, oc=oc, m_=m_):
                        last = None
                        for kc in range(8):
                            last = e.matmul(ps[4 + oc][0:m_, :], lhsT=wdkv[:, kc, oc * 128:oc * 128 + m_], rhs=hT[:, kc, sl], start=(kc == 0), stop=(kc == 7))
                        return last
                    P.op("pe", mm, r=["wdkv", "hT"], w=["ps%d" % (4 + oc)])

            def evac_tile(ti):
                sl = slice(ti * 512, (ti + 1) * 512)
                for oc in range(4):
                    P.op("act", lambda e, oc=oc: e.activation(out=rawq[:, oc, :], in_=ps[oc], func=AF.Copy), x=["ps%d" % oc], w=["rawq"])
                for oc in range(3):
                    if oc < 2:
                        P.op("act", lambda e, oc=oc: e.activation(out=ckf[:, oc, :], in_=ps[4 + oc], func=AF.Copy), x=["ps%d" % (4 + oc)], w=["ckf"])
                    else:
                        P.op("act", lambda e: e.activation(out=krT[0:64, sl], in_=ps[6][0:64, :], func=AF.Copy), x=["ps6"], w=["krT"])

            def chain_tile(ti):
                sl = slice(ti * 512, (ti + 1) * 512)
                P.op("act", lambda e: e.activation(out=sq, in_=rawq, func=AF.Square), r=["rawq"], w=["sq"])

                def mms(e):
                    last = None
                    for oc in range(4):
                        last = e.matmul(ps[7], lhsT=onesb, rhs=sq[:, oc, :], start=(oc == 0), stop=(oc == 3))
                    return last
                P.op("pe", mms, r=["sq", "onesb"], w=["ps7"])
                P.op("act", lambda e: e.activation(out=rs, in_=ps[7], func=AF.Sqrt, scale=1.0 / 512, bias=epsT[:, 0:1]), x=["ps7"], r=["eps"], w=["rs"])
                P.op("dve", lambda e: e.reciprocal(out=rs, in_=rs), r=["rs"], w=["rs"])
                for oc in range(4):
                    P.op("dve", lambda e, oc=oc: e.tensor_tensor(out=rawq[:, oc, :], in0=rawq[:, oc, :], in1=rs, op=ALU.mult), r=["rawq", "rs"], w=["rawq"])
                    P.op("act", lambda e, oc=oc: e.activation(out=qnb[ti % 2][:, oc, :], in_=rawq[:, oc, :], func=AF.Identity, scale=pvc("qn%d" % j, oc)), r=["rawq", "pvt"], w=["qnb%d" % (ti % 2)])
                P.op("sp", lambda e: e.dma_start(out=qnT_d.rearrange("(c p) t -> p c t", p=128)[:, :, ti * 512:(ti + 1) * 512], in_=qnb[ti % 2]),
                     r=["qnb%d" % (ti % 2)], w=["d:qnT"], dma="st:qnb%d" % (ti % 2))
                P.op("act", lambda e: e.activation(out=sq[:, 0:2, :], in_=ckf, func=AF.Square), r=["ckf"], w=["sq"])

                def mms2(e):
                    e.matmul(ps[7], lhsT=onesb, rhs=sq[:, 0, :], start=True, stop=False)
                    return e.matmul(ps[7], lhsT=onesb, rhs=sq[:, 1, :], start=False, stop=True)
                P.op("pe", mms2, r=["sq", "onesb"], w=["ps7"])
                P.op("act", lambda e: e.activation(out=rs, in_=ps[7], func=AF.Sqrt, scale=1.0 / 256, bias=epsT[:, 0:1]), x=["ps7"], r=["eps"], w=["rs"])
                P.op("dve", lambda e: e.reciprocal(out=rs, in_=rs), r=["rs"], w=["rs"])
                for oc in range(2):
                    P.op("dve", lambda e, oc=oc: e.tensor_tensor(out=ckf[:, oc, :], in0=ckf[:, oc, :], in1=rs, op=ALU.mult), r=["ckf", "rs"], w=["ckf"])
                    P.op("dve", lambda e, oc=oc: e.tensor_scalar(out=ckf[:, oc, :], in0=ckf[:, oc, :], scalar1=pvc("kn%d" % j, oc), scalar2=None, op0=ALU.mult), r=["ckf", "pvt"], w=["ckf"])
                    P.op("act", lambda e, oc=oc: e.activation(out=ckvT[:, oc, sl], in_=ckf[:, oc, :], func=AF.Copy), r=["ckf"], w=["ckvT"])
                if ti >= 8:
                    for tc in range(4):
                        sqi = (ti - 8) * 2 + tc // 2
                        r0 = (tc % 2) * 128
                        b = tc % 2

                        def tr(e, tc=tc):
                            e.transpose(ps[7][:, 0:128], ckf[:, 0, tc * 128:(tc + 1) * 128], ident)
                            e.transpose(ps[7][:, 128:256], ckf[:, 1, tc * 128:(tc + 1) * 128], ident)
                            return e.transpose(ps[7][:, 256:320], krT[0:64, ti * 512 + tc * 128:ti * 512 + (tc + 1) * 128], ident[0:64, 0:64])
                        P.op("pe", tr, r=["ckf", "krT", "ident"], w=["ps7"])
                        P.op("dve", lambda e, b=b: e.tensor_copy(out=tok[b], in_=ps[7][:, 0:256]), x=["ps7"], w=["tok%d" % b])
                        P.op("dve", lambda e, b=b: e.tensor_copy(out=tokr[b], in_=ps[7][:, 256:320]), x=["ps7"], w=["tokr%d" % b])
                        P.op("sp", lambda e, b=b, sqi=sqi, r0=r0: e.dma_start(out=nckv_d[sqi, j, r0:r0 + 128, :], in_=tok[b]), r=["tok%d" % b], dma="st:tok%d" % b)
                        P.op("sp", lambda e, b=b, sqi=sqi, r0=r0: e.dma_start(out=nkr_d[sqi, j, r0:r0 + 128, :], in_=tokr[b]), r=["tokr%d" % b], dma="st:tokr%d" % b)
            mm_tile(0)
            for ti in range(NT):
                evac_tile(ti)
                if ti + 1 < NT:
                    mm_tile(ti + 1)
                chain_tile(ti)
            P.op("sp", lambda e: e.dma_start(out=cin, in_=cckv_d[j].rearrange("(c p) n -> p c n", p=128)), w=["cin"], dma="cin")
            P.op("sp", lambda e: e.dma_start(out=cinr, in_=ckr_d[j].rearrange("(c p) n -> p c n", p=128)), w=["cinr"], dma="cin")
            for tc in range(2):
                def tr(e, tc=tc):
                    e.transpose(ps[6][:, 0:128], cin[:, tc, 0:128], ident)
                    e.transpose(ps[6][:, 128:256], cin[:, tc, 128:256], ident)
                    return e.transpose(ps[6][0:64, 256:384], cinr[:, tc, :], ident)
                P.op("pe", tr, r=["cin", "cinr", "ident"], w=["ps6"])
                for cc in range(2):
                    P.op("act", lambda e, tc=tc, cc=cc: e.activation(out=ckvT[:, cc, T + tc * 128:T + (tc + 1) * 128], in_=ps[6][:, cc * 128:(cc + 1) * 128], func=AF.Copy),
                         x=["ps6"], w=["ckvT"])
                P.op("act", lambda e, tc=tc: e.activation(out=krT[0:64, T + tc * 128:T + (tc + 1) * 128], in_=ps[6][0:64, 256:384], func=AF.Copy), x=["ps6"], w=["krT"])
            P.barrier()
            A.release(m_persist)
            wuq = A.alloc([4, 1536], BF16)
            load_w(wuq, W["mla_w_uq"][j].rearrange("(c p) n -> p c n", p=128), "wuq")
            wukv = A.alloc([2, 2048], BF16)
            load_w(wukv, W["mla_w_ukv"][j].rearrange("(c p) n -> p c n", p=128), "wukv")
            ropec = A.alloc([4096], F32)
            ropes = A.alloc([4096], F32)
            P.op("sp", lambda e: e.dma_start(out=ropec[0:64, :], in_=cd["ropec"]), w=["ropec"], dma="rope")
            P.op("sp", lambda e: e.dma_start(out=ropes[0:64, :], in_=cd["ropes"]), w=["ropes"], dma="rope")
            NKC = 34
            NCH = NKEY // 128
            Khr = A.alloc([NKEY], BF16)
            krss = A.alloc([NCH], F32)
            KhnA = [A.alloc([NKC * 128], BF16) for _ in range(2)]
            VhA = [A.alloc([NKC, 128], BF16) for _ in range(2)]
            sclA = [A.alloc([NKC], F32) for _ in range(2)]
            KhnB = [A.alloc([256], BF16) for _ in range(4)]
            VhB = [A.alloc([2, 128], BF16) for _ in range(4)]
            sclB = [A.alloc([2], F32) for _ in range(4)]
            sqk = A.alloc([512], BF16)
            tk = A.alloc([4], F32)
            sqa = A.alloc([512], BF16)
            sqr = A.alloc([512], BF16)
            rsa = A.alloc([512], F32)
            tn = A.alloc([512], F32)
            tr_ = A.alloc([512], F32)
            tr2 = A.alloc([512], F32)
            Qn = [A.alloc([512], BF16) for _ in range(2)]
            Qr = [A.alloc([512], BF16) for _ in range(2)]
            qin = [A.alloc([4, 512], BF16) for _ in range(2)]
            PT = [A.alloc([512], BF16) for _ in range(4)]
            rec = A.alloc([512], F32)
            ob = [A.alloc([512], BF16) for _ in range(2)]
            sc_ = 1.0 / math.sqrt(192.0)
            kgr = "khr%d" % j

            for c0 in range(0, NCH, 4):
                cn = min(4, NCH - c0)
                P.op("act", lambda e, c0=c0, cn=cn: e.activation(out=sqr[0:64, 0:cn * 128], in_=krT[0:64, c0 * 128:(c0 + cn) * 128], func=AF.Square), r=["krT"], w=["sqr"])

                def mm(e, c0=c0, cn=cn):
                    last = None
                    for a_ in range(cn):
                        last = e.matmul(ps[4][:, c0 + a_:c0 + a_ + 1], lhsT=sqr[0:64, a_ * 128:(a_ + 1) * 128], rhs=onesb[0:64, 0:1], start=True, stop=True)
                    return last
                P.op("pe", mm, r=["sqr", "onesb"], w=["ps4"])
            P.op("dve", lambda e: e.tensor_copy(out=krss, in_=ps[4][:, 0:NCH]), x=["ps4"], w=["krss"])
            for c0 in range(0, NKEY, 512):
                n = min(512, NKEY - c0)
                cs_ = slice(c0, c0 + n)
                P.op("dve", lambda e, cs_=cs_, n=n: e.tensor_scalar(out=tr2[0:64, 0:n], in0=krT[0:64, cs_], scalar1=pvc(kgr, 0, 64), scalar2=None, op0=ALU.mult), r=["krT", "pvt"], w=["tr2"])
                if c0 < 4096:
                    P.op("pe", lambda e, n=n: e.matmul(ps[6][0:64, 0:n], lhsT=rotT[0:64, :], rhs=tr2[0:64, 0:n], start=True, stop=True), r=["rotT", "tr2"], w=["ps6"])
                    P.op("dve", lambda e, cs_=cs_, n=n: e.tensor_tensor(out=tn[0:64, 0:n], in0=ps[6][0:64, 0:n], in1=ropes[0:64, cs_], op=ALU.mult), x=["ps6"], r=["ropes"], w=["tn"])
                    P.op("dve", lambda e, cs_=cs_, n=n: e.tensor_tensor(out=tr2[0:64, 0:n], in0=tr2[0:64, 0:n], in1=ropec[0:64, cs_], op=ALU.mult), r=["tr2", "ropec"], w=["tr2"])
                    P.op("dve", lambda e, cs_=cs_, n=n: e.tensor_tensor(out=Khr[0:64, cs_], in0=tr2[0:64, 0:n], in1=tn[0:64, 0:n], op=ALU.add), r=["tr2", "tn"], w=["Khr"])
                else:
                    P.op("dve", lambda e, cs_=cs_, n=n: e.tensor_copy(out=Khr[0:64, cs_], in_=tr2[0:64, 0:n]), r=["tr2"], w=["Khr"])

            def kprep(hd, S, Khn_, Vh_, scl_, tag):
                kch = S["kch"]
                groups = [kch[a:a + 4] for a in range(0, len(kch), 4)]
                for gi, grp in enumerate(groups):
                    ng = len(grp)
                    n = ng * 128
                    c0 = grp[0] * 128
                    assert grp[-1] == grp[0] + ng - 1
                    lo = gi * 512

                    def mm(e, c0=c0, n=n):
                        last = None
                        for kc in range(2):
                            last = e.matmul(ps[4][:, 0:n], lhsT=wukv[:, kc, hd * 256:hd * 256 + 128], rhs=ckvT[:, kc, c0:c0 + n], start=(kc == 0), stop=(kc == 1))
                        return last
                    P.op("pe", mm, r=["wukv", "ckvT"], w=["ps4"])
                    yield
                    P.op("act", lambda e, n=n: e.activation(out=sqk[:, 0:n], in_=ps[4][:, 0:n], func=AF.Square), x=["ps4"], w=["sqk"])
                    yield
                    P.op("dve", lambda e, lo=lo, n=n: e.tensor_scalar(out=Khn_[:, lo:lo + n], in0=ps[4][:, 0:n], scalar1=pvc("khn%d" % j), scalar2=None, op0=ALU.mult),
                         x=["ps4"], r=["pvt"], w=["Khn" + tag])
                    yield

                    def mm1(e, ng=ng):
                        last = None
                        for a_ in range(ng):
                            last = e.matmul(ps[6][:, a_:a_ + 1], lhsT=sqk[:, a_ * 128:(a_ + 1) * 128], rhs=onesb[:, 0:1], start=True, stop=True)
                        return last
                    P.op("pe", mm1, r=["sqk", "onesb"], w=["ps6"])
                    yield
                    P.op("dve", lambda e, ng=ng, g0=grp[0]: e.tensor_tensor(out=tk[:, 0:ng], in0=ps[6][:, 0:ng], in1=krss[:, g0:g0 + ng], op=ALU.add), x=["ps6"], r=["krss"], w=["tk"])
                    yield
                    P.op("act", lambda e, ng=ng: e.activation(out=tk[:, 0:ng], in_=tk[:, 0:ng], func=AF.Ln, scale=1.0 / 192, bias=epsT[:, 0:1]), r=["tk", "eps"], w=["tk"])
                    yield
                    P.op("act", lambda e, ng=ng: e.activation(out=tk[:, 0:ng], in_=tk[:, 0:ng], func=AF.Exp, scale=-0.5), r=["tk"], w=["tk"])
                    yield
                    P.op("dve", lambda e, ng=ng, gi=gi: e.tensor_scalar(out=scl_[:, gi * 4:gi * 4 + ng], in0=tk[:, 0:ng], scalar1=sc_, scalar2=None, op0=ALU.mult), r=["tk"], w=["scl" + tag])
                    yield

                    def mmv(e, grp=grp):
                        last = None
                        for a_, ch in enumerate(grp):
                            for kc in range(2):
                                last = e.matmul(ps[5][:, a_ * 128:(a_ + 1) * 128], lhsT=ckvT[:, kc, ch * 128:(ch + 1) * 128],
                                                rhs=wukv[:, kc, hd * 256 + 128:hd * 256 + 256], start=(kc == 0), stop=(kc == 1))
                        return last
                    P.op("pe", mmv, r=["wukv", "ckvT"], w=["ps5"])
                    yield
                    P.op("dve", lambda e, gi=gi, ng=ng, n=n: e.tensor_copy(out=Vh_[:, gi * 4:gi * 4 + ng, :], in_=ps[5][:, 0:n].rearrange("p (a d) -> p a d", d=128)),
                         x=["ps5"], w=["Vh" + tag])
                    yield

            def qprep(hd, S, qt, slot):
                qw = min(512, S["nq"])
                q0 = S["q0"] + qt * qw
                rope = S["rope"]
                qb_ = qin[slot]
                qn_, qr_ = Qn[slot], Qr[slot]
                qres = "Qh%d" % slot
                P.op("sp", lambda e: e.dma_start(out=qb_[:, :, 0:qw], in_=qnT_d.rearrange("(c p) t -> p c t", p=128)[:, :, q0:q0 + qw]),
                     r=["d:qnT"], w=["qin%d" % slot], dma="qin%d" % slot)
                yield

                def mmq(e):
                    last = None
                    for kc in range(4):
                        e.matmul(ps[4][:, 0:qw], lhsT=wuq[:, kc, hd * 192:hd * 192 + 128], rhs=qb_[:, kc, 0:qw], start=(kc == 0), stop=(kc == 3))
                    for kc in range(4):
                        last = e.matmul(ps[5][0:64, 0:qw], lhsT=wuq[:, kc, hd * 192 + 128:hd * 192 + 192], rhs=qb_[:, kc, 0:qw], start=(kc == 0), stop=(kc == 3))
                    return last
                P.op("pe", mmq, r=["wuq", "qin%d" % slot], w=["ps4", "ps5"])
                yield
                P.op("act", lambda e: e.activation(out=sqa[:, 0:qw], in_=ps[4][:, 0:qw], func=AF.Square), x=["ps4"], w=["sqa"])
                yield
                P.op("act", lambda e: e.activation(out=tr_[0:64, 0:qw], in_=ps[5][0:64, 0:qw], func=AF.Copy), x=["ps5"], w=["tr"])
                yield
                P.op("act", lambda e: e.activation(out=sqr[0:64, 0:qw], in_=tr_[0:64, 0:qw], func=AF.Square), r=["tr"], w=["sqr"])
                yield

                def mm(e):
                    e.matmul(ps[6][:, 0:qw], lhsT=onesb, rhs=sqa[:, 0:qw], start=True, stop=False)
                    return e.matmul(ps[6][:, 0:qw], lhsT=onesb[0:64, :], rhs=sqr[0:64, 0:qw], start=False, stop=True)
                P.op("pe", mm, r=["sqa", "sqr", "onesb"], w=["ps6"])
                yield
                P.op("act", lambda e: e.activation(out=rsa[:, 0:qw], in_=ps[6][:, 0:qw], func=AF.Ln, scale=1.0 / 192, bias=epsT[:, 0:1]), x=["ps6"], r=["eps"], w=["rsa"])
                yield
                P.op("act", lambda e: e.activation(out=rsa[:, 0:qw], in_=rsa[:, 0:qw], func=AF.Exp, scale=-0.5), r=["rsa"], w=["rsa"])
                yield
                P.op("dve", lambda e: e.tensor_tensor(out=tn[:, 0:qw], in0=ps[4][:, 0:qw], in1=rsa[:, 0:qw], op=ALU.mult), x=["ps4"], r=["rsa"], w=["tn"])
                yield
                P.op("dve", lambda e: e.tensor_scalar(out=qn_[:, 0:qw], in0=tn[:, 0:qw], scalar1=pvc("qhn%d" % j), scalar2=None, op0=ALU.mult), r=["tn", "pvt"], w=[qres])
                yield
                P.op("dve", lambda e: e.tensor_tensor(out=tr2[0:64, 0:qw], in0=tr_[0:64, 0:qw], in1=rsa[0:64, 0:qw], op=ALU.mult), r=["tr", "rsa"], w=["tr2"])
                yield
                if not rope:
                    P.op("dve", lambda e: e.tensor_scalar(out=qr_[0:64, 0:qw], in0=tr2[0:64, 0:qw], scalar1=pvc("qhr%d" % j, 0, 64), scalar2=None, op0=ALU.mult), r=["tr2", "pvt"], w=[qres])
                    yield
                else:
                    tc_ = slice(q0, q0 + qw)
                    P.op("dve", lambda e: e.tensor_scalar(out=tr2[0:64, 0:qw], in0=tr2[0:64, 0:qw], scalar1=pvc("qhr%d" % j, 0, 64), scalar2=None, op0=ALU.mult), r=["tr2", "pvt"], w=["tr2"])
                    yield
                    P.op("pe", lambda e: e.matmul(ps[6][0:64, 0:qw], lhsT=rotT[0:64, :], rhs=tr2[0:64, 0:qw], start=True, stop=True), r=["rotT", "tr2"], w=["ps6"])
                    yield
                    P.op("dve", lambda e: e.tensor_tensor(out=tn[0:64, 0:qw], in0=ps[6][0:64, 0:qw], in1=ropes[0:64, tc_], op=ALU.mult), x=["ps6"], r=["ropes"], w=["tn"])
                    yield
                    P.op("dve", lambda e: e.tensor_tensor(out=tr2[0:64, 0:qw], in0=tr2[0:64, 0:qw], in1=ropec[0:64, tc_], op=ALU.mult), r=["tr2", "ropec"], w=["tr2"])
                    yield
                    P.op("dve", lambda e: e.tensor_tensor(out=qr_[0:64, 0:qw], in0=tr2[0:64, 0:qw], in1=tn[0:64, 0:qw], op=ALU.add), r=["tr2", "tn"], w=[qres])
                    yield

            def drain(g):
                for _ in g:
                    pass

            def core(hd, S, qt, Khn_, Vh_, scl_, tag, slot, pending, oidx):
                kch = S["kch"]
                nk = len(kch)
                qw = min(512, S["nq"])
                q0 = S["q0"] + qt * qw
                qn_, qr_ = Qn[slot], Qr[slot]
                qres = "Qh%d" % slot
                SB = (0, 1, 7)

                def qk(a):
                    sb = SB[a % 3]
                    pt = PT[a % 4]
                    kc0 = kch[a] * 128

                    def mms_(e):
                        e.matmul(ps[sb][:, 0:qw], lhsT=Khn_[:, a * 128:(a + 1) * 128], rhs=qn_[:, 0:qw], start=True, stop=False)
                        return e.matmul(ps[sb][:, 0:qw], lhsT=Khr[0:64, kc0:kc0 + 128], rhs=qr_[0:64, 0:qw], start=False, stop=True)
                    P.op("pe", mms_, r=["Khn" + tag, "Khr", qres], w=["ps%d" % sb])
                    P.op("act", lambda e: e.activation(out=pt[:, 0:qw], in_=ps[sb][:, 0:qw], func=AF.Exp, scale=scl_[:, a:a + 1]),
                         x=["ps%d" % sb], r=["scl" + tag], w=["PT%d" % (a % 4)])

                def pv(a):
                    pt = PT[a % 4]

                    def mmo(e):
                        e.matmul(ps[2][:, 0:qw], lhsT=Vh_[:, a, :], rhs=pt[:, 0:qw], start=(a == 0), stop=(a == nk - 1))
                        return e.matmul(ps[3][:, 0:qw], lhsT=onesb, rhs=pt[:, 0:qw], start=(a == 0), stop=(a == nk - 1))
                    P.op("pe", mmo, r=["Vh" + tag, "PT%d" % (a % 4), "onesb"], w=["ps2", "ps3"])

                def drip(k):
                    for _ in range(k):
                        while pending:
                            try:
                                next(pending[0])
                                break
                            except StopIteration:
                                pending.pop(0)
                qk(0)
                if nk > 1:
                    qk(1)
                for a in range(nk):
                    if a + 2 < nk:
                        qk(a + 2)
                    pv(a)
                    drip(2 if a % 2 else 1)
                o_ = ob[oidx % 2]
                P.op("dve", lambda e: e.reciprocal(out=rec[:, 0:qw], in_=ps[3][:, 0:qw]), x=["ps3"], w=["rec"])
                P.op("dve", lambda e: e.tensor_tensor(out=o_[:, 0:qw], in0=ps[2][:, 0:qw], in1=rec[:, 0:qw], op=ALU.mult), x=["ps2"], r=["rec"], w=["ob%d" % (oidx % 2)])
                P.op("sp", lambda e: e.dma_start(out=oT_d[hd * 128:(hd + 1) * 128, q0:q0 + qw], in_=o_[:, 0:qw]), r=["ob%d" % (oidx % 2)], w=["d:oT"],
                     dma="st:aob%d" % (oidx % 2))

            seqs = [dict(q0=0, nq=4096, kch=list(range(32)) + [40, 41], rope=True)]
            for sq_ in range(4):
                seqs.append(dict(q0=4096 + 256 * sq_, nq=256, kch=[32 + 2 * sq_, 33 + 2 * sq_], rope=False))
            units = []
            for hd in range(HEADS):
                for si, S in enumerate(seqs):
                    if si == 0:
                        bufs = (KhnA[hd % 2], VhA[hd % 2], sclA[hd % 2], "A%d" % (hd % 2))
                    else:
                        bufs = (KhnB[si - 1], VhB[si - 1], sclB[si - 1], "B%d" % (si - 1))
                    units.append(dict(hd=hd, S=S, si=si, bufs=bufs))
            for u in units:
                u["kgen"] = kprep(u["hd"], u["S"], *u["bufs"])
            items = []
            for ui, u in enumerate(units):
                qw = min(512, u["S"]["nq"])
                for qt in range(u["S"]["nq"] // qw):
                    items.append(dict(ui=ui, qt=qt))
            for ii, it in enumerate(items):
                u = units[it["ui"]]
                it["qgen"] = qprep(u["hd"], u["S"], it["qt"], ii % 2)
            for ii, it in enumerate(items):
                ui = it["ui"]
                u = units[ui]
                drain(u["kgen"])
                drain(it["qgen"])
                pending = []
                if ii + 1 < len(items):
                    pending.append(items[ii + 1]["qgen"])
                if u["si"] == 0:
                    for k in range(1, 5):
                        pending.append(units[ui + k]["kgen"])
                    if ui + 5 < len(units):
                        pending.append(units[ui + 5]["kgen"])
                core(u["hd"], u["S"], it["qt"], *u["bufs"], ii % 2, pending, ii)
            phase_end()

            def in_tiles(ti, dst, key):
                P.op("sp", lambda e: e.dma_start(out=dst, in_=oT_d.rearrange("(c p) t -> p c t", p=128)[:, :, ti * 512:(ti + 1) * 512]), r=["d:oT"], w=[key], dma=key)
            out_proj(l, 0, W["mla_w_o"][j], 8, in_tiles)
            phase_end()

        for l in range(depth):
            if l % 2 == 0:
                even_mixer(l)
            else:
                mla(l)
            ffn(l)

        xt = [A.alloc([8, 512], F32) for _ in range(2)]
        yo = [A.alloc([1024], F32) for _ in range(2)]

        def ldx(ti):
            P.op("sp", lambda e: e.dma_start(out=xt[ti % 2], in_=xT_tile_ap(ti)), r=["d:xT"], w=["fx%d" % (ti % 2)], dma="fx%d" % (ti % 2))
        ldx(0)
        for ti in range(NT):
            if ti + 1 < NT:
                ldx(ti + 1)
            b = ti % 2
            for tc in range(4):
                yb = tc % 2
                for hh in range(2):
                    pb = hh

                    def tr(e, b=b, tc=tc, hh=hh, pb=pb):
                        last = None
                        for d4 in range(4):
                            dc = hh * 4 + d4
                            last = e.transpose(ps[pb][:, d4 * 128:(d4 + 1) * 128], xt[b][:, dc, tc * 128:(tc + 1) * 128], ident)
                        return last
                    P.op("pe", tr, r=["fx%d" % b, "ident"], w=["ps%d" % pb])
                    if hh == 0:
                        P.op("act", lambda e, yb=yb, pb=pb: e.activation(out=yo[yb][:, 0:512], in_=ps[pb], func=AF.Copy), x=["ps%d" % pb], w=["yo%d" % yb])
                    else:
                        P.op("dve", lambda e, yb=yb, pb=pb: e.tensor_copy(out=yo[yb][:, 512:1024], in_=ps[pb]), x=["ps%d" % pb], w=["yo%d" % yb])
                P.op("sp", lambda e, yb=yb, ti=ti, tc=tc: e.dma_start(out=y_rows(ti)[tc * 128:(tc + 1) * 128, :], in_=yo[yb]), r=["yo%d" % yb], dma="st:yo%d" % yb)
        P.barrier()
        P.emit()
        build.n_inst = P.n_inst
    return nc


_NC = {}


def make_in_maps(inp):
    C = host_consts()
    pv = pv_layout(inp).array()
    bf = ml_dtypes.bfloat16
    shared = {k: np.ascontiguousarray(inp[k], dtype=np.float32) for k in WEIGHTS}
    shared["pv"] = pv
    shared["sguT"] = np.ascontiguousarray(np.transpose(inp["sgu_w"], (0, 3, 1, 2)))
    shared["sgub"] = np.ascontiguousarray(inp["sgu_b"].reshape(2, 1, 512))
    shared["decbc"] = np.ascontiguousarray(np.broadcast_to(inp["hy_decay"].reshape(2, 1, 1024), (2, 128, 1024)))
    for k, v in C.items():
        shared["c_" + k] = v
    maps = []
    for c in range(8):
        m = dict(shared)
        m["xs"] = np.ascontiguousarray(inp["x_sample"][c])
        m["xp"] = np.ascontiguousarray(inp["x_prompt"][4 * c:4 * c + 4].reshape(1024, 1024))
        m["cckv"] = np.ascontiguousarray(inp["cache_ckv"][c])
        m["ckr"] = np.ascontiguousarray(inp["cache_krope"][c])
        cond = np.stack([inp["c"][c], inp["c_ctx"]], axis=1)
        m["condT"] = np.ascontiguousarray(cond.reshape(8, 128, 2).transpose(1, 0, 2))
        maps.append(m)
    return maps


def kernel(**inputs):
    inp = {k: np.asarray(v) for k, v in inputs.items()}
    if "nc" not in _NC:
        _NC["nc"] = build()
    nc = _NC["nc"]
    maps = make_in_maps(inp)
    res = run_bass_kernel_spmd(nc, maps, core_ids=list(range(8)))
    R = res.results
    y_sample = np.stack([np.asarray(R[c]["ys"], np.float32) for c in range(8)], 0)
    y_prompt = np.concatenate([np.asarray(R[c]["yp"], np.float32).reshape(4, 256, 1024) for c in range(8)], 0)
    nckv = np.concatenate([np.asarray(R[c]["nckv"], np.float32) for c in range(8)], 0)
    nkr = np.concatenate([np.asarray(R[c]["nkr"], np.float32) for c in range(8)], 0)
    return (y_prompt, y_sample, nckv, nkr)
```

```python
import contextlib
import math
import numpy as np
import ml_dtypes
import concourse.bass as bass
import concourse.mybir as mybir
from concourse.bass_utils import run_bass_kernel_spmd

F32 = mybir.dt.float32
BF16 = mybir.dt.bfloat16
U8 = mybir.dt.uint8
AF = mybir.ActivationFunctionType
ALU = mybir.AluOpType

COMPUTE = ("pe", "act", "dve", "pool")
STREAMS = ("pe", "act", "dve", "pool", "sp")


class Prog:
    def __init__(self, nc, strict=True):
        self.nc = nc
        self.strict = strict
        self.ops = []
        self.res = {}
        self.dma_cnt = {}
        self.last_real = {s: None for s in STREAMS}

    def _dep_entry(self, a):
        A = self.ops[a]
        if A["dma"] is not None:
            return ("d", A["dma"], 16 * self.dma_cnt[A["dma"]])
        return ("c", a)

    def op(self, eng, fn, r=(), w=(), dma=None, x=()):
        idx = len(self.ops)
        deps = set()
        for name in x:
            st = self.res.setdefault(name, [None, []])
            if st[0] is not None:
                deps.add(st[0])
            for rd in st[1]:
                if self.ops[rd]["eng"] != eng:
                    deps.add(rd)
        for name in r:
            st = self.res.setdefault(name, [None, []])
            if st[0] is not None:
                deps.add(st[0])
        for name in w:
            st = self.res.setdefault(name, [None, []])
            if st[0] is not None:
                deps.add(st[0])
            for rd in st[1]:
                deps.add(rd)
        if dma is not None:
            self.dma_cnt[dma] = self.dma_cnt.get(dma, 0)
        dep_entries = []
        for a in sorted(deps):
            A = self.ops[a]
            if A["dma"] is None and A["eng"] == eng and dma is None:
                if eng == "pe" or not self.strict:
                    continue
            dep_entries.append(self._dep_entry(a))
        if dma is not None:
            self.dma_cnt[dma] += 1
        self.ops.append(dict(eng=eng, fn=fn, deps=dep_entries, dma=dma, sig=False, val=None))
        for name in list(r) + list(x):
            self.res[name][1].append(idx)
        for name in w:
            self.res[name] = [idx, []]
        if fn is not None and dma is None:
            self.last_real[eng] = idx
        return idx

    def barrier(self):
        ents = []
        for s in COMPUTE:
            a = self.last_real.get(s)
            if a is not None:
                ents.append(("c", a))
        for key, cnt in self.dma_cnt.items():
            if cnt:
                ents.append(("d", key, 16 * cnt))
        for s in STREAMS:
            self.ops.append(dict(eng=s, fn=None, deps=list(ents), dma=None, sig=False, val=None))
        self.res = {}

    def emit(self):
        nc = self.nc
        ops = self.ops
        for o in ops:
            for d in o["deps"]:
                if d[0] == "c":
                    ops[d[1]]["sig"] = True
        cnt = {s: 0 for s in COMPUTE}
        for o in ops:
            if o["dma"] is None and o["sig"]:
                cnt[o["eng"]] += 1
                o["val"] = cnt[o["eng"]]
        dma_keys = list(self.dma_cnt.keys())
        with contextlib.ExitStack() as es:
            esem = {s: es.enter_context(nc.semaphore("sem_" + s)) for s in COMPUTE}
            dsem = {k: es.enter_context(nc.semaphore("dsem_%d" % i)) for i, k in enumerate(dma_keys)}
            block = es.enter_context(nc.Block())
            self.n_inst = {s: 0 for s in STREAMS}

            def run_stream(s, engine):
                waited = {}
                for o in ops:
                    if o["eng"] != s:
                        continue
                    for d in o["deps"]:
                        if d[0] == "c":
                            A = ops[d[1]]
                            sem, val = esem[A["eng"]], A["val"]
                            key = ("c", A["eng"])
                        else:
                            sem, val = dsem[d[1]], d[2]
                            key = ("d", d[1])
                        if waited.get(key, 0) >= val:
                            continue
                        waited[key] = val
                        engine.wait_ge(sem, val)
                        self.n_inst[s] += 1
                    if o["fn"] is None:
                        continue
                    ins = o["fn"](engine)
                    self.n_inst[s] += 1
                    if o["dma"] is not None:
                        ins.then_inc(dsem[o["dma"]], 16)
                    elif o["sig"]:
                        ins.then_inc(esem[s], 1)
                if s == "sp":
                    for k, c in self.dma_cnt.items():
                        if c and waited.get(("d", k), 0) < 16 * c:
                            engine.wait_ge(dsem[k], 16 * c)

            @block.tensor
            def _(e):
                run_stream("pe", e)

            @block.scalar
            def _(e):
                run_stream("act", e)

            @block.vector
            def _(e):
                run_stream("dve", e)

            @block.gpsimd
            def _(e):
                run_stream("pool", e)

            @block.sync
            def _(e):
                run_stream("sp", e)


class Arena:
    def __init__(self, nc, es, nbytes):
        self.t = es.enter_context(nc.sbuf_tensor("arena", [128, nbytes], U8))
        self.n = nbytes
        self.off = 0

    def alloc(self, shape_free, dtype, parts=128):
        size = {F32: 4, BF16: 2, U8: 1}[dtype]
        n = int(np.prod(shape_free)) * size
        off = (self.off + 63) // 64 * 64
        assert off + n <= self.n, ("SBUF arena overflow", off, n, self.n)
        self.off = off + n
        ap = self.t[0:parts, off:off + n].bitcast(dtype)
        if len(shape_free) == 1:
            return ap
        names = " ".join("d%d" % i for i in range(len(shape_free)))
        kw = {"d%d" % i: int(s) for i, s in enumerate(shape_free)}
        return ap.rearrange("p (%s) -> p %s" % (names, names), **kw)

    def mark(self):
        return self.off

    def release(self, m):
        self.off = m


D = 1024
DEPTH = 4
T = 5120
NT = 10
TS = 4096
LP = 256
SEGS = [(0, 4096)] + [(4096 + 256 * i, 4096 + 256 * (i + 1)) for i in range(4)]
DFF = 2816
EPS = 1e-6
HEADS = 8
NKEY = T + 256

class PV:
    def __init__(self):
        self.cols = {}
        self.n = 0
        self.data = []

    def add(self, name, vec):
        v = np.asarray(vec, np.float32).reshape(-1)
        if v.size % 128:
            v = np.concatenate([v, np.zeros(128 - v.size % 128, np.float32)])
        c = v.size // 128
        self.cols[name] = (self.n, c)
        self.data.append(v.reshape(c, 128).T)
        self.n += c

    def array(self):
        return np.ascontiguousarray(np.concatenate(self.data, axis=1))


def pv_layout(inp=None):
    pv = PV()

    def g(name, shape):
        return inp[name] if inp is not None else np.zeros(shape, np.float32)
    ng = g("norm_g", (4, 2, 1024)); ab = g("ada_b", (4, 6144))
    fw = g("ffn_conv_w", (4, 3, 5632)); fb = g("ffn_conv_b", (4, 5632))
    hw = g("hy_conv_w", (2, 3, 1536)); hb = g("hy_conv_b", (2, 1536)); hd = g("hy_d", (2, 2, 512))
    qn = g("mla_q_norm", (2, 512)); kn = g("mla_kv_norm", (2, 256))
    qh = g("mla_q_head_norm", (2, 192)); kh = g("mla_k_head_norm", (2, 192))
    b1 = g("hy_f_b1", (2, 64)); b2 = g("hy_f_b2", (2, 64)); fr = g("hy_f_freq", (2, 64))
    for l in range(4):
        for s in range(2):
            pv.add("ng%d%d" % (l, s), ng[l, s])
        pv.add("ab%d" % l, ab[l])
        for k in range(3):
            pv.add("fw%d%d" % (l, k), fw[l, k])
        pv.add("fb%d" % l, fb[l])
    for i in range(2):
        for k in range(3):
            pv.add("hw%d%d" % (i, k), hw[i, k])
        pv.add("hb%d" % i, hb[i])
        for o in range(2):
            pv.add("hd%d%d" % (i, o), hd[i, o])
        pv.add("qn%d" % i, qn[i]); pv.add("kn%d" % i, kn[i])
        pv.add("qhn%d" % i, qh[i, :128]); pv.add("qhr%d" % i, qh[i, 128:])
        pv.add("khn%d" % i, kh[i, :128]); pv.add("khr%d" % i, kh[i, 128:])
        pv.add("b1%d" % i, b1[i]); pv.add("b2%d" % i, b2[i]); pv.add("fr%d" % i, fr[i])
    return pv


_CONST = {}


def host_consts():
    if _CONST:
        return _CONST
    bf = ml_dtypes.bfloat16
    c = {}
    c["ident"] = np.eye(128, dtype=np.float32)
    t = np.arange(4096)
    row, col = t // 64, t % 64
    inv = 1.0 / (10000.0 ** (np.arange(0, 32, 2, dtype=np.float32) / 32))
    ang = np.zeros((64, 4096), np.float32)
    for p in range(64):
        pos = row if p < 32 else col
        ang[p] = pos.astype(np.float32) * inv[p % 16]
    c["ropec"] = np.cos(ang).astype(np.float32)
    c["ropes"] = np.sin(ang).astype(np.float32)
    RT = np.zeros((64, 64), np.float32)
    for base in (0, 32):
        for i in range(16):
            RT[base + 16 + i, base + i] = -1.0
            RT[base + i, base + 16 + i] = 1.0
    c["rotT"] = RT
    for L in (4096, 256):
        N = 2 * L
        tt = np.arange(L, dtype=np.float32)
        tn = tt / L
        bands = np.arange(1, 17, dtype=np.float32)
        angf = (2.0 * math.pi) * tn[:, None] * bands[None]
        z = np.concatenate([tn[:, None], np.sin(angf), np.cos(angf)], axis=-1).astype(np.float32)
        c["zf%d" % L] = np.ascontiguousarray(z.T)
        dist = np.abs(tt - L // 2) / L
        c["nd%d" % L] = np.ascontiguousarray((-dist).astype(np.float32).reshape(L // 128, 128).T)
        n = np.arange(L, dtype=np.int64)
        k = np.arange(L, dtype=np.int64)
        nch = L // 128
        m = ((2 * k[None, :] + 1) * n[:, None]) % (2 * N)
        th = (math.pi / N) * m.astype(np.float64)
        Fc = np.cos(th).astype(np.float32); Fs = np.sin(th).astype(np.float32)
        F = np.stack([Fc, Fs], 0).reshape(2, nch, 128, nch, 128)
        c["F%d" % L] = np.ascontiguousarray(F.transpose(3, 2, 1, 0, 4)).astype(bf)
        m = ((2 * k[:, None] + 1) * (n[None, :] + L // 2)) % (2 * N)
        th = (math.pi / N) * m.astype(np.float64)
        Gc = (np.cos(th) * (2.0 / N)).astype(np.float32); Gs = (np.sin(th) * (2.0 / N)).astype(np.float32)
        tw = min(512, L)
        kgn = max(1, nch // 8); kl = nch // kgn
        G = np.stack([Gc, Gs], 0).reshape(2, kgn, kl, 128, L // tw, tw)
        c["G%d" % L] = np.ascontiguousarray(G.transpose(4, 1, 3, 2, 0, 5)).astype(bf)
        if L == 4096:
            H = L // 2
            Fh = np.stack([Fc[:, :H], Fs[:, :H]], 0)
            Fh = Fh.reshape(2, 16, 128, 2, 16, 128)
            c["FE"] = np.ascontiguousarray(Fh.transpose(4, 2, 3, 1, 0, 5)).astype(bf)
            Gh = np.stack([Gc[:H], Gs[:H]], 0)
            Gh = Gh.reshape(2, 2, 8, 128, 4, 512, 2)
            c["GE"] = np.ascontiguousarray(Gh.transpose(4, 6, 1, 3, 2, 0, 5)).astype(bf)
            ndv = (-dist).astype(np.float32).reshape(16, 128, 2)
            c["ndE"] = np.ascontiguousarray(ndv.transpose(1, 2, 0).reshape(128, 32))
            del c["F4096"], c["G4096"], c["nd4096"]
    _CONST.update(c)
    return _CONST


WEIGHTS = ["ada_w", "mix_w_in", "hy_f_w1", "hy_f_w2", "hy_f_w3", "mix_w_out", "mla_w_dq", "mla_w_uq",
           "mla_w_dkv", "mla_w_ukv", "mla_w_o", "ffn_w_up", "ffn_w_down"]
WSHAPE = {"ada_w": (4, 1024, 6144), "mix_w_in": (2, 1024, 2560), "hy_f_w1": (2, 33, 64), "hy_f_w2": (2, 64, 64),
          "hy_f_w3": (2, 64, 1024), "mix_w_out": (2, 1024, 1024), "mla_w_dq": (2, 1024, 512),
          "mla_w_uq": (2, 512, 1536), "mla_w_dkv": (2, 1024, 320), "mla_w_ukv": (2, 256, 2048),
          "mla_w_o": (2, 1024, 1024), "ffn_w_up": (4, 1024, 5632), "ffn_w_down": (4, 2816, 1024)}


def build(dbg=(), depth=DEPTH):
    nc = bass.Bass("TRN2", target_bir_lowering=False)
    pvl = pv_layout()
    C = host_consts()

    def din(name, shape, dt=F32):
        return nc.dram_tensor(name, list(shape), dt, kind="ExternalInput").ap()

    def dscr(name, shape, dt):
        kind = "ExternalOutput" if name in dbg else "Internal"
        return nc.dram_tensor(name, list(shape), dt, kind=kind).ap()

    xs_d = din("xs", (4096, 1024)); xp_d = din("xp", (1024, 1024))
    cckv_d = din("cckv", (2, 256, 256)); ckr_d = din("ckr", (2, 256, 64))
    cond_d = din("condT", (128, 8, 2)); pv_d = din("pv", (128, pvl.n))
    sguT_d = din("sguT", (2, 128, 4, 128)); sgub_d = din("sgub", (2, 1, 512)); dec_d = din("decbc", (2, 128, 1024))
    W = {k: din(k, WSHAPE[k]) for k in WEIGHTS}
    cd = {k: din("c_" + k, v.shape, BF16 if v.dtype != np.float32 else F32) for k, v in C.items()}
    ys_d = nc.dram_tensor("ys", [4096, 1024], F32, kind="ExternalOutput").ap()
    yp_d = nc.dram_tensor("yp", [1024, 1024], F32, kind="ExternalOutput").ap()
    nckv_d = nc.dram_tensor("nckv", [4, 2, 256, 256], F32, kind="ExternalOutput").ap()
    nkr_d = nc.dram_tensor("nkr", [4, 2, 256, 64], F32, kind="ExternalOutput").ap()
    xT_d = dscr("xT", (1024, T), F32)
    act_d = dscr("actT", (DFF, T), BF16)
    pT_d = dscr("pT", (2048, T), BF16)
    vbtm_d = dscr("vbtm", (T, 512), BF16)
    aT_d = dscr("aT", (512, T), BF16)
    z1T_d = dscr("z1T", (512, T), BF16)
    zT_d = dscr("zT", (512, T), BF16)
    hf_d = {4096: dscr("hf4096", (2048, 4, 1024), F32), 256: dscr("hf256", (256, 2, 1024), F32)}
    oT_d = dscr("oT", (1024, T), BF16)
    qnT_d = dscr("qnT", (512, T), BF16)

    es = contextlib.ExitStack()
    with es:
        A = Arena(nc, es, 190 * 1024)
        psl = [es.enter_context(nc.psum_tensor("ps%d" % i, [128, 512], F32)) for i in range(8)]
        ps = [p_[:, :] for p_ in psl]
        P = Prog(nc)
        uid = [0]

        def U(s):
            uid[0] += 1
            return "%s#%d" % (s, uid[0])

        pvt = A.alloc([pvl.n], F32)
        ident = A.alloc([128], F32)
        ones = A.alloc([128], F32)
        onesb = A.alloc([128], BF16)
        epsT = A.alloc([1], F32)
        condT = A.alloc([8, 2], F32)
        scond = A.alloc([8, 2], BF16)
        mods = A.alloc([DEPTH, 48, 2], F32)
        gm = A.alloc([DEPTH, 2, 8, 2], F32)
        rotT = A.alloc([64], F32)
        P.op("sp", lambda e: e.dma_start(out=pvt, in_=pv_d), w=["pvt"], dma="c0")
        P.op("sp", lambda e: e.dma_start(out=ident, in_=cd["ident"]), w=["ident"], dma="c0")
        P.op("sp", lambda e: e.dma_start(out=condT, in_=cond_d), w=["condT"], dma="c0")
        P.op("sp", lambda e: e.dma_start(out=rotT[0:64, :], in_=cd["rotT"]), w=["rotT"], dma="c0")
        P.op("dve", lambda e: e.memset(ones, 1.0), w=["ones"])
        P.op("dve", lambda e: e.memset(onesb, 1.0), w=["onesb"])
        P.op("dve", lambda e: e.memset(epsT, EPS), w=["eps"])
        P.op("act", lambda e: e.activation(out=scond, in_=condT, func=AF.Silu), r=["condT"], w=["scond"])
        P.barrier()
        base_mark = A.mark()

        def pvc(name, j=0, parts=128):
            o, c = pvl.cols[name]
            return pvt[0:parts, o + j:o + j + 1]

        def phase_end():
            P.barrier()
            A.release(base_mark)

        def load_w(dst, src, key):
            P.op("pool", lambda e: e.dma_start(out=dst, in_=src), w=[key], dma="w:" + key)

        def rstd_from(psb, n, out_sb, psname):
            P.op("act", lambda e: e.activation(out=out_sb, in_=psb, func=AF.Sqrt, scale=1.0 / n, bias=epsT[0:out_sb.shape[0], 0:1]),
                 x=[psname], r=["eps"], w=[U("rs")])
            nm = P.ops
            P.op("dve", lambda e: e.reciprocal(out=out_sb, in_=out_sb), r=[], w=[])

        adab = A.alloc([2, 8, 1536], BF16)
        for l in range(depth):
            for pc in range(4):
                slot = (l * 4 + pc) % 2
                key = "adaw%d" % slot
                load_w(adab[:, slot], W["ada_w"][l].rearrange("(c p) n -> p c n", p=128)[:, :, pc * 1536:(pc + 1) * 1536], key)

                def mm(e, slot=slot, pc=pc):
                    last = None
                    for f in range(12):
                        for kc in range(8):
                            last = e.matmul(ps[0][:, (pc * 12 + f) * 2:(pc * 12 + f) * 2 + 2], lhsT=adab[:, slot, kc, f * 128:(f + 1) * 128],
                                            rhs=scond[:, kc, :], start=(kc == 0), stop=(kc == 7))
                    return last
                P.op("pe", mm, r=[key, "scond"], w=["ps0"])
            ao, _ = pvl.cols["ab%d" % l]
            for ci in range(2):
                P.op("dve", lambda e, l=l, ci=ci, ao=ao: e.tensor_tensor(out=mods[:, l, :, ci], in0=ps[0][:, ci:96:2], in1=pvt[:, ao:ao + 48], op=ALU.add),
                     x=["ps0"], r=["pvt"], w=["mods"])
            for s in range(2):
                go, _ = pvl.cols["ng%d%d" % (l, s)]
                for ci in range(2):
                    P.op("dve", lambda e, l=l, s=s, ci=ci, go=go: e.scalar_tensor_tensor(
                        out=gm[:, l, s, :, ci], in0=mods[:, l, (3 * s + 1) * 8:(3 * s + 2) * 8, ci], scalar=1.0,
                        in1=pvt[:, go:go + 8], op0=ALU.add, op1=ALU.mult), r=["mods", "pvt"], w=["gm"])
        phase_end()

        def mod_sh(l, s, dc, ci):
            return mods[:, l, (3 * s) * 8 + dc, ci:ci + 1]

        def mod_gate(l, s, dc, ci):
            return mods[:, l, (3 * s + 2) * 8 + dc, ci:ci + 1]

        def ci_of_tile(ti):
            return 0 if ti < 8 else 1

        def xT_tile_ap(ti):
            return xT_d.rearrange("(c p) t -> p c t", p=128)[:, :, ti * 512:(ti + 1) * 512]

        xin = [A.alloc([4, 1024], F32) for _ in range(2)]
        xTt = [A.alloc([8, 512], F32) for _ in range(2)]

        def x_rows(ti):
            if ti < 8:
                return xs_d[ti * 512:(ti + 1) * 512, :]
            return xp_d[(ti - 8) * 512:(ti - 7) * 512, :]

        def y_rows(ti):
            if ti < 8:
                return ys_d[ti * 512:(ti + 1) * 512, :]
            return yp_d[(ti - 8) * 512:(ti - 7) * 512, :]

        def ld_x(ti):
            P.op("sp", lambda e: e.dma_start(out=xin[ti % 2], in_=x_rows(ti).rearrange("(c p) d -> p c d", p=128)),
                 w=["xin%d" % (ti % 2)], dma="xin%d" % (ti % 2))
        ld_x(0)
        for ti in range(NT):
            if ti + 1 < NT:
                ld_x(ti + 1)
            s = ti % 2
            for dc in range(8):
                b = dc % 2

                def tr(e, s=s, dc=dc, b=b):
                    last = None
                    for tc in range(4):
                        last = e.transpose(ps[b][:, tc * 128:(tc + 1) * 128], xin[s][:, tc, dc * 128:(dc + 1) * 128], ident)
                    return last
                P.op("pe", tr, r=["xin%d" % s, "ident"], w=["ps%d" % b])
                eng = "act" if dc % 2 == 0 else "dve"
                if eng == "act":
                    P.op("act", lambda e, s=s, dc=dc, b=b: e.activation(out=xTt[s][:, dc, :], in_=ps[b], func=AF.Copy), x=["ps%d" % b], w=["xTt%d" % s])
                else:
                    P.op("dve", lambda e, s=s, dc=dc, b=b: e.tensor_copy(out=xTt[s][:, dc, :], in_=ps[b]), x=["ps%d" % b], w=["xTt%d" % s])
            P.op("sp", lambda e, s=s, ti=ti: e.dma_start(out=xT_tile_ap(ti), in_=xTt[s]), r=["xTt%d" % s], w=["d:xT"], dma="st:xTt%d" % s)
        phase_end()

        def prologue(l, s, hT):
            xt = [A.alloc([8, 512], F32) for _ in range(2)]
            sqs = [A.alloc([8, 512], BF16) for _ in range(2)]
            rss = [A.alloc([512], F32) for _ in range(2)]
            tmp = [A.alloc([512], F32) for _ in range(2)]

            def ld(ti):
                P.op("sp", lambda e: e.dma_start(out=xt[ti % 2], in_=xT_tile_ap(ti)), r=["d:xT"], w=["pxt%d" % (ti % 2)], dma="pxt%d" % (ti % 2))
            ld(0)
            for ti in range(NT):
                if ti + 1 < NT:
                    ld(ti + 1)
                b = ti % 2
                ci = ci_of_tile(ti)
                sq, rs = sqs[b], rss[b]
                pbk = 2 + b
                P.op("act", lambda e, b=b, sq=sq: e.activation(out=sq, in_=xt[b], func=AF.Square), r=["pxt%d" % b], w=["psq%d" % b])

                def mm(e, sq=sq, pbk=pbk):
                    last = None
                    for dc in range(8):
                        last = e.matmul(ps[pbk], lhsT=onesb, rhs=sq[:, dc, :], start=(dc == 0), stop=(dc == 7))
                    return last
                P.op("pe", mm, r=["psq%d" % b, "onesb"], w=["ps%d" % pbk])
                P.op("act", lambda e, rs=rs, pbk=pbk: e.activation(out=rs, in_=ps[pbk], func=AF.Sqrt, scale=1.0 / D, bias=epsT[:, 0:1]), x=["ps%d" % pbk], r=["eps"], w=["prs%d" % b])
                P.op("dve", lambda e, rs=rs: e.reciprocal(out=rs, in_=rs), r=["prs%d" % b], w=["prs%d" % b])
                for dc in range(8):
                    tb = dc % 2
                    P.op("dve", lambda e, b=b, dc=dc, tb=tb, rs=rs: e.tensor_tensor(out=tmp[tb], in0=xt[b][:, dc, :], in1=rs, op=ALU.mult),
                         r=["pxt%d" % b, "prs%d" % b], w=["ptmp%d" % tb])
                    P.op("act", lambda e, dc=dc, tb=tb, ti=ti, ci=ci: e.activation(
                        out=hT[:, dc, ti * 512:(ti + 1) * 512], in_=tmp[tb], func=AF.Identity,
                        scale=gm[:, l, s, dc, ci:ci + 1], bias=mod_sh(l, s, dc, ci)), r=["ptmp%d" % tb, "gm", "mods"], w=["hT"])

        def out_proj(l, s, wsrc, nk, in_tiles_fn):
            wt = A.alloc([nk, 1024], BF16)
            load_w(wt, wsrc.rearrange("(c p) n -> p c n", p=128), "opw")
            it = [A.alloc([nk, 512], BF16) for _ in range(2)]
            xt = [A.alloc([8, 512], F32) for _ in range(2)]

            def ld(ti):
                b = ti % 2
                in_tiles_fn(ti, it[b], "opin%d" % b)
                P.op("sp", lambda e: e.dma_start(out=xt[b], in_=xT_tile_ap(ti)), r=["d:xT"], w=["opx%d" % b], dma="opx%d" % b)
            ld(0)
            for ti in range(NT):
                if ti + 1 < NT:
                    ld(ti + 1)
                b = ti % 2
                ci = ci_of_tile(ti)
                for dc in range(8):
                    pb = 4 + dc % 4

                    def mm(e, b=b, dc=dc, pb=pb):
                        last = None
                        for kc in range(nk):
                            last = e.matmul(ps[pb], lhsT=wt[:, kc, dc * 128:(dc + 1) * 128], rhs=it[b][:, kc, :], start=(kc == 0), stop=(kc == nk - 1))
                        return last
                    P.op("pe", mm, r=["opw", "opin%d" % b], w=["ps%d" % pb])
                    P.op("dve", lambda e, b=b, dc=dc, pb=pb, ci=ci: e.scalar_tensor_tensor(
                        out=xt[b][:, dc, :], in0=ps[pb], scalar=mod_gate(l, s, dc, ci), in1=xt[b][:, dc, :], op0=ALU.mult, op1=ALU.add),
                        x=["ps%d" % pb], r=["mods"], w=["opx%d" % b])
                P.op("sp", lambda e, b=b, ti=ti: e.dma_start(out=xT_tile_ap(ti), in_=xt[b]), r=["opx%d" % b], w=["d:xT"], dma="st:opx%d" % b)

        def dwconv(raw, co, wname, bname, j, segs, rawname="raw"):
            P.op("dve", lambda e: e.tensor_scalar(out=co, in0=raw, scalar1=pvc(wname + "1", j), scalar2=pvc(bname, j), op0=ALU.mult, op1=ALU.add),
                 r=[rawname, "pvt"], w=["co"])
            for (s0, s1) in segs:
                P.op("dve", lambda e, s0=s0, s1=s1: e.scalar_tensor_tensor(out=co[:, s0 + 1:s1], in0=raw[:, s0:s1 - 1], scalar=pvc(wname + "0", j),
                                                                           in1=co[:, s0 + 1:s1], op0=ALU.mult, op1=ALU.add), r=[rawname, "pvt"], w=["co"])
                P.op("dve", lambda e, s0=s0, s1=s1: e.scalar_tensor_tensor(out=co[:, s0:s1 - 1], in0=raw[:, s0 + 1:s1], scalar=pvc(wname + "2", j),
                                                                           in1=co[:, s0:s1 - 1], op0=ALU.mult, op1=ALU.add), r=[rawname, "pvt"], w=["co"])

        def gemm_rows(wt, hT, raw, wkey, rawname="raw"):
            for ti in range(NT):
                pb = ti % 4

                def mm(e, ti=ti, pb=pb):
                    last = None
                    for kc in range(8):
                        last = e.matmul(ps[pb], lhsT=wt[:, kc, :], rhs=hT[:, kc, ti * 512:(ti + 1) * 512], start=(kc == 0), stop=(kc == 7))
                    return last
                P.op("pe", mm, r=[wkey, "hT"], w=["ps%d" % pb])
                P.op("act", lambda e, ti=ti, pb=pb: e.activation(out=raw[:, ti * 512:(ti + 1) * 512], in_=ps[pb], func=AF.Copy), x=["ps%d" % pb], w=[rawname])

        def ffn(l):
            hT = A.alloc([8, T], BF16)
            m0 = A.mark()
            prologue(l, 1, hT)
            P.barrier()
            A.release(m0)
            wt = [A.alloc([8, 128], BF16) for _ in range(4)]
            raws = [A.alloc([T], F32) for _ in range(2)]
            co = A.alloc([T], F32)
            sg = A.alloc([T], BF16)
            ab = [A.alloc([T], BF16) for _ in range(2)]
            wup = W["ffn_w_up"][l].rearrange("(c p) n -> p c n", p=128)
            NJ = DFF // 128

            def ldw(j):
                for h in range(2):
                    k = (j % 2) * 2 + h
                    load_w(wt[k], wup[:, :, h * DFF + j * 128:h * DFF + (j + 1) * 128], "fw%d" % k)

            def gemm_n(n):
                j, h = n // 2, n % 2
                if h == 0 and j + 1 < NJ:
                    ldw(j + 1)
                k = (j % 2) * 2 + h
                gemm_rows(wt[k], hT, raws[n % 2], "fw%d" % k, "raw%d" % (n % 2))

            def post_n(n):
                j, h = n // 2, n % 2
                dwconv(raws[n % 2], co, "fw%d" % l, "fb%d" % l, h * 22 + j, SEGS, "raw%d" % (n % 2))
                if h == 0:
                    P.op("act", lambda e: e.activation(out=sg, in_=co, func=AF.Silu), r=["co"], w=["sg"])
                else:
                    a = ab[j % 2]
                    P.op("dve", lambda e, a=a: e.tensor_tensor(out=a, in0=co, in1=sg, op=ALU.mult), r=["co", "sg"], w=["ab%d" % (j % 2)])
                    P.op("sp", lambda e, a=a, j=j: e.dma_start(out=act_d[j * 128:(j + 1) * 128, :], in_=a), r=["ab%d" % (j % 2)], w=["d:act"],
                         dma="st:ab%d" % (j % 2))
            ldw(0)
            gemm_n(0)
            for n in range(2 * NJ):
                if n + 1 < 2 * NJ:
                    gemm_n(n + 1)
                post_n(n)
            phase_end()

            def in_tiles(ti, dst, key):
                P.op("sp", lambda e: e.dma_start(out=dst, in_=act_d.rearrange("(c p) t -> p c t", p=128)[:, :, ti * 512:(ti + 1) * 512]),
                     r=["d:act"], w=[key], dma=key)
            out_proj(l, 1, W["ffn_w_down"][l], 22, in_tiles)
            phase_end()

        def gelu_from(src_ap, src_res, src_x, out_ap, out_res, n, graw, gt, tag):
            raw_, t_ = graw[:, 0:n], gt[:, 0:n]
            kr_, kt_ = "glraw" + tag, "glt" + tag
            P.op("act", lambda e: e.activation(out=raw_, in_=src_ap, func=AF.Copy), x=src_x, r=src_res, w=[kr_])
            P.op("dve", lambda e: e.tensor_tensor(out=t_, in0=raw_, in1=raw_, op=ALU.mult), r=[kr_], w=[kt_])
            P.op("dve", lambda e: e.tensor_scalar(out=t_, in0=t_, scalar1=0.044715, scalar2=1.0, op0=ALU.mult, op1=ALU.add), r=[kt_], w=[kt_])
            P.op("dve", lambda e: e.tensor_tensor(out=t_, in0=t_, in1=raw_, op=ALU.mult), r=[kt_, kr_], w=[kt_])
            P.op("act", lambda e: e.activation(out=t_, in_=t_, func=AF.Sigmoid, scale=2.0 * math.sqrt(2.0 / math.pi)), r=[kt_], w=[kt_])
            P.op("dve", lambda e: e.tensor_tensor(out=out_ap, in0=t_, in1=raw_, op=ALU.mult), r=[kt_, kr_], w=out_res)


        def even_mixer(l):
            i = l // 2
            hT = A.alloc([8, T], BF16)
            m0 = A.mark()
            prologue(l, 0, hT)
            P.barrier()
            A.release(m0)
            wt = [A.alloc([8, 128], BF16) for _ in range(2)]
            raws = [A.alloc([T], F32) for _ in range(2)]
            co = A.alloc([T], F32)
            ob = [A.alloc([T], BF16) for _ in range(2)]
            vtm = [A.alloc([4, 128], BF16) for _ in range(2)]
            win = W["mix_w_in"][i].rearrange("(c p) n -> p c n", p=128)
            cols = [c * 128 for c in range(4)] + [1024 + c * 128 for c in range(12)]

            def gemm_q(q):
                load_w(wt[q % 2], win[:, :, cols[q]:cols[q] + 128], "mw%d" % (q % 2))
                gemm_rows(wt[q % 2], hT, raws[q % 2], "mw%d" % (q % 2), "raw%d" % (q % 2))

            def post_q(q):
                raw = raws[q % 2]
                rn = "raw%d" % (q % 2)
                o_ = ob[q % 2]
                okey = "ob%d" % (q % 2)
                if q < 4:
                    P.op("dve", lambda e: e.tensor_tensor(out=co, in0=raw, in1=raw, op=ALU.mult), r=[rn], w=["co"])
                    P.op("dve", lambda e: e.tensor_scalar(out=co, in0=co, scalar1=0.044715, scalar2=1.0, op0=ALU.mult, op1=ALU.add), r=["co"], w=["co"])
                    P.op("dve", lambda e: e.tensor_tensor(out=co, in0=co, in1=raw, op=ALU.mult), r=["co", rn], w=["co"])
                    P.op("act", lambda e: e.activation(out=co, in_=co, func=AF.Sigmoid, scale=2.0 * math.sqrt(2.0 / math.pi)), r=["co"], w=["co"])
                    P.op("dve", lambda e: e.tensor_tensor(out=o_, in0=co, in1=raw, op=ALU.mult), r=["co", rn], w=[okey])
                else:
                    dwconv(raw, co, "hw%d" % i, "hb%d" % i, q - 4, SEGS, rn)
                    P.op("act", lambda e: e.activation(out=o_, in_=co, func=AF.Copy), r=["co"], w=[okey])
                    if q < 8:
                        for tc in range(T // 128):
                            pb = 4 + tc % 2
                            P.op("pe", lambda e, tc=tc, pb=pb: e.transpose(ps[pb][:, 0:128], co[:, tc * 128:(tc + 1) * 128], ident), r=["co", "ident"], w=["ps%d" % pb])
                            vb_ = vtm[tc % 2]
                            P.op("dve", lambda e, pb=pb, vb_=vb_: e.tensor_copy(out=vb_[:, 0, :], in_=ps[pb][:, 0:128]), x=["ps%d" % pb], w=["vtm%d" % (tc % 2)])
                            P.op("sp", lambda e, tc=tc, vb_=vb_: e.dma_start(out=vbtm_d[tc * 128:(tc + 1) * 128, (q - 4) * 128:(q - 3) * 128], in_=vb_[:, 0, :]),
                                 r=["vtm%d" % (tc % 2)], w=["d:vbtm"], dma="st:vtm%d" % (tc % 2))
                P.op("sp", lambda e: e.dma_start(out=pT_d[q * 128:(q + 1) * 128, :], in_=o_), r=[okey], w=["d:pT"], dma="st:" + okey)
            gemm_q(0)
            for q in range(16):
                if q + 1 < 16:
                    gemm_q(q + 1)
                post_q(q)
            P.barrier()
            A.release(m0)
            wv = A.alloc([8, 512], BF16)
            load_w(wv, win[:, :, 512:1024], "wv")
            sgw = A.alloc([4, 128], BF16)
            load_w(sgw, sguT_d[i], "sgw")
            sgb = A.alloc([512], BF16, parts=1)
            load_w(sgb, sgub_d[i], "sgb")
            glr = [A.alloc([512], F32) for _ in range(2)]
            glt = [A.alloc([512], F32) for _ in range(2)]
            vbs = [A.alloc([512], BF16) for _ in range(2)]
            ut = [A.alloc([4, 512], BF16) for _ in range(2)]
            at = [A.alloc([4, 512], BF16) for _ in range(2)]

            def ldu(ti):
                P.op("sp", lambda e: e.dma_start(out=ut[ti % 2], in_=pT_d[0:512, :].rearrange("(c p) t -> p c t", p=128)[:, :, ti * 512:(ti + 1) * 512]),
                     r=["d:pT"], w=["ut%d" % (ti % 2)], dma="ut%d" % (ti % 2))
            ldu(0)

            def sgu_mm(n):
                t0 = n * 128
                pa = 2 * (n % 2)

                def mm(e):
                    last = None
                    for kc in range(8):
                        last = e.matmul(ps[pa], lhsT=hT[:, kc, t0:t0 + 128], rhs=wv[:, kc, :], start=(kc == 0), stop=(kc == 7))
                    return last
                P.op("pe", mm, r=["hT", "wv"], w=["ps%d" % pa])

            def sgu_post(n):
                ti, tc = divmod(n, 4)
                b = ti % 2
                pp = n % 2
                pa, pbb = 2 * pp, 2 * pp + 1
                vb16 = vbs[pp]
                vk = "vb16_%d" % pp
                gelu_from(ps[pa], [], ["ps%d" % pa], vb16, [vk], 512, glr[pp], glt[pp], str(pp))

                def mm2(e):
                    last = None
                    for g in range(4):
                        e.matmul(ps[pbb][:, g * 128:(g + 1) * 128], lhsT=vb16[:, g * 128:(g + 1) * 128], rhs=sgw[:, g, :], start=True, stop=False)
                        last = e.matmul(ps[pbb][:, g * 128:(g + 1) * 128], lhsT=onesb[0:1, :], rhs=sgb[0:1, g * 128:(g + 1) * 128], start=False, stop=True)
                    return last
                P.op("pe", mm2, r=[vk, "sgw", "sgb", "onesb"], w=["ps%d" % pbb])
                P.op("dve", lambda e: e.tensor_tensor(out=at[b][:, :, tc * 128:(tc + 1) * 128], in0=ut[b][:, :, tc * 128:(tc + 1) * 128],
                                                      in1=ps[pbb].rearrange("p (g q) -> p g q", g=4), op=ALU.mult),
                     x=["ps%d" % pbb], r=["ut%d" % b], w=["at%d" % b])
                if tc == 3:
                    P.op("sp", lambda e: e.dma_start(out=aT_d.rearrange("(c p) t -> p c t", p=128)[:, :, ti * 512:(ti + 1) * 512], in_=at[b]),
                         r=["at%d" % b], w=["d:aT"], dma="st:at%d" % b)
            NCHK = T // 128
            sgu_mm(0)
            for n in range(NCHK):
                if n % 4 == 0 and n // 4 + 1 < NT:
                    ldu(n // 4 + 1)
                if n + 1 < NCHK:
                    sgu_mm(n + 1)
                sgu_post(n)
            phase_end()
            for L in (4096, 256):
                hyena_filters(i, L)
                phase_end()
            hyena_conv_eo(i, 0)
            phase_end()
            for sq_ in range(4):
                hyena_conv(i, 256, 4096 + 256 * sq_, tg="_%d" % (sq_ % 2))
            phase_end()

            def in_tiles(ti, dst, key):
                P.op("sp", lambda e: e.dma_start(out=dst[:, 0:4, :], in_=aT_d.rearrange("(c p) t -> p c t", p=128)[:, :, ti * 512:(ti + 1) * 512]),
                     r=["d:aT"], w=[key], dma=key)
                P.op("sp", lambda e: e.dma_start(out=dst[:, 4:8, :], in_=zT_d.rearrange("(c p) t -> p c t", p=128)[:, :, ti * 512:(ti + 1) * 512]),
                     r=["d:zT"], w=[key], dma=key)
            out_proj(l, 0, W["mix_w_out"][i], 8, in_tiles)
            phase_end()

        def sin_rr(arg, out_ap, n, res_in, res_out):
            a_, b_ = sr_a[0:64, 0:n], sr_b[0:64, 0:n]
            P.op("act", lambda e: e.activation(out=a_, in_=arg, func=AF.Sin, scale=0.5), r=res_in, w=["sra"])
            P.op("act", lambda e: e.activation(out=b_, in_=arg, func=AF.Sin, scale=0.25), r=res_in, w=["srb"])
            P.op("dve", lambda e: e.tensor_tensor(out=b_, in0=b_, in1=b_, op=ALU.mult), r=["srb"], w=["srb"])
            P.op("dve", lambda e: e.tensor_scalar(out=b_, in0=b_, scalar1=-4.0, scalar2=2.0, op0=ALU.mult, op1=ALU.add), r=["srb"], w=["srb"])
            P.op("dve", lambda e: e.tensor_tensor(out=out_ap, in0=a_, in1=b_, op=ALU.mult), r=["sra", "srb"], w=res_out)

        sr_a = sr_b = None

        def hyena_filters(i, L):
            nonlocal sr_a, sr_b
            nch = L // 128
            h2 = A.alloc([L], F32)
            hf_tm = A.alloc([nch, 1024], BF16)
            w3 = A.alloc([1024], F32)
            dec = A.alloc([1024], F32)
            nd = A.alloc([nch], F32)
            m1 = A.mark()
            zf = A.alloc([L], F32)
            w1 = A.alloc([64], F32)
            w2 = A.alloc([64], F32)
            frb = A.alloc([2], F32)
            h1 = A.alloc([L], F32)
            arg = A.alloc([512], F32)
            sr_a = A.alloc([512], F32)
            sr_b = A.alloc([512], F32)
            P.op("sp", lambda e: e.dma_start(out=zf[0:33, :], in_=cd["zf%d" % L]), w=["zf"], dma="hfl")
            P.op("sp", lambda e: e.dma_start(out=w1[0:33, :], in_=W["hy_f_w1"][i]), w=["w1"], dma="hfl")
            P.op("sp", lambda e: e.dma_start(out=w2[0:64, :], in_=W["hy_f_w2"][i]), w=["w2"], dma="hfl")
            P.op("sp", lambda e: e.dma_start(out=w3[0:64, :], in_=W["hy_f_w3"][i]), w=["w3"], dma="hfl")
            P.op("sp", lambda e: e.dma_start(out=dec, in_=dec_d[i]), w=["dec"], dma="hfl")
            P.op("sp", lambda e: e.dma_start(out=nd, in_=cd["ndE" if L == 4096 else "nd%d" % L]), w=["nd"], dma="hfl")
            P.op("act", lambda e: e.activation(out=dec, in_=dec, func=AF.Abs), r=["dec"], w=["dec"])
            for q, bn in enumerate(("b1", "b2")):
                P.op("dve", lambda e, q=q, bn=bn: e.tensor_tensor(out=frb[0:64, q:q + 1], in0=pvc("fr%d" % i, 0, 64), in1=pvc("%s%d" % (bn, i), 0, 64), op=ALU.mult),
                     r=["pvt"], w=["frb"])
            tw = min(512, L)
            for (wmat, kk, src, dst, q) in ((w1, 33, zf, h1, 0), (w2, 64, h1, h2, 1)):
                for tt in range(L // tw):
                    P.op("pe", lambda e, wmat=wmat, kk=kk, src=src, tt=tt: e.matmul(ps[0][0:64, 0:tw], lhsT=wmat[0:kk, :], rhs=src[0:kk, tt * tw:(tt + 1) * tw], start=True, stop=True),
                         r=["w1", "w2", "zf", "h1"], w=["ps0"])
                    P.op("dve", lambda e, q=q: e.tensor_scalar(out=arg[0:64, 0:tw], in0=ps[0][0:64, 0:tw], scalar1=pvc("fr%d" % i, 0, 64), scalar2=frb[0:64, q:q + 1],
                                                               op0=ALU.mult, op1=ALU.add), x=["ps0"], r=["pvt", "frb"], w=["arg"])
                    sin_rr(arg[0:64, 0:tw], dst[0:64, tt * tw:(tt + 1) * tw], tw, ["arg"], ["h1" if q == 0 else "h2"])
            P.barrier()
            A.release(m1)
            win_ = A.alloc([1024], F32)
            hwf = A.alloc([1024], F32)
            hab = A.alloc([1024], F32)
            rec = A.alloc([1024], F32)
            EO = (L == 4096)
            for sc in range(nch):
                if EO:
                    par_, mc_ = divmod(sc, 16)
                    h2s = h2[0:64, 256 * mc_ + par_:256 * mc_ + par_ + 255:2]
                else:
                    h2s = h2[0:64, sc * 128:(sc + 1) * 128]

                def mm(e, h2s=h2s):
                    e.matmul(ps[1], lhsT=h2s, rhs=w3[0:64, 0:512], start=True, stop=True)
                    return e.matmul(ps[2], lhsT=h2s, rhs=w3[0:64, 512:1024], start=True, stop=True)
                P.op("pe", mm, r=["h2", "w3"], w=["ps1", "ps2"])
                P.op("act", lambda e, sc=sc: e.activation(out=win_, in_=dec, func=AF.Exp, scale=nd[:, sc:sc + 1]), r=["dec", "nd"], w=["win"])
                for hh in range(2):
                    P.op("dve", lambda e, hh=hh: e.tensor_tensor(out=hwf[:, hh * 512:(hh + 1) * 512], in0=ps[1 + hh], in1=win_[:, hh * 512:(hh + 1) * 512], op=ALU.mult),
                         x=["ps%d" % (1 + hh)], r=["win"], w=["hwf"])
                P.op("act", lambda e: e.activation(out=hab, in_=hwf, func=AF.Abs), r=["hwf"], w=["hab"])
                P.op("dve", lambda e, sc=sc: e.tensor_copy(out=hf_tm[:, sc, :], in_=hwf), r=["hwf"], w=["hftm"])

                def mm3(e, sc=sc):
                    e.matmul(ps[3], lhsT=ones, rhs=hab[:, 0:512], start=(sc == 0), stop=(sc == nch - 1))
                    return e.matmul(ps[4], lhsT=ones, rhs=hab[:, 512:1024], start=(sc == 0), stop=(sc == nch - 1))
                P.op("pe", mm3, r=["hab", "ones"], w=["ps3", "ps4"])
            for hh in range(2):
                P.op("dve", lambda e, hh=hh: e.tensor_scalar(out=rec[:, hh * 512:(hh + 1) * 512], in0=ps[3 + hh], scalar1=EPS, scalar2=None, op0=ALU.add),
                     x=["ps%d" % (3 + hh)], w=["rec"])
            P.op("dve", lambda e: e.reciprocal(out=rec, in_=rec), r=["rec"], w=["rec"])
            if EO:
                ft = [A.alloc([2, 16, 2, 128], BF16) for _ in range(2)]
                hfo = [A.alloc([4, 1024], F32) for _ in range(2)]
                osb = [A.alloc([512], F32) for _ in range(2)]
                tf = A.alloc([512], F32)

                def ldfe(kc):
                    P.op("sp", lambda e: e.dma_start(out=ft[kc % 2], in_=cd["FE"][kc]), w=["ft%d" % (kc % 2)], dma="ft%d" % (kc % 2))
                ldfe(0)
                for kc in range(16):
                    if kc + 1 < 16:
                        ldfe(kc + 1)
                    b = kc % 2
                    for hh in range(2):
                        hs = slice(hh * 512, (hh + 1) * 512)

                        def mm(e, b=b, hs=hs):
                            last = None
                            for bank, (par, cs) in enumerate(((0, 0), (1, 0), (0, 1), (1, 1))):
                                for mc in range(16):
                                    last = e.matmul(ps[bank], lhsT=ft[b][:, par, mc, cs, :], rhs=hf_tm[:, par * 16 + mc, hs], start=(mc == 0), stop=(mc == 15))
                            return last
                        P.op("pe", mm, r=["ft%d" % b, "hftm"], w=["ps0", "ps1", "ps2", "ps3"])
                        P.op("act", lambda e: e.activation(out=osb[0], in_=ps[1], func=AF.Copy), x=["ps1"], w=["osb0"])
                        P.op("act", lambda e: e.activation(out=osb[1], in_=ps[3], func=AF.Copy), x=["ps3"], w=["osb1"])
                        for slot, (pe_, ob_, op_, rev) in enumerate(((0, 0, ALU.add, False), (2, 1, ALU.add, False), (0, 0, ALU.subtract, False), (2, 1, ALU.subtract, True))):
                            if rev:
                                P.op("dve", lambda e, pe_=pe_, ob_=ob_: e.tensor_tensor(out=tf, in0=osb[ob_], in1=ps[pe_], op=ALU.subtract), x=["ps%d" % pe_], r=["osb%d" % ob_], w=["tf"])
                            else:
                                P.op("dve", lambda e, pe_=pe_, ob_=ob_, op_=op_: e.tensor_tensor(out=tf, in0=ps[pe_], in1=osb[ob_], op=op_), x=["ps%d" % pe_], r=["osb%d" % ob_], w=["tf"])
                            P.op("dve", lambda e, b=b, slot=slot, hs=hs: e.tensor_tensor(out=hfo[b][:, slot, hs], in0=tf, in1=rec[:, hs], op=ALU.mult), r=["tf", "rec"], w=["hfo%d" % b])
                    P.op("sp", lambda e, b=b, kc=kc: e.dma_start(out=hf_d[L][kc * 128:(kc + 1) * 128], in_=hfo[b]), r=["hfo%d" % b], w=["d:hf"], dma="st:hfo%d" % b)
                return
            ft = [A.alloc([nch, 2, 128], BF16) for _ in range(2)]
            hfo = [A.alloc([2, 1024], F32) for _ in range(2)]

            def ldf(kc):
                P.op("sp", lambda e: e.dma_start(out=ft[kc % 2], in_=cd["F%d" % L][kc]), w=["ft%d" % (kc % 2)], dma="ft%d" % (kc % 2))
            ldf(0)
            for kc in range(nch):
                if kc + 1 < nch:
                    ldf(kc + 1)
                b = kc % 2
                for cs in range(2):
                    for hh in range(2):
                        pb = 5 + (cs * 2 + hh) % 3

                        def mm(e, b=b, cs=cs, hh=hh, pb=pb):
                            last = None
                            for sc in range(nch):
                                last = e.matmul(ps[pb], lhsT=ft[b][:, sc, cs, :], rhs=hf_tm[:, sc, hh * 512:(hh + 1) * 512], start=(sc == 0), stop=(sc == nch - 1))
                            return last
                        P.op("pe", mm, r=["ft%d" % b, "hftm"], w=["ps%d" % pb])
                        P.op("dve", lambda e, b=b, cs=cs, hh=hh, pb=pb: e.tensor_tensor(out=hfo[b][:, cs, hh * 512:(hh + 1) * 512], in0=ps[pb],
                                                                                      in1=rec[:, hh * 512:(hh + 1) * 512], op=ALU.mult),
                             x=["ps%d" % pb], r=["rec"], w=["hfo%d" % b])
                P.op("sp", lambda e, b=b, kc=kc: e.dma_start(out=hf_d[L][kc * 128:(kc + 1) * 128], in_=hfo[b]), r=["hfo%d" % b], w=["d:hf"], dma="st:hfo%d" % b)

        def hyena_conv_eo(i, s0):
            L = 4096
            intm = A.alloc([2, 16, 512], BF16)
            Ypq = A.alloc([16, 4, 512], BF16)
            fg = [A.alloc([8192], BF16) for _ in range(2)]
            ftv = [f.rearrange("p (a m c k) -> p a m c k", a=2, m=16, c=2, k=128) for f in fg]
            gtv = [f.rearrange("p (l c t) -> p l c t", l=8, c=2, t=512) for f in fg]
            hft = [A.alloc([4, 512], F32) for _ in range(2)]
            TA, TB, T1, T2, T3, T4, T5, T6 = [A.alloc([512], F32) for _ in range(8)]
            dti = [A.alloc([1024], BF16) for _ in range(2)]
            xti = [A.alloc([1024], BF16) for _ in range(2)]
            zo = A.alloc([4, 1024], BF16)
            zf32 = A.alloc([512], F32)
            vsrc = vbtm_d[s0:s0 + L, :].rearrange("(mc p two) n -> two p mc n", p=128, two=2)
            for par in range(2):
                P.op("sp", lambda e, par=par: e.dma_start(out=intm[:, par], in_=vsrc[par]), r=["d:vbtm"], w=["intm"], dma="intm")

            def tt_(out, a, b, op, xa=(), xb=(), ra=(), rb=(), wname=None):
                P.op("dve", lambda e: e.tensor_tensor(out=out, in0=a, in1=b, op=op), x=list(xa) + list(xb), r=list(ra) + list(rb), w=[wname])

            def run_order(o):
                dsrc = pT_d[512:1024, :] if o == 0 else z1T_d
                xsrc = pT_d[1024 + 512 * o:1536 + 512 * o, :]
                zdst = z1T_d if o == 0 else zT_d

                def ldf(kc):
                    b = kc % 2
                    P.op("sp", lambda e: e.dma_start(out=fg[b], in_=cd["FE"][kc].rearrange("p a m c k -> p (a m c k)")), w=["fg%d" % b], dma="fg%d" % b)
                    P.op("sp", lambda e: e.dma_start(out=hft[b], in_=hf_d[L][kc * 128:(kc + 1) * 128, :, o * 512:(o + 1) * 512]), r=["d:hf"], w=["hft%d" % b], dma="hft%d" % b)
                ldf(0)
                for kc in range(16):
                    if kc + 1 < 16:
                        ldf(kc + 1)
                    b = kc % 2

                    bk = 4 * (kc % 2)

                    def mm(e, b=b, bk=bk):
                        last = None
                        for bank, (par, cs) in enumerate(((0, 0), (1, 0), (0, 1), (1, 1))):
                            for mc in range(16):
                                last = e.matmul(ps[bk + bank], lhsT=ftv[b][:, par, mc, cs, :], rhs=intm[:, par, mc, :], start=(mc == 0), stop=(mc == 15))
                        return last
                    P.op("pe", mm, r=["fg%d" % b, "intm"], w=["ps%d" % (bk + q_) for q_ in range(4)])
                    pE0, pE1, pE2, pE3 = (ps[bk + q_] for q_ in range(4))
                    nE = ["ps%d" % (bk + q_) for q_ in range(4)]
                    hk = "hft%d" % b
                    Hc, Hs, Hc2, Hs2 = (hft[b][:, q_, :] for q_ in range(4))
                    P.op("act", lambda e, pE1=pE1: e.activation(out=TA, in_=pE1, func=AF.Copy), x=[nE[1]], w=["TA"])
                    P.op("act", lambda e, pE3=pE3: e.activation(out=TB, in_=pE3, func=AF.Copy), x=[nE[3]], w=["TB"])
                    tt_(T1, pE0, TA, ALU.add, xa=[nE[0]], rb=["TA"], wname="T1")
                    tt_(T2, pE0, TA, ALU.subtract, xa=[nE[0]], rb=["TA"], wname="T2")
                    tt_(T3, pE2, TB, ALU.add, xa=[nE[2]], rb=["TB"], wname="T3")
                    tt_(T4, TB, pE2, ALU.subtract, xb=[nE[2]], ra=["TB"], wname="T4")
                    tt_(TA, T1, Hc, ALU.mult, ra=["T1"], rb=[hk], wname="TA")
                    tt_(TB, T3, Hs, ALU.mult, ra=["T3"], rb=[hk], wname="TB")
                    tt_(T5, TA, TB, ALU.subtract, ra=["TA"], rb=["TB"], wname="T5")
                    tt_(TA, T1, Hs, ALU.mult, ra=["T1"], rb=[hk], wname="TA")
                    tt_(TB, T3, Hc, ALU.mult, ra=["T3"], rb=[hk], wname="TB")
                    tt_(T6, TA, TB, ALU.add, ra=["TA"], rb=["TB"], wname="T6")
                    tt_(TA, T2, Hc2, ALU.mult, ra=["T2"], rb=[hk], wname="TA")
                    tt_(TB, T4, Hs2, ALU.mult, ra=["T4"], rb=[hk], wname="TB")
                    tt_(T1, TA, TB, ALU.subtract, ra=["TA"], rb=["TB"], wname="T1")
                    tt_(TA, T2, Hs2, ALU.mult, ra=["T2"], rb=[hk], wname="TA")
                    tt_(TB, T4, Hc2, ALU.mult, ra=["T4"], rb=[hk], wname="TB")
                    tt_(T3, TA, TB, ALU.add, ra=["TA"], rb=["TB"], wname="T3")
                    tt_(Ypq[:, kc, 0, :], T5, T1, ALU.add, ra=["T5"], rb=["T1"], wname="Y")
                    tt_(Ypq[:, kc, 1, :], T6, T3, ALU.subtract, ra=["T6"], rb=["T3"], wname="Y")
                    tt_(Ypq[:, kc, 2, :], T5, T1, ALU.subtract, ra=["T5"], rb=["T1"], wname="Y")
                    tt_(Ypq[:, kc, 3, :], T6, T3, ALU.add, ra=["T6"], rb=["T3"], wname="Y")
                seq = [(tt, par, kg) for tt in range(4) for par in range(2) for kg in range(2)]

                def ldg(q):
                    tt, par, kg = seq[q]
                    P.op("sp", lambda e: e.dma_start(out=fg[q % 2], in_=cd["GE"][tt, par, kg].rearrange("p l c t -> p (l c t)")), w=["fg%d" % (q % 2)], dma="fg%d" % (q % 2))
                ldg(0)
                for q, (tt, par, kg) in enumerate(seq):
                    if q + 1 < len(seq):
                        ldg(q + 1)
                    b = q % 2

                    def mm(e, b=b, kg=kg, par=par):
                        last = None
                        for klc in range(8):
                            kc = kg * 8 + klc
                            for cs in range(2):
                                for cch in range(4):
                                    last = e.matmul(ps[4 + cch], lhsT=Ypq[:, kc, 2 * par + cs, cch * 128:(cch + 1) * 128], rhs=gtv[b][:, klc, cs, :],
                                                    start=(kc == 0 and cs == 0), stop=(kc == 15 and cs == 1))
                        return last
                    P.op("pe", mm, r=["Y", "fg%d" % b], w=["ps4", "ps5", "ps6", "ps7"])
                    if kg == 1:
                        t0 = s0 + tt * 1024
                        for cch in range(4):
                            bb = cch % 2
                            P.op("sp", lambda e, bb=bb, cch=cch, t0=t0: e.dma_start(out=dti[bb], in_=dsrc[cch * 128:(cch + 1) * 128, t0:t0 + 1024]),
                                 r=["d:pT", "d:z1T"], w=["dti%d" % bb], dma="dti%d" % bb)
                            P.op("sp", lambda e, bb=bb, cch=cch, t0=t0: e.dma_start(out=xti[bb], in_=xsrc[cch * 128:(cch + 1) * 128, t0:t0 + 1024]),
                                 r=["d:pT"], w=["xti%d" % bb], dma="xti%d" % bb)
                            P.op("dve", lambda e, bb=bb, cch=cch, par=par: e.scalar_tensor_tensor(out=zf32, in0=dti[bb][:, par:1024:2], scalar=pvc("hd%d%d" % (i, o), cch), in1=ps[4 + cch],
                                                                                               op0=ALU.mult, op1=ALU.add), x=["ps%d" % (4 + cch)], r=["dti%d" % bb, "pvt"], w=["zf32"])
                            P.op("dve", lambda e, bb=bb, par=par: e.tensor_tensor(out=zf32, in0=zf32, in1=xti[bb][:, par:1024:2], op=ALU.mult), r=["zf32", "xti%d" % bb], w=["zf32"])
                            P.op("act", lambda e, cch=cch, par=par: e.activation(out=zo[:, cch, par:1024:2], in_=zf32, func=AF.Copy), r=["zf32"], w=["zo%d" % cch])
                            if par == 1:
                                P.op("sp", lambda e, cch=cch, t0=t0: e.dma_start(out=zdst[cch * 128:(cch + 1) * 128, t0:t0 + 1024], in_=zo[:, cch, :]),
                                     r=["zo%d" % cch], w=["d:z1T" if o == 0 else "d:zT"], dma="st:zo%d" % cch)
                            if o == 0:
                                for blk in range(4):
                                    P.op("pe", lambda e, blk=blk: e.transpose(ps[0][:, blk * 128:(blk + 1) * 128], zf32[:, blk * 128:(blk + 1) * 128], ident), r=["zf32", "ident"], w=["ps0"])
                                for blk in range(4):
                                    P.op("act", lambda e, blk=blk, cch=cch, tt=tt, par=par: e.activation(out=intm[:, par, tt * 4 + blk, cch * 128:(cch + 1) * 128],
                                                                                                     in_=ps[0][:, blk * 128:(blk + 1) * 128], func=AF.Copy), x=["ps0"], w=["intm"])
            run_order(0)
            run_order(1)


        def hyena_conv(i, L, s0, tg=""):
            keep = ("ps", "d:pT", "d:hf", "d:vbtm", "pvt", "ident")

            def rn(n):
                return n if n.startswith(keep) else n + tg

            def op(eng, fn, r=(), w=(), dma=None, x=()):
                return P.op(eng, fn, r=[rn(n) for n in r], w=[rn(n) for n in w], dma=(None if dma is None else dma + tg), x=list(x))
            nch = L // 128
            tw = min(512, L)
            ntt = L // tw
            kgn = max(1, nch // 8)
            kl = nch // kgn
            intm = A.alloc([nch, 512], BF16)
            Y = A.alloc([nch, 2, 512], BF16)
            ft = [A.alloc([nch, 2, 128], BF16) for _ in range(2)]
            gt = [A.alloc([kl, 2, tw], BF16) for _ in range(2)]
            hft = [A.alloc([2, 512], F32) for _ in range(2)]
            t1 = A.alloc([512], F32)
            t2 = A.alloc([512], F32)
            dti = [A.alloc([tw], BF16) for _ in range(2)]
            xti = [A.alloc([tw], BF16) for _ in range(2)]
            zf32 = A.alloc([tw], F32)
            zo = [A.alloc([tw], BF16) for _ in range(2)]
            op("sp", lambda e: e.dma_start(out=intm, in_=vbtm_d[s0:s0 + L, :].rearrange("(c p) n -> p c n", p=128)), r=["d:vbtm"], w=["intm"], dma="intm")
            def run_order(o):
                dsrc = pT_d[512:1024, :] if o == 0 else z1T_d
                xsrc = pT_d[1024 + 512 * o:1536 + 512 * o, :]
                zdst = z1T_d if o == 0 else zT_d

                def ldf(kc):
                    b = kc % 2
                    op("sp", lambda e: e.dma_start(out=ft[b], in_=cd["F%d" % L][kc]), w=["cft%d" % b], dma="cft%d" % b)
                    op("sp", lambda e: e.dma_start(out=hft[b], in_=hf_d[L][kc * 128:(kc + 1) * 128, :, o * 512:(o + 1) * 512]), r=["d:hf"], w=["hft%d" % b], dma="hft%d" % b)
                ldf(0)
                for kc in range(nch):
                    if kc + 1 < nch:
                        ldf(kc + 1)
                    b = kc % 2

                    def mm(e, b=b):
                        last = None
                        for cs in range(2):
                            for sc in range(nch):
                                last = e.matmul(ps[cs], lhsT=ft[b][:, sc, cs, :], rhs=intm[:, sc, :], start=(sc == 0), stop=(sc == nch - 1))
                        return last
                    op("pe", mm, r=["cft%d" % b, "intm"], w=["ps0", "ps1"])
                    Hc, Hs = hft[b][:, 0, :], hft[b][:, 1, :]
                    hk = "hft%d" % b
                    op("dve", lambda e, Hc=Hc: e.tensor_tensor(out=t1, in0=ps[0], in1=Hc, op=ALU.mult), x=["ps0"], r=[hk], w=["t1"])
                    op("dve", lambda e, Hs=Hs: e.tensor_tensor(out=t2, in0=ps[1], in1=Hs, op=ALU.mult), x=["ps1"], r=[hk], w=["t2"])
                    op("dve", lambda e, kc=kc: e.tensor_tensor(out=Y[:, kc, 0, :], in0=t1, in1=t2, op=ALU.subtract), r=["t1", "t2"], w=["Y"])
                    op("dve", lambda e, Hs=Hs: e.tensor_tensor(out=t1, in0=ps[0], in1=Hs, op=ALU.mult), x=["ps0"], r=[hk], w=["t1"])
                    op("dve", lambda e, Hc=Hc: e.tensor_tensor(out=t2, in0=ps[1], in1=Hc, op=ALU.mult), x=["ps1"], r=[hk], w=["t2"])
                    op("dve", lambda e, kc=kc: e.tensor_tensor(out=Y[:, kc, 1, :], in0=t1, in1=t2, op=ALU.add), r=["t1", "t2"], w=["Y"])
                seq = [(tt, kg) for tt in range(ntt) for kg in range(kgn)]

                def ldg(q):
                    tt, kg = seq[q]
                    op("sp", lambda e: e.dma_start(out=gt[q % 2], in_=cd["G%d" % L][tt, kg]), w=["gt%d" % (q % 2)], dma="gt%d" % (q % 2))
                ldg(0)
                for q, (tt, kg) in enumerate(seq):
                    if q + 1 < len(seq):
                        ldg(q + 1)
                    b = q % 2

                    def mm(e, b=b, kg=kg):
                        last = None
                        for klc in range(kl):
                            kc = kg * kl + klc
                            for cs in range(2):
                                for cch in range(4):
                                    last = e.matmul(ps[2 + cch][:, 0:tw], lhsT=Y[:, kc, cs, cch * 128:(cch + 1) * 128], rhs=gt[b][:, klc, cs, :],
                                                    start=(kc == 0 and cs == 0), stop=(kc == nch - 1 and cs == 1))
                        return last
                    op("pe", mm, r=["Y", "gt%d" % b], w=["ps2", "ps3", "ps4", "ps5"])
                    if kg == kgn - 1:
                        t0 = s0 + tt * tw
                        for cch in range(4):
                            bb = cch % 2
                            op("sp", lambda e, bb=bb, cch=cch, t0=t0: e.dma_start(out=dti[bb], in_=dsrc[cch * 128:(cch + 1) * 128, t0:t0 + tw]),
                                 r=["d:pT", "d:z1T"], w=["dti%d" % bb], dma="dti%d" % bb)
                            op("sp", lambda e, bb=bb, cch=cch, t0=t0: e.dma_start(out=xti[bb], in_=xsrc[cch * 128:(cch + 1) * 128, t0:t0 + tw]),
                                 r=["d:pT"], w=["xti%d" % bb], dma="xti%d" % bb)
                            op("dve", lambda e, bb=bb, cch=cch: e.scalar_tensor_tensor(out=zf32, in0=dti[bb], scalar=pvc("hd%d%d" % (i, o), cch), in1=ps[2 + cch][:, 0:tw],
                                                                                        op0=ALU.mult, op1=ALU.add), x=["ps%d" % (2 + cch)], r=["dti%d" % bb, "pvt"], w=["zf32"])
                            op("dve", lambda e, bb=bb: e.tensor_tensor(out=zf32, in0=zf32, in1=xti[bb], op=ALU.mult), r=["zf32", "xti%d" % bb], w=["zf32"])
                            op("act", lambda e, bb=bb: e.activation(out=zo[bb], in_=zf32, func=AF.Copy), r=["zf32"], w=["zo%d" % bb])
                            op("sp", lambda e, bb=bb, cch=cch, t0=t0: e.dma_start(out=zdst[cch * 128:(cch + 1) * 128, t0:t0 + tw], in_=zo[bb]),
                                 r=["zo%d" % bb], w=["d:z1T" if o == 0 else "d:zT"], dma="st:zo%d" % bb)
                            if o == 0:
                                for tc in range(tw // 128):
                                    op("pe", lambda e, tc=tc: e.transpose(ps[6][:, tc * 128:(tc + 1) * 128], zf32[:, tc * 128:(tc + 1) * 128], ident), r=["zf32", "ident"], w=["ps6"])
                                for tc in range(tw // 128):
                                    op("act", lambda e, tc=tc, cch=cch, tt=tt: e.activation(out=intm[:, tt * (tw // 128) + tc, cch * 128:(cch + 1) * 128],
                                                                                           in_=ps[6][:, tc * 128:(tc + 1) * 128], func=AF.Copy), x=["ps6"], w=["intm"])
            run_order(0)
            run_order(1)

        def mla(l):
            j = l // 2
            ckvT = A.alloc([2, NKEY], BF16)
            krT = A.alloc([NKEY], F32)
            m_persist = A.mark()
            hT = A.alloc([8, T], BF16)
            m0 = A.mark()
            prologue(l, 0, hT)
            P.barrier()
            A.release(m0)
            wdq = A.alloc([8, 512], BF16)
            load_w(wdq, W["mla_w_dq"][j].rearrange("(c p) n -> p c n", p=128), "wdq")
            wdkv = A.alloc([8, 320], BF16)
            load_w(wdkv, W["mla_w_dkv"][j].rearrange("(c p) n -> p c n", p=128), "wdkv")
            rawq = A.alloc([4, 512], F32)
            qnb = [A.alloc([4, 512], BF16) for _ in range(2)]
            sq = A.alloc([4, 512], BF16)
            rs = A.alloc([512], F32)
            ckf = A.alloc([2, 512], F32)
            tok = [A.alloc([256], F32) for _ in range(2)]
            tokr = [A.alloc([64], F32) for _ in range(2)]
            cin = A.alloc([2, 256], F32)
            cinr = A.alloc([2, 64], F32)
            def proj_tile(ti):
                sl = slice(ti * 512, (ti + 1) * 512)
                for oc in range(4):
                    pb = oc % 2

                    def mm(e, oc=oc, pb=pb):
                        last = None
                        for kc in range(8):
                            last = e.matmul(ps[pb], lhsT=wdq[:, kc, oc * 128:(oc + 1) * 128], rhs=hT[:, kc, sl], start=(kc == 0), stop=(kc == 7))
                        return last
                    P.op("pe", mm, r=["wdq", "hT"], w=["ps%d" % pb])
                    P.op("act", lambda e, oc=oc, pb=pb: e.activation(out=rawq[:, oc, :], in_=ps[pb], func=AF.Copy), x=["ps%d" % pb], w=["rawq"])
                P.op("act", lambda e: e.activation(out=sq, in_=rawq, func=AF.Square), r=["rawq"], w=["sq"])

                def mms(e):
                    last = None
                    for oc in range(4):
                        last = e.matmul(ps[2], lhsT=onesb, rhs=sq[:, oc, :], start=(oc == 0), stop=(oc == 3))
                    return last
                P.op("pe", mms, r=["sq", "onesb"], w=["ps2"])
                P.op("act", lambda e: e.activation(out=rs, in_=ps[2], func=AF.Sqrt, scale=1.0 / 512, bias=epsT[:, 0:1]), x=["ps2"], r=["eps"], w=["rs"])
                P.op("dve", lambda e: e.reciprocal(out=rs, in_=rs), r=["rs"], w=["rs"])
                for oc in range(4):
                    P.op("dve", lambda e, oc=oc: e.tensor_tensor(out=rawq[:, oc, :], in0=rawq[:, oc, :], in1=rs, op=ALU.mult), r=["rawq", "rs"], w=["rawq"])
                    P.op("act", lambda e, oc=oc, ti=ti: e.activation(out=qnb[ti % 2][:, oc, :], in_=rawq[:, oc, :], func=AF.Identity, scale=pvc("qn%d" % j, oc)), r=["rawq", "pvt"], w=["qnb%d" % (ti % 2)])
                P.op("sp", lambda e, ti=ti: e.dma_start(out=qnT_d.rearrange("(c p) t -> p c t", p=128)[:, :, ti * 512:(ti + 1) * 512], in_=qnb[ti % 2]),
                     r=["qnb%d" % (ti % 2)], w=["d:qnT"], dma="st:qnb%d" % (ti % 2))
                for oc in range(3):
                    m_ = 128 if oc < 2 else 64
                    pb = 3 + oc

                    def mm(e, oc=oc, pb=pb, m_=m_):
                        last = None
                        for kc in range(8):
                            last = e.matmul(ps[pb][0:m_, :], lhsT=wdkv[:, kc, oc * 128:oc * 128 + m_], rhs=hT[:, kc, sl], start=(kc == 0), stop=(kc == 7))
                        return last
                    P.op("pe", mm, r=["wdkv", "hT"], w=["ps%d" % pb])
                    if oc < 2:
                        P.op("act", lambda e, oc=oc, pb=pb: e.activation(out=ckf[:, oc, :], in_=ps[pb], func=AF.Copy), x=["ps%d" % pb], w=["ckf"])
                    else:
                        P.op("act", lambda e, pb=pb: e.activation(out=krT[0:64, sl], in_=ps[pb][0:64, :], func=AF.Copy), x=["ps%d" % pb], w=["krT"])
                P.op("act", lambda e: e.activation(out=sq[:, 0:2, :], in_=ckf, func=AF.Square), r=["ckf"], w=["sq"])

                def mms2(e):
                    e.matmul(ps[2], lhsT=onesb, rhs=sq[:, 0, :], start=True, stop=False)
                    return e.matmul(ps[2], lhsT=onesb, rhs=sq[:, 1, :], start=False, stop=True)
                P.op("pe", mms2, r=["sq", "onesb"], w=["ps2"])
                P.op("act", lambda e: e.activation(out=rs, in_=ps[2], func=AF.Sqrt, scale=1.0 / 256, bias=epsT[:, 0:1]), x=["ps2"], r=["eps"], w=["rs"])
                P.op("dve", lambda e: e.reciprocal(out=rs, in_=rs), r=["rs"], w=["rs"])
                for oc in range(2):
                    P.op("dve", lambda e, oc=oc: e.tensor_tensor(out=ckf[:, oc, :], in0=ckf[:, oc, :], in1=rs, op=ALU.mult), r=["ckf", "rs"], w=["ckf"])
                    P.op("dve", lambda e, oc=oc: e.tensor_scalar(out=ckf[:, oc, :], in0=ckf[:, oc, :], scalar1=pvc("kn%d" % j, oc), scalar2=None, op0=ALU.mult), r=["ckf", "pvt"], w=["ckf"])
                    P.op("act", lambda e, oc=oc: e.activation(out=ckvT[:, oc, sl], in_=ckf[:, oc, :], func=AF.Copy), r=["ckf"], w=["ckvT"])
                if ti >= 8:
                    for tc in range(4):
                        sqi = (ti - 8) * 2 + tc // 2
                        r0 = (tc % 2) * 128
                        b = tc % 2

                        def tr(e, tc=tc):
                            e.transpose(ps[6][:, 0:128], ckf[:, 0, tc * 128:(tc + 1) * 128], ident)
                            e.transpose(ps[6][:, 128:256], ckf[:, 1, tc * 128:(tc + 1) * 128], ident)
                            return e.transpose(ps[6][:, 256:320], krT[0:64, ti * 512 + tc * 128:ti * 512 + (tc + 1) * 128], ident[0:64, 0:64])
                        P.op("pe", tr, r=["ckf", "krT", "ident"], w=["ps6"])
                        P.op("dve", lambda e, b=b: e.tensor_copy(out=tok[b], in_=ps[6][:, 0:256]), x=["ps6"], w=["tok%d" % b])
                        P.op("dve", lambda e, b=b: e.tensor_copy(out=tokr[b], in_=ps[6][:, 256:320]), x=["ps6"], w=["tokr%d" % b])
                        P.op("sp", lambda e, b=b, sqi=sqi, r0=r0: e.dma_start(out=nckv_d[sqi, j, r0:r0 + 128, :], in_=tok[b]), r=["tok%d" % b], dma="st:tok%d" % b)
                        P.op("sp", lambda e, b=b, sqi=sqi, r0=r0: e.dma_start(out=nkr_d[sqi, j, r0:r0 + 128, :], in_=tokr[b]), r=["tokr%d" % b], dma="st:tokr%d" % b)
            for ti in range(NT):
                proj_tile(ti)
            P.op("sp", lambda e: e.dma_start(out=cin, in_=cckv_d[j].rearrange("(c p) n -> p c n", p=128)), w=["cin"], dma="cin")
            P.op("sp", lambda e: e.dma_start(out=cinr, in_=ckr_d[j].rearrange("(c p) n -> p c n", p=128)), w=["cinr"], dma="cin")
            for tc in range(2):
                def tr(e, tc=tc):
                    e.transpose(ps[6][:, 0:128], cin[:, tc, 0:128], ident)
                    e.transpose(ps[6][:, 128:256], cin[:, tc, 128:256], ident)
                    return e.transpose(ps[6][0:64, 256:384], cinr[:, tc, :], ident)
                P.op("pe", tr, r=["cin", "cinr", "ident"], w=["ps6"])
                for cc in range(2):
                    P.op("act", lambda e, tc=tc, cc=cc: e.activation(out=ckvT[:, cc, T + tc * 128:T + (tc + 1) * 128], in_=ps[6][:, cc * 128:(cc + 1) * 128], func=AF.Copy),
                         x=["ps6"], w=["ckvT"])
                P.op("act", lambda e, tc=tc: e.activation(out=krT[0:64, T + tc * 128:T + (tc + 1) * 128], in_=ps[6][0:64, 256:384], func=AF.Copy), x=["ps6"], w=["krT"])
            P.barrier()
            A.release(m_persist)
            wuq = A.alloc([4, 1536], BF16)
            load_w(wuq, W["mla_w_uq"][j].rearrange("(c p) n -> p c n", p=128), "wuq")
            wukv = A.alloc([2, 2048], BF16)
            load_w(wukv, W["mla_w_ukv"][j].rearrange("(c p) n -> p c n", p=128), "wukv")
            ropec = A.alloc([4096], F32)
            ropes = A.alloc([4096], F32)
            P.op("sp", lambda e: e.dma_start(out=ropec[0:64, :], in_=cd["ropec"]), w=["ropec"], dma="rope")
            P.op("sp", lambda e: e.dma_start(out=ropes[0:64, :], in_=cd["ropes"]), w=["ropes"], dma="rope")
            NKC = 34
            NCH = NKEY // 128
            Khr = A.alloc([NKEY], BF16)
            krss = A.alloc([NCH], F32)
            KhnA = [A.alloc([NKC * 128], BF16) for _ in range(2)]
            VhA = [A.alloc([NKC, 128], BF16) for _ in range(2)]
            sclA = [A.alloc([NKC], F32) for _ in range(2)]
            KhnB = [A.alloc([256], BF16) for _ in range(4)]
            VhB = [A.alloc([2, 128], BF16) for _ in range(4)]
            sclB = [A.alloc([2], F32) for _ in range(4)]
            sqk = A.alloc([512], BF16)
            tk = A.alloc([4], F32)
            sqa = A.alloc([512], BF16)
            sqr = A.alloc([512], BF16)
            rsa = A.alloc([512], F32)
            tn = A.alloc([512], F32)
            tr_ = A.alloc([512], F32)
            tr2 = A.alloc([512], F32)
            Qn = [A.alloc([512], BF16) for _ in range(2)]
            Qr = [A.alloc([512], BF16) for _ in range(2)]
            qin = [A.alloc([4, 512], BF16) for _ in range(2)]
            PT = [A.alloc([512], BF16) for _ in range(4)]
            rec = A.alloc([512], F32)
            ob = [A.alloc([512], BF16) for _ in range(2)]
            sc_ = 1.0 / math.sqrt(192.0)
            kgr = "khr%d" % j

            for c0 in range(0, NCH, 4):
                cn = min(4, NCH - c0)
                P.op("act", lambda e, c0=c0, cn=cn: e.activation(out=sqr[0:64, 0:cn * 128], in_=krT[0:64, c0 * 128:(c0 + cn) * 128], func=AF.Square), r=["krT"], w=["sqr"])

                def mm(e, c0=c0, cn=cn):
                    last = None
                    for a_ in range(cn):
                        last = e.matmul(ps[4][:, c0 + a_:c0 + a_ + 1], lhsT=sqr[0:64, a_ * 128:(a_ + 1) * 128], rhs=onesb[0:64, 0:1], start=True, stop=True)
                    return last
                P.op("pe", mm, r=["sqr", "onesb"], w=["ps4"])
            P.op("dve", lambda e: e.tensor_copy(out=krss, in_=ps[4][:, 0:NCH]), x=["ps4"], w=["krss"])
            for c0 in range(0, NKEY, 512):
                n = min(512, NKEY - c0)
                cs_ = slice(c0, c0 + n)
                P.op("dve", lambda e, cs_=cs_, n=n: e.tensor_scalar(out=tr2[0:64, 0:n], in0=krT[0:64, cs_], scalar1=pvc(kgr, 0, 64), scalar2=None, op0=ALU.mult), r=["krT", "pvt"], w=["tr2"])
                if c0 < 4096:
                    P.op("pe", lambda e, n=n: e.matmul(ps[6][0:64, 0:n], lhsT=rotT[0:64, :], rhs=tr2[0:64, 0:n], start=True, stop=True), r=["rotT", "tr2"], w=["ps6"])
                    P.op("dve", lambda e, cs_=cs_, n=n: e.tensor_tensor(out=tn[0:64, 0:n], in0=ps[6][0:64, 0:n], in1=ropes[0:64, cs_], op=ALU.mult), x=["ps6"], r=["ropes"], w=["tn"])
                    P.op("dve", lambda e, cs_=cs_, n=n: e.tensor_tensor(out=tr2[0:64, 0:n], in0=tr2[0:64, 0:n], in1=ropec[0:64, cs_], op=ALU.mult), r=["tr2", "ropec"], w=["tr2"])
                    P.op("dve", lambda e, cs_=cs_, n=n: e.tensor_tensor(out=Khr[0:64, cs_], in0=tr2[0:64, 0:n], in1=tn[0:64, 0:n], op=ALU.add), r=["tr2", "tn"], w=["Khr"])
                else:
                    P.op("dve", lambda e, cs_=cs_, n=n: e.tensor_copy(out=Khr[0:64, cs_], in_=tr2[0:64, 0:n]), r=["tr2"], w=["Khr"])

            def kprep(hd, S, Khn_, Vh_, scl_, tag):
                kch = S["kch"]
                groups = [kch[a:a + 4] for a in range(0, len(kch), 4)]
                for gi, grp in enumerate(groups):
                    ng = len(grp)
                    n = ng * 128
                    c0 = grp[0] * 128
                    assert grp[-1] == grp[0] + ng - 1
                    lo = gi * 512

                    def mm(e, c0=c0, n=n):
                        last = None
                        for kc in range(2):
                            last = e.matmul(ps[4][:, 0:n], lhsT=wukv[:, kc, hd * 256:hd * 256 + 128], rhs=ckvT[:, kc, c0:c0 + n], start=(kc == 0), stop=(kc == 1))
                        return last
                    P.op("pe", mm, r=["wukv", "ckvT"], w=["ps4"])
                    yield
                    P.op("act", lambda e, n=n: e.activation(out=sqk[:, 0:n], in_=ps[4][:, 0:n], func=AF.Square), x=["ps4"], w=["sqk"])
                    yield
                    P.op("dve", lambda e, lo=lo, n=n: e.tensor_scalar(out=Khn_[:, lo:lo + n], in0=ps[4][:, 0:n], scalar1=pvc("khn%d" % j), scalar2=None, op0=ALU.mult),
                         x=["ps4"], r=["pvt"], w=["Khn" + tag])
                    yield

                    def mm1(e, ng=ng):
                        last = None
                        for a_ in range(ng):
                            last = e.matmul(ps[6][:, a_:a_ + 1], lhsT=sqk[:, a_ * 128:(a_ + 1) * 128], rhs=onesb[:, 0:1], start=True, stop=True)
                        return last
                    P.op("pe", mm1, r=["sqk", "onesb"], w=["ps6"])
                    yield
                    P.op("dve", lambda e, ng=ng, g0=grp[0]: e.tensor_tensor(out=tk[:, 0:ng], in0=ps[6][:, 0:ng], in1=krss[:, g0:g0 + ng], op=ALU.add), x=["ps6"], r=["krss"], w=["tk"])
                    yield
                    P.op("act", lambda e, ng=ng: e.activation(out=tk[:, 0:ng], in_=tk[:, 0:ng], func=AF.Ln, scale=1.0 / 192, bias=epsT[:, 0:1]), r=["tk", "eps"], w=["tk"])
                    yield
                    P.op("act", lambda e, ng=ng: e.activation(out=tk[:, 0:ng], in_=tk[:, 0:ng], func=AF.Exp, scale=-0.5), r=["tk"], w=["tk"])
                    yield
                    P.op("dve", lambda e, ng=ng, gi=gi: e.tensor_scalar(out=scl_[:, gi * 4:gi * 4 + ng], in0=tk[:, 0:ng], scalar1=sc_, scalar2=None, op0=ALU.mult), r=["tk"], w=["scl" + tag])
                    yield

                    def mmv(e, grp=grp):
                        last = None
                        for a_, ch in enumerate(grp):
                            for kc in range(2):
                                last = e.matmul(ps[5][:, a_ * 128:(a_ + 1) * 128], lhsT=ckvT[:, kc, ch * 128:(ch + 1) * 128],
                                                rhs=wukv[:, kc, hd * 256 + 128:hd * 256 + 256], start=(kc == 0), stop=(kc == 1))
                        return last
                    P.op("pe", mmv, r=["wukv", "ckvT"], w=["ps5"])
                    yield
                    P.op("dve", lambda e, gi=gi, ng=ng, n=n: e.tensor_copy(out=Vh_[:, gi * 4:gi * 4 + ng, :], in_=ps[5][:, 0:n].rearrange("p (a d) -> p a d", d=128)),
                         x=["ps5"], w=["Vh" + tag])
                    yield

            def qprep(hd, S, qt, slot):
                qw = min(512, S["nq"])
                q0 = S["q0"] + qt * qw
                rope = S["rope"]
                qb_ = qin[slot]
                qn_, qr_ = Qn[slot], Qr[slot]
                qres = "Qh%d" % slot
                P.op("sp", lambda e: e.dma_start(out=qb_[:, :, 0:qw], in_=qnT_d.rearrange("(c p) t -> p c t", p=128)[:, :, q0:q0 + qw]),
                     r=["d:qnT"], w=["qin%d" % slot], dma="qin%d" % slot)
                yield

                def mmq(e):
                    last = None
                    for kc in range(4):
                        e.matmul(ps[4][:, 0:qw], lhsT=wuq[:, kc, hd * 192:hd * 192 + 128], rhs=qb_[:, kc, 0:qw], start=(kc == 0), stop=(kc == 3))
                    for kc in range(4):
                        last = e.matmul(ps[5][0:64, 0:qw], lhsT=wuq[:, kc, hd * 192 + 128:hd * 192 + 192], rhs=qb_[:, kc, 0:qw], start=(kc == 0), stop=(kc == 3))
                    return last
                P.op("pe", mmq, r=["wuq", "qin%d" % slot], w=["ps4", "ps5"])
                yield
                P.op("act", lambda e: e.activation(out=sqa[:, 0:qw], in_=ps[4][:, 0:qw], func=AF.Square), x=["ps4"], w=["sqa"])
                yield
                P.op("act", lambda e: e.activation(out=tr_[0:64, 0:qw], in_=ps[5][0:64, 0:qw], func=AF.Copy), x=["ps5"], w=["tr"])
                yield
                P.op("act", lambda e: e.activation(out=sqr[0:64, 0:qw], in_=tr_[0:64, 0:qw], func=AF.Square), r=["tr"], w=["sqr"])
                yield

                def mm(e):
                    e.matmul(ps[6][:, 0:qw], lhsT=onesb, rhs=sqa[:, 0:qw], start=True, stop=False)
                    return e.matmul(ps[6][:, 0:qw], lhsT=onesb[0:64, :], rhs=sqr[0:64, 0:qw], start=False, stop=True)
                P.op("pe", mm, r=["sqa", "sqr", "onesb"], w=["ps6"])
                yield
                P.op("act", lambda e: e.activation(out=rsa[:, 0:qw], in_=ps[6][:, 0:qw], func=AF.Ln, scale=1.0 / 192, bias=epsT[:, 0:1]), x=["ps6"], r=["eps"], w=["rsa"])
                yield
                P.op("act", lambda e: e.activation(out=rsa[:, 0:qw], in_=rsa[:, 0:qw], func=AF.Exp, scale=-0.5), r=["rsa"], w=["rsa"])
                yield
                P.op("dve", lambda e: e.tensor_tensor(out=tn[:, 0:qw], in0=ps[4][:, 0:qw], in1=rsa[:, 0:qw], op=ALU.mult), x=["ps4"], r=["rsa"], w=["tn"])
                yield
                P.op("dve", lambda e: e.tensor_scalar(out=qn_[:, 0:qw], in0=tn[:, 0:qw], scalar1=pvc("qhn%d" % j), scalar2=None, op0=ALU.mult), r=["tn", "pvt"], w=[qres])
                yield
                P.op("dve", lambda e: e.tensor_tensor(out=tr2[0:64, 0:qw], in0=tr_[0:64, 0:qw], in1=rsa[0:64, 0:qw], op=ALU.mult), r=["tr", "rsa"], w=["tr2"])
                yield
                if not rope:
                    P.op("dve", lambda e: e.tensor_scalar(out=qr_[0:64, 0:qw], in0=tr2[0:64, 0:qw], scalar1=pvc("qhr%d" % j, 0, 64), scalar2=None, op0=ALU.mult), r=["tr2", "pvt"], w=[qres])
                    yield
                else:
                    tc_ = slice(q0, q0 + qw)
                    P.op("dve", lambda e: e.tensor_scalar(out=tr2[0:64, 0:qw], in0=tr2[0:64, 0:qw], scalar1=pvc("qhr%d" % j, 0, 64), scalar2=None, op0=ALU.mult), r=["tr2", "pvt"], w=["tr2"])
                    yield
                    P.op("pe", lambda e: e.matmul(ps[6][0:64, 0:qw], lhsT=rotT[0:64, :], rhs=tr2[0:64, 0:qw], start=True, stop=True), r=["rotT", "tr2"], w=["ps6"])
                    yield
                    P.op("dve", lambda e: e.tensor_tensor(out=tn[0:64, 0:qw], in0=ps[6][0:64, 0:qw], in1=ropes[0:64, tc_], op=ALU.mult), x=["ps6"], r=["ropes"], w=["tn"])
                    yield
                    P.op("dve", lambda e: e.tensor_tensor(out=tr2[0:64, 0:qw], in0=tr2[0:64, 0:qw], in1=ropec[0:64, tc_], op=ALU.mult), r=["tr2", "ropec"], w=["tr2"])
                    yield
                    P.op("dve", lambda e: e.tensor_tensor(out=qr_[0:64, 0:qw], in0=tr2[0:64, 0:qw], in1=tn[0:64, 0:qw], op=ALU.add), r=["tr2", "tn"], w=[qres])
                    yield

            def drain(g):
                for _ in g:
                    pass

            def core(hd, S, qt, Khn_, Vh_, scl_, tag, slot, pending, oidx):
                kch = S["kch"]
                nk = len(kch)
                qw = min(512, S["nq"])
                q0 = S["q0"] + qt * qw
                qn_, qr_ = Qn[slot], Qr[slot]
                qres = "Qh%d" % slot
                SB = (0, 1, 7)

                def qk(a):
                    sb = SB[a % 3]
                    pt = PT[a % 4]
                    kc0 = kch[a] * 128

                    def mms_(e):
                        e.matmul(ps[sb][:, 0:qw], lhsT=Khn_[:, a * 128:(a + 1) * 128], rhs=qn_[:, 0:qw], start=True, stop=False)
                        return e.matmul(ps[sb][:, 0:qw], lhsT=Khr[0:64, kc0:kc0 + 128], rhs=qr_[0:64, 0:qw], start=False, stop=True)
                    P.op("pe", mms_, r=["Khn" + tag, "Khr", qres], w=["ps%d" % sb])
                    P.op("act", lambda e: e.activation(out=pt[:, 0:qw], in_=ps[sb][:, 0:qw], func=AF.Exp, scale=scl_[:, a:a + 1]),
                         x=["ps%d" % sb], r=["scl" + tag], w=["PT%d" % (a % 4)])

                def pv(a):
                    pt = PT[a % 4]

                    def mmo(e):
                        e.matmul(ps[2][:, 0:qw], lhsT=Vh_[:, a, :], rhs=pt[:, 0:qw], start=(a == 0), stop=(a == nk - 1))
                        return e.matmul(ps[3][:, 0:qw], lhsT=onesb, rhs=pt[:, 0:qw], start=(a == 0), stop=(a == nk - 1))
                    P.op("pe", mmo, r=["Vh" + tag, "PT%d" % (a % 4), "onesb"], w=["ps2", "ps3"])

                def drip(k):
                    for _ in range(k):
                        while pending:
                            try:
                                next(pending[0])
                                break
                            except StopIteration:
                                pending.pop(0)
                qk(0)
                if nk > 1:
                    qk(1)
                for a in range(nk):
                    if a + 2 < nk:
                        qk(a + 2)
                    pv(a)
                    drip(2 if a % 2 else 1)
                o_ = ob[oidx % 2]
                P.op("dve", lambda e: e.reciprocal(out=rec[:, 0:qw], in_=ps[3][:, 0:qw]), x=["ps3"], w=["rec"])
                P.op("dve", lambda e: e.tensor_tensor(out=o_[:, 0:qw], in0=ps[2][:, 0:qw], in1=rec[:, 0:qw], op=ALU.mult), x=["ps2"], r=["rec"], w=["ob%d" % (oidx % 2)])
                P.op("sp", lambda e: e.dma_start(out=oT_d[hd * 128:(hd + 1) * 128, q0:q0 + qw], in_=o_[:, 0:qw]), r=["ob%d" % (oidx % 2)], w=["d:oT"],
                     dma="st:aob%d" % (oidx % 2))

            seqs = [dict(q0=0, nq=4096, kch=list(range(32)) + [40, 41], rope=True)]
            for sq_ in range(4):
                seqs.append(dict(q0=4096 + 256 * sq_, nq=256, kch=[32 + 2 * sq_, 33 + 2 * sq_], rope=False))
            units = []
            for hd in range(HEADS):
                for si, S in enumerate(seqs):
                    if si == 0:
                        bufs = (KhnA[hd % 2], VhA[hd % 2], sclA[hd % 2], "A%d" % (hd % 2))
                    else:
                        bufs = (KhnB[si - 1], VhB[si - 1], sclB[si - 1], "B%d" % (si - 1))
                    units.append(dict(hd=hd, S=S, si=si, bufs=bufs))
            for u in units:
                u["kgen"] = kprep(u["hd"], u["S"], *u["bufs"])
            items = []
            for ui, u in enumerate(units):
                qw = min(512, u["S"]["nq"])
                for qt in range(u["S"]["nq"] // qw):
                    items.append(dict(ui=ui, qt=qt))
            for ii, it in enumerate(items):
                u = units[it["ui"]]
                it["qgen"] = qprep(u["hd"], u["S"], it["qt"], ii % 2)
            for ii, it in enumerate(items):
                ui = it["ui"]
                u = units[ui]
                drain(u["kgen"])
                drain(it["qgen"])
                pending = []
                if ii + 1 < len(items):
                    pending.append(items[ii + 1]["qgen"])
                if u["si"] == 0:
                    for k in range(1, 5):
                        pending.append(units[ui + k]["kgen"])
                    if ui + 5 < len(units):
                        pending.append(units[ui + 5]["kgen"])
                core(u["hd"], u["S"], it["qt"], *u["bufs"], ii % 2, pending, ii)
            phase_end()

            def in_tiles(ti, dst, key):
                P.op("sp", lambda e: e.dma_start(out=dst, in_=oT_d.rearrange("(c p) t -> p c t", p=128)[:, :, ti * 512:(ti + 1) * 512]), r=["d:oT"], w=[key], dma=key)
            out_proj(l, 0, W["mla_w_o"][j], 8, in_tiles)
            phase_end()

        for l in range(depth):
            if l % 2 == 0:
                even_mixer(l)
            else:
                mla(l)
            ffn(l)

        xt = [A.alloc([8, 512], F32) for _ in range(2)]
        yo = [A.alloc([1024], F32) for _ in range(2)]

        def ldx(ti):
            P.op("sp", lambda e: e.dma_start(out=xt[ti % 2], in_=xT_tile_ap(ti)), r=["d:xT"], w=["fx%d" % (ti % 2)], dma="fx%d" % (ti % 2))
        ldx(0)
        for ti in range(NT):
            if ti + 1 < NT:
                ldx(ti + 1)
            b = ti % 2
            for tc in range(4):
                yb = tc % 2
                for hh in range(2):
                    pb = hh

                    def tr(e, b=b, tc=tc, hh=hh, pb=pb):
                        last = None
                        for d4 in range(4):
                            dc = hh * 4 + d4
                            last = e.transpose(ps[pb][:, d4 * 128:(d4 + 1) * 128], xt[b][:, dc, tc * 128:(tc + 1) * 128], ident)
                        return last
                    P.op("pe", tr, r=["fx%d" % b, "ident"], w=["ps%d" % pb])
                    if hh == 0:
                        P.op("act", lambda e, yb=yb, pb=pb: e.activation(out=yo[yb][:, 0:512], in_=ps[pb], func=AF.Copy), x=["ps%d" % pb], w=["yo%d" % yb])
                    else:
                        P.op("dve", lambda e, yb=yb, pb=pb: e.tensor_copy(out=yo[yb][:, 512:1024], in_=ps[pb]), x=["ps%d" % pb], w=["yo%d" % yb])
                P.op("sp", lambda e, yb=yb, ti=ti, tc=tc: e.dma_start(out=y_rows(ti)[tc * 128:(tc + 1) * 128, :], in_=yo[yb]), r=["yo%d" % yb], dma="st:yo%d" % yb)
        P.barrier()
        P.emit()
        build.n_inst = P.n_inst
    return nc


_NC = {}


def make_in_maps(inp):
    C = host_consts()
    pv = pv_layout(inp).array()
    bf = ml_dtypes.bfloat16
    shared = {k: np.ascontiguousarray(inp[k], dtype=np.float32) for k in WEIGHTS}
    shared["pv"] = pv
    shared["sguT"] = np.ascontiguousarray(np.transpose(inp["sgu_w"], (0, 3, 1, 2)))
    shared["sgub"] = np.ascontiguousarray(inp["sgu_b"].reshape(2, 1, 512))
    shared["decbc"] = np.ascontiguousarray(np.broadcast_to(inp["hy_decay"].reshape(2, 1, 1024), (2, 128, 1024)))
    for k, v in C.items():
        shared["c_" + k] = v
    maps = []
    for c in range(8):
        m = dict(shared)
        m["xs"] = np.ascontiguousarray(inp["x_sample"][c])
        m["xp"] = np.ascontiguousarray(inp["x_prompt"][4 * c:4 * c + 4].reshape(1024, 1024))
        m["cckv"] = np.ascontiguousarray(inp["cache_ckv"][c])
        m["ckr"] = np.ascontiguousarray(inp["cache_krope"][c])
        cond = np.stack([inp["c"][c], inp["c_ctx"]], axis=1)
        m["condT"] = np.ascontiguousarray(cond.reshape(8, 128, 2).transpose(1, 0, 2))
        maps.append(m)
    return maps


def kernel(**inputs):
    inp = {k: np.asarray(v) for k, v in inputs.items()}
    if "nc" not in _NC:
        _NC["nc"] = build()
    nc = _NC["nc"]
    maps = make_in_maps(inp)
    res = run_bass_kernel_spmd(nc, maps, core_ids=list(range(8)))
    R = res.results
    y_sample = np.stack([np.asarray(R[c]["ys"], np.float32) for c in range(8)], 0)
    y_prompt = np.concatenate([np.asarray(R[c]["yp"], np.float32).reshape(4, 256, 1024) for c in range(8)], 0)
    nckv = np.concatenate([np.asarray(R[c]["nckv"], np.float32) for c in range(8)], 0)
    nkr = np.concatenate([np.asarray(R[c]["nkr"], np.float32) for c in range(8)], 0)
    return (y_prompt, y_sample, nckv, nkr)
```

```python
import contextlib
import math
import numpy as np
import ml_dtypes
import concourse.bass as bass
import concourse.mybir as mybir
from concourse.bass_utils import run_bass_kernel_spmd

F32 = mybir.dt.float32
BF16 = mybir.dt.bfloat16
U8 = mybir.dt.uint8
AF = mybir.ActivationFunctionType
ALU = mybir.AluOpType

COMPUTE = ("pe", "act", "dve", "pool")
STREAMS = ("pe", "act", "dve", "pool", "sp")


class Prog:
    def __init__(self, nc, strict=True):
        self.nc = nc
        self.strict = strict
        self.ops = []
        self.res = {}
        self.dma_cnt = {}
        self.last_real = {s: None for s in STREAMS}

    def _dep_entry(self, a):
        A = self.ops[a]
        if A["dma"] is not None:
            return ("d", A["dma"], 16 * self.dma_cnt[A["dma"]])
        return ("c", a)

    def op(self, eng, fn, r=(), w=(), dma=None, x=()):
        idx = len(self.ops)
        deps = set()
        for name in x:
            st = self.res.setdefault(name, [None, []])
            if st[0] is not None:
                deps.add(st[0])
            for rd in st[1]:
                if self.ops[rd]["eng"] != eng:
                    deps.add(rd)
        for name in r:
            st = self.res.setdefault(name, [None, []])
            if st[0] is not None:
                deps.add(st[0])
        for name in w:
            st = self.res.setdefault(name, [None, []])
            if st[0] is not None:
                deps.add(st[0])
            for rd in st[1]:
                deps.add(rd)
        if dma is not None:
            self.dma_cnt[dma] = self.dma_cnt.get(dma, 0)
        dep_entries = []
        for a in sorted(deps):
            A = self.ops[a]
            if A["dma"] is None and A["eng"] == eng and dma is None:
                if eng == "pe" or not self.strict:
                    continue
            dep_entries.append(self._dep_entry(a))
        if dma is not None:
            self.dma_cnt[dma] += 1
        self.ops.append(dict(eng=eng, fn=fn, deps=dep_entries, dma=dma, sig=False, val=None))
        for name in list(r) + list(x):
            self.res[name][1].append(idx)
        for name in w:
            self.res[name] = [idx, []]
        if fn is not None and dma is None:
            self.last_real[eng] = idx
        return idx

    def barrier(self):
        ents = []
        for s in COMPUTE:
            a = self.last_real.get(s)
            if a is not None:
                ents.append(("c", a))
        for key, cnt in self.dma_cnt.items():
            if cnt:
                ents.append(("d", key, 16 * cnt))
        for s in STREAMS:
            self.ops.append(dict(eng=s, fn=None, deps=list(ents), dma=None, sig=False, val=None))
        self.res = {}

    def emit(self):
        nc = self.nc
        ops = self.ops
        for o in ops:
            for d in o["deps"]:
                if d[0] == "c":
                    ops[d[1]]["sig"] = True
        cnt = {s: 0 for s in COMPUTE}
        for o in ops:
            if o["dma"] is None and o["sig"]:
                cnt[o["eng"]] += 1
                o["val"] = cnt[o["eng"]]
        dma_keys = list(self.dma_cnt.keys())
        with contextlib.ExitStack() as es:
            esem = {s: es.enter_context(nc.semaphore("sem_" + s)) for s in COMPUTE}
            dsem = {k: es.enter_context(nc.semaphore("dsem_%d" % i)) for i, k in enumerate(dma_keys)}
            block = es.enter_context(nc.Block())
            self.n_inst = {s: 0 for s in STREAMS}

            def run_stream(s, engine):
                waited = {}
                for o in ops:
                    if o["eng"] != s:
                        continue
                    for d in o["deps"]:
                        if d[0] == "c":
                            A = ops[d[1]]
                            sem, val = esem[A["eng"]], A["val"]
                            key = ("c", A["eng"])
                        else:
                            sem, val = dsem[d[1]], d[2]
                            key = ("d", d[1])
                        if waited.get(key, 0) >= val:
                            continue
                        waited[key] = val
                        engine.wait_ge(sem, val)
                        self.n_inst[s] += 1
                    if o["fn"] is None:
                        continue
                    ins = o["fn"](engine)
                    self.n_inst[s] += 1
                    if o["dma"] is not None:
                        ins.then_inc(dsem[o["dma"]], 16)
                    elif o["sig"]:
                        ins.then_inc(esem[s], 1)
                if s == "sp":
                    for k, c in self.dma_cnt.items():
                        if c and waited.get(("d", k), 0) < 16 * c:
                            engine.wait_ge(dsem[k], 16 * c)

            @block.tensor
            def _(e):
                run_stream("pe", e)

            @block.scalar
            def _(e):
                run_stream("act", e)

            @block.vector
            def _(e):
                run_stream("dve", e)

            @block.gpsimd
            def _(e):
                run_stream("pool", e)

            @block.sync
            def _(e):
                run_stream("sp", e)


class Arena:
    def __init__(self, nc, es, nbytes):
        self.t = es.enter_context(nc.sbuf_tensor("arena", [128, nbytes], U8))
        self.n = nbytes
        self.off = 0

    def alloc(self, shape_free, dtype, parts=128):
        size = {F32: 4, BF16: 2, U8: 1}[dtype]
        n = int(np.prod(shape_free)) * size
        off = (self.off + 63) // 64 * 64
        assert off + n <= self.n, ("SBUF arena overflow", off, n, self.n)
        self.off = off + n
        ap = self.t[0:parts, off:off + n].bitcast(dtype)
        if len(shape_free) == 1:
            return ap
        names = " ".join("d%d" % i for i in range(len(shape_free)))
        kw = {"d%d" % i: int(s) for i, s in enumerate(shape_free)}
        return ap.rearrange("p (%s) -> p %s" % (names, names), **kw)

    def mark(self):
        return self.off

    def release(self, m):
        self.off = m


D = 1024
DEPTH = 4
T = 5120
NT = 10
TS = 4096
LP = 256
SEGS = [(0, 4096)] + [(4096 + 256 * i, 4096 + 256 * (i + 1)) for i in range(4)]
DFF = 2816
EPS = 1e-6
HEADS = 8
NKEY = T + 256

class PV:
    def __init__(self):
        self.cols = {}
        self.n = 0
        self.data = []

    def add(self, name, vec):
        v = np.asarray(vec, np.float32).reshape(-1)
        if v.size % 128:
            v = np.concatenate([v, np.zeros(128 - v.size % 128, np.float32)])
        c = v.size // 128
        self.cols[name] = (self.n, c)
        self.data.append(v.reshape(c, 128).T)
        self.n += c

    def array(self):
        return np.ascontiguousarray(np.concatenate(self.data, axis=1))


def pv_layout(inp=None):
    pv = PV()

    def g(name, shape):
        return inp[name] if inp is not None else np.zeros(shape, np.float32)
    ng = g("norm_g", (4, 2, 1024)); ab = g("ada_b", (4, 6144))
    fw = g("ffn_conv_w", (4, 3, 5632)); fb = g("ffn_conv_b", (4, 5632))
    hw = g("hy_conv_w", (2, 3, 1536)); hb = g("hy_conv_b", (2, 1536)); hd = g("hy_d", (2, 2, 512))
    qn = g("mla_q_norm", (2, 512)); kn = g("mla_kv_norm", (2, 256))
    qh = g("mla_q_head_norm", (2, 192)); kh = g("mla_k_head_norm", (2, 192))
    b1 = g("hy_f_b1", (2, 64)); b2 = g("hy_f_b2", (2, 64)); fr = g("hy_f_freq", (2, 64))
    for l in range(4):
        for s in range(2):
            pv.add("ng%d%d" % (l, s), ng[l, s])
        pv.add("ab%d" % l, ab[l])
        for k in range(3):
            pv.add("fw%d%d" % (l, k), fw[l, k])
        pv.add("fb%d" % l, fb[l])
    for i in range(2):
        for k in range(3):
            pv.add("hw%d%d" % (i, k), hw[i, k])
        pv.add("hb%d" % i, hb[i])
        for o in range(2):
            pv.add("hd%d%d" % (i, o), hd[i, o])
        pv.add("qn%d" % i, qn[i]); pv.add("kn%d" % i, kn[i])
        pv.add("qhn%d" % i, qh[i, :128]); pv.add("qhr%d" % i, qh[i, 128:])
        pv.add("khn%d" % i, kh[i, :128]); pv.add("khr%d" % i, kh[i, 128:])
        pv.add("b1%d" % i, b1[i]); pv.add("b2%d" % i, b2[i]); pv.add("fr%d" % i, fr[i])
    return pv


_CONST = {}


def host_consts():
    if _CONST:
        return _CONST
    bf = ml_dtypes.bfloat16
    c = {}
    c["ident"] = np.eye(128, dtype=np.float32)
    t = np.arange(4096)
    row, col = t // 64, t % 64
    inv = 1.0 / (10000.0 ** (np.arange(0, 32, 2, dtype=np.float32) / 32))
    ang = np.zeros((64, 4096), np.float32)
    for p in range(64):
        pos = row if p < 32 else col
        ang[p] = pos.astype(np.float32) * inv[p % 16]
    c["ropec"] = np.cos(ang).astype(np.float32)
    c["ropes"] = np.sin(ang).astype(np.float32)
    RT = np.zeros((64, 64), np.float32)
    for base in (0, 32):
        for i in range(16):
            RT[base + 16 + i, base + i] = -1.0
            RT[base + i, base + 16 + i] = 1.0
    c["rotT"] = RT
    for L in (4096, 256):
        N = 2 * L
        tt = np.arange(L, dtype=np.float32)
        tn = tt / L
        bands = np.arange(1, 17, dtype=np.float32)
        angf = (2.0 * math.pi) * tn[:, None] * bands[None]
        z = np.concatenate([tn[:, None], np.sin(angf), np.cos(angf)], axis=-1).astype(np.float32)
        c["zf%d" % L] = np.ascontiguousarray(z.T)
        dist = np.abs(tt - L // 2) / L
        c["nd%d" % L] = np.ascontiguousarray((-dist).astype(np.float32).reshape(L // 128, 128).T)
        n = np.arange(L, dtype=np.int64)
        k = np.arange(L, dtype=np.int64)
        nch = L // 128
        m = ((2 * k[None, :] + 1) * n[:, None]) % (2 * N)
        th = (math.pi / N) * m.astype(np.float64)
        Fc = np.cos(th).astype(np.float32); Fs = np.sin(th).astype(np.float32)
        F = np.stack([Fc, Fs], 0).reshape(2, nch, 128, nch, 128)
        c["F%d" % L] = np.ascontiguousarray(F.transpose(3, 2, 1, 0, 4)).astype(bf)
        m = ((2 * k[:, None] + 1) * (n[None, :] + L // 2)) % (2 * N)
        th = (math.pi / N) * m.astype(np.float64)
        Gc = (np.cos(th) * (2.0 / N)).astype(np.float32); Gs = (np.sin(th) * (2.0 / N)).astype(np.float32)
        tw = min(512, L)
        kgn = max(1, nch // 8); kl = nch // kgn
        G = np.stack([Gc, Gs], 0).reshape(2, kgn, kl, 128, L // tw, tw)
        c["G%d" % L] = np.ascontiguousarray(G.transpose(4, 1, 3, 2, 0, 5)).astype(bf)
        if L == 4096:
            H = L // 2
            Fh = np.stack([Fc[:, :H], Fs[:, :H]], 0)
            Fh = Fh.reshape(2, 16, 128, 2, 16, 128)
            c["FE"] = np.ascontiguousarray(Fh.transpose(4, 2, 3, 1, 0, 5)).astype(bf)
            Gh = np.stack([Gc[:H], Gs[:H]], 0)
            Gh = Gh.reshape(2, 2, 8, 128, 4, 512, 2)
            c["GE"] = np.ascontiguousarray(Gh.transpose(4, 6, 1, 3, 2, 0, 5)).astype(bf)
            ndv = (-dist).astype(np.float32).reshape(16, 128, 2)
            c["ndE"] = np.ascontiguousarray(ndv.transpose(1, 2, 0).reshape(128, 32))
            del c["F4096"], c["G4096"], c["nd4096"]
    _CONST.update(c)
    return _CONST


WEIGHTS = ["ada_w", "mix_w_in", "hy_f_w1", "hy_f_w2", "hy_f_w3", "mix_w_out", "mla_w_dq", "mla_w_uq",
           "mla_w_dkv", "mla_w_ukv", "mla_w_o", "ffn_w_up", "ffn_w_down"]
WSHAPE = {"ada_w": (4, 1024, 6144), "mix_w_in": (2, 1024, 2560), "hy_f_w1": (2, 33, 64), "hy_f_w2": (2, 64, 64),
          "hy_f_w3": (2, 64, 1024), "mix_w_out": (2, 1024, 1024), "mla_w_dq": (2, 1024, 512),
          "mla_w_uq": (2, 512, 1536), "mla_w_dkv": (2, 1024, 320), "mla_w_ukv": (2, 256, 2048),
          "mla_w_o": (2, 1024, 1024), "ffn_w_up": (4, 1024, 5632), "ffn_w_down": (4, 2816, 1024)}


def build(dbg=(), depth=DEPTH):
    nc = bass.Bass("TRN2", target_bir_lowering=False)
    pvl = pv_layout()
    C = host_consts()

    def din(name, shape, dt=F32):
        return nc.dram_tensor(name, list(shape), dt, kind="ExternalInput").ap()

    def dscr(name, shape, dt):
        kind = "ExternalOutput" if name in dbg else "Internal"
        return nc.dram_tensor(name, list(shape), dt, kind=kind).ap()

    xs_d = din("xs", (4096, 1024)); xp_d = din("xp", (1024, 1024))
    cckv_d = din("cckv", (2, 256, 256)); ckr_d = din("ckr", (2, 256, 64))
    cond_d = din("condT", (128, 8, 2)); pv_d = din("pv", (128, pvl.n))
    sguT_d = din("sguT", (2, 128, 4, 128)); sgub_d = din("sgub", (2, 1, 512)); dec_d = din("decbc", (2, 128, 1024))
    W = {k: din(k, WSHAPE[k]) for k in WEIGHTS}
    cd = {k: din("c_" + k, v.shape, BF16 if v.dtype != np.float32 else F32) for k, v in C.items()}
    ys_d = nc.dram_tensor("ys", [4096, 1024], F32, kind="ExternalOutput").ap()
    yp_d = nc.dram_tensor("yp", [1024, 1024], F32, kind="ExternalOutput").ap()
    nckv_d = nc.dram_tensor("nckv", [4, 2, 256, 256], F32, kind="ExternalOutput").ap()
    nkr_d = nc.dram_tensor("nkr", [4, 2, 256, 64], F32, kind="ExternalOutput").ap()
    xT_d = dscr("xT", (1024, T), F32)
    act_d = dscr("actT", (DFF, T), BF16)
    pT_d = dscr("pT", (2048, T), BF16)
    vbtm_d = dscr("vbtm", (T, 512), BF16)
    aT_d = dscr("aT", (512, T), BF16)
    z1T_d = dscr("z1T", (512, T), BF16)
    zT_d = dscr("zT", (512, T), BF16)
    hf_d = {4096: dscr("hf4096", (2048, 4, 1024), F32), 256: dscr("hf256", (256, 2, 1024), F32)}
    oT_d = dscr("oT", (1024, T), BF16)
    qnT_d = dscr("qnT", (512, T), BF16)

    es = contextlib.ExitStack()
    with es:
        A = Arena(nc, es, 190 * 1024)
        psl = [es.enter_context(nc.psum_tensor("ps%d" % i, [128, 512], F32)) for i in range(8)]
        ps = [p_[:, :] for p_ in psl]
        P = Prog(nc)
        uid = [0]

        def U(s):
            uid[0] += 1
            return "%s#%d" % (s, uid[0])

        pvt = A.alloc([pvl.n], F32)
        ident = A.alloc([128], F32)
        ones = A.alloc([128], F32)
        onesb = A.alloc([128], BF16)
        epsT = A.alloc([1], F32)
        condT = A.alloc([8, 2], F32)
        scond = A.alloc([8, 2], BF16)
        mods = A.alloc([DEPTH, 48, 2], F32)
        gm = A.alloc([DEPTH, 2, 8, 2], F32)
        rotT = A.alloc([64], F32)
        P.op("sp", lambda e: e.dma_start(out=pvt, in_=pv_d), w=["pvt"], dma="c0")
        P.op("sp", lambda e: e.dma_start(out=ident, in_=cd["ident"]), w=["ident"], dma="c0")
        P.op("sp", lambda e: e.dma_start(out=condT, in_=cond_d), w=["condT"], dma="c0")
        P.op("sp", lambda e: e.dma_start(out=rotT[0:64, :], in_=cd["rotT"]), w=["rotT"], dma="c0")
        P.op("dve", lambda e: e.memset(ones, 1.0), w=["ones"])
        P.op("dve", lambda e: e.memset(onesb, 1.0), w=["onesb"])
        P.op("dve", lambda e: e.memset(epsT, EPS), w=["eps"])
        P.op("act", lambda e: e.activation(out=scond, in_=condT, func=AF.Silu), r=["condT"], w=["scond"])
        P.barrier()
        base_mark = A.mark()

        def pvc(name, j=0, parts=128):
            o, c = pvl.cols[name]
            return pvt[0:parts, o + j:o + j + 1]

        def phase_end():
            P.barrier()
            A.release(base_mark)

        def load_w(dst, src, key):
            P.op("pool", lambda e: e.dma_start(out=dst, in_=src), w=[key], dma="w:" + key)

        def rstd_from(psb, n, out_sb, psname):
            P.op("act", lambda e: e.activation(out=out_sb, in_=psb, func=AF.Sqrt, scale=1.0 / n, bias=epsT[0:out_sb.shape[0], 0:1]),
                 x=[psname], r=["eps"], w=[U("rs")])
            nm = P.ops
            P.op("dve", lambda e: e.reciprocal(out=out_sb, in_=out_sb), r=[], w=[])

        adab = A.alloc([2, 8, 1536], BF16)
        for l in range(depth):
            for pc in range(4):
                slot = (l * 4 + pc) % 2
                key = "adaw%d" % slot
                load_w(adab[:, slot], W["ada_w"][l].rearrange("(c p) n -> p c n", p=128)[:, :, pc * 1536:(pc + 1) * 1536], key)

                def mm(e, slot=slot, pc=pc):
                    last = None
                    for f in range(12):
                        for kc in range(8):
                            last = e.matmul(ps[0][:, (pc * 12 + f) * 2:(pc * 12 + f) * 2 + 2], lhsT=adab[:, slot, kc, f * 128:(f + 1) * 128],
                                            rhs=scond[:, kc, :], start=(kc == 0), stop=(kc == 7))
                    return last
                P.op("pe", mm, r=[key, "scond"], w=["ps0"])
            ao, _ = pvl.cols["ab%d" % l]
            for ci in range(2):
                P.op("dve", lambda e, l=l, ci=ci, ao=ao: e.tensor_tensor(out=mods[:, l, :, ci], in0=ps[0][:, ci:96:2], in1=pvt[:, ao:ao + 48], op=ALU.add),
                     x=["ps0"], r=["pvt"], w=["mods"])
            for s in range(2):
                go, _ = pvl.cols["ng%d%d" % (l, s)]
                for ci in range(2):
                    P.op("dve", lambda e, l=l, s=s, ci=ci, go=go: e.scalar_tensor_tensor(
                        out=gm[:, l, s, :, ci], in0=mods[:, l, (3 * s + 1) * 8:(3 * s + 2) * 8, ci], scalar=1.0,
                        in1=pvt[:, go:go + 8], op0=ALU.add, op1=ALU.mult), r=["mods", "pvt"], w=["gm"])
        phase_end()

        def mod_sh(l, s, dc, ci):
            return mods[:, l, (3 * s) * 8 + dc, ci:ci + 1]

        def mod_gate(l, s, dc, ci):
            return mods[:, l, (3 * s + 2) * 8 + dc, ci:ci + 1]

        def ci_of_tile(ti):
            return 0 if ti < 8 else 1

        def xT_tile_ap(ti):
            return xT_d.rearrange("(c p) t -> p c t", p=128)[:, :, ti * 512:(ti + 1) * 512]

        xin = [A.alloc([4, 1024], F32) for _ in range(2)]
        xTt = [A.alloc([8, 512], F32) for _ in range(2)]

        def x_rows(ti):
            if ti < 8:
                return xs_d[ti * 512:(ti + 1) * 512, :]
            return xp_d[(ti - 8) * 512:(ti - 7) * 512, :]

        def y_rows(ti):
            if ti < 8:
                return ys_d[ti * 512:(ti + 1) * 512, :]
            return yp_d[(ti - 8) * 512:(ti - 7) * 512, :]

        def ld_x(ti):
            P.op("sp", lambda e: e.dma_start(out=xin[ti % 2], in_=x_rows(ti).rearrange("(c p) d -> p c d", p=128)),
                 w=["xin%d" % (ti % 2)], dma="xin%d" % (ti % 2))
        ld_x(0)
        for ti in range(NT):
            if ti + 1 < NT:
                ld_x(ti + 1)
            s = ti % 2
            for dc in range(8):
                b = dc % 2

                def tr(e, s=s, dc=dc, b=b):
                    last = None
                    for tc in range(4):
                        last = e.transpose(ps[b][:, tc * 128:(tc + 1) * 128], xin[s][:, tc, dc * 128:(dc + 1) * 128], ident)
                    return last
                P.op("pe", tr, r=["xin%d" % s, "ident"], w=["ps%d" % b])
                eng = "act" if dc % 2 == 0 else "dve"
                if eng == "act":
                    P.op("act", lambda e, s=s, dc=dc, b=b: e.activation(out=xTt[s][:, dc, :], in_=ps[b], func=AF.Copy), x=["ps%d" % b], w=["xTt%d" % s])
                else:
                    P.op("dve", lambda e, s=s, dc=dc, b=b: e.tensor_copy(out=xTt[s][:, dc, :], in_=ps[b]), x=["ps%d" % b], w=["xTt%d" % s])
            P.op("sp", lambda e, s=s, ti=ti: e.dma_start(out=xT_tile_ap(ti), in_=xTt[s]), r=["xTt%d" % s], w=["d:xT"], dma="st:xTt%d" % s)
        phase_end()

        def prologue(l, s, hT):
            xt = [A.alloc([8, 512], F32) for _ in range(2)]
            sqs = [A.alloc([8, 512], BF16) for _ in range(2)]
            rss = [A.alloc([512], F32) for _ in range(2)]
            tmp = [A.alloc([512], F32) for _ in range(2)]

            def ld(ti):
                P.op("sp", lambda e: e.dma_start(out=xt[ti % 2], in_=xT_tile_ap(ti)), r=["d:xT"], w=["pxt%d" % (ti % 2)], dma="pxt%d" % (ti % 2))
            ld(0)
            for ti in range(NT):
                if ti + 1 < NT:
                    ld(ti + 1)
                b = ti % 2
                ci = ci_of_tile(ti)
                sq, rs = sqs[b], rss[b]
                pbk = 2 + b
                P.op("act", lambda e, b=b, sq=sq: e.activation(out=sq, in_=xt[b], func=AF.Square), r=["pxt%d" % b], w=["psq%d" % b])

                def mm(e, sq=sq, pbk=pbk):
                    last = None
                    for dc in range(8):
                        last = e.matmul(ps[pbk], lhsT=onesb, rhs=sq[:, dc, :], start=(dc == 0), stop=(dc == 7))
                    return last
                P.op("pe", mm, r=["psq%d" % b, "onesb"], w=["ps%d" % pbk])
                P.op("act", lambda e, rs=rs, pbk=pbk: e.activation(out=rs, in_=ps[pbk], func=AF.Sqrt, scale=1.0 / D, bias=epsT[:, 0:1]), x=["ps%d" % pbk], r=["eps"], w=["prs%d" % b])
                P.op("dve", lambda e, rs=rs: e.reciprocal(out=rs, in_=rs), r=["prs%d" % b], w=["prs%d" % b])
                for dc in range(8):
                    tb = dc % 2
                    P.op("dve", lambda e, b=b, dc=dc, tb=tb, rs=rs: e.tensor_tensor(out=tmp[tb], in0=xt[b][:, dc, :], in1=rs, op=ALU.mult),
                         r=["pxt%d" % b, "prs%d" % b], w=["ptmp%d" % tb])
                    P.op("act", lambda e, dc=dc, tb=tb, ti=ti, ci=ci: e.activation(
                        out=hT[:, dc, ti * 512:(ti + 1) * 512], in_=tmp[tb], func=AF.Identity,
                        scale=gm[:, l, s, dc, ci:ci + 1], bias=mod_sh(l, s, dc, ci)), r=["ptmp%d" % tb, "gm", "mods"], w=["hT"])

        def out_proj(l, s, wsrc, nk, in_tiles_fn):
            wt = A.alloc([nk, 1024], BF16)
            load_w(wt, wsrc.rearrange("(c p) n -> p c n", p=128), "opw")
            it = [A.alloc([nk, 512], BF16) for _ in range(2)]
            xt = [A.alloc([8, 512], F32) for _ in range(2)]

            def ld(ti):
                b = ti % 2
                in_tiles_fn(ti, it[b], "opin%d" % b)
                P.op("sp", lambda e: e.dma_start(out=xt[b], in_=xT_tile_ap(ti)), r=["d:xT"], w=["opx%d" % b], dma="opx%d" % b)
            ld(0)
            for ti in range(NT):
                if ti + 1 < NT:
                    ld(ti + 1)
                b = ti % 2
                ci = ci_of_tile(ti)
                for dc in range(8):
                    pb = 4 + dc % 4

                    def mm(e, b=b, dc=dc, pb=pb):
                        last = None
                        for kc in range(nk):
                            last = e.matmul(ps[pb], lhsT=wt[:, kc, dc * 128:(dc + 1) * 128], rhs=it[b][:, kc, :], start=(kc == 0), stop=(kc == nk - 1))
                        return last
                    P.op("pe", mm, r=["opw", "opin%d" % b], w=["ps%d" % pb])
                    P.op("dve", lambda e, b=b, dc=dc, pb=pb, ci=ci: e.scalar_tensor_tensor(
                        out=xt[b][:, dc, :], in0=ps[pb], scalar=mod_gate(l, s, dc, ci), in1=xt[b][:, dc, :], op0=ALU.mult, op1=ALU.add),
                        x=["ps%d" % pb], r=["mods"], w=["opx%d" % b])
                P.op("sp", lambda e, b=b, ti=ti: e.dma_start(out=xT_tile_ap(ti), in_=xt[b]), r=["opx%d" % b], w=["d:xT"], dma="st:opx%d" % b)

        def dwconv(raw, co, wname, bname, j, segs, rawname="raw"):
            P.op("dve", lambda e: e.tensor_scalar(out=co, in0=raw, scalar1=pvc(wname + "1", j), scalar2=pvc(bname, j), op0=ALU.mult, op1=ALU.add),
                 r=[rawname, "pvt"], w=["co"])
            for (s0, s1) in segs:
                P.op("dve", lambda e, s0=s0, s1=s1: e.scalar_tensor_tensor(out=co[:, s0 + 1:s1], in0=raw[:, s0:s1 - 1], scalar=pvc(wname + "0", j),
                                                                           in1=co[:, s0 + 1:s1], op0=ALU.mult, op1=ALU.add), r=[rawname, "pvt"], w=["co"])
                P.op("dve", lambda e, s0=s0, s1=s1: e.scalar_tensor_tensor(out=co[:, s0:s1 - 1], in0=raw[:, s0 + 1:s1], scalar=pvc(wname + "2", j),
                                                                           in1=co[:, s0:s1 - 1], op0=ALU.mult, op1=ALU.add), r=[rawname, "pvt"], w=["co"])

        def gemm_rows(wt, hT, raw, wkey, rawname="raw"):
            for ti in range(NT):
                pb = ti % 4

                def mm(e, ti=ti, pb=pb):
                    last = None
                    for kc in range(8):
                        last = e.matmul(ps[pb], lhsT=wt[:, kc, :], rhs=hT[:, kc, ti * 512:(ti + 1) * 512], start=(kc == 0), stop=(kc == 7))
                    return last
                P.op("pe", mm, r=[wkey, "hT"], w=["ps%d" % pb])
                P.op("act", lambda e, ti=ti, pb=pb: e.activation(out=raw[:, ti * 512:(ti + 1) * 512], in_=ps[pb], func=AF.Copy), x=["ps%d" % pb], w=[rawname])

        def ffn(l):
            hT = A.alloc([8, T], BF16)
            m0 = A.mark()
            prologue(l, 1, hT)
            P.barrier()
            A.release(m0)
            wt = [A.alloc([8, 128], BF16) for _ in range(4)]
            raws = [A.alloc([T], F32) for _ in range(2)]
            co = A.alloc([T], F32)
            sg = A.alloc([T], BF16)
            ab = [A.alloc([T], BF16) for _ in range(2)]
            wup = W["ffn_w_up"][l].rearrange("(c p) n -> p c n", p=128)
            NJ = DFF // 128

            def ldw(j):
                for h in range(2):
                    k = (j % 2) * 2 + h
                    load_w(wt[k], wup[:, :, h * DFF + j * 128:h * DFF + (j + 1) * 128], "fw%d" % k)

            def gemm_n(n):
                j, h = n // 2, n % 2
                if h == 0 and j + 1 < NJ:
                    ldw(j + 1)
                k = (j % 2) * 2 + h
                gemm_rows(wt[k], hT, raws[n % 2], "fw%d" % k, "raw%d" % (n % 2))

            def post_n(n):
                j, h = n // 2, n % 2
                dwconv(raws[n % 2], co, "fw%d" % l, "fb%d" % l, h * 22 + j, SEGS, "raw%d" % (n % 2))
                if h == 0:
                    P.op("act", lambda e: e.activation(out=sg, in_=co, func=AF.Silu), r=["co"], w=["sg"])
                else:
                    a = ab[j % 2]
                    P.op("dve", lambda e, a=a: e.tensor_tensor(out=a, in0=co, in1=sg, op=ALU.mult), r=["co", "sg"], w=["ab%d" % (j % 2)])
                    P.op("sp", lambda e, a=a, j=j: e.dma_start(out=act_d[j * 128:(j + 1) * 128, :], in_=a), r=["ab%d" % (j % 2)], w=["d:act"],
                         dma="st:ab%d" % (j % 2))
            ldw(0)
            gemm_n(0)
            for n in range(2 * NJ):
                if n + 1 < 2 * NJ:
                    gemm_n(n + 1)
                post_n(n)
            phase_end()

            def in_tiles(ti, dst, key):
                P.op("sp", lambda e: e.dma_start(out=dst, in_=act_d.rearrange("(c p) t -> p c t", p=128)[:, :, ti * 512:(ti + 1) * 512]),
                     r=["d:act"], w=[key], dma=key)
            out_proj(l, 1, W["ffn_w_down"][l], 22, in_tiles)
            phase_end()

        def gelu_from(src_ap, src_res, src_x, out_ap, out_res, n, graw, gt, tag):
            raw_, t_ = graw[:, 0:n], gt[:, 0:n]
            kr_, kt_ = "glraw" + tag, "glt" + tag
            P.op("act", lambda e: e.activation(out=raw_, in_=src_ap, func=AF.Copy), x=src_x, r=src_res, w=[kr_])
            P.op("dve", lambda e: e.tensor_tensor(out=t_, in0=raw_, in1=raw_, op=ALU.mult), r=[kr_], w=[kt_])
            P.op("dve", lambda e: e.tensor_scalar(out=t_, in0=t_, scalar1=0.044715, scalar2=1.0, op0=ALU.mult, op1=ALU.add), r=[kt_], w=[kt_])
            P.op("dve", lambda e: e.tensor_tensor(out=t_, in0=t_, in1=raw_, op=ALU.mult), r=[kt_, kr_], w=[kt_])
            P.op("act", lambda e: e.activation(out=t_, in_=t_, func=AF.Sigmoid, scale=2.0 * math.sqrt(2.0 / math.pi)), r=[kt_], w=[kt_])
            P.op("dve", lambda e: e.tensor_tensor(out=out_ap, in0=t_, in1=raw_, op=ALU.mult), r=[kt_, kr_], w=out_res)


        def even_mixer(l):
            i = l // 2
            hT = A.alloc([8, T], BF16)
            m0 = A.mark()
            prologue(l, 0, hT)
            P.barrier()
            A.release(m0)
            wt = [A.alloc([8, 128], BF16) for _ in range(2)]
            raws = [A.alloc([T], F32) for _ in range(2)]
            co = A.alloc([T], F32)
            ob = [A.alloc([T], BF16) for _ in range(2)]
            vtm = [A.alloc([4, 128], BF16) for _ in range(2)]
            win = W["mix_w_in"][i].rearrange("(c p) n -> p c n", p=128)
            cols = [c * 128 for c in range(4)] + [1024 + c * 128 for c in range(12)]

            def gemm_q(q):
                load_w(wt[q % 2], win[:, :, cols[q]:cols[q] + 128], "mw%d" % (q % 2))
                gemm_rows(wt[q % 2], hT, raws[q % 2], "mw%d" % (q % 2), "raw%d" % (q % 2))

            def post_q(q):
                raw = raws[q % 2]
                rn = "raw%d" % (q % 2)
                o_ = ob[q % 2]
                okey = "ob%d" % (q % 2)
                if q < 4:
                    P.op("dve", lambda e: e.tensor_tensor(out=co, in0=raw, in1=raw, op=ALU.mult), r=[rn], w=["co"])
                    P.op("dve", lambda e: e.tensor_scalar(out=co, in0=co, scalar1=0.044715, scalar2=1.0, op0=ALU.mult, op1=ALU.add), r=["co"], w=["co"])
                    P.op("dve", lambda e: e.tensor_tensor(out=co, in0=co, in1=raw, op=ALU.mult), r=["co", rn], w=["co"])
                    P.op("act", lambda e: e.activation(out=co, in_=co, func=AF.Sigmoid, scale=2.0 * math.sqrt(2.0 / math.pi)), r=["co"], w=["co"])
                    P.op("dve", lambda e: e.tensor_tensor(out=o_, in0=co, in1=raw, op=ALU.mult), r=["co", rn], w=[okey])
                else:
                    dwconv(raw, co, "hw%d" % i, "hb%d" % i, q - 4, SEGS, rn)
                    P.op("act", lambda e: e.activation(out=o_, in_=co, func=AF.Copy), r=["co"], w=[okey])
                    if q < 8:
                        for tc in range(T // 128):
                            pb = 4 + tc % 2
                            P.op("pe", lambda e, tc=tc, pb=pb: e.transpose(ps[pb][:, 0:128], co[:, tc * 128:(tc + 1) * 128], ident), r=["co", "ident"], w=["ps%d" % pb])
                            vb_ = vtm[tc % 2]
                            P.op("dve", lambda e, pb=pb, vb_=vb_: e.tensor_copy(out=vb_[:, 0, :], in_=ps[pb][:, 0:128]), x=["ps%d" % pb], w=["vtm%d" % (tc % 2)])
                            P.op("sp", lambda e, tc=tc, vb_=vb_: e.dma_start(out=vbtm_d[tc * 128:(tc + 1) * 128, (q - 4) * 128:(q - 3) * 128], in_=vb_[:, 0, :]),
                                 r=["vtm%d" % (tc % 2)], w=["d:vbtm"], dma="st:vtm%d" % (tc % 2))
                P.op("sp", lambda e: e.dma_start(out=pT_d[q * 128:(q + 1) * 128, :], in_=o_), r=[okey], w=["d:pT"], dma="st:" + okey)
            gemm_q(0)
            for q in range(16):
                if q + 1 < 16:
                    gemm_q(q + 1)
                post_q(q)
            P.barrier()
            A.release(m0)
            wv = A.alloc([8, 512], BF16)
            load_w(wv, win[:, :, 512:1024], "wv")
            sgw = A.alloc([4, 128], BF16)
            load_w(sgw, sguT_d[i], "sgw")
            sgb = A.alloc([512], BF16, parts=1)
            load_w(sgb, sgub_d[i], "sgb")
            glr = [A.alloc([512], F32) for _ in range(2)]
            glt = [A.alloc([512], F32) for _ in range(2)]
            vbs = [A.alloc([512], BF16) for _ in range(2)]
            ut = [A.alloc([4, 512], BF16) for _ in range(2)]
            at = [A.alloc([4, 512], BF16) for _ in range(2)]

            def ldu(ti):
                P.op("sp", lambda e: e.dma_start(out=ut[ti % 2], in_=pT_d[0:512, :].rearrange("(c p) t -> p c t", p=128)[:, :, ti * 512:(ti + 1) * 512]),
                     r=["d:pT"], w=["ut%d" % (ti % 2)], dma="ut%d" % (ti % 2))
            ldu(0)

            def sgu_mm(n):
                t0 = n * 128
                pa = 2 * (n % 2)

                def mm(e):
                    last = None
                    for kc in range(8):
                        last = e.matmul(ps[pa], lhsT=hT[:, kc, t0:t0 + 128], rhs=wv[:, kc, :], start=(kc == 0), stop=(kc == 7))
                    return last
                P.op("pe", mm, r=["hT", "wv"], w=["ps%d" % pa])

            def sgu_post(n):
                ti, tc = divmod(n, 4)
                b = ti % 2
                pp = n % 2
                pa, pbb = 2 * pp, 2 * pp + 1
                vb16 = vbs[pp]
                vk = "vb16_%d" % pp
                gelu_from(ps[pa], [], ["ps%d" % pa], vb16, [vk], 512, glr[pp], glt[pp], str(pp))

                def mm2(e):
                    last = None
                    for g in range(4):
                        e.matmul(ps[pbb][:, g * 128:(g + 1) * 128], lhsT=vb16[:, g * 128:(g + 1) * 128], rhs=sgw[:, g, :], start=True, stop=False)
                        last = e.matmul(ps[pbb][:, g * 128:(g + 1) * 128], lhsT=onesb[0:1, :], rhs=sgb[0:1, g * 128:(g + 1) * 128], start=False, stop=True)
                    return last
                P.op("pe", mm2, r=[vk, "sgw", "sgb", "onesb"], w=["ps%d" % pbb])
                P.op("dve", lambda e: e.tensor_tensor(out=at[b][:, :, tc * 128:(tc + 1) * 128], in0=ut[b][:, :, tc * 128:(tc + 1) * 128],
                                                      in1=ps[pbb].rearrange("p (g q) -> p g q", g=4), op=ALU.mult),
                     x=["ps%d" % pbb], r=["ut%d" % b], w=["at%d" % b])
                if tc == 3:
                    P.op("sp", lambda e: e.dma_start(out=aT_d.rearrange("(c p) t -> p c t", p=128)[:, :, ti * 512:(ti + 1) * 512], in_=at[b]),
                         r=["at%d" % b], w=["d:aT"], dma="st:at%d" % b)
            NCHK = T // 128
            sgu_mm(0)
            for n in range(NCHK):
                if n % 4 == 0 and n // 4 + 1 < NT:
                    ldu(n // 4 + 1)
                if n + 1 < NCHK:
                    sgu_mm(n + 1)
                sgu_post(n)
            phase_end()
            for L in (4096, 256):
                hyena_filters(i, L)
                phase_end()
            hyena_conv_eo(i, 0)
            phase_end()
            for sq_ in range(4):
                hyena_conv(i, 256, 4096 + 256 * sq_, tg="_%d" % (sq_ % 2))
            phase_end()

            def in_tiles(ti, dst, key):
                P.op("sp", lambda e: e.dma_start(out=dst[:, 0:4, :], in_=aT_d.rearrange("(c p) t -> p c t", p=128)[:, :, ti * 512:(ti + 1) * 512]),
                     r=["d:aT"], w=[key], dma=key)
                P.op("sp", lambda e: e.dma_start(out=dst[:, 4:8, :], in_=zT_d.rearrange("(c p) t -> p c t", p=128)[:, :, ti * 512:(ti + 1) * 512]),
                     r=["d:zT"], w=[key], dma=key)
            out_proj(l, 0, W["mix_w_out"][i], 8, in_tiles)
            phase_end()

        def sin_rr(arg, out_ap, n, res_in, res_out):
            a_, b_ = sr_a[0:64, 0:n], sr_b[0:64, 0:n]
            P.op("act", lambda e: e.activation(out=a_, in_=arg, func=AF.Sin, scale=0.5), r=res_in, w=["sra"])
            P.op("act", lambda e: e.activation(out=b_, in_=arg, func=AF.Sin, scale=0.25), r=res_in, w=["srb"])
            P.op("dve", lambda e: e.tensor_tensor(out=b_, in0=b_, in1=b_, op=ALU.mult), r=["srb"], w=["srb"])
            P.op("dve", lambda e: e.tensor_scalar(out=b_, in0=b_, scalar1=-4.0, scalar2=2.0, op0=ALU.mult, op1=ALU.add), r=["srb"], w=["srb"])
            P.op("dve", lambda e: e.tensor_tensor(out=out_ap, in0=a_, in1=b_, op=ALU.mult), r=["sra", "srb"], w=res_out)

        sr_a = sr_b = None

        def hyena_filters(i, L):
            nonlocal sr_a, sr_b
            nch = L // 128
            h2 = A.alloc([L], F32)
            hf_tm = A.alloc([nch, 1024], BF16)
            w3 = A.alloc([1024], F32)
            dec = A.alloc([1024], F32)
            nd = A.alloc([nch], F32)
            m1 = A.mark()
            zf = A.alloc([L], F32)
            w1 = A.alloc([64], F32)
            w2 = A.alloc([64], F32)
            frb = A.alloc([2], F32)
            h1 = A.alloc([L], F32)
            arg = A.alloc([512], F32)
            sr_a = A.alloc([512], F32)
            sr_b = A.alloc([512], F32)
            P.op("sp", lambda e: e.dma_start(out=zf[0:33, :], in_=cd["zf%d" % L]), w=["zf"], dma="hfl")
            P.op("sp", lambda e: e.dma_start(out=w1[0:33, :], in_=W["hy_f_w1"][i]), w=["w1"], dma="hfl")
            P.op("sp", lambda e: e.dma_start(out=w2[0:64, :], in_=W["hy_f_w2"][i]), w=["w2"], dma="hfl")
            P.op("sp", lambda e: e.dma_start(out=w3[0:64, :], in_=W["hy_f_w3"][i]), w=["w3"], dma="hfl")
            P.op("sp", lambda e: e.dma_start(out=dec, in_=dec_d[i]), w=["dec"], dma="hfl")
            P.op("sp", lambda e: e.dma_start(out=nd, in_=cd["ndE" if L == 4096 else "nd%d" % L]), w=["nd"], dma="hfl")
            P.op("act", lambda e: e.activation(out=dec, in_=dec, func=AF.Abs), r=["dec"], w=["dec"])
            for q, bn in enumerate(("b1", "b2")):
                P.op("dve", lambda e, q=q, bn=bn: e.tensor_tensor(out=frb[0:64, q:q + 1], in0=pvc("fr%d" % i, 0, 64), in1=pvc("%s%d" % (bn, i), 0, 64), op=ALU.mult),
                     r=["pvt"], w=["frb"])
            tw = min(512, L)
            for (wmat, kk, src, dst, q) in ((w1, 33, zf, h1, 0), (w2, 64, h1, h2, 1)):
                for tt in range(L // tw):
                    P.op("pe", lambda e, wmat=wmat, kk=kk, src=src, tt=tt: e.matmul(ps[0][0:64, 0:tw], lhsT=wmat[0:kk, :], rhs=src[0:kk, tt * tw:(tt + 1) * tw], start=True, stop=True),
                         r=["w1", "w2", "zf", "h1"], w=["ps0"])
                    P.op("dve", lambda e, q=q: e.tensor_scalar(out=arg[0:64, 0:tw], in0=ps[0][0:64, 0:tw], scalar1=pvc("fr%d" % i, 0, 64), scalar2=frb[0:64, q:q + 1],
                                                               op0=ALU.mult, op1=ALU.add), x=["ps0"], r=["pvt", "frb"], w=["arg"])
                    sin_rr(arg[0:64, 0:tw], dst[0:64, tt * tw:(tt + 1) * tw], tw, ["arg"], ["h1" if q == 0 else "h2"])
            P.barrier()
            A.release(m1)
            win_ = A.alloc([1024], F32)
            hwf = A.alloc([1024], F32)
            hab = A.alloc([1024], F32)
            rec = A.alloc([1024], F32)
            EO = (L == 4096)
            for sc in range(nch):
                if EO:
                    par_, mc_ = divmod(sc, 16)
                    h2s = h2[0:64, 256 * mc_ + par_:256 * mc_ + par_ + 255:2]
                else:
                    h2s = h2[0:64, sc * 128:(sc + 1) * 128]

                def mm(e, h2s=h2s):
                    e.matmul(ps[1], lhsT=h2s, rhs=w3[0:64, 0:512], start=True, stop=True)
                    return e.matmul(ps[2], lhsT=h2s, rhs=w3[0:64, 512:1024], start=True, stop=True)
                P.op("pe", mm, r=["h2", "w3"], w=["ps1", "ps2"])
                P.op("act", lambda e, sc=sc: e.activation(out=win_, in_=dec, func=AF.Exp, scale=nd[:, sc:sc + 1]), r=["dec", "nd"], w=["win"])
                for hh in range(2):
                    P.op("dve", lambda e, hh=hh: e.tensor_tensor(out=hwf[:, hh * 512:(hh + 1) * 512], in0=ps[1 + hh], in1=win_[:, hh * 512:(hh + 1) * 512], op=ALU.mult),
                         x=["ps%d" % (1 + hh)], r=["win"], w=["hwf"])
                P.op("act", lambda e: e.activation(out=hab, in_=hwf, func=AF.Abs), r=["hwf"], w=["hab"])
                P.op("dve", lambda e, sc=sc: e.tensor_copy(out=hf_tm[:, sc, :], in_=hwf), r=["hwf"], w=["hftm"])

                def mm3(e, sc=sc):
                    e.matmul(ps[3], lhsT=ones, rhs=hab[:, 0:512], start=(sc == 0), stop=(sc == nch - 1))
                    return e.matmul(ps[4], lhsT=ones, rhs=hab[:, 512:1024], start=(sc == 0), stop=(sc == nch - 1))
                P.op("pe", mm3, r=["hab", "ones"], w=["ps3", "ps4"])
            for hh in range(2):
                P.op("dve", lambda e, hh=hh: e.tensor_scalar(out=rec[:, hh * 512:(hh + 1) * 512], in0=ps[3 + hh], scalar1=EPS, scalar2=None, op0=ALU.add),
                     x=["ps%d" % (3 + hh)], w=["rec"])
            P.op("dve", lambda e: e.reciprocal(out=rec, in_=rec), r=["rec"], w=["rec"])
            if EO:
                ft = [A.alloc([2, 16, 2, 128], BF16) for _ in range(2)]
                hfo = [A.alloc([4, 1024], F32) for _ in range(2)]
                osb = [A.alloc([512], F32) for _ in range(2)]
                tf = A.alloc([512], F32)

                def ldfe(kc):
                    P.op("sp", lambda e: e.dma_start(out=ft[kc % 2], in_=cd["FE"][kc]), w=["ft%d" % (kc % 2)], dma="ft%d" % (kc % 2))
                ldfe(0)
                for kc in range(16):
                    if kc + 1 < 16:
                        ldfe(kc + 1)
                    b = kc % 2
                    for hh in range(2):
                        hs = slice(hh * 512, (hh + 1) * 512)
                        bk = 4 * hh

                        def mm(e, b=b, hs=hs, bk=bk):
                            last = None
                            for bank, (par, cs) in enumerate(((0, 0), (1, 0), (0, 1), (1, 1))):
                                for mc in range(16):
                                    last = e.matmul(ps[bk + bank], lhsT=ft[b][:, par, mc, cs, :], rhs=hf_tm[:, par * 16 + mc, hs], start=(mc == 0), stop=(mc == 15))
                            return last
                        P.op("pe", mm, r=["ft%d" % b, "hftm"], w=["ps%d" % (bk + q_) for q_ in range(4)])
                        P.op("act", lambda e, bk=bk: e.activation(out=osb[0], in_=ps[bk + 1], func=AF.Copy), x=["ps%d" % (bk + 1)], w=["osb0"])
                        P.op("act", lambda e, bk=bk: e.activation(out=osb[1], in_=ps[bk + 3], func=AF.Copy), x=["ps%d" % (bk + 3)], w=["osb1"])
                        for slot, (pe_, ob_, op_, rev) in enumerate(((0, 0, ALU.add, False), (2, 1, ALU.add, False), (0, 0, ALU.subtract, False), (2, 1, ALU.subtract, True))):
                            pe_ = bk + pe_
                            if rev:
                                P.op("dve", lambda e, pe_=pe_, ob_=ob_: e.tensor_tensor(out=tf, in0=osb[ob_], in1=ps[pe_], op=ALU.subtract), x=["ps%d" % pe_], r=["osb%d" % ob_], w=["tf"])
                            else:
                                P.op("dve", lambda e, pe_=pe_, ob_=ob_, op_=op_: e.tensor_tensor(out=tf, in0=ps[pe_], in1=osb[ob_], op=op_), x=["ps%d" % pe_], r=["osb%d" % ob_], w=["tf"])
                            P.op("dve", lambda e, b=b, slot=slot, hs=hs: e.tensor_tensor(out=hfo[b][:, slot, hs], in0=tf, in1=rec[:, hs], op=ALU.mult), r=["tf", "rec"], w=["hfo%d" % b])
                    P.op("sp", lambda e, b=b, kc=kc: e.dma_start(out=hf_d[L][kc * 128:(kc + 1) * 128], in_=hfo[b]), r=["hfo%d" % b], w=["d:hf"], dma="st:hfo%d" % b)
                return
            ft = [A.alloc([nch, 2, 128], BF16) for _ in range(2)]
            hfo = [A.alloc([2, 1024], F32) for _ in range(2)]

            def ldf(kc):
                P.op("sp", lambda e: e.dma_start(out=ft[kc % 2], in_=cd["F%d" % L][kc]), w=["ft%d" % (kc % 2)], dma="ft%d" % (kc % 2))
            ldf(0)
            for kc in range(nch):
                if kc + 1 < nch:
                    ldf(kc + 1)
                b = kc % 2
                for cs in range(2):
                    for hh in range(2):
                        pb = 5 + (cs * 2 + hh) % 3

                        def mm(e, b=b, cs=cs, hh=hh, pb=pb):
                            last = None
                            for sc in range(nch):
                                last = e.matmul(ps[pb], lhsT=ft[b][:, sc, cs, :], rhs=hf_tm[:, sc, hh * 512:(hh + 1) * 512], start=(sc == 0), stop=(sc == nch - 1))
                            return last
                        P.op("pe", mm, r=["ft%d" % b, "hftm"], w=["ps%d" % pb])
                        P.op("dve", lambda e, b=b, cs=cs, hh=hh, pb=pb: e.tensor_tensor(out=hfo[b][:, cs, hh * 512:(hh + 1) * 512], in0=ps[pb],
                                                                                      in1=rec[:, hh * 512:(hh + 1) * 512], op=ALU.mult),
                             x=["ps%d" % pb], r=["rec"], w=["hfo%d" % b])
                P.op("sp", lambda e, b=b, kc=kc: e.dma_start(out=hf_d[L][kc * 128:(kc + 1) * 128], in_=hfo[b]), r=["hfo%d" % b], w=["d:hf"], dma="st:hfo%d" % b)

        def hyena_conv_eo(i, s0):
            L = 4096
            intm = A.alloc([2, 16, 512], BF16)
            Ypq = A.alloc([16, 4, 512], BF16)
            fg = [A.alloc([8192], BF16) for _ in range(2)]
            ftv = [f.rearrange("p (a m c k) -> p a m c k", a=2, m=16, c=2, k=128) for f in fg]
            gtv = [f.rearrange("p (l c t) -> p l c t", l=8, c=2, t=512) for f in fg]
            hft = [A.alloc([4, 512], F32) for _ in range(2)]
            TA, TB, T1, T2, T3, T4, T5, T6 = [A.alloc([512], F32) for _ in range(8)]
            dti = [A.alloc([1024], BF16) for _ in range(2)]
            xti = [A.alloc([1024], BF16) for _ in range(2)]
            zo = A.alloc([4, 1024], BF16)
            zf32 = A.alloc([512], F32)
            vsrc = vbtm_d[s0:s0 + L, :].rearrange("(mc p two) n -> two p mc n", p=128, two=2)
            for par in range(2):
                P.op("sp", lambda e, par=par: e.dma_start(out=intm[:, par], in_=vsrc[par]), r=["d:vbtm"], w=["intm"], dma="intm")

            def tt_(out, a, b, op, xa=(), xb=(), ra=(), rb=(), wname=None):
                P.op("dve", lambda e: e.tensor_tensor(out=out, in0=a, in1=b, op=op), x=list(xa) + list(xb), r=list(ra) + list(rb), w=[wname])

            def run_order(o):
                dsrc = pT_d[512:1024, :] if o == 0 else z1T_d
                xsrc = pT_d[1024 + 512 * o:1536 + 512 * o, :]
                zdst = z1T_d if o == 0 else zT_d

                def ldf(kc):
                    b = kc % 2
                    P.op("sp", lambda e: e.dma_start(out=fg[b], in_=cd["FE"][kc].rearrange("p a m c k -> p (a m c k)")), w=["fg%d" % b], dma="fg%d" % b)
                    P.op("sp", lambda e: e.dma_start(out=hft[b], in_=hf_d[L][kc * 128:(kc + 1) * 128, :, o * 512:(o + 1) * 512]), r=["d:hf"], w=["hft%d" % b], dma="hft%d" % b)
                ldf(0)
                for kc in range(16):
                    if kc + 1 < 16:
                        ldf(kc + 1)
                    b = kc % 2

                    bk = 4 * (kc % 2)

                    def mm(e, b=b, bk=bk):
                        last = None
                        for bank, (par, cs) in enumerate(((0, 0), (1, 0), (0, 1), (1, 1))):
                            for mc in range(16):
                                last = e.matmul(ps[bk + bank], lhsT=ftv[b][:, par, mc, cs, :], rhs=intm[:, par, mc, :], start=(mc == 0), stop=(mc == 15))
                        return last
                    P.op("pe", mm, r=["fg%d" % b, "intm"], w=["ps%d" % (bk + q_) for q_ in range(4)])
                    pE0, pE1, pE2, pE3 = (ps[bk + q_] for q_ in range(4))
                    nE = ["ps%d" % (bk + q_) for q_ in range(4)]
                    hk = "hft%d" % b
                    Hc, Hs, Hc2, Hs2 = (hft[b][:, q_, :] for q_ in range(4))
                    P.op("act", lambda e, pE1=pE1: e.activation(out=TA, in_=pE1, func=AF.Copy), x=[nE[1]], w=["TA"])
                    P.op("act", lambda e, pE3=pE3: e.activation(out=TB, in_=pE3, func=AF.Copy), x=[nE[3]], w=["TB"])
                    tt_(T1, pE0, TA, ALU.add, xa=[nE[0]], rb=["TA"], wname="T1")
                    tt_(T2, pE0, TA, ALU.subtract, xa=[nE[0]], rb=["TA"], wname="T2")
                    tt_(T3, pE2, TB, ALU.add, xa=[nE[2]], rb=["TB"], wname="T3")
                    tt_(T4, TB, pE2, ALU.subtract, xb=[nE[2]], ra=["TB"], wname="T4")
                    tt_(TA, T1, Hc, ALU.mult, ra=["T1"], rb=[hk], wname="TA")
                    tt_(TB, T3, Hs, ALU.mult, ra=["T3"], rb=[hk], wname="TB")
                    tt_(T5, TA, TB, ALU.subtract, ra=["TA"], rb=["TB"], wname="T5")
                    tt_(TA, T1, Hs, ALU.mult, ra=["T1"], rb=[hk], wname="TA")
                    tt_(TB, T3, Hc, ALU.mult, ra=["T3"], rb=[hk], wname="TB")
                    tt_(T6, TA, TB, ALU.add, ra=["TA"], rb=["TB"], wname="T6")
                    tt_(TA, T2, Hc2, ALU.mult, ra=["T2"], rb=[hk], wname="TA")
                    tt_(TB, T4, Hs2, ALU.mult, ra=["T4"], rb=[hk], wname="TB")
                    tt_(T1, TA, TB, ALU.subtract, ra=["TA"], rb=["TB"], wname="T1")
                    tt_(TA, T2, Hs2, ALU.mult, ra=["T2"], rb=[hk], wname="TA")
                    tt_(TB, T4, Hc2, ALU.mult, ra=["T4"], rb=[hk], wname="TB")
                    tt_(T3, TA, TB, ALU.add, ra=["TA"], rb=["TB"], wname="T3")
                    tt_(Ypq[:, kc, 0, :], T5, T1, ALU.add, ra=["T5"], rb=["T1"], wname="Y")
                    tt_(Ypq[:, kc, 1, :], T6, T3, ALU.subtract, ra=["T6"], rb=["T3"], wname="Y")
                    tt_(Ypq[:, kc, 2, :], T5, T1, ALU.subtract, ra=["T5"], rb=["T1"], wname="Y")
                    tt_(Ypq[:, kc, 3, :], T6, T3, ALU.add, ra=["T6"], rb=["T3"], wname="Y")
                seq = [(tt, par, kg) for tt in range(4) for par in range(2) for kg in range(2)]

                def ldg(q):
                    tt, par, kg = seq[q]
                    P.op("sp", lambda e: e.dma_start(out=fg[q % 2], in_=cd["GE"][tt, par, kg].rearrange("p l c t -> p (l c t)")), w=["fg%d" % (q % 2)], dma="fg%d" % (q % 2))
                ldg(0)
                for q, (tt, par, kg) in enumerate(seq):
                    if q + 1 < len(seq):
                        ldg(q + 1)
                    b = q % 2

                    def mm(e, b=b, kg=kg, par=par):
                        last = None
                        for klc in range(8):
                            kc = kg * 8 + klc
                            for cs in range(2):
                                for cch in range(4):
                                    last = e.matmul(ps[4 + cch], lhsT=Ypq[:, kc, 2 * par + cs, cch * 128:(cch + 1) * 128], rhs=gtv[b][:, klc, cs, :],
                                                    start=(kc == 0 and cs == 0), stop=(kc == 15 and cs == 1))
                        return last
                    P.op("pe", mm, r=["Y", "fg%d" % b], w=["ps4", "ps5", "ps6", "ps7"])
                    if kg == 1:
                        t0 = s0 + tt * 1024
                        for cch in range(4):
                            bb = cch % 2
                            P.op("sp", lambda e, bb=bb, cch=cch, t0=t0: e.dma_start(out=dti[bb], in_=dsrc[cch * 128:(cch + 1) * 128, t0:t0 + 1024]),
                                 r=["d:pT", "d:z1T"], w=["dti%d" % bb], dma="dti%d" % bb)
                            P.op("sp", lambda e, bb=bb, cch=cch, t0=t0: e.dma_start(out=xti[bb], in_=xsrc[cch * 128:(cch + 1) * 128, t0:t0 + 1024]),
                                 r=["d:pT"], w=["xti%d" % bb], dma="xti%d" % bb)
                            P.op("dve", lambda e, bb=bb, cch=cch, par=par: e.scalar_tensor_tensor(out=zf32, in0=dti[bb][:, par:1024:2], scalar=pvc("hd%d%d" % (i, o), cch), in1=ps[4 + cch],
                                                                                               op0=ALU.mult, op1=ALU.add), x=["ps%d" % (4 + cch)], r=["dti%d" % bb, "pvt"], w=["zf32"])
                            P.op("dve", lambda e, bb=bb, par=par: e.tensor_tensor(out=zf32, in0=zf32, in1=xti[bb][:, par:1024:2], op=ALU.mult), r=["zf32", "xti%d" % bb], w=["zf32"])
                            P.op("act", lambda e, cch=cch, par=par: e.activation(out=zo[:, cch, par:1024:2], in_=zf32, func=AF.Copy), r=["zf32"], w=["zo%d" % cch])
                            if par == 1:
                                P.op("sp", lambda e, cch=cch, t0=t0: e.dma_start(out=zdst[cch * 128:(cch + 1) * 128, t0:t0 + 1024], in_=zo[:, cch, :]),
                                     r=["zo%d" % cch], w=["d:z1T" if o == 0 else "d:zT"], dma="st:zo%d" % cch)
                            if o == 0:
                                for blk in range(4):
                                    P.op("pe", lambda e, blk=blk: e.transpose(ps[0][:, blk * 128:(blk + 1) * 128], zf32[:, blk * 128:(blk + 1) * 128], ident), r=["zf32", "ident"], w=["ps0"])
                                for blk in range(4):
                                    P.op("act", lambda e, blk=blk, cch=cch, tt=tt, par=par: e.activation(out=intm[:, par, tt * 4 + blk, cch * 128:(cch + 1) * 128],
                                                                                                     in_=ps[0][:, blk * 128:(blk + 1) * 128], func=AF.Copy), x=["ps0"], w=["intm"])
            run_order(0)
            run_order(1)


        def hyena_conv(i, L, s0, tg=""):
            keep = ("ps", "d:pT", "d:hf", "d:vbtm", "pvt", "ident")

            def rn(n):
                return n if n.startswith(keep) else n + tg

            def op(eng, fn, r=(), w=(), dma=None, x=()):
                return P.op(eng, fn, r=[rn(n) for n in r], w=[rn(n) for n in w], dma=(None if dma is None else dma + tg), x=list(x))
            nch = L // 128
            tw = min(512, L)
            ntt = L // tw
            kgn = max(1, nch // 8)
            kl = nch // kgn
            intm = A.alloc([nch, 512], BF16)
            Y = A.alloc([nch, 2, 512], BF16)
            ft = [A.alloc([nch, 2, 128], BF16) for _ in range(2)]
            gt = [A.alloc([kl, 2, tw], BF16) for _ in range(2)]
            hft = [A.alloc([2, 512], F32) for _ in range(2)]
            t1 = A.alloc([512], F32)
            t2 = A.alloc([512], F32)
            dti = [A.alloc([tw], BF16) for _ in range(2)]
            xti = [A.alloc([tw], BF16) for _ in range(2)]
            zf32 = A.alloc([tw], F32)
            zo = [A.alloc([tw], BF16) for _ in range(2)]
            op("sp", lambda e: e.dma_start(out=intm, in_=vbtm_d[s0:s0 + L, :].rearrange("(c p) n -> p c n", p=128)), r=["d:vbtm"], w=["intm"], dma="intm")
            def run_order(o):
                dsrc = pT_d[512:1024, :] if o == 0 else z1T_d
                xsrc = pT_d[1024 + 512 * o:1536 + 512 * o, :]
                zdst = z1T_d if o == 0 else zT_d

                def ldf(kc):
                    b = kc % 2
                    op("sp", lambda e: e.dma_start(out=ft[b], in_=cd["F%d" % L][kc]), w=["cft%d" % b], dma="cft%d" % b)
                    op("sp", lambda e: e.dma_start(out=hft[b], in_=hf_d[L][kc * 128:(kc + 1) * 128, :, o * 512:(o + 1) * 512]), r=["d:hf"], w=["hft%d" % b], dma="hft%d" % b)
                ldf(0)
                for kc in range(nch):
                    if kc + 1 < nch:
                        ldf(kc + 1)
                    b = kc % 2

                    def mm(e, b=b):
                        last = None
                        for cs in range(2):
                            for sc in range(nch):
                                last = e.matmul(ps[cs], lhsT=ft[b][:, sc, cs, :], rhs=intm[:, sc, :], start=(sc == 0), stop=(sc == nch - 1))
                        return last
                    op("pe", mm, r=["cft%d" % b, "intm"], w=["ps0", "ps1"])
                    Hc, Hs = hft[b][:, 0, :], hft[b][:, 1, :]
                    hk = "hft%d" % b
                    op("dve", lambda e, Hc=Hc: e.tensor_tensor(out=t1, in0=ps[0], in1=Hc, op=ALU.mult), x=["ps0"], r=[hk], w=["t1"])
                    op("dve", lambda e, Hs=Hs: e.tensor_tensor(out=t2, in0=ps[1], in1=Hs, op=ALU.mult), x=["ps1"], r=[hk], w=["t2"])
                    op("dve", lambda e, kc=kc: e.tensor_tensor(out=Y[:, kc, 0, :], in0=t1, in1=t2, op=ALU.subtract), r=["t1", "t2"], w=["Y"])
                    op("dve", lambda e, Hs=Hs: e.tensor_tensor(out=t1, in0=ps[0], in1=Hs, op=ALU.mult), x=["ps0"], r=[hk], w=["t1"])
                    op("dve", lambda e, Hc=Hc: e.tensor_tensor(out=t2, in0=ps[1], in1=Hc, op=ALU.mult), x=["ps1"], r=[hk], w=["t2"])
                    op("dve", lambda e, kc=kc: e.tensor_tensor(out=Y[:, kc, 1, :], in0=t1, in1=t2, op=ALU.add), r=["t1", "t2"], w=["Y"])
                seq = [(tt, kg) for tt in range(ntt) for kg in range(kgn)]

                def ldg(q):
                    tt, kg = seq[q]
                    op("sp", lambda e: e.dma_start(out=gt[q % 2], in_=cd["G%d" % L][tt, kg]), w=["gt%d" % (q % 2)], dma="gt%d" % (q % 2))
                ldg(0)
                for q, (tt, kg) in enumerate(seq):
                    if q + 1 < len(seq):
                        ldg(q + 1)
                    b = q % 2

                    def mm(e, b=b, kg=kg):
                        last = None
                        for klc in range(kl):
                            kc = kg * kl + klc
                            for cs in range(2):
                                for cch in range(4):
                                    last = e.matmul(ps[2 + cch][:, 0:tw], lhsT=Y[:, kc, cs, cch * 128:(cch + 1) * 128], rhs=gt[b][:, klc, cs, :],
                                                    start=(kc == 0 and cs == 0), stop=(kc == nch - 1 and cs == 1))
                        return last
                    op("pe", mm, r=["Y", "gt%d" % b], w=["ps2", "ps3", "ps4", "ps5"])
                    if kg == kgn - 1:
                        t0 = s0 + tt * tw
                        for cch in range(4):
                            bb = cch % 2
                            op("sp", lambda e, bb=bb, cch=cch, t0=t0: e.dma_start(out=dti[bb], in_=dsrc[cch * 128:(cch + 1) * 128, t0:t0 + tw]),
                                 r=["d:pT", "d:z1T"], w=["dti%d" % bb], dma="dti%d" % bb)
                            op("sp", lambda e, bb=bb, cch=cch, t0=t0: e.dma_start(out=xti[bb], in_=xsrc[cch * 128:(cch + 1) * 128, t0:t0 + tw]),
                                 r=["d:pT"], w=["xti%d" % bb], dma="xti%d" % bb)
                            op("dve", lambda e, bb=bb, cch=cch: e.scalar_tensor_tensor(out=zf32, in0=dti[bb], scalar=pvc("hd%d%d" % (i, o), cch), in1=ps[2 + cch][:, 0:tw],
                                                                                        op0=ALU.mult, op1=ALU.add), x=["ps%d" % (2 + cch)], r=["dti%d" % bb, "pvt"], w=["zf32"])
                            op("dve", lambda e, bb=bb: e.tensor_tensor(out=zf32, in0=zf32, in1=xti[bb], op=ALU.mult), r=["zf32", "xti%d" % bb], w=["zf32"])
                            op("act", lambda e, bb=bb: e.activation(out=zo[bb], in_=zf32, func=AF.Copy), r=["zf32"], w=["zo%d" % bb])
                            op("sp", lambda e, bb=bb, cch=cch, t0=t0: e.dma_start(out=zdst[cch * 128:(cch + 1) * 128, t0:t0 + tw], in_=zo[bb]),
                                 r=["zo%d" % bb], w=["d:z1T" if o == 0 else "d:zT"], dma="st:zo%d" % bb)
                            if o == 0:
                                for tc in range(tw // 128):
                                    op("pe", lambda e, tc=tc: e.transpose(ps[6][:, tc * 128:(tc + 1) * 128], zf32[:, tc * 128:(tc + 1) * 128], ident), r=["zf32", "ident"], w=["ps6"])
                                for tc in range(tw // 128):
                                    op("act", lambda e, tc=tc, cch=cch, tt=tt: e.activation(out=intm[:, tt * (tw // 128) + tc, cch * 128:(cch + 1) * 128],
                                                                                           in_=ps[6][:, tc * 128:(tc + 1) * 128], func=AF.Copy), x=["ps6"], w=["intm"])
            run_order(0)
            run_order(1)

        def mla(l):
            j = l // 2
            ckvT = A.alloc([2, NKEY], BF16)
            krT = A.alloc([NKEY], F32)
            m_persist = A.mark()
            hT = A.alloc([8, T], BF16)
            m0 = A.mark()
            prologue(l, 0, hT)
            P.barrier()
            A.release(m0)
            wdq = A.alloc([8, 512], BF16)
            load_w(wdq, W["mla_w_dq"][j].rearrange("(c p) n -> p c n", p=128), "wdq")
            wdkv = A.alloc([8, 320], BF16)
            load_w(wdkv, W["mla_w_dkv"][j].rearrange("(c p) n -> p c n", p=128), "wdkv")
            rawq = A.alloc([4, 512], F32)
            qnb = [A.alloc([4, 512], BF16) for _ in range(2)]
            sq = A.alloc([4, 512], BF16)
            rs = A.alloc([512], F32)
            ckf = A.alloc([2, 512], F32)
            tok = [A.alloc([256], F32) for _ in range(2)]
            tokr = [A.alloc([64], F32) for _ in range(2)]
            cin = A.alloc([2, 256], F32)
            cinr = A.alloc([2, 64], F32)
            def proj_tile(ti):
                sl = slice(ti * 512, (ti + 1) * 512)
                for oc in range(4):
                    pb = oc % 2

                    def mm(e, oc=oc, pb=pb):
                        last = None
                        for kc in range(8):
                            last = e.matmul(ps[pb], lhsT=wdq[:, kc, oc * 128:(oc + 1) * 128], rhs=hT[:, kc, sl], start=(kc == 0), stop=(kc == 7))
                        return last
                    P.op("pe", mm, r=["wdq", "hT"], w=["ps%d" % pb])
                    P.op("act", lambda e, oc=oc, pb=pb: e.activation(out=rawq[:, oc, :], in_=ps[pb], func=AF.Copy), x=["ps%d" % pb], w=["rawq"])
                P.op("act", lambda e: e.activation(out=sq, in_=rawq, func=AF.Square), r=["rawq"], w=["sq"])

                def mms(e):
                    last = None
                    for oc in range(4):
                        last = e.matmul(ps[2], lhsT=onesb, rhs=sq[:, oc, :], start=(oc == 0), stop=(oc == 3))
                    return last
                P.op("pe", mms, r=["sq", "onesb"], w=["ps2"])
                P.op("act", lambda e: e.activation(out=rs, in_=ps[2], func=AF.Sqrt, scale=1.0 / 512, bias=epsT[:, 0:1]), x=["ps2"], r=["eps"], w=["rs"])
                P.op("dve", lambda e: e.reciprocal(out=rs, in_=rs), r=["rs"], w=["rs"])
                for oc in range(4):
                    P.op("dve", lambda e, oc=oc: e.tensor_tensor(out=rawq[:, oc, :], in0=rawq[:, oc, :], in1=rs, op=ALU.mult), r=["rawq", "rs"], w=["rawq"])
                    P.op("act", lambda e, oc=oc, ti=ti: e.activation(out=qnb[ti % 2][:, oc, :], in_=rawq[:, oc, :], func=AF.Identity, scale=pvc("qn%d" % j, oc)), r=["rawq", "pvt"], w=["qnb%d" % (ti % 2)])
                P.op("sp", lambda e, ti=ti: e.dma_start(out=qnT_d.rearrange("(c p) t -> p c t", p=128)[:, :, ti * 512:(ti + 1) * 512], in_=qnb[ti % 2]),
                     r=["qnb%d" % (ti % 2)], w=["d:qnT"], dma="st:qnb%d" % (ti % 2))
                for oc in range(3):
                    m_ = 128 if oc < 2 else 64
                    pb = 3 + oc

                    def mm(e, oc=oc, pb=pb, m_=m_):
                        last = None
                        for kc in range(8):
                            last = e.matmul(ps[pb][0:m_, :], lhsT=wdkv[:, kc, oc * 128:oc * 128 + m_], rhs=hT[:, kc, sl], start=(kc == 0), stop=(kc == 7))
                        return last
                    P.op("pe", mm, r=["wdkv", "hT"], w=["ps%d" % pb])
                    if oc < 2:
                        P.op("act", lambda e, oc=oc, pb=pb: e.activation(out=ckf[:, oc, :], in_=ps[pb], func=AF.Copy), x=["ps%d" % pb], w=["ckf"])
                    else:
                        P.op("act", lambda e, pb=pb: e.activation(out=krT[0:64, sl], in_=ps[pb][0:64, :], func=AF.Copy), x=["ps%d" % pb], w=["krT"])
                P.op("act", lambda e: e.activation(out=sq[:, 0:2, :], in_=ckf, func=AF.Square), r=["ckf"], w=["sq"])

                def mms2(e):
                    e.matmul(ps[2], lhsT=onesb, rhs=sq[:, 0, :], start=True, stop=False)
                    return e.matmul(ps[2], lhsT=onesb, rhs=sq[:, 1, :], start=False, stop=True)
                P.op("pe", mms2, r=["sq", "onesb"], w=["ps2"])
                P.op("act", lambda e: e.activation(out=rs, in_=ps[2], func=AF.Sqrt, scale=1.0 / 256, bias=epsT[:, 0:1]), x=["ps2"], r=["eps"], w=["rs"])
                P.op("dve", lambda e: e.reciprocal(out=rs, in_=rs), r=["rs"], w=["rs"])
                for oc in range(2):
                    P.op("dve", lambda e, oc=oc: e.tensor_tensor(out=ckf[:, oc, :], in0=ckf[:, oc, :], in1=rs, op=ALU.mult), r=["ckf", "rs"], w=["ckf"])
                    P.op("dve", lambda e, oc=oc: e.tensor_scalar(out=ckf[:, oc, :], in0=ckf[:, oc, :], scalar1=pvc("kn%d" % j, oc), scalar2=None, op0=ALU.mult), r=["ckf", "pvt"], w=["ckf"])
                    P.op("act", lambda e, oc=oc: e.activation(out=ckvT[:, oc, sl], in_=ckf[:, oc, :], func=AF.Copy), r=["ckf"], w=["ckvT"])
                if ti >= 8:
                    for tc in range(4):
                        sqi = (ti - 8) * 2 + tc // 2
                        r0 = (tc % 2) * 128
                        b = tc % 2

                        def tr(e, tc=tc):
                            e.transpose(ps[6][:, 0:128], ckf[:, 0, tc * 128:(tc + 1) * 128], ident)
                            e.transpose(ps[6][:, 128:256], ckf[:, 1, tc * 128:(tc + 1) * 128], ident)
                            return e.transpose(ps[6][:, 256:320], krT[0:64, ti * 512 + tc * 128:ti * 512 + (tc + 1) * 128], ident[0:64, 0:64])
                        P.op("pe", tr, r=["ckf", "krT", "ident"], w=["ps6"])
                        P.op("dve", lambda e, b=b: e.tensor_copy(out=tok[b], in_=ps[6][:, 0:256]), x=["ps6"], w=["tok%d" % b])
                        P.op("dve", lambda e, b=b: e.tensor_copy(out=tokr[b], in_=ps[6][:, 256:320]), x=["ps6"], w=["tokr%d" % b])
                        P.op("sp", lambda e, b=b, sqi=sqi, r0=r0: e.dma_start(out=nckv_d[sqi, j, r0:r0 + 128, :], in_=tok[b]), r=["tok%d" % b], dma="st:tok%d" % b)
                        P.op("sp", lambda e, b=b, sqi=sqi, r0=r0: e.dma_start(out=nkr_d[sqi, j, r0:r0 + 128, :], in_=tokr[b]), r=["tokr%d" % b], dma="st:tokr%d" % b)
            for ti in range(NT):
                proj_tile(ti)
            P.op("sp", lambda e: e.dma_start(out=cin, in_=cckv_d[j].rearrange("(c p) n -> p c n", p=128)), w=["cin"], dma="cin")
            P.op("sp", lambda e: e.dma_start(out=cinr, in_=ckr_d[j].rearrange("(c p) n -> p c n", p=128)), w=["cinr"], dma="cin")
            for tc in range(2):
                def tr(e, tc=tc):
                    e.transpose(ps[6][:, 0:128], cin[:, tc, 0:128], ident)
                    e.transpose(ps[6][:, 128:256], cin[:, tc, 128:256], ident)
                    return e.transpose(ps[6][0:64, 256:384], cinr[:, tc, :], ident)
                P.op("pe", tr, r=["cin", "cinr", "ident"], w=["ps6"])
                for cc in range(2):
                    P.op("act", lambda e, tc=tc, cc=cc: e.activation(out=ckvT[:, cc, T + tc * 128:T + (tc + 1) * 128], in_=ps[6][:, cc * 128:(cc + 1) * 128], func=AF.Copy),
                         x=["ps6"], w=["ckvT"])
                P.op("act", lambda e, tc=tc: e.activation(out=krT[0:64, T + tc * 128:T + (tc + 1) * 128], in_=ps[6][0:64, 256:384], func=AF.Copy), x=["ps6"], w=["krT"])
            P.barrier()
            A.release(m_persist)
            wuq = A.alloc([4, 1536], BF16)
            load_w(wuq, W["mla_w_uq"][j].rearrange("(c p) n -> p c n", p=128), "wuq")
            wukv = A.alloc([2, 2048], BF16)
            load_w(wukv, W["mla_w_ukv"][j].rearrange("(c p) n -> p c n", p=128), "wukv")
            ropec = A.alloc([4096], F32)
            ropes = A.alloc([4096], F32)
            P.op("sp", lambda e: e.dma_start(out=ropec[0:64, :], in_=cd["ropec"]), w=["ropec"], dma="rope")
            P.op("sp", lambda e: e.dma_start(out=ropes[0:64, :], in_=cd["ropes"]), w=["ropes"], dma="rope")
            NKC = 34
            NCH = NKEY // 128
            Khr = A.alloc([NKEY], BF16)
            krss = A.alloc([NCH], F32)
            KhnA = [A.alloc([NKC * 128], BF16) for _ in range(2)]
            VhA = [A.alloc([NKC, 128], BF16) for _ in range(2)]
            sclA = [A.alloc([NKC], F32) for _ in range(2)]
            KhnB = [A.alloc([256], BF16) for _ in range(4)]
            VhB = [A.alloc([2, 128], BF16) for _ in range(4)]
            sclB = [A.alloc([2], F32) for _ in range(4)]
            sqk = A.alloc([512], BF16)
            tk = A.alloc([4], F32)
            sqa = A.alloc([512], BF16)
            sqr = A.alloc([512], BF16)
            rsa = A.alloc([512], F32)
            tn = A.alloc([512], F32)
            tr_ = A.alloc([512], F32)
            tr2 = A.alloc([512], F32)
            Qn = [A.alloc([512], BF16) for _ in range(2)]
            Qr = [A.alloc([512], BF16) for _ in range(2)]
            qin = [A.alloc([4, 512], BF16) for _ in range(2)]
            PT = [A.alloc([512], BF16) for _ in range(4)]
            rec = A.alloc([512], F32)
            ob = [A.alloc([512], BF16) for _ in range(2)]
            sc_ = 1.0 / math.sqrt(192.0)
            kgr = "khr%d" % j

            for c0 in range(0, NCH, 4):
                cn = min(4, NCH - c0)
                P.op("act", lambda e, c0=c0, cn=cn: e.activation(out=sqr[0:64, 0:cn * 128], in_=krT[0:64, c0 * 128:(c0 + cn) * 128], func=AF.Square), r=["krT"], w=["sqr"])

                def mm(e, c0=c0, cn=cn):
                    last = None
                    for a_ in range(cn):
                        last = e.matmul(ps[4][:, c0 + a_:c0 + a_ + 1], lhsT=sqr[0:64, a_ * 128:(a_ + 1) * 128], rhs=onesb[0:64, 0:1], start=True, stop=True)
                    return last
                P.op("pe", mm, r=["sqr", "onesb"], w=["ps4"])
            P.op("dve", lambda e: e.tensor_copy(out=krss, in_=ps[4][:, 0:NCH]), x=["ps4"], w=["krss"])
            for c0 in range(0, NKEY, 512):
                n = min(512, NKEY - c0)
                cs_ = slice(c0, c0 + n)
                P.op("dve", lambda e, cs_=cs_, n=n: e.tensor_scalar(out=tr2[0:64, 0:n], in0=krT[0:64, cs_], scalar1=pvc(kgr, 0, 64), scalar2=None, op0=ALU.mult), r=["krT", "pvt"], w=["tr2"])
                if c0 < 4096:
                    P.op("pe", lambda e, n=n: e.matmul(ps[6][0:64, 0:n], lhsT=rotT[0:64, :], rhs=tr2[0:64, 0:n], start=True, stop=True), r=["rotT", "tr2"], w=["ps6"])
                    P.op("dve", lambda e, cs_=cs_, n=n: e.tensor_tensor(out=tn[0:64, 0:n], in0=ps[6][0:64, 0:n], in1=ropes[0:64, cs_], op=ALU.mult), x=["ps6"], r=["ropes"], w=["tn"])
                    P.op("dve", lambda e, cs_=cs_, n=n: e.tensor_tensor(out=tr2[0:64, 0:n], in0=tr2[0:64, 0:n], in1=ropec[0:64, cs_], op=ALU.mult), r=["tr2", "ropec"], w=["tr2"])
                    P.op("dve", lambda e, cs_=cs_, n=n: e.tensor_tensor(out=Khr[0:64, cs_], in0=tr2[0:64, 0:n], in1=tn[0:64, 0:n], op=ALU.add), r=["tr2", "tn"], w=["Khr"])
                else:
                    P.op("dve", lambda e, cs_=cs_, n=n: e.tensor_copy(out=Khr[0:64, cs_], in_=tr2[0:64, 0:n]), r=["tr2"], w=["Khr"])

            def kprep(hd, S, Khn_, Vh_, scl_, tag):
                kch = S["kch"]
                groups = [kch[a:a + 4] for a in range(0, len(kch), 4)]
                for gi, grp in enumerate(groups):
                    ng = len(grp)
                    n = ng * 128
                    c0 = grp[0] * 128
                    assert grp[-1] == grp[0] + ng - 1
                    lo = gi * 512

                    def mm(e, c0=c0, n=n):
                        last = None
                        for kc in range(2):
                            last = e.matmul(ps[4][:, 0:n], lhsT=wukv[:, kc, hd * 256:hd * 256 + 128], rhs=ckvT[:, kc, c0:c0 + n], start=(kc == 0), stop=(kc == 1))
                        return last
                    P.op("pe", mm, r=["wukv", "ckvT"], w=["ps4"])
                    yield
                    P.op("act", lambda e, n=n: e.activation(out=sqk[:, 0:n], in_=ps[4][:, 0:n], func=AF.Square), x=["ps4"], w=["sqk"])
                    yield
                    P.op("dve", lambda e, lo=lo, n=n: e.tensor_scalar(out=Khn_[:, lo:lo + n], in0=ps[4][:, 0:n], scalar1=pvc("khn%d" % j), scalar2=None, op0=ALU.mult),
                         x=["ps4"], r=["pvt"], w=["Khn" + tag])
                    yield

                    def mm1(e, ng=ng):
                        last = None
                        for a_ in range(ng):
                            last = e.matmul(ps[6][:, a_:a_ + 1], lhsT=sqk[:, a_ * 128:(a_ + 1) * 128], rhs=onesb[:, 0:1], start=True, stop=True)
                        return last
                    P.op("pe", mm1, r=["sqk", "onesb"], w=["ps6"])
                    yield
                    P.op("dve", lambda e, ng=ng, g0=grp[0]: e.tensor_tensor(out=tk[:, 0:ng], in0=ps[6][:, 0:ng], in1=krss[:, g0:g0 + ng], op=ALU.add), x=["ps6"], r=["krss"], w=["tk"])
                    yield
                    P.op("act", lambda e, ng=ng: e.activation(out=tk[:, 0:ng], in_=tk[:, 0:ng], func=AF.Ln, scale=1.0 / 192, bias=epsT[:, 0:1]), r=["tk", "eps"], w=["tk"])
                    yield
                    P.op("act", lambda e, ng=ng: e.activation(out=tk[:, 0:ng], in_=tk[:, 0:ng], func=AF.Exp, scale=-0.5), r=["tk"], w=["tk"])
                    yield
                    P.op("dve", lambda e, ng=ng, gi=gi: e.tensor_scalar(out=scl_[:, gi * 4:gi * 4 + ng], in0=tk[:, 0:ng], scalar1=sc_, scalar2=None, op0=ALU.mult), r=["tk"], w=["scl" + tag])
                    yield

                    def mmv(e, grp=grp):
                        last = None
                        for a_, ch in enumerate(grp):
                            for kc in range(2):
                                last = e.matmul(ps[5][:, a_ * 128:(a_ + 1) * 128], lhsT=ckvT[:, kc, ch * 128:(ch + 1) * 128],
                                                rhs=wukv[:, kc, hd * 256 + 128:hd * 256 + 256], start=(kc == 0), stop=(kc == 1))
                        return last
                    P.op("pe", mmv, r=["wukv", "ckvT"], w=["ps5"])
                    yield
                    P.op("dve", lambda e, gi=gi, ng=ng, n=n: e.tensor_copy(out=Vh_[:, gi * 4:gi * 4 + ng, :], in_=ps[5][:, 0:n].rearrange("p (a d) -> p a d", d=128)),
                         x=["ps5"], w=["Vh" + tag])
                    yield

            def qprep(hd, S, qt, slot):
                qw = min(512, S["nq"])
                q0 = S["q0"] + qt * qw
                rope = S["rope"]
                qb_ = qin[slot]
                qn_, qr_ = Qn[slot], Qr[slot]
                qres = "Qh%d" % slot
                P.op("sp", lambda e: e.dma_start(out=qb_[:, :, 0:qw], in_=qnT_d.rearrange("(c p) t -> p c t", p=128)[:, :, q0:q0 + qw]),
                     r=["d:qnT"], w=["qin%d" % slot], dma="qin%d" % slot)
                yield

                def mmq(e):
                    last = None
                    for kc in range(4):
                        e.matmul(ps[4][:, 0:qw], lhsT=wuq[:, kc, hd * 192:hd * 192 + 128], rhs=qb_[:, kc, 0:qw], start=(kc == 0), stop=(kc == 3))
                    for kc in range(4):
                        last = e.matmul(ps[5][0:64, 0:qw], lhsT=wuq[:, kc, hd * 192 + 128:hd * 192 + 192], rhs=qb_[:, kc, 0:qw], start=(kc == 0), stop=(kc == 3))
                    return last
                P.op("pe", mmq, r=["wuq", "qin%d" % slot], w=["ps4", "ps5"])
                yield
                P.op("act", lambda e: e.activation(out=sqa[:, 0:qw], in_=ps[4][:, 0:qw], func=AF.Square), x=["ps4"], w=["sqa"])
                yield
                P.op("act", lambda e: e.activation(out=tr_[0:64, 0:qw], in_=ps[5][0:64, 0:qw], func=AF.Copy), x=["ps5"], w=["tr"])
                yield
                P.op("act", lambda e: e.activation(out=sqr[0:64, 0:qw], in_=tr_[0:64, 0:qw], func=AF.Square), r=["tr"], w=["sqr"])
                yield

                def mm(e):
                    e.matmul(ps[6][:, 0:qw], lhsT=onesb, rhs=sqa[:, 0:qw], start=True, stop=False)
                    return e.matmul(ps[6][:, 0:qw], lhsT=onesb[0:64, :], rhs=sqr[0:64, 0:qw], start=False, stop=True)
                P.op("pe", mm, r=["sqa", "sqr", "onesb"], w=["ps6"])
                yield
                P.op("act", lambda e: e.activation(out=rsa[:, 0:qw], in_=ps[6][:, 0:qw], func=AF.Ln, scale=1.0 / 192, bias=epsT[:, 0:1]), x=["ps6"], r=["eps"], w=["rsa"])
                yield
                P.op("act", lambda e: e.activation(out=rsa[:, 0:qw], in_=rsa[:, 0:qw], func=AF.Exp, scale=-0.5), r=["rsa"], w=["rsa"])
                yield
                P.op("dve", lambda e: e.tensor_tensor(out=tn[:, 0:qw], in0=ps[4][:, 0:qw], in1=rsa[:, 0:qw], op=ALU.mult), x=["ps4"], r=["rsa"], w=["tn"])
                yield
                P.op("dve", lambda e: e.tensor_scalar(out=qn_[:, 0:qw], in0=tn[:, 0:qw], scalar1=pvc("qhn%d" % j), scalar2=None, op0=ALU.mult), r=["tn", "pvt"], w=[qres])
                yield
                P.op("dve", lambda e: e.tensor_tensor(out=tr2[0:64, 0:qw], in0=tr_[0:64, 0:qw], in1=rsa[0:64, 0:qw], op=ALU.mult), r=["tr", "rsa"], w=["tr2"])
                yield
                if not rope:
                    P.op("dve", lambda e: e.tensor_scalar(out=qr_[0:64, 0:qw], in0=tr2[0:64, 0:qw], scalar1=pvc("qhr%d" % j, 0, 64), scalar2=None, op0=ALU.mult), r=["tr2", "pvt"], w=[qres])
                    yield
                else:
                    tc_ = slice(q0, q0 + qw)
                    P.op("dve", lambda e: e.tensor_scalar(out=tr2[0:64, 0:qw], in0=tr2[0:64, 0:qw], scalar1=pvc("qhr%d" % j, 0, 64), scalar2=None, op0=ALU.mult), r=["tr2", "pvt"], w=["tr2"])
                    yield
                    P.op("pe", lambda e: e.matmul(ps[6][0:64, 0:qw], lhsT=rotT[0:64, :], rhs=tr2[0:64, 0:qw], start=True, stop=True), r=["rotT", "tr2"], w=["ps6"])
                    yield
                    P.op("dve", lambda e: e.tensor_tensor(out=tn[0:64, 0:qw], in0=ps[6][0:64, 0:qw], in1=ropes[0:64, tc_], op=ALU.mult), x=["ps6"], r=["ropes"], w=["tn"])
                    yield
                    P.op("dve", lambda e: e.tensor_tensor(out=tr2[0:64, 0:qw], in0=tr2[0:64, 0:qw], in1=ropec[0:64, tc_], op=ALU.mult), r=["tr2", "ropec"], w=["tr2"])
                    yield
                    P.op("dve", lambda e: e.tensor_tensor(out=qr_[0:64, 0:qw], in0=tr2[0:64, 0:qw], in1=tn[0:64, 0:qw], op=ALU.add), r=["tr2", "tn"], w=[qres])
                    yield

            def drain(g):
                for _ in g:
                    pass

            def core(hd, S, qt, Khn_, Vh_, scl_, tag, slot, pending, oidx):
                kch = S["kch"]
                nk = len(kch)
                qw = min(512, S["nq"])
                q0 = S["q0"] + qt * qw
                qn_, qr_ = Qn[slot], Qr[slot]
                qres = "Qh%d" % slot
                SB = (0, 1, 7)

                def qk(a):
                    sb = SB[a % 3]
                    pt = PT[a % 4]
                    kc0 = kch[a] * 128

                    def mms_(e):
                        e.matmul(ps[sb][:, 0:qw], lhsT=Khn_[:, a * 128:(a + 1) * 128], rhs=qn_[:, 0:qw], start=True, stop=False)
                        return e.matmul(ps[sb][:, 0:qw], lhsT=Khr[0:64, kc0:kc0 + 128], rhs=qr_[0:64, 0:qw], start=False, stop=True)
                    P.op("pe", mms_, r=["Khn" + tag, "Khr", qres], w=["ps%d" % sb])
                    P.op("act", lambda e: e.activation(out=pt[:, 0:qw], in_=ps[sb][:, 0:qw], func=AF.Exp, scale=scl_[:, a:a + 1]),
                         x=["ps%d" % sb], r=["scl" + tag], w=["PT%d" % (a % 4)])

                def pv(a):
                    pt = PT[a % 4]

                    def mmo(e):
                        e.matmul(ps[2][:, 0:qw], lhsT=Vh_[:, a, :], rhs=pt[:, 0:qw], start=(a == 0), stop=(a == nk - 1))
                        return e.matmul(ps[3][:, 0:qw], lhsT=onesb, rhs=pt[:, 0:qw], start=(a == 0), stop=(a == nk - 1))
                    P.op("pe", mmo, r=["Vh" + tag, "PT%d" % (a % 4), "onesb"], w=["ps2", "ps3"])

                def drip(k):
                    for _ in range(k):
                        while pending:
                            try:
                                next(pending[0])
                                break
                            except StopIteration:
                                pending.pop(0)
                qk(0)
                if nk > 1:
                    qk(1)
                for a in range(nk):
                    if a + 2 < nk:
                        qk(a + 2)
                    pv(a)
                    drip(2 if a % 2 else 1)
                o_ = ob[oidx % 2]
                P.op("dve", lambda e: e.reciprocal(out=rec[:, 0:qw], in_=ps[3][:, 0:qw]), x=["ps3"], w=["rec"])
                P.op("dve", lambda e: e.tensor_tensor(out=o_[:, 0:qw], in0=ps[2][:, 0:qw], in1=rec[:, 0:qw], op=ALU.mult), x=["ps2"], r=["rec"], w=["ob%d" % (oidx % 2)])
                P.op("sp", lambda e: e.dma_start(out=oT_d[hd * 128:(hd + 1) * 128, q0:q0 + qw], in_=o_[:, 0:qw]), r=["ob%d" % (oidx % 2)], w=["d:oT"],
                     dma="st:aob%d" % (oidx % 2))

            seqs = [dict(q0=0, nq=4096, kch=list(range(32)) + [40, 41], rope=True)]
            for sq_ in range(4):
                seqs.append(dict(q0=4096 + 256 * sq_, nq=256, kch=[32 + 2 * sq_, 33 + 2 * sq_], rope=False))
            units = []
            for hd in range(HEADS):
                for si, S in enumerate(seqs):
                    if si == 0:
                        bufs = (KhnA[hd % 2], VhA[hd % 2], sclA[hd % 2], "A%d" % (hd % 2))
                    else:
                        bufs = (KhnB[si - 1], VhB[si - 1], sclB[si - 1], "B%d" % (si - 1))
                    units.append(dict(hd=hd, S=S, si=si, bufs=bufs))
            for u in units:
                u["kgen"] = kprep(u["hd"], u["S"], *u["bufs"])
            items = []
            for ui, u in enumerate(units):
                qw = min(512, u["S"]["nq"])
                for qt in range(u["S"]["nq"] // qw):
                    items.append(dict(ui=ui, qt=qt))
            for ii, it in enumerate(items):
                u = units[it["ui"]]
                it["qgen"] = qprep(u["hd"], u["S"], it["qt"], ii % 2)
            for ii, it in enumerate(items):
                ui = it["ui"]
                u = units[ui]
                drain(u["kgen"])
                drain(it["qgen"])
                pending = []
                if ii + 1 < len(items):
                    pending.append(items[ii + 1]["qgen"])
                if u["si"] == 0:
                    for k in range(1, 5):
                        pending.append(units[ui + k]["kgen"])
                    if ui + 5 < len(units):
                        pending.append(units[ui + 5]["kgen"])
                core(u["hd"], u["S"], it["qt"], *u["bufs"], ii % 2, pending, ii)
            phase_end()

            def in_tiles(ti, dst, key):
                P.op("sp", lambda e: e.dma_start(out=dst, in_=oT_d.rearrange("(c p) t -> p c t", p=128)[:, :, ti * 512:(ti + 1) * 512]), r=["d:oT"], w=[key], dma=key)
            out_proj(l, 0, W["mla_w_o"][j], 8, in_tiles)
            phase_end()

        for l in range(depth):
            if l % 2 == 0:
                even_mixer(l)
            else:
                mla(l)
            ffn(l)

        xt = [A.alloc([8, 512], F32) for _ in range(2)]
        yo = [A.alloc([1024], F32) for _ in range(2)]

        def ldx(ti):
            P.op("sp", lambda e: e.dma_start(out=xt[ti % 2], in_=xT_tile_ap(ti)), r=["d:xT"], w=["fx%d" % (ti % 2)], dma="fx%d" % (ti % 2))
        ldx(0)
        for ti in range(NT):
            if ti + 1 < NT:
                ldx(ti + 1)
            b = ti % 2
            for tc in range(4):
                yb = tc % 2
                for hh in range(2):
                    pb = hh

                    def tr(e, b=b, tc=tc, hh=hh, pb=pb):
                        last = None
                        for d4 in range(4):
                            dc = hh * 4 + d4
                            last = e.transpose(ps[pb][:, d4 * 128:(d4 + 1) * 128], xt[b][:, dc, tc * 128:(tc + 1) * 128], ident)
                        return last
                    P.op("pe", tr, r=["fx%d" % b, "ident"], w=["ps%d" % pb])
                    if hh == 0:
                        P.op("act", lambda e, yb=yb, pb=pb: e.activation(out=yo[yb][:, 0:512], in_=ps[pb], func=AF.Copy), x=["ps%d" % pb], w=["yo%d" % yb])
                    else:
                        P.op("dve", lambda e, yb=yb, pb=pb: e.tensor_copy(out=yo[yb][:, 512:1024], in_=ps[pb]), x=["ps%d" % pb], w=["yo%d" % yb])
                P.op("sp", lambda e, yb=yb, ti=ti, tc=tc: e.dma_start(out=y_rows(ti)[tc * 128:(tc + 1) * 128, :], in_=yo[yb]), r=["yo%d" % yb], dma="st:yo%d" % yb)
        P.barrier()
        P.emit()
        build.n_inst = P.n_inst
    return nc


_NC = {}


def make_in_maps(inp):
    C = host_consts()
    pv = pv_layout(inp).array()
    bf = ml_dtypes.bfloat16
    shared = {k: np.ascontiguousarray(inp[k], dtype=np.float32) for k in WEIGHTS}
    shared["pv"] = pv
    shared["sguT"] = np.ascontiguousarray(np.transpose(inp["sgu_w"], (0, 3, 1, 2)))
    shared["sgub"] = np.ascontiguousarray(inp["sgu_b"].reshape(2, 1, 512))
    shared["decbc"] = np.ascontiguousarray(np.broadcast_to(inp["hy_decay"].reshape(2, 1, 1024), (2, 128, 1024)))
    for k, v in C.items():
        shared["c_" + k] = v
    maps = []
    for c in range(8):
        m = dict(shared)
        m["xs"] = np.ascontiguousarray(inp["x_sample"][c])
        m["xp"] = np.ascontiguousarray(inp["x_prompt"][4 * c:4 * c + 4].reshape(1024, 1024))
        m["cckv"] = np.ascontiguousarray(inp["cache_ckv"][c])
        m["ckr"] = np.ascontiguousarray(inp["cache_krope"][c])
        cond = np.stack([inp["c"][c], inp["c_ctx"]], axis=1)
        m["condT"] = np.ascontiguousarray(cond.reshape(8, 128, 2).transpose(1, 0, 2))
        maps.append(m)
    return maps


def kernel(**inputs):
    inp = {k: np.asarray(v) for k, v in inputs.items()}
    if "nc" not in _NC:
        _NC["nc"] = build()
    nc = _NC["nc"]
    maps = make_in_maps(inp)
    res = run_bass_kernel_spmd(nc, maps, core_ids=list(range(8)))
    R = res.results
    y_sample = np.stack([np.asarray(R[c]["ys"], np.float32) for c in range(8)], 0)
    y_prompt = np.concatenate([np.asarray(R[c]["yp"], np.float32).reshape(4, 256, 1024) for c in range(8)], 0)
    nckv = np.concatenate([np.asarray(R[c]["nckv"], np.float32) for c in range(8)], 0)
    nkr = np.concatenate([np.asarray(R[c]["nkr"], np.float32) for c in range(8)], 0)
    return (y_prompt, y_sample, nckv, nkr)
```

```python
import contextlib
import math
import numpy as np
import ml_dtypes
import concourse.bass as bass
import concourse.mybir as mybir
from concourse.bass_utils import run_bass_kernel_spmd

F32 = mybir.dt.float32
BF16 = mybir.dt.bfloat16
U8 = mybir.dt.uint8
AF = mybir.ActivationFunctionType
ALU = mybir.AluOpType

COMPUTE = ("pe", "act", "dve", "pool")
STREAMS = ("pe", "act", "dve", "pool", "sp")


class Prog:
    def __init__(self, nc, strict=True):
        self.nc = nc
        self.strict = strict
        self.ops = []
        self.res = {}
        self.dma_cnt = {}
        self.last_real = {s: None for s in STREAMS}

    def _dep_entry(self, a):
        A = self.ops[a]
        if A["dma"] is not None:
            return ("d", A["dma"], 16 * self.dma_cnt[A["dma"]])
        return ("c", a)

    def op(self, eng, fn, r=(), w=(), dma=None, x=()):
        idx = len(self.ops)
        deps = set()
        for name in x:
            st = self.res.setdefault(name, [None, []])
            if st[0] is not None:
                deps.add(st[0])
            for rd in st[1]:
                if self.ops[rd]["eng"] != eng:
                    deps.add(rd)
        for name in r:
            st = self.res.setdefault(name, [None, []])
            if st[0] is not None:
                deps.add(st[0])
        for name in w:
            st = self.res.setdefault(name, [None, []])
            if st[0] is not None:
                deps.add(st[0])
            for rd in st[1]:
                deps.add(rd)
        if dma is not None:
            self.dma_cnt[dma] = self.dma_cnt.get(dma, 0)
        dep_entries = []
        for a in sorted(deps):
            A = self.ops[a]
            if A["dma"] is None and A["eng"] == eng and dma is None:
                if eng == "pe" or not self.strict:
                    continue
            dep_entries.append(self._dep_entry(a))
        if dma is not None:
            self.dma_cnt[dma] += 1
        self.ops.append(dict(eng=eng, fn=fn, deps=dep_entries, dma=dma, sig=False, val=None))
        for name in list(r) + list(x):
            self.res[name][1].append(idx)
        for name in w:
            self.res[name] = [idx, []]
        if fn is not None and dma is None:
            self.last_real[eng] = idx
        return idx

    def barrier(self):
        ents = []
        for s in COMPUTE:
            a = self.last_real.get(s)
            if a is not None:
                ents.append(("c", a))
        for key, cnt in self.dma_cnt.items():
            if cnt:
                ents.append(("d", key, 16 * cnt))
        for s in STREAMS:
            self.ops.append(dict(eng=s, fn=None, deps=list(ents), dma=None, sig=False, val=None))
        self.res = {}

    def emit(self):
        nc = self.nc
        ops = self.ops
        for o in ops:
            for d in o["deps"]:
                if d[0] == "c":
                    ops[d[1]]["sig"] = True
        cnt = {s: 0 for s in COMPUTE}
        for o in ops:
            if o["dma"] is None and o["sig"]:
                cnt[o["eng"]] += 1
                o["val"] = cnt[o["eng"]]
        dma_keys = list(self.dma_cnt.keys())
        with contextlib.ExitStack() as es:
            esem = {s: es.enter_context(nc.semaphore("sem_" + s)) for s in COMPUTE}
            dsem = {k: es.enter_context(nc.semaphore("dsem_%d" % i)) for i, k in enumerate(dma_keys)}
            block = es.enter_context(nc.Block())
            self.n_inst = {s: 0 for s in STREAMS}

            def run_stream(s, engine):
                waited = {}
                for o in ops:
                    if o["eng"] != s:
                        continue
                    for d in o["deps"]:
                        if d[0] == "c":
                            A = ops[d[1]]
                            sem, val = esem[A["eng"]], A["val"]
                            key = ("c", A["eng"])
                        else:
                            sem, val = dsem[d[1]], d[2]
                            key = ("d", d[1])
                        if waited.get(key, 0) >= val:
                            continue
                        waited[key] = val
                        engine.wait_ge(sem, val)
                        self.n_inst[s] += 1
                    if o["fn"] is None:
                        continue
                    ins = o["fn"](engine)
                    self.n_inst[s] += 1
                    if o["dma"] is not None:
                        ins.then_inc(dsem[o["dma"]], 16)
                    elif o["sig"]:
                        ins.then_inc(esem[s], 1)
                if s == "sp":
                    for k, c in self.dma_cnt.items():
                        if c and waited.get(("d", k), 0) < 16 * c:
                            engine.wait_ge(dsem[k], 16 * c)

            @block.tensor
            def _(e):
                run_stream("pe", e)

            @block.scalar
            def _(e):
                run_stream("act", e)

            @block.vector
            def _(e):
                run_stream("dve", e)

            @block.gpsimd
            def _(e):
                run_stream("pool", e)

            @block.sync
            def _(e):
                run_stream("sp", e)


class Arena:
    def __init__(self, nc, es, nbytes):
        self.t = es.enter_context(nc.sbuf_tensor("arena", [128, nbytes], U8))
        self.n = nbytes
        self.off = 0

    def alloc(self, shape_free, dtype, parts=128):
        size = {F32: 4, BF16: 2, U8: 1}[dtype]
        n = int(np.prod(shape_free)) * size
        off = (self.off + 63) // 64 * 64
        assert off + n <= self.n, ("SBUF arena overflow", off, n, self.n)
        self.off = off + n
        ap = self.t[0:parts, off:off + n].bitcast(dtype)
        if len(shape_free) == 1:
            return ap
        names = " ".join("d%d" % i for i in range(len(shape_free)))
        kw = {"d%d" % i: int(s) for i, s in enumerate(shape_free)}
        return ap.rearrange("p (%s) -> p %s" % (names, names), **kw)

    def mark(self):
        return self.off

    def release(self, m):
        self.off = m


D = 1024
DEPTH = 4
T = 5120
NT = 10
TS = 4096
LP = 256
SEGS = [(0, 4096)] + [(4096 + 256 * i, 4096 + 256 * (i + 1)) for i in range(4)]
DFF = 2816
EPS = 1e-6
HEADS = 8
NKEY = T + 256

class PV:
    def __init__(self):
        self.cols = {}
        self.n = 0
        self.data = []

    def add(self, name, vec):
        v = np.asarray(vec, np.float32).reshape(-1)
        if v.size % 128:
            v = np.concatenate([v, np.zeros(128 - v.size % 128, np.float32)])
        c = v.size // 128
        self.cols[name] = (self.n, c)
        self.data.append(v.reshape(c, 128).T)
        self.n += c

    def array(self):
        return np.ascontiguousarray(np.concatenate(self.data, axis=1))


def pv_layout(inp=None):
    pv = PV()

    def g(name, shape):
        return inp[name] if inp is not None else np.zeros(shape, np.float32)
    ng = g("norm_g", (4, 2, 1024)); ab = g("ada_b", (4, 6144))
    fw = g("ffn_conv_w", (4, 3, 5632)); fb = g("ffn_conv_b", (4, 5632))
    hw = g("hy_conv_w", (2, 3, 1536)); hb = g("hy_conv_b", (2, 1536)); hd = g("hy_d", (2, 2, 512))
    qn = g("mla_q_norm", (2, 512)); kn = g("mla_kv_norm", (2, 256))
    qh = g("mla_q_head_norm", (2, 192)); kh = g("mla_k_head_norm", (2, 192))
    b1 = g("hy_f_b1", (2, 64)); b2 = g("hy_f_b2", (2, 64)); fr = g("hy_f_freq", (2, 64))
    for l in range(4):
        for s in range(2):
            pv.add("ng%d%d" % (l, s), ng[l, s])
        pv.add("ab%d" % l, ab[l])
        for k in range(3):
            pv.add("fw%d%d" % (l, k), fw[l, k])
        pv.add("fb%d" % l, fb[l])
    for i in range(2):
        for k in range(3):
            pv.add("hw%d%d" % (i, k), hw[i, k])
        pv.add("hb%d" % i, hb[i])
        for o in range(2):
            pv.add("hd%d%d" % (i, o), hd[i, o])
        pv.add("qn%d" % i, qn[i]); pv.add("kn%d" % i, kn[i])
        pv.add("qhn%d" % i, qh[i, :128]); pv.add("qhr%d" % i, qh[i, 128:])
        pv.add("khn%d" % i, kh[i, :128]); pv.add("khr%d" % i, kh[i, 128:])
        pv.add("b1%d" % i, b1[i]); pv.add("b2%d" % i, b2[i]); pv.add("fr%d" % i, fr[i])
    return pv


_CONST = {}


def host_consts():
    if _CONST:
        return _CONST
    bf = ml_dtypes.bfloat16
    c = {}
    c["ident"] = np.eye(128, dtype=np.float32)
    t = np.arange(4096)
    row, col = t // 64, t % 64
    inv = 1.0 / (10000.0 ** (np.arange(0, 32, 2, dtype=np.float32) / 32))
    ang = np.zeros((64, 4096), np.float32)
    for p in range(64):
        pos = row if p < 32 else col
        ang[p] = pos.astype(np.float32) * inv[p % 16]
    c["ropec"] = np.cos(ang).astype(np.float32)
    c["ropes"] = np.sin(ang).astype(np.float32)
    RT = np.zeros((64, 64), np.float32)
    for base in (0, 32):
        for i in range(16):
            RT[base + 16 + i, base + i] = -1.0
            RT[base + i, base + 16 + i] = 1.0
    c["rotT"] = RT
    for L in (4096, 256):
        N = 2 * L
        tt = np.arange(L, dtype=np.float32)
        tn = tt / L
        bands = np.arange(1, 17, dtype=np.float32)
        angf = (2.0 * math.pi) * tn[:, None] * bands[None]
        z = np.concatenate([tn[:, None], np.sin(angf), np.cos(angf)], axis=-1).astype(np.float32)
        c["zf%d" % L] = np.ascontiguousarray(z.T)
        dist = np.abs(tt - L // 2) / L
        c["nd%d" % L] = np.ascontiguousarray((-dist).astype(np.float32).reshape(L // 128, 128).T)
        n = np.arange(L, dtype=np.int64)
        k = np.arange(L, dtype=np.int64)
        nch = L // 128
        m = ((2 * k[None, :] + 1) * n[:, None]) % (2 * N)
        th = (math.pi / N) * m.astype(np.float64)
        Fc = np.cos(th).astype(np.float32); Fs = np.sin(th).astype(np.float32)
        F = np.stack([Fc, Fs], 0).reshape(2, nch, 128, nch, 128)
        c["F%d" % L] = np.ascontiguousarray(F.transpose(3, 2, 1, 0, 4)).astype(bf)
        m = ((2 * k[:, None] + 1) * (n[None, :] + L // 2)) % (2 * N)
        th = (math.pi / N) * m.astype(np.float64)
        Gc = (np.cos(th) * (2.0 / N)).astype(np.float32); Gs = (np.sin(th) * (2.0 / N)).astype(np.float32)
        tw = min(512, L)
        kgn = max(1, nch // 8); kl = nch // kgn
        G = np.stack([Gc, Gs], 0).reshape(2, kgn, kl, 128, L // tw, tw)
        c["G%d" % L] = np.ascontiguousarray(G.transpose(4, 1, 3, 2, 0, 5)).astype(bf)
        if L == 4096:
            H = L // 2
            Fh = np.stack([Fc[:, :H], Fs[:, :H]], 0)
            Fh = Fh.reshape(2, 16, 128, 2, 16, 128)
            c["FE"] = np.ascontiguousarray(Fh.transpose(4, 2, 3, 1, 0, 5)).astype(bf)
            Gh = np.stack([Gc[:H], Gs[:H]], 0)
            Gh = Gh.reshape(2, 2, 8, 128, 4, 512, 2)
            c["GE"] = np.ascontiguousarray(Gh.transpose(4, 6, 1, 3, 2, 0, 5)).astype(bf)
            ndv = (-dist).astype(np.float32).reshape(16, 128, 2)
            c["ndE"] = np.ascontiguousarray(ndv.transpose(1, 2, 0).reshape(128, 32))
            del c["F4096"], c["G4096"], c["nd4096"]
    _CONST.update(c)
    return _CONST


WEIGHTS = ["ada_w", "mix_w_in", "hy_f_w1", "hy_f_w2", "hy_f_w3", "mix_w_out", "mla_w_dq", "mla_w_uq",
           "mla_w_dkv", "mla_w_ukv", "mla_w_o", "ffn_w_up", "ffn_w_down"]
WSHAPE = {"ada_w": (4, 1024, 6144), "mix_w_in": (2, 1024, 2560), "hy_f_w1": (2, 33, 64), "hy_f_w2": (2, 64, 64),
          "hy_f_w3": (2, 64, 1024), "mix_w_out": (2, 1024, 1024), "mla_w_dq": (2, 1024, 512),
          "mla_w_uq": (2, 512, 1536), "mla_w_dkv": (2, 1024, 320), "mla_w_ukv": (2, 256, 2048),
          "mla_w_o": (2, 1024, 1024), "ffn_w_up": (4, 1024, 5632), "ffn_w_down": (4, 2816, 1024)}


def build(dbg=(), depth=DEPTH):
    nc = bass.Bass("TRN2", target_bir_lowering=False)
    pvl = pv_layout()
    C = host_consts()

    def din(name, shape, dt=F32):
        return nc.dram_tensor(name, list(shape), dt, kind="ExternalInput").ap()

    def dscr(name, shape, dt):
        kind = "ExternalOutput" if name in dbg else "Internal"
        return nc.dram_tensor(name, list(shape), dt, kind=kind).ap()

    xs_d = din("xs", (4096, 1024)); xp_d = din("xp", (1024, 1024))
    cckv_d = din("cckv", (2, 256, 256)); ckr_d = din("ckr", (2, 256, 64))
    cond_d = din("condT", (128, 8, 2)); pv_d = din("pv", (128, pvl.n))
    sguT_d = din("sguT", (2, 128, 4, 128)); sgub_d = din("sgub", (2, 1, 512)); dec_d = din("decbc", (2, 128, 1024))
    W = {k: din(k, WSHAPE[k]) for k in WEIGHTS}
    cd = {k: din("c_" + k, v.shape, BF16 if v.dtype != np.float32 else F32) for k, v in C.items()}
    ys_d = nc.dram_tensor("ys", [4096, 1024], F32, kind="ExternalOutput").ap()
    yp_d = nc.dram_tensor("yp", [1024, 1024], F32, kind="ExternalOutput").ap()
    nckv_d = nc.dram_tensor("nckv", [4, 2, 256, 256], F32, kind="ExternalOutput").ap()
    nkr_d = nc.dram_tensor("nkr", [4, 2, 256, 64], F32, kind="ExternalOutput").ap()
    xT_d = dscr("xT", (1024, T), F32)
    act_d = dscr("actT", (DFF, T), BF16)
    pT_d = dscr("pT", (2048, T), BF16)
    vbtm_d = dscr("vbtm", (T, 512), BF16)
    aT_d = dscr("aT", (512, T), BF16)
    z1T_d = dscr("z1T", (512, T), BF16)
    zT_d = dscr("zT", (512, T), BF16)
    hf_d = {4096: dscr("hf4096", (2048, 4, 1024), F32), 256: dscr("hf256", (256, 2, 1024), F32)}
    oT_d = dscr("oT", (1024, T), BF16)
    qnT_d = dscr("qnT", (512, T), BF16)

    es = contextlib.ExitStack()
    with es:
        A = Arena(nc, es, 190 * 1024)
        psl = [es.enter_context(nc.psum_tensor("ps%d" % i, [128, 512], F32)) for i in range(8)]
        ps = [p_[:, :] for p_ in psl]
        P = Prog(nc)
        uid = [0]

        def U(s):
            uid[0] += 1
            return "%s#%d" % (s, uid[0])

        pvt = A.alloc([pvl.n], F32)
        ident = A.alloc([128], F32)
        ones = A.alloc([128], F32)
        onesb = A.alloc([128], BF16)
        epsT = A.alloc([1], F32)
        condT = A.alloc([8, 2], F32)
        scond = A.alloc([8, 2], BF16)
        mods = A.alloc([DEPTH, 48, 2], F32)
        gm = A.alloc([DEPTH, 2, 8, 2], F32)
        rotT = A.alloc([64], F32)
        P.op("sp", lambda e: e.dma_start(out=pvt, in_=pv_d), w=["pvt"], dma="c0")
        P.op("sp", lambda e: e.dma_start(out=ident, in_=cd["ident"]), w=["ident"], dma="c0")
        P.op("sp", lambda e: e.dma_start(out=condT, in_=cond_d), w=["condT"], dma="c0")
        P.op("sp", lambda e: e.dma_start(out=rotT[0:64, :], in_=cd["rotT"]), w=["rotT"], dma="c0")
        P.op("dve", lambda e: e.memset(ones, 1.0), w=["ones"])
        P.op("dve", lambda e: e.memset(onesb, 1.0), w=["onesb"])
        P.op("dve", lambda e: e.memset(epsT, EPS), w=["eps"])
        P.op("act", lambda e: e.activation(out=scond, in_=condT, func=AF.Silu), r=["condT"], w=["scond"])
        P.barrier()
        base_mark = A.mark()

        def pvc(name, j=0, parts=128):
            o, c = pvl.cols[name]
            return pvt[0:parts, o + j:o + j + 1]

        def phase_end():
            P.barrier()
            A.release(base_mark)

        def load_w(dst, src, key):
            P.op("pool", lambda e: e.dma_start(out=dst, in_=src), w=[key], dma="w:" + key)

        def rstd_from(psb, n, out_sb, psname):
            P.op("act", lambda e: e.activation(out=out_sb, in_=psb, func=AF.Sqrt, scale=1.0 / n, bias=epsT[0:out_sb.shape[0], 0:1]),
                 x=[psname], r=["eps"], w=[U("rs")])
            nm = P.ops
            P.op("dve", lambda e: e.reciprocal(out=out_sb, in_=out_sb), r=[], w=[])

        adab = A.alloc([2, 8, 1536], BF16)
        for l in range(depth):
            for pc in range(4):
                slot = (l * 4 + pc) % 2
                key = "adaw%d" % slot
                load_w(adab[:, slot], W["ada_w"][l].rearrange("(c p) n -> p c n", p=128)[:, :, pc * 1536:(pc + 1) * 1536], key)

                def mm(e, slot=slot, pc=pc):
                    last = None
                    for f in range(12):
                        for kc in range(8):
                            last = e.matmul(ps[0][:, (pc * 12 + f) * 2:(pc * 12 + f) * 2 + 2], lhsT=adab[:, slot, kc, f * 128:(f + 1) * 128],
                                            rhs=scond[:, kc, :], start=(kc == 0), stop=(kc == 7))
                    return last
                P.op("pe", mm, r=[key, "scond"], w=["ps0"])
            ao, _ = pvl.cols["ab%d" % l]
            for ci in range(2):
                P.op("dve", lambda e, l=l, ci=ci, ao=ao: e.tensor_tensor(out=mods[:, l, :, ci], in0=ps[0][:, ci:96:2], in1=pvt[:, ao:ao + 48], op=ALU.add),
                     x=["ps0"], r=["pvt"], w=["mods"])
            for s in range(2):
                go, _ = pvl.cols["ng%d%d" % (l, s)]
                for ci in range(2):
                    P.op("dve", lambda e, l=l, s=s, ci=ci, go=go: e.scalar_tensor_tensor(
                        out=gm[:, l, s, :, ci], in0=mods[:, l, (3 * s + 1) * 8:(3 * s + 2) * 8, ci], scalar=1.0,
                        in1=pvt[:, go:go + 8], op0=ALU.add, op1=ALU.mult), r=["mods", "pvt"], w=["gm"])
        phase_end()

        def mod_sh(l, s, dc, ci):
            return mods[:, l, (3 * s) * 8 + dc, ci:ci + 1]

        def mod_gate(l, s, dc, ci):
            return mods[:, l, (3 * s + 2) * 8 + dc, ci:ci + 1]

        def ci_of_tile(ti):
            return 0 if ti < 8 else 1

        def xT_tile_ap(ti):
            return xT_d.rearrange("(c p) t -> p c t", p=128)[:, :, ti * 512:(ti + 1) * 512]

        xin = [A.alloc([4, 1024], F32) for _ in range(2)]
        xTt = [A.alloc([8, 512], F32) for _ in range(2)]

        def x_rows(ti):
            if ti < 8:
                return xs_d[ti * 512:(ti + 1) * 512, :]
            return xp_d[(ti - 8) * 512:(ti - 7) * 512, :]

        def y_rows(ti):
            if ti < 8:
                return ys_d[ti * 512:(ti + 1) * 512, :]
            return yp_d[(ti - 8) * 512:(ti - 7) * 512, :]

        def ld_x(ti):
            P.op("sp", lambda e: e.dma_start(out=xin[ti % 2], in_=x_rows(ti).rearrange("(c p) d -> p c d", p=128)),
                 w=["xin%d" % (ti % 2)], dma="xin%d" % (ti % 2))
        ld_x(0)
        for ti in range(NT):
            if ti + 1 < NT:
                ld_x(ti + 1)
            s = ti % 2
            for dc in range(8):
                b = dc % 2

                def tr(e, s=s, dc=dc, b=b):
                    last = None
                    for tc in range(4):
                        last = e.transpose(ps[b][:, tc * 128:(tc + 1) * 128], xin[s][:, tc, dc * 128:(dc + 1) * 128], ident)
                    return last
                P.op("pe", tr, r=["xin%d" % s, "ident"], w=["ps%d" % b])
                eng = "act" if dc % 2 == 0 else "dve"
                if eng == "act":
                    P.op("act", lambda e, s=s, dc=dc, b=b: e.activation(out=xTt[s][:, dc, :], in_=ps[b], func=AF.Copy), x=["ps%d" % b], w=["xTt%d" % s])
                else:
                    P.op("dve", lambda e, s=s, dc=dc, b=b: e.tensor_copy(out=xTt[s][:, dc, :], in_=ps[b]), x=["ps%d" % b], w=["xTt%d" % s])
            P.op("sp", lambda e, s=s, ti=ti: e.dma_start(out=xT_tile_ap(ti), in_=xTt[s]), r=["xTt%d" % s], w=["d:xT"], dma="st:xTt%d" % s)
        phase_end()

        def prologue(l, s, hT):
            xt = [A.alloc([8, 512], F32) for _ in range(2)]
            sqs = [A.alloc([8, 512], BF16) for _ in range(2)]
            rss = [A.alloc([512], F32) for _ in range(2)]
            tmp = [A.alloc([512], F32) for _ in range(2)]

            def ld(ti):
                P.op("sp", lambda e: e.dma_start(out=xt[ti % 2], in_=xT_tile_ap(ti)), r=["d:xT"], w=["pxt%d" % (ti % 2)], dma="pxt%d" % (ti % 2))
            ld(0)
            for ti in range(NT):
                if ti + 1 < NT:
                    ld(ti + 1)
                b = ti % 2
                ci = ci_of_tile(ti)
                sq, rs = sqs[b], rss[b]
                pbk = 2 + b
                P.op("act", lambda e, b=b, sq=sq: e.activation(out=sq, in_=xt[b], func=AF.Square), r=["pxt%d" % b], w=["psq%d" % b])

                def mm(e, sq=sq, pbk=pbk):
                    last = None
                    for dc in range(8):
                        last = e.matmul(ps[pbk], lhsT=onesb, rhs=sq[:, dc, :], start=(dc == 0), stop=(dc == 7))
                    return last
                P.op("pe", mm, r=["psq%d" % b, "onesb"], w=["ps%d" % pbk])
                P.op("act", lambda e, rs=rs, pbk=pbk: e.activation(out=rs, in_=ps[pbk], func=AF.Sqrt, scale=1.0 / D, bias=epsT[:, 0:1]), x=["ps%d" % pbk], r=["eps"], w=["prs%d" % b])
                P.op("dve", lambda e, rs=rs: e.reciprocal(out=rs, in_=rs), r=["prs%d" % b], w=["prs%d" % b])
                for dc in range(8):
                    tb = dc % 2
                    P.op("dve", lambda e, b=b, dc=dc, tb=tb, rs=rs: e.tensor_tensor(out=tmp[tb], in0=xt[b][:, dc, :], in1=rs, op=ALU.mult),
                         r=["pxt%d" % b, "prs%d" % b], w=["ptmp%d" % tb])
                    P.op("act", lambda e, dc=dc, tb=tb, ti=ti, ci=ci: e.activation(
                        out=hT[:, dc, ti * 512:(ti + 1) * 512], in_=tmp[tb], func=AF.Identity,
                        scale=gm[:, l, s, dc, ci:ci + 1], bias=mod_sh(l, s, dc, ci)), r=["ptmp%d" % tb, "gm", "mods"], w=["hT"])

        def out_proj(l, s, wsrc, nk, in_tiles_fn):
            wt = A.alloc([nk, 1024], BF16)
            load_w(wt, wsrc.rearrange("(c p) n -> p c n", p=128), "opw")
            it = [A.alloc([nk, 512], BF16) for _ in range(2)]
            xt = [A.alloc([8, 512], F32) for _ in range(2)]

            def ld(ti):
                b = ti % 2
                in_tiles_fn(ti, it[b], "opin%d" % b)
                P.op("sp", lambda e: e.dma_start(out=xt[b], in_=xT_tile_ap(ti)), r=["d:xT"], w=["opx%d" % b], dma="opx%d" % b)
            ld(0)
            for ti in range(NT):
                if ti + 1 < NT:
                    ld(ti + 1)
                b = ti % 2
                ci = ci_of_tile(ti)
                for dc in range(8):
                    pb = 4 + dc % 4

                    def mm(e, b=b, dc=dc, pb=pb):
                        last = None
                        for kc in range(nk):
                            last = e.matmul(ps[pb], lhsT=wt[:, kc, dc * 128:(dc + 1) * 128], rhs=it[b][:, kc, :], start=(kc == 0), stop=(kc == nk - 1))
                        return last
                    P.op("pe", mm, r=["opw", "opin%d" % b], w=["ps%d" % pb])
                    P.op("dve", lambda e, b=b, dc=dc, pb=pb, ci=ci: e.scalar_tensor_tensor(
                        out=xt[b][:, dc, :], in0=ps[pb], scalar=mod_gate(l, s, dc, ci), in1=xt[b][:, dc, :], op0=ALU.mult, op1=ALU.add),
                        x=["ps%d" % pb], r=["mods"], w=["opx%d" % b])
                P.op("sp", lambda e, b=b, ti=ti: e.dma_start(out=xT_tile_ap(ti), in_=xt[b]), r=["opx%d" % b], w=["d:xT"], dma="st:opx%d" % b)

        def dwconv(raw, co, wname, bname, j, segs, rawname="raw"):
            P.op("dve", lambda e: e.tensor_scalar(out=co, in0=raw, scalar1=pvc(wname + "1", j), scalar2=pvc(bname, j), op0=ALU.mult, op1=ALU.add),
                 r=[rawname, "pvt"], w=["co"])
            for (s0, s1) in segs:
                P.op("dve", lambda e, s0=s0, s1=s1: e.scalar_tensor_tensor(out=co[:, s0 + 1:s1], in0=raw[:, s0:s1 - 1], scalar=pvc(wname + "0", j),
                                                                           in1=co[:, s0 + 1:s1], op0=ALU.mult, op1=ALU.add), r=[rawname, "pvt"], w=["co"])
                P.op("dve", lambda e, s0=s0, s1=s1: e.scalar_tensor_tensor(out=co[:, s0:s1 - 1], in0=raw[:, s0 + 1:s1], scalar=pvc(wname + "2", j),
                                                                           in1=co[:, s0:s1 - 1], op0=ALU.mult, op1=ALU.add), r=[rawname, "pvt"], w=["co"])

        def gemm_rows(wt, hT, raw, wkey, rawname="raw"):
            for ti in range(NT):
                pb = ti % 4

                def mm(e, ti=ti, pb=pb):
                    last = None
                    for kc in range(8):
                        last = e.matmul(ps[pb], lhsT=wt[:, kc, :], rhs=hT[:, kc, ti * 512:(ti + 1) * 512], start=(kc == 0), stop=(kc == 7))
                    return last
                P.op("pe", mm, r=[wkey, "hT"], w=["ps%d" % pb])
                P.op("act", lambda e, ti=ti, pb=pb: e.activation(out=raw[:, ti * 512:(ti + 1) * 512], in_=ps[pb], func=AF.Copy), x=["ps%d" % pb], w=[rawname])

        def ffn(l):
            hT = A.alloc([8, T], BF16)
            m0 = A.mark()
            prologue(l, 1, hT)
            P.barrier()
            A.release(m0)
            wt = [A.alloc([8, 128], BF16) for _ in range(4)]
            raws = [A.alloc([T], F32) for _ in range(2)]
            co = A.alloc([T], F32)
            sg = A.alloc([T], BF16)
            ab = [A.alloc([T], BF16) for _ in range(2)]
            wup = W["ffn_w_up"][l].rearrange("(c p) n -> p c n", p=128)
            NJ = DFF // 128

            def ldw(j):
                for h in range(2):
                    k = (j % 2) * 2 + h
                    load_w(wt[k], wup[:, :, h * DFF + j * 128:h * DFF + (j + 1) * 128], "fw%d" % k)

            def gemm_n(n):
                j, h = n // 2, n % 2
                if h == 0 and j + 1 < NJ:
                    ldw(j + 1)
                k = (j % 2) * 2 + h
                gemm_rows(wt[k], hT, raws[n % 2], "fw%d" % k, "raw%d" % (n % 2))

            def post_n(n):
                j, h = n // 2, n % 2
                dwconv(raws[n % 2], co, "fw%d" % l, "fb%d" % l, h * 22 + j, SEGS, "raw%d" % (n % 2))
                if h == 0:
                    P.op("act", lambda e: e.activation(out=sg, in_=co, func=AF.Silu), r=["co"], w=["sg"])
                else:
                    a = ab[j % 2]
                    P.op("dve", lambda e, a=a: e.tensor_tensor(out=a, in0=co, in1=sg, op=ALU.mult), r=["co", "sg"], w=["ab%d" % (j % 2)])
                    P.op("sp", lambda e, a=a, j=j: e.dma_start(out=act_d[j * 128:(j + 1) * 128, :], in_=a), r=["ab%d" % (j % 2)], w=["d:act"],
                         dma="st:ab%d" % (j % 2))
            ldw(0)
            gemm_n(0)
            for n in range(2 * NJ):
                if n + 1 < 2 * NJ:
                    gemm_n(n + 1)
                post_n(n)
            phase_end()

            def in_tiles(ti, dst, key):
                P.op("sp", lambda e: e.dma_start(out=dst, in_=act_d.rearrange("(c p) t -> p c t", p=128)[:, :, ti * 512:(ti + 1) * 512]),
                     r=["d:act"], w=[key], dma=key)
            out_proj(l, 1, W["ffn_w_down"][l], 22, in_tiles)
            phase_end()

        def gelu_from(src_ap, src_res, src_x, out_ap, out_res, n, graw, gt, tag):
            raw_, t_ = graw[:, 0:n], gt[:, 0:n]
            kr_, kt_ = "glraw" + tag, "glt" + tag
            P.op("act", lambda e: e.activation(out=raw_, in_=src_ap, func=AF.Copy), x=src_x, r=src_res, w=[kr_])
            P.op("dve", lambda e: e.tensor_tensor(out=t_, in0=raw_, in1=raw_, op=ALU.mult), r=[kr_], w=[kt_])
            P.op("dve", lambda e: e.tensor_scalar(out=t_, in0=t_, scalar1=0.044715, scalar2=1.0, op0=ALU.mult, op1=ALU.add), r=[kt_], w=[kt_])
            P.op("dve", lambda e: e.tensor_tensor(out=t_, in0=t_, in1=raw_, op=ALU.mult), r=[kt_, kr_], w=[kt_])
            P.op("act", lambda e: e.activation(out=t_, in_=t_, func=AF.Sigmoid, scale=2.0 * math.sqrt(2.0 / math.pi)), r=[kt_], w=[kt_])
            P.op("dve", lambda e: e.tensor_tensor(out=out_ap, in0=t_, in1=raw_, op=ALU.mult), r=[kt_, kr_], w=out_res)


        def even_mixer(l):
            i = l // 2
            hT = A.alloc([8, T], BF16)
            m0 = A.mark()
            prologue(l, 0, hT)
            P.barrier()
            A.release(m0)
            wt = [A.alloc([8, 128], BF16) for _ in range(2)]
            raws = [A.alloc([T], F32) for _ in range(2)]
            co = A.alloc([T], F32)
            ob = [A.alloc([T], BF16) for _ in range(2)]
            vtm = [A.alloc([4, 128], BF16) for _ in range(2)]
            win = W["mix_w_in"][i].rearrange("(c p) n -> p c n", p=128)
            cols = [c * 128 for c in range(4)] + [1024 + c * 128 for c in range(12)]

            def gemm_q(q):
                load_w(wt[q % 2], win[:, :, cols[q]:cols[q] + 128], "mw%d" % (q % 2))
                gemm_rows(wt[q % 2], hT, raws[q % 2], "mw%d" % (q % 2), "raw%d" % (q % 2))

            def post_q(q):
                raw = raws[q % 2]
                rn = "raw%d" % (q % 2)
                o_ = ob[q % 2]
                okey = "ob%d" % (q % 2)
                if q < 4:
                    P.op("dve", lambda e: e.tensor_tensor(out=co, in0=raw, in1=raw, op=ALU.mult), r=[rn], w=["co"])
                    P.op("dve", lambda e: e.tensor_scalar(out=co, in0=co, scalar1=0.044715, scalar2=1.0, op0=ALU.mult, op1=ALU.add), r=["co"], w=["co"])
                    P.op("dve", lambda e: e.tensor_tensor(out=co, in0=co, in1=raw, op=ALU.mult), r=["co", rn], w=["co"])
                    P.op("act", lambda e: e.activation(out=co, in_=co, func=AF.Sigmoid, scale=2.0 * math.sqrt(2.0 / math.pi)), r=["co"], w=["co"])
                    P.op("dve", lambda e: e.tensor_tensor(out=o_, in0=co, in1=raw, op=ALU.mult), r=["co", rn], w=[okey])
                else:
                    dwconv(raw, co, "hw%d" % i, "hb%d" % i, q - 4, SEGS, rn)
                    P.op("act", lambda e: e.activation(out=o_, in_=co, func=AF.Copy), r=["co"], w=[okey])
                    if q < 8:
                        for tc in range(T // 128):
                            pb = 4 + tc % 2
                            P.op("pe", lambda e, tc=tc, pb=pb: e.transpose(ps[pb][:, 0:128], co[:, tc * 128:(tc + 1) * 128], ident), r=["co", "ident"], w=["ps%d" % pb])
                            vb_ = vtm[tc % 2]
                            P.op("dve", lambda e, pb=pb, vb_=vb_: e.tensor_copy(out=vb_[:, 0, :], in_=ps[pb][:, 0:128]), x=["ps%d" % pb], w=["vtm%d" % (tc % 2)])
                            P.op("sp", lambda e, tc=tc, vb_=vb_: e.dma_start(out=vbtm_d[tc * 128:(tc + 1) * 128, (q - 4) * 128:(q - 3) * 128], in_=vb_[:, 0, :]),
                                 r=["vtm%d" % (tc % 2)], w=["d:vbtm"], dma="st:vtm%d" % (tc % 2))
                P.op("sp", lambda e: e.dma_start(out=pT_d[q * 128:(q + 1) * 128, :], in_=o_), r=[okey], w=["d:pT"], dma="st:" + okey)
            gemm_q(0)
            for q in range(16):
                if q + 1 < 16:
                    gemm_q(q + 1)
                post_q(q)
            P.barrier()
            A.release(m0)
            wv = A.alloc([8, 512], BF16)
            load_w(wv, win[:, :, 512:1024], "wv")
            sgw = A.alloc([4, 128], BF16)
            load_w(sgw, sguT_d[i], "sgw")
            sgb = A.alloc([512], BF16, parts=1)
            load_w(sgb, sgub_d[i], "sgb")
            glr = [A.alloc([512], F32) for _ in range(2)]
            glt = [A.alloc([512], F32) for _ in range(2)]
            vbs = [A.alloc([512], BF16) for _ in range(2)]
            ut = [A.alloc([4, 512], BF16) for _ in range(2)]
            at = [A.alloc([4, 512], BF16) for _ in range(2)]

            def ldu(ti):
                P.op("sp", lambda e: e.dma_start(out=ut[ti % 2], in_=pT_d[0:512, :].rearrange("(c p) t -> p c t", p=128)[:, :, ti * 512:(ti + 1) * 512]),
                     r=["d:pT"], w=["ut%d" % (ti % 2)], dma="ut%d" % (ti % 2))
            ldu(0)

            def sgu_mm(n):
                t0 = n * 128
                pa = 2 * (n % 2)

                def mm(e):
                    last = None
                    for kc in range(8):
                        last = e.matmul(ps[pa], lhsT=hT[:, kc, t0:t0 + 128], rhs=wv[:, kc, :], start=(kc == 0), stop=(kc == 7))
                    return last
                P.op("pe", mm, r=["hT", "wv"], w=["ps%d" % pa])

            def sgu_post(n):
                ti, tc = divmod(n, 4)
                b = ti % 2
                pp = n % 2
                pa, pbb = 2 * pp, 2 * pp + 1
                vb16 = vbs[pp]
                vk = "vb16_%d" % pp
                gelu_from(ps[pa], [], ["ps%d" % pa], vb16, [vk], 512, glr[pp], glt[pp], str(pp))

                def mm2(e):
                    last = None
                    for g in range(4):
                        e.matmul(ps[pbb][:, g * 128:(g + 1) * 128], lhsT=vb16[:, g * 128:(g + 1) * 128], rhs=sgw[:, g, :], start=True, stop=False)
                        last = e.matmul(ps[pbb][:, g * 128:(g + 1) * 128], lhsT=onesb[0:1, :], rhs=sgb[0:1, g * 128:(g + 1) * 128], start=False, stop=True)
                    return last
                P.op("pe", mm2, r=[vk, "sgw", "sgb", "onesb"], w=["ps%d" % pbb])
                P.op("dve", lambda e: e.tensor_tensor(out=at[b][:, :, tc * 128:(tc + 1) * 128], in0=ut[b][:, :, tc * 128:(tc + 1) * 128],
                                                      in1=ps[pbb].rearrange("p (g q) -> p g q", g=4), op=ALU.mult),
                     x=["ps%d" % pbb], r=["ut%d" % b], w=["at%d" % b])
                if tc == 3:
                    P.op("sp", lambda e: e.dma_start(out=aT_d.rearrange("(c p) t -> p c t", p=128)[:, :, ti * 512:(ti + 1) * 512], in_=at[b]),
                         r=["at%d" % b], w=["d:aT"], dma="st:at%d" % b)
            NCHK = T // 128
            sgu_mm(0)
            for n in range(NCHK):
                if n % 4 == 0 and n // 4 + 1 < NT:
                    ldu(n // 4 + 1)
                if n + 1 < NCHK:
                    sgu_mm(n + 1)
                sgu_post(n)
            phase_end()
            for L in (4096, 256):
                hyena_filters(i, L)
                phase_end()
            hyena_conv_eo(i, 0)
            phase_end()
            for sq_ in range(4):
                hyena_conv(i, 256, 4096 + 256 * sq_, tg="_%d" % (sq_ % 2))
            phase_end()

            def in_tiles(ti, dst, key):
                P.op("sp", lambda e: e.dma_start(out=dst[:, 0:4, :], in_=aT_d.rearrange("(c p) t -> p c t", p=128)[:, :, ti * 512:(ti + 1) * 512]),
                     r=["d:aT"], w=[key], dma=key)
                P.op("sp", lambda e: e.dma_start(out=dst[:, 4:8, :], in_=zT_d.rearrange("(c p) t -> p c t", p=128)[:, :, ti * 512:(ti + 1) * 512]),
                     r=["d:zT"], w=[key], dma=key)
            out_proj(l, 0, W["mix_w_out"][i], 8, in_tiles)
            phase_end()

        def sin_rr(arg, out_ap, n, res_in, res_out):
            a_, b_ = sr_a[0:64, 0:n], sr_b[0:64, 0:n]
            P.op("act", lambda e: e.activation(out=a_, in_=arg, func=AF.Sin, scale=0.5), r=res_in, w=["sra"])
            P.op("act", lambda e: e.activation(out=b_, in_=arg, func=AF.Sin, scale=0.25), r=res_in, w=["srb"])
            P.op("dve", lambda e: e.tensor_tensor(out=b_, in0=b_, in1=b_, op=ALU.mult), r=["srb"], w=["srb"])
            P.op("dve", lambda e: e.tensor_scalar(out=b_, in0=b_, scalar1=-4.0, scalar2=2.0, op0=ALU.mult, op1=ALU.add), r=["srb"], w=["srb"])
            P.op("dve", lambda e: e.tensor_tensor(out=out_ap, in0=a_, in1=b_, op=ALU.mult), r=["sra", "srb"], w=res_out)

        sr_a = sr_b = None

        def hyena_filters(i, L):
            nonlocal sr_a, sr_b
            nch = L // 128
            h2 = A.alloc([L], F32)
            hf_tm = A.alloc([nch, 1024], BF16)
            w3 = A.alloc([1024], F32)
            dec = A.alloc([1024], F32)
            nd = A.alloc([nch], F32)
            m1 = A.mark()
            zf = A.alloc([L], F32)
            w1 = A.alloc([64], F32)
            w2 = A.alloc([64], F32)
            frb = A.alloc([2], F32)
            h1 = A.alloc([L], F32)
            arg = A.alloc([512], F32)
            sr_a = A.alloc([512], F32)
            sr_b = A.alloc([512], F32)
            P.op("sp", lambda e: e.dma_start(out=zf[0:33, :], in_=cd["zf%d" % L]), w=["zf"], dma="hfl")
            P.op("sp", lambda e: e.dma_start(out=w1[0:33, :], in_=W["hy_f_w1"][i]), w=["w1"], dma="hfl")
            P.op("sp", lambda e: e.dma_start(out=w2[0:64, :], in_=W["hy_f_w2"][i]), w=["w2"], dma="hfl")
            P.op("sp", lambda e: e.dma_start(out=w3[0:64, :], in_=W["hy_f_w3"][i]), w=["w3"], dma="hfl")
            P.op("sp", lambda e: e.dma_start(out=dec, in_=dec_d[i]), w=["dec"], dma="hfl")
            P.op("sp", lambda e: e.dma_start(out=nd, in_=cd["ndE" if L == 4096 else "nd%d" % L]), w=["nd"], dma="hfl")
            P.op("act", lambda e: e.activation(out=dec, in_=dec, func=AF.Abs), r=["dec"], w=["dec"])
            for q, bn in enumerate(("b1", "b2")):
                P.op("dve", lambda e, q=q, bn=bn: e.tensor_tensor(out=frb[0:64, q:q + 1], in0=pvc("fr%d" % i, 0, 64), in1=pvc("%s%d" % (bn, i), 0, 64), op=ALU.mult),
                     r=["pvt"], w=["frb"])
            tw = min(512, L)
            for (wmat, kk, src, dst, q) in ((w1, 33, zf, h1, 0), (w2, 64, h1, h2, 1)):
                for tt in range(L // tw):
                    P.op("pe", lambda e, wmat=wmat, kk=kk, src=src, tt=tt: e.matmul(ps[0][0:64, 0:tw], lhsT=wmat[0:kk, :], rhs=src[0:kk, tt * tw:(tt + 1) * tw], start=True, stop=True),
                         r=["w1", "w2", "zf", "h1"], w=["ps0"])
                    P.op("dve", lambda e, q=q: e.tensor_scalar(out=arg[0:64, 0:tw], in0=ps[0][0:64, 0:tw], scalar1=pvc("fr%d" % i, 0, 64), scalar2=frb[0:64, q:q + 1],
                                                               op0=ALU.mult, op1=ALU.add), x=["ps0"], r=["pvt", "frb"], w=["arg"])
                    sin_rr(arg[0:64, 0:tw], dst[0:64, tt * tw:(tt + 1) * tw], tw, ["arg"], ["h1" if q == 0 else "h2"])
            P.barrier()
            A.release(m1)
            win_ = A.alloc([1024], F32)
            hwf = A.alloc([1024], F32)
            hab = A.alloc([1024], F32)
            rec = A.alloc([1024], F32)
            EO = (L == 4096)

            def w_mm(sc):
                if EO:
                    par_, mc_ = divmod(sc, 16)
                    h2s = h2[0:64, 256 * mc_ + par_:256 * mc_ + par_ + 255:2]
                else:
                    h2s = h2[0:64, sc * 128:(sc + 1) * 128]
                b0 = 1 + 4 * (sc % 2)

                def mm(e):
                    e.matmul(ps[b0], lhsT=h2s, rhs=w3[0:64, 0:512], start=True, stop=True)
                    return e.matmul(ps[b0 + 1], lhsT=h2s, rhs=w3[0:64, 512:1024], start=True, stop=True)
                P.op("pe", mm, r=["h2", "w3"], w=["ps%d" % b0, "ps%d" % (b0 + 1)])

            def w_post(sc):
                b0 = 1 + 4 * (sc % 2)
                P.op("act", lambda e: e.activation(out=win_, in_=dec, func=AF.Exp, scale=nd[:, sc:sc + 1]), r=["dec", "nd"], w=["win"])
                for hh in range(2):
                    P.op("dve", lambda e, hh=hh: e.tensor_tensor(out=hwf[:, hh * 512:(hh + 1) * 512], in0=ps[b0 + hh], in1=win_[:, hh * 512:(hh + 1) * 512], op=ALU.mult),
                         x=["ps%d" % (b0 + hh)], r=["win"], w=["hwf"])
                P.op("act", lambda e: e.activation(out=hab, in_=hwf, func=AF.Abs), r=["hwf"], w=["hab"])
                P.op("dve", lambda e: e.tensor_copy(out=hf_tm[:, sc, :], in_=hwf), r=["hwf"], w=["hftm"])

                def mm3(e):
                    e.matmul(ps[3], lhsT=ones, rhs=hab[:, 0:512], start=(sc == 0), stop=(sc == nch - 1))
                    return e.matmul(ps[4], lhsT=ones, rhs=hab[:, 512:1024], start=(sc == 0), stop=(sc == nch - 1))
                P.op("pe", mm3, r=["hab", "ones"], w=["ps3", "ps4"])
            w_mm(0)
            for sc in range(nch):
                if sc + 1 < nch:
                    w_mm(sc + 1)
                w_post(sc)
            for hh in range(2):
                P.op("dve", lambda e, hh=hh: e.tensor_scalar(out=rec[:, hh * 512:(hh + 1) * 512], in0=ps[3 + hh], scalar1=EPS, scalar2=None, op0=ALU.add),
                     x=["ps%d" % (3 + hh)], w=["rec"])
            P.op("dve", lambda e: e.reciprocal(out=rec, in_=rec), r=["rec"], w=["rec"])
            if EO:
                ft = [A.alloc([2, 16, 2, 128], BF16) for _ in range(2)]
                hfo = [A.alloc([4, 1024], F32) for _ in range(2)]
                osb = [A.alloc([512], F32) for _ in range(2)]
                tf = A.alloc([512], F32)

                def ldfe(kc):
                    P.op("sp", lambda e: e.dma_start(out=ft[kc % 2], in_=cd["FE"][kc]), w=["ft%d" % (kc % 2)], dma="ft%d" % (kc % 2))
                ldfe(0)
                for kc in range(16):
                    if kc + 1 < 16:
                        ldfe(kc + 1)
                    b = kc % 2
                    for hh in range(2):
                        hs = slice(hh * 512, (hh + 1) * 512)
                        bk = 4 * hh

                        def mm(e, b=b, hs=hs, bk=bk):
                            last = None
                            for bank, (par, cs) in enumerate(((0, 0), (1, 0), (0, 1), (1, 1))):
                                for mc in range(16):
                                    last = e.matmul(ps[bk + bank], lhsT=ft[b][:, par, mc, cs, :], rhs=hf_tm[:, par * 16 + mc, hs], start=(mc == 0), stop=(mc == 15))
                            return last
                        P.op("pe", mm, r=["ft%d" % b, "hftm"], w=["ps%d" % (bk + q_) for q_ in range(4)])
                        P.op("act", lambda e, bk=bk: e.activation(out=osb[0], in_=ps[bk + 1], func=AF.Copy), x=["ps%d" % (bk + 1)], w=["osb0"])
                        P.op("act", lambda e, bk=bk: e.activation(out=osb[1], in_=ps[bk + 3], func=AF.Copy), x=["ps%d" % (bk + 3)], w=["osb1"])
                        for slot, (pe_, ob_, op_, rev) in enumerate(((0, 0, ALU.add, False), (2, 1, ALU.add, False), (0, 0, ALU.subtract, False), (2, 1, ALU.subtract, True))):
                            pe_ = bk + pe_
                            if rev:
                                P.op("dve", lambda e, pe_=pe_, ob_=ob_: e.tensor_tensor(out=tf, in0=osb[ob_], in1=ps[pe_], op=ALU.subtract), x=["ps%d" % pe_], r=["osb%d" % ob_], w=["tf"])
                            else:
                                P.op("dve", lambda e, pe_=pe_, ob_=ob_, op_=op_: e.tensor_tensor(out=tf, in0=ps[pe_], in1=osb[ob_], op=op_), x=["ps%d" % pe_], r=["osb%d" % ob_], w=["tf"])
                            P.op("dve", lambda e, b=b, slot=slot, hs=hs: e.tensor_tensor(out=hfo[b][:, slot, hs], in0=tf, in1=rec[:, hs], op=ALU.mult), r=["tf", "rec"], w=["hfo%d" % b])
                    P.op("sp", lambda e, b=b, kc=kc: e.dma_start(out=hf_d[L][kc * 128:(kc + 1) * 128], in_=hfo[b]), r=["hfo%d" % b], w=["d:hf"], dma="st:hfo%d" % b)
                return
            ft = [A.alloc([nch, 2, 128], BF16) for _ in range(2)]
            hfo = [A.alloc([2, 1024], F32) for _ in range(2)]

            def ldf(kc):
                P.op("sp", lambda e: e.dma_start(out=ft[kc % 2], in_=cd["F%d" % L][kc]), w=["ft%d" % (kc % 2)], dma="ft%d" % (kc % 2))
            ldf(0)
            for kc in range(nch):
                if kc + 1 < nch:
                    ldf(kc + 1)
                b = kc % 2
                for cs in range(2):
                    for hh in range(2):
                        pb = 5 + (cs * 2 + hh) % 3

                        def mm(e, b=b, cs=cs, hh=hh, pb=pb):
                            last = None
                            for sc in range(nch):
                                last = e.matmul(ps[pb], lhsT=ft[b][:, sc, cs, :], rhs=hf_tm[:, sc, hh * 512:(hh + 1) * 512], start=(sc == 0), stop=(sc == nch - 1))
                            return last
                        P.op("pe", mm, r=["ft%d" % b, "hftm"], w=["ps%d" % pb])
                        P.op("dve", lambda e, b=b, cs=cs, hh=hh, pb=pb: e.tensor_tensor(out=hfo[b][:, cs, hh * 512:(hh + 1) * 512], in0=ps[pb],
                                                                                      in1=rec[:, hh * 512:(hh + 1) * 512], op=ALU.mult),
                             x=["ps%d" % pb], r=["rec"], w=["hfo%d" % b])
                P.op("sp", lambda e, b=b, kc=kc: e.dma_start(out=hf_d[L][kc * 128:(kc + 1) * 128], in_=hfo[b]), r=["hfo%d" % b], w=["d:hf"], dma="st:hfo%d" % b)

        def hyena_conv_eo(i, s0):
            L = 4096
            intm = A.alloc([2, 16, 512], BF16)
            Ypq = A.alloc([16, 4, 512], BF16)
            fg = [A.alloc([8192], BF16) for _ in range(2)]
            ftv = [f.rearrange("p (a m c k) -> p a m c k", a=2, m=16, c=2, k=128) for f in fg]
            gtv = [f.rearrange("p (l c t) -> p l c t", l=8, c=2, t=512) for f in fg]
            hft = [A.alloc([4, 512], F32) for _ in range(2)]
            TA, TB, T1, T2, T3, T4, T5, T6 = [A.alloc([512], F32) for _ in range(8)]
            dti = [A.alloc([1024], BF16) for _ in range(2)]
            xti = [A.alloc([1024], BF16) for _ in range(2)]
            zo = A.alloc([4, 1024], BF16)
            zf32 = A.alloc([512], F32)
            vsrc = vbtm_d[s0:s0 + L, :].rearrange("(mc p two) n -> two p mc n", p=128, two=2)
            for par in range(2):
                P.op("sp", lambda e, par=par: e.dma_start(out=intm[:, par], in_=vsrc[par]), r=["d:vbtm"], w=["intm"], dma="intm")

            def tt_(out, a, b, op, xa=(), xb=(), ra=(), rb=(), wname=None):
                P.op("dve", lambda e: e.tensor_tensor(out=out, in0=a, in1=b, op=op), x=list(xa) + list(xb), r=list(ra) + list(rb), w=[wname])

            def run_order(o):
                dsrc = pT_d[512:1024, :] if o == 0 else z1T_d
                xsrc = pT_d[1024 + 512 * o:1536 + 512 * o, :]
                zdst = z1T_d if o == 0 else zT_d

                def ldf(kc):
                    b = kc % 2
                    P.op("sp", lambda e: e.dma_start(out=fg[b], in_=cd["FE"][kc].rearrange("p a m c k -> p (a m c k)")), w=["fg%d" % b], dma="fg%d" % b)
                    P.op("sp", lambda e: e.dma_start(out=hft[b], in_=hf_d[L][kc * 128:(kc + 1) * 128, :, o * 512:(o + 1) * 512]), r=["d:hf"], w=["hft%d" % b], dma="hft%d" % b)
                ldf(0)
                for kc in range(16):
                    if kc + 1 < 16:
                        ldf(kc + 1)
                    b = kc % 2

                    bk = 4 * (kc % 2)

                    def mm(e, b=b, bk=bk):
                        last = None
                        for bank, (par, cs) in enumerate(((0, 0), (1, 0), (0, 1), (1, 1))):
                            for mc in range(16):
                                last = e.matmul(ps[bk + bank], lhsT=ftv[b][:, par, mc, cs, :], rhs=intm[:, par, mc, :], start=(mc == 0), stop=(mc == 15))
                        return last
                    P.op("pe", mm, r=["fg%d" % b, "intm"], w=["ps%d" % (bk + q_) for q_ in range(4)])
                    pE0, pE1, pE2, pE3 = (ps[bk + q_] for q_ in range(4))
                    nE = ["ps%d" % (bk + q_) for q_ in range(4)]
                    hk = "hft%d" % b
                    Hc, Hs, Hc2, Hs2 = (hft[b][:, q_, :] for q_ in range(4))
                    P.op("act", lambda e, pE1=pE1: e.activation(out=TA, in_=pE1, func=AF.Copy), x=[nE[1]], w=["TA"])
                    P.op("act", lambda e, pE3=pE3: e.activation(out=TB, in_=pE3, func=AF.Copy), x=[nE[3]], w=["TB"])
                    tt_(T1, pE0, TA, ALU.add, xa=[nE[0]], rb=["TA"], wname="T1")
                    tt_(T2, pE0, TA, ALU.subtract, xa=[nE[0]], rb=["TA"], wname="T2")
                    tt_(T3, pE2, TB, ALU.add, xa=[nE[2]], rb=["TB"], wname="T3")
                    tt_(T4, TB, pE2, ALU.subtract, xb=[nE[2]], ra=["TB"], wname="T4")
                    tt_(TA, T1, Hc, ALU.mult, ra=["T1"], rb=[hk], wname="TA")
                    tt_(TB, T3, Hs, ALU.mult, ra=["T3"], rb=[hk], wname="TB")
                    tt_(T5, TA, TB, ALU.subtract, ra=["TA"], rb=["TB"], wname="T5")
                    tt_(TA, T1, Hs, ALU.mult, ra=["T1"], rb=[hk], wname="TA")
                    tt_(TB, T3, Hc, ALU.mult, ra=["T3"], rb=[hk], wname="TB")
                    tt_(T6, TA, TB, ALU.add, ra=["TA"], rb=["TB"], wname="T6")
                    tt_(TA, T2, Hc2, ALU.mult, ra=["T2"], rb=[hk], wname="TA")
                    tt_(TB, T4, Hs2, ALU.mult, ra=["T4"], rb=[hk], wname="TB")
                    tt_(T1, TA, TB, ALU.subtract, ra=["TA"], rb=["TB"], wname="T1")
                    tt_(TA, T2, Hs2, ALU.mult, ra=["T2"], rb=[hk], wname="TA")
                    tt_(TB, T4, Hc2, ALU.mult, ra=["T4"], rb=[hk], wname="TB")
                    tt_(T3, TA, TB, ALU.add, ra=["TA"], rb=["TB"], wname="T3")
                    tt_(Ypq[:, kc, 0, :], T5, T1, ALU.add, ra=["T5"], rb=["T1"], wname="Y")
                    tt_(Ypq[:, kc, 1, :], T6, T3, ALU.subtract, ra=["T6"], rb=["T3"], wname="Y")
                    tt_(Ypq[:, kc, 2, :], T5, T1, ALU.subtract, ra=["T5"], rb=["T1"], wname="Y")
                    tt_(Ypq[:, kc, 3, :], T6, T3, ALU.add, ra=["T6"], rb=["T3"], wname="Y")
                seq = [(tt, par, kg) for tt in range(4) for par in range(2) for kg in range(2)]

                def ldg(q):
                    tt, par, kg = seq[q]
                    P.op("sp", lambda e: e.dma_start(out=fg[q % 2], in_=cd["GE"][tt, par, kg].rearrange("p l c t -> p (l c t)")), w=["fg%d" % (q % 2)], dma="fg%d" % (q % 2))
                ldg(0)
                for q, (tt, par, kg) in enumerate(seq):
                    if q + 1 < len(seq):
                        ldg(q + 1)
                    b = q % 2

                    def mm(e, b=b, kg=kg, par=par):
                        last = None
                        for klc in range(8):
                            kc = kg * 8 + klc
                            for cs in range(2):
                                for cch in range(4):
                                    last = e.matmul(ps[4 + cch], lhsT=Ypq[:, kc, 2 * par + cs, cch * 128:(cch + 1) * 128], rhs=gtv[b][:, klc, cs, :],
                                                    start=(kc == 0 and cs == 0), stop=(kc == 15 and cs == 1))
                        return last
                    P.op("pe", mm, r=["Y", "fg%d" % b], w=["ps4", "ps5", "ps6", "ps7"])
                    if kg == 1:
                        t0 = s0 + tt * 1024
                        for cch in range(4):
                            bb = cch % 2
                            P.op("sp", lambda e, bb=bb, cch=cch, t0=t0: e.dma_start(out=dti[bb], in_=dsrc[cch * 128:(cch + 1) * 128, t0:t0 + 1024]),
                                 r=["d:pT", "d:z1T"], w=["dti%d" % bb], dma="dti%d" % bb)
                            P.op("sp", lambda e, bb=bb, cch=cch, t0=t0: e.dma_start(out=xti[bb], in_=xsrc[cch * 128:(cch + 1) * 128, t0:t0 + 1024]),
                                 r=["d:pT"], w=["xti%d" % bb], dma="xti%d" % bb)
                            P.op("dve", lambda e, bb=bb, cch=cch, par=par: e.scalar_tensor_tensor(out=zf32, in0=dti[bb][:, par:1024:2], scalar=pvc("hd%d%d" % (i, o), cch), in1=ps[4 + cch],
                                                                                               op0=ALU.mult, op1=ALU.add), x=["ps%d" % (4 + cch)], r=["dti%d" % bb, "pvt"], w=["zf32"])
                            P.op("dve", lambda e, bb=bb, par=par: e.tensor_tensor(out=zf32, in0=zf32, in1=xti[bb][:, par:1024:2], op=ALU.mult), r=["zf32", "xti%d" % bb], w=["zf32"])
                            P.op("act", lambda e, cch=cch, par=par: e.activation(out=zo[:, cch, par:1024:2], in_=zf32, func=AF.Copy), r=["zf32"], w=["zo%d" % cch])
                            if par == 1:
                                P.op("sp", lambda e, cch=cch, t0=t0: e.dma_start(out=zdst[cch * 128:(cch + 1) * 128, t0:t0 + 1024], in_=zo[:, cch, :]),
                                     r=["zo%d" % cch], w=["d:z1T" if o == 0 else "d:zT"], dma="st:zo%d" % cch)
                            if o == 0:
                                for blk in range(4):
                                    P.op("pe", lambda e, blk=blk: e.transpose(ps[0][:, blk * 128:(blk + 1) * 128], zf32[:, blk * 128:(blk + 1) * 128], ident), r=["zf32", "ident"], w=["ps0"])
                                for blk in range(4):
                                    P.op("act", lambda e, blk=blk, cch=cch, tt=tt, par=par: e.activation(out=intm[:, par, tt * 4 + blk, cch * 128:(cch + 1) * 128],
                                                                                                     in_=ps[0][:, blk * 128:(blk + 1) * 128], func=AF.Copy), x=["ps0"], w=["intm"])
            run_order(0)
            run_order(1)


        def hyena_conv(i, L, s0, tg=""):
            keep = ("ps", "d:pT", "d:hf", "d:vbtm", "pvt", "ident")

            def rn(n):
                return n if n.startswith(keep) else n + tg

            def op(eng, fn, r=(), w=(), dma=None, x=()):
                return P.op(eng, fn, r=[rn(n) for n in r], w=[rn(n) for n in w], dma=(None if dma is None else dma + tg), x=list(x))
            nch = L // 128
            tw = min(512, L)
            ntt = L // tw
            kgn = max(1, nch // 8)
            kl = nch // kgn
            intm = A.alloc([nch, 512], BF16)
            Y = A.alloc([nch, 2, 512], BF16)
            ft = [A.alloc([nch, 2, 128], BF16) for _ in range(2)]
            gt = [A.alloc([kl, 2, tw], BF16) for _ in range(2)]
            hft = [A.alloc([2, 512], F32) for _ in range(2)]
            t1 = A.alloc([512], F32)
            t2 = A.alloc([512], F32)
            dti = [A.alloc([tw], BF16) for _ in range(2)]
            xti = [A.alloc([tw], BF16) for _ in range(2)]
            zf32 = A.alloc([tw], F32)
            zo = [A.alloc([tw], BF16) for _ in range(2)]
            op("sp", lambda e: e.dma_start(out=intm, in_=vbtm_d[s0:s0 + L, :].rearrange("(c p) n -> p c n", p=128)), r=["d:vbtm"], w=["intm"], dma="intm")
            def run_order(o):
                dsrc = pT_d[512:1024, :] if o == 0 else z1T_d
                xsrc = pT_d[1024 + 512 * o:1536 + 512 * o, :]
                zdst = z1T_d if o == 0 else zT_d

                def ldf(kc):
                    b = kc % 2
                    op("sp", lambda e: e.dma_start(out=ft[b], in_=cd["F%d" % L][kc]), w=["cft%d" % b], dma="cft%d" % b)
                    op("sp", lambda e: e.dma_start(out=hft[b], in_=hf_d[L][kc * 128:(kc + 1) * 128, :, o * 512:(o + 1) * 512]), r=["d:hf"], w=["hft%d" % b], dma="hft%d" % b)
                ldf(0)
                for kc in range(nch):
                    if kc + 1 < nch:
                        ldf(kc + 1)
                    b = kc % 2

                    def mm(e, b=b):
                        last = None
                        for cs in range(2):
                            for sc in range(nch):
                                last = e.matmul(ps[cs], lhsT=ft[b][:, sc, cs, :], rhs=intm[:, sc, :], start=(sc == 0), stop=(sc == nch - 1))
                        return last
                    op("pe", mm, r=["cft%d" % b, "intm"], w=["ps0", "ps1"])
                    Hc, Hs = hft[b][:, 0, :], hft[b][:, 1, :]
                    hk = "hft%d" % b
                    op("dve", lambda e, Hc=Hc: e.tensor_tensor(out=t1, in0=ps[0], in1=Hc, op=ALU.mult), x=["ps0"], r=[hk], w=["t1"])
                    op("dve", lambda e, Hs=Hs: e.tensor_tensor(out=t2, in0=ps[1], in1=Hs, op=ALU.mult), x=["ps1"], r=[hk], w=["t2"])
                    op("dve", lambda e, kc=kc: e.tensor_tensor(out=Y[:, kc, 0, :], in0=t1, in1=t2, op=ALU.subtract), r=["t1", "t2"], w=["Y"])
                    op("dve", lambda e, Hs=Hs: e.tensor_tensor(out=t1, in0=ps[0], in1=Hs, op=ALU.mult), x=["ps0"], r=[hk], w=["t1"])
                    op("dve", lambda e, Hc=Hc: e.tensor_tensor(out=t2, in0=ps[1], in1=Hc, op=ALU.mult), x=["ps1"], r=[hk], w=["t2"])
                    op("dve", lambda e, kc=kc: e.tensor_tensor(out=Y[:, kc, 1, :], in0=t1, in1=t2, op=ALU.add), r=["t1", "t2"], w=["Y"])
                seq = [(tt, kg) for tt in range(ntt) for kg in range(kgn)]

                def ldg(q):
                    tt, kg = seq[q]
                    op("sp", lambda e: e.dma_start(out=gt[q % 2], in_=cd["G%d" % L][tt, kg]), w=["gt%d" % (q % 2)], dma="gt%d" % (q % 2))
                ldg(0)
                for q, (tt, kg) in enumerate(seq):
                    if q + 1 < len(seq):
                        ldg(q + 1)
                    b = q % 2

                    def mm(e, b=b, kg=kg):
                        last = None
                        for klc in range(kl):
                            kc = kg * kl + klc
                            for cs in range(2):
                                for cch in range(4):
                                    last = e.matmul(ps[2 + cch][:, 0:tw], lhsT=Y[:, kc, cs, cch * 128:(cch + 1) * 128], rhs=gt[b][:, klc, cs, :],
                                                    start=(kc == 0 and cs == 0), stop=(kc == nch - 1 and cs == 1))
                        return last
                    op("pe", mm, r=["Y", "gt%d" % b], w=["ps2", "ps3", "ps4", "ps5"])
                    if kg == kgn - 1:
                        t0 = s0 + tt * tw
                        for cch in range(4):
                            bb = cch % 2
                            op("sp", lambda e, bb=bb, cch=cch, t0=t0: e.dma_start(out=dti[bb], in_=dsrc[cch * 128:(cch + 1) * 128, t0:t0 + tw]),
                                 r=["d:pT", "d:z1T"], w=["dti%d" % bb], dma="dti%d" % bb)
                            op("sp", lambda e, bb=bb, cch=cch, t0=t0: e.dma_start(out=xti[bb], in_=xsrc[cch * 128:(cch + 1) * 128, t0:t0 + tw]),
                                 r=["d:pT"], w=["xti%d" % bb], dma="xti%d" % bb)
                            op("dve", lambda e, bb=bb, cch=cch: e.scalar_tensor_tensor(out=zf32, in0=dti[bb], scalar=pvc("hd%d%d" % (i, o), cch), in1=ps[2 + cch][:, 0:tw],
                                                                                        op0=ALU.mult, op1=ALU.add), x=["ps%d" % (2 + cch)], r=["dti%d" % bb, "pvt"], w=["zf32"])
                            op("dve", lambda e, bb=bb: e.tensor_tensor(out=zf32, in0=zf32, in1=xti[bb], op=ALU.mult), r=["zf32", "xti%d" % bb], w=["zf32"])
                            op("act", lambda e, bb=bb: e.activation(out=zo[bb], in_=zf32, func=AF.Copy), r=["zf32"], w=["zo%d" % bb])
                            op("sp", lambda e, bb=bb, cch=cch, t0=t0: e.dma_start(out=zdst[cch * 128:(cch + 1) * 128, t0:t0 + tw], in_=zo[bb]),
                                 r=["zo%d" % bb], w=["d:z1T" if o == 0 else "d:zT"], dma="st:zo%d" % bb)
                            if o == 0:
                                for tc in range(tw // 128):
                                    op("pe", lambda e, tc=tc: e.transpose(ps[6][:, tc * 128:(tc + 1) * 128], zf32[:, tc * 128:(tc + 1) * 128], ident), r=["zf32", "ident"], w=["ps6"])
                                for tc in range(tw // 128):
                                    op("act", lambda e, tc=tc, cch=cch, tt=tt: e.activation(out=intm[:, tt * (tw // 128) + tc, cch * 128:(cch + 1) * 128],
                                                                                           in_=ps[6][:, tc * 128:(tc + 1) * 128], func=AF.Copy), x=["ps6"], w=["intm"])
            run_order(0)
            run_order(1)

        def mla(l):
            j = l // 2
            ckvT = A.alloc([2, NKEY], BF16)
            krT = A.alloc([NKEY], F32)
            m_persist = A.mark()
            hT = A.alloc([8, T], BF16)
            m0 = A.mark()
            prologue(l, 0, hT)
            P.barrier()
            A.release(m0)
            wdq = A.alloc([8, 512], BF16)
            load_w(wdq, W["mla_w_dq"][j].rearrange("(c p) n -> p c n", p=128), "wdq")
            wdkv = A.alloc([8, 320], BF16)
            load_w(wdkv, W["mla_w_dkv"][j].rearrange("(c p) n -> p c n", p=128), "wdkv")
            rawq = A.alloc([4, 512], F32)
            qnb = [A.alloc([4, 512], BF16) for _ in range(2)]
            sq = A.alloc([4, 512], BF16)
            rs = A.alloc([512], F32)
            ckf = A.alloc([2, 512], F32)
            tok = [A.alloc([256], F32) for _ in range(2)]
            tokr = [A.alloc([64], F32) for _ in range(2)]
            cin = A.alloc([2, 256], F32)
            cinr = A.alloc([2, 64], F32)
            def proj_tile(ti):
                sl = slice(ti * 512, (ti + 1) * 512)
                for oc in range(4):
                    pb = oc % 2

                    def mm(e, oc=oc, pb=pb):
                        last = None
                        for kc in range(8):
                            last = e.matmul(ps[pb], lhsT=wdq[:, kc, oc * 128:(oc + 1) * 128], rhs=hT[:, kc, sl], start=(kc == 0), stop=(kc == 7))
                        return last
                    P.op("pe", mm, r=["wdq", "hT"], w=["ps%d" % pb])
                    P.op("act", lambda e, oc=oc, pb=pb: e.activation(out=rawq[:, oc, :], in_=ps[pb], func=AF.Copy), x=["ps%d" % pb], w=["rawq"])
                P.op("act", lambda e: e.activation(out=sq, in_=rawq, func=AF.Square), r=["rawq"], w=["sq"])

                def mms(e):
                    last = None
                    for oc in range(4):
                        last = e.matmul(ps[2], lhsT=onesb, rhs=sq[:, oc, :], start=(oc == 0), stop=(oc == 3))
                    return last
                P.op("pe", mms, r=["sq", "onesb"], w=["ps2"])
                P.op("act", lambda e: e.activation(out=rs, in_=ps[2], func=AF.Sqrt, scale=1.0 / 512, bias=epsT[:, 0:1]), x=["ps2"], r=["eps"], w=["rs"])
                P.op("dve", lambda e: e.reciprocal(out=rs, in_=rs), r=["rs"], w=["rs"])
                for oc in range(4):
                    P.op("dve", lambda e, oc=oc: e.tensor_tensor(out=rawq[:, oc, :], in0=rawq[:, oc, :], in1=rs, op=ALU.mult), r=["rawq", "rs"], w=["rawq"])
                    P.op("act", lambda e, oc=oc, ti=ti: e.activation(out=qnb[ti % 2][:, oc, :], in_=rawq[:, oc, :], func=AF.Identity, scale=pvc("qn%d" % j, oc)), r=["rawq", "pvt"], w=["qnb%d" % (ti % 2)])
                P.op("sp", lambda e, ti=ti: e.dma_start(out=qnT_d.rearrange("(c p) t -> p c t", p=128)[:, :, ti * 512:(ti + 1) * 512], in_=qnb[ti % 2]),
                     r=["qnb%d" % (ti % 2)], w=["d:qnT"], dma="st:qnb%d" % (ti % 2))
                for oc in range(3):
                    m_ = 128 if oc < 2 else 64
                    pb = 3 + oc

                    def mm(e, oc=oc, pb=pb, m_=m_):
                        last = None
                        for kc in range(8):
                            last = e.matmul(ps[pb][0:m_, :], lhsT=wdkv[:, kc, oc * 128:oc * 128 + m_], rhs=hT[:, kc, sl], start=(kc == 0), stop=(kc == 7))
                        return last
                    P.op("pe", mm, r=["wdkv", "hT"], w=["ps%d" % pb])
                    if oc < 2:
                        P.op("act", lambda e, oc=oc, pb=pb: e.activation(out=ckf[:, oc, :], in_=ps[pb], func=AF.Copy), x=["ps%d" % pb], w=["ckf"])
                    else:
                        P.op("act", lambda e, pb=pb: e.activation(out=krT[0:64, sl], in_=ps[pb][0:64, :], func=AF.Copy), x=["ps%d" % pb], w=["krT"])
                P.op("act", lambda e: e.activation(out=sq[:, 0:2, :], in_=ckf, func=AF.Square), r=["ckf"], w=["sq"])

                def mms2(e):
                    e.matmul(ps[2], lhsT=onesb, rhs=sq[:, 0, :], start=True, stop=False)
                    return e.matmul(ps[2], lhsT=onesb, rhs=sq[:, 1, :], start=False, stop=True)
                P.op("pe", mms2, r=["sq", "onesb"], w=["ps2"])
                P.op("act", lambda e: e.activation(out=rs, in_=ps[2], func=AF.Sqrt, scale=1.0 / 256, bias=epsT[:, 0:1]), x=["ps2"], r=["eps"], w=["rs"])
                P.op("dve", lambda e: e.reciprocal(out=rs, in_=rs), r=["rs"], w=["rs"])
                for oc in range(2):
                    P.op("dve", lambda e, oc=oc: e.tensor_tensor(out=ckf[:, oc, :], in0=ckf[:, oc, :], in1=rs, op=ALU.mult), r=["ckf", "rs"], w=["ckf"])
                    P.op("dve", lambda e, oc=oc: e.tensor_scalar(out=ckf[:, oc, :], in0=ckf[:, oc, :], scalar1=pvc("kn%d" % j, oc), scalar2=None, op0=ALU.mult), r=["ckf", "pvt"], w=["ckf"])
                    P.op("act", lambda e, oc=oc: e.activation(out=ckvT[:, oc, sl], in_=ckf[:, oc, :], func=AF.Copy), r=["ckf"], w=["ckvT"])
                if ti >= 8:
                    for tc in range(4):
                        sqi = (ti - 8) * 2 + tc // 2
                        r0 = (tc % 2) * 128
                        b = tc % 2

                        def tr(e, tc=tc):
                            e.transpose(ps[6][:, 0:128], ckf[:, 0, tc * 128:(tc + 1) * 128], ident)
                            e.transpose(ps[6][:, 128:256], ckf[:, 1, tc * 128:(tc + 1) * 128], ident)
                            return e.transpose(ps[6][:, 256:320], krT[0:64, ti * 512 + tc * 128:ti * 512 + (tc + 1) * 128], ident[0:64, 0:64])
                        P.op("pe", tr, r=["ckf", "krT", "ident"], w=["ps6"])
                        P.op("dve", lambda e, b=b: e.tensor_copy(out=tok[b], in_=ps[6][:, 0:256]), x=["ps6"], w=["tok%d" % b])
                        P.op("dve", lambda e, b=b: e.tensor_copy(out=tokr[b], in_=ps[6][:, 256:320]), x=["ps6"], w=["tokr%d" % b])
                        P.op("sp", lambda e, b=b, sqi=sqi, r0=r0: e.dma_start(out=nckv_d[sqi, j, r0:r0 + 128, :], in_=tok[b]), r=["tok%d" % b], dma="st:tok%d" % b)
                        P.op("sp", lambda e, b=b, sqi=sqi, r0=r0: e.dma_start(out=nkr_d[sqi, j, r0:r0 + 128, :], in_=tokr[b]), r=["tokr%d" % b], dma="st:tokr%d" % b)
            for ti in range(NT):
                proj_tile(ti)
            P.op("sp", lambda e: e.dma_start(out=cin, in_=cckv_d[j].rearrange("(c p) n -> p c n", p=128)), w=["cin"], dma="cin")
            P.op("sp", lambda e: e.dma_start(out=cinr, in_=ckr_d[j].rearrange("(c p) n -> p c n", p=128)), w=["cinr"], dma="cin")
            for tc in range(2):
                def tr(e, tc=tc):
                    e.transpose(ps[6][:, 0:128], cin[:, tc, 0:128], ident)
                    e.transpose(ps[6][:, 128:256], cin[:, tc, 128:256], ident)
                    return e.transpose(ps[6][0:64, 256:384], cinr[:, tc, :], ident)
                P.op("pe", tr, r=["cin", "cinr", "ident"], w=["ps6"])
                for cc in range(2):
                    P.op("act", lambda e, tc=tc, cc=cc: e.activation(out=ckvT[:, cc, T + tc * 128:T + (tc + 1) * 128], in_=ps[6][:, cc * 128:(cc + 1) * 128], func=AF.Copy),
                         x=["ps6"], w=["ckvT"])
                P.op("act", lambda e, tc=tc: e.activation(out=krT[0:64, T + tc * 128:T + (tc + 1) * 128], in_=ps[6][0:64, 256:384], func=AF.Copy), x=["ps6"], w=["krT"])
            P.barrier()
            A.release(m_persist)
            wuq = A.alloc([4, 1536], BF16)
            load_w(wuq, W["mla_w_uq"][j].rearrange("(c p) n -> p c n", p=128), "wuq")
            wukv = A.alloc([2, 2048], BF16)
            load_w(wukv, W["mla_w_ukv"][j].rearrange("(c p) n -> p c n", p=128), "wukv")
            ropec = A.alloc([4096], F32)
            ropes = A.alloc([4096], F32)
            P.op("sp", lambda e: e.dma_start(out=ropec[0:64, :], in_=cd["ropec"]), w=["ropec"], dma="rope")
            P.op("sp", lambda e: e.dma_start(out=ropes[0:64, :], in_=cd["ropes"]), w=["ropes"], dma="rope")
            NKC = 34
            NCH = NKEY // 128
            Khr = A.alloc([NKEY], BF16)
            krss = A.alloc([NCH], F32)
            KhnA = [A.alloc([NKC * 128], BF16) for _ in range(2)]
            VhA = [A.alloc([NKC, 128], BF16) for _ in range(2)]
            sclA = [A.alloc([NKC], F32) for _ in range(2)]
            KhnB = [A.alloc([256], BF16) for _ in range(4)]
            VhB = [A.alloc([2, 128], BF16) for _ in range(4)]
            sclB = [A.alloc([2], F32) for _ in range(4)]
            sqk = A.alloc([512], BF16)
            tk = A.alloc([4], F32)
            sqa = A.alloc([512], BF16)
            sqr = A.alloc([512], BF16)
            rsa = A.alloc([512], F32)
            tn = A.alloc([512], F32)
            tr_ = A.alloc([512], F32)
            tr2 = A.alloc([512], F32)
            Qn = [A.alloc([512], BF16) for _ in range(2)]
            Qr = [A.alloc([512], BF16) for _ in range(2)]
            qin = [A.alloc([4, 512], BF16) for _ in range(2)]
            PT = [A.alloc([512], BF16) for _ in range(4)]
            rec = A.alloc([512], F32)
            ob = [A.alloc([512], BF16) for _ in range(2)]
            sc_ = 1.0 / math.sqrt(192.0)
            kgr = "khr%d" % j

            for c0 in range(0, NCH, 4):
                cn = min(4, NCH - c0)
                P.op("act", lambda e, c0=c0, cn=cn: e.activation(out=sqr[0:64, 0:cn * 128], in_=krT[0:64, c0 * 128:(c0 + cn) * 128], func=AF.Square), r=["krT"], w=["sqr"])

                def mm(e, c0=c0, cn=cn):
                    last = None
                    for a_ in range(cn):
                        last = e.matmul(ps[4][:, c0 + a_:c0 + a_ + 1], lhsT=sqr[0:64, a_ * 128:(a_ + 1) * 128], rhs=onesb[0:64, 0:1], start=True, stop=True)
                    return last
                P.op("pe", mm, r=["sqr", "onesb"], w=["ps4"])
            P.op("dve", lambda e: e.tensor_copy(out=krss, in_=ps[4][:, 0:NCH]), x=["ps4"], w=["krss"])
            for c0 in range(0, NKEY, 512):
                n = min(512, NKEY - c0)
                cs_ = slice(c0, c0 + n)
                P.op("dve", lambda e, cs_=cs_, n=n: e.tensor_scalar(out=tr2[0:64, 0:n], in0=krT[0:64, cs_], scalar1=pvc(kgr, 0, 64), scalar2=None, op0=ALU.mult), r=["krT", "pvt"], w=["tr2"])
                if c0 < 4096:
                    P.op("pe", lambda e, n=n: e.matmul(ps[6][0:64, 0:n], lhsT=rotT[0:64, :], rhs=tr2[0:64, 0:n], start=True, stop=True), r=["rotT", "tr2"], w=["ps6"])
                    P.op("dve", lambda e, cs_=cs_, n=n: e.tensor_tensor(out=tn[0:64, 0:n], in0=ps[6][0:64, 0:n], in1=ropes[0:64, cs_], op=ALU.mult), x=["ps6"], r=["ropes"], w=["tn"])
                    P.op("dve", lambda e, cs_=cs_, n=n: e.tensor_tensor(out=tr2[0:64, 0:n], in0=tr2[0:64, 0:n], in1=ropec[0:64, cs_], op=ALU.mult), r=["tr2", "ropec"], w=["tr2"])
                    P.op("dve", lambda e, cs_=cs_, n=n: e.tensor_tensor(out=Khr[0:64, cs_], in0=tr2[0:64, 0:n], in1=tn[0:64, 0:n], op=ALU.add), r=["tr2", "tn"], w=["Khr"])
                else:
                    P.op("dve", lambda e, cs_=cs_, n=n: e.tensor_copy(out=Khr[0:64, cs_], in_=tr2[0:64, 0:n]), r=["tr2"], w=["Khr"])

            def kprep(hd, S, Khn_, Vh_, scl_, tag):
                kch = S["kch"]
                groups = [kch[a:a + 4] for a in range(0, len(kch), 4)]
                for gi, grp in enumerate(groups):
                    ng = len(grp)
                    n = ng * 128
                    c0 = grp[0] * 128
                    assert grp[-1] == grp[0] + ng - 1
                    lo = gi * 512

                    def mm(e, c0=c0, n=n):
                        last = None
                        for kc in range(2):
                            last = e.matmul(ps[4][:, 0:n], lhsT=wukv[:, kc, hd * 256:hd * 256 + 128], rhs=ckvT[:, kc, c0:c0 + n], start=(kc == 0), stop=(kc == 1))
                        return last
                    P.op("pe", mm, r=["wukv", "ckvT"], w=["ps4"])
                    yield
                    P.op("act", lambda e, n=n: e.activation(out=sqk[:, 0:n], in_=ps[4][:, 0:n], func=AF.Square), x=["ps4"], w=["sqk"])
                    yield
                    P.op("dve", lambda e, lo=lo, n=n: e.tensor_scalar(out=Khn_[:, lo:lo + n], in0=ps[4][:, 0:n], scalar1=pvc("khn%d" % j), scalar2=None, op0=ALU.mult),
                         x=["ps4"], r=["pvt"], w=["Khn" + tag])
                    yield

                    def mm1(e, ng=ng):
                        last = None
                        for a_ in range(ng):
                            last = e.matmul(ps[6][:, a_:a_ + 1], lhsT=sqk[:, a_ * 128:(a_ + 1) * 128], rhs=onesb[:, 0:1], start=True, stop=True)
                        return last
                    P.op("pe", mm1, r=["sqk", "onesb"], w=["ps6"])
                    yield
                    P.op("dve", lambda e, ng=ng, g0=grp[0]: e.tensor_tensor(out=tk[:, 0:ng], in0=ps[6][:, 0:ng], in1=krss[:, g0:g0 + ng], op=ALU.add), x=["ps6"], r=["krss"], w=["tk"])
                    yield
                    P.op("act", lambda e, ng=ng: e.activation(out=tk[:, 0:ng], in_=tk[:, 0:ng], func=AF.Ln, scale=1.0 / 192, bias=epsT[:, 0:1]), r=["tk", "eps"], w=["tk"])
                    yield
                    P.op("act", lambda e, ng=ng: e.activation(out=tk[:, 0:ng], in_=tk[:, 0:ng], func=AF.Exp, scale=-0.5), r=["tk"], w=["tk"])
                    yield
                    P.op("dve", lambda e, ng=ng, gi=gi: e.tensor_scalar(out=scl_[:, gi * 4:gi * 4 + ng], in0=tk[:, 0:ng], scalar1=sc_, scalar2=None, op0=ALU.mult), r=["tk"], w=["scl" + tag])
                    yield

                    def mmv(e, grp=grp):
                        last = None
                        for a_, ch in enumerate(grp):
                            for kc in range(2):
                                last = e.matmul(ps[5][:, a_ * 128:(a_ + 1) * 128], lhsT=ckvT[:, kc, ch * 128:(ch + 1) * 128],
                                                rhs=wukv[:, kc, hd * 256 + 128:hd * 256 + 256], start=(kc == 0), stop=(kc == 1))
                        return last
                    P.op("pe", mmv, r=["wukv", "ckvT"], w=["ps5"])
                    yield
                    P.op("dve", lambda e, gi=gi, ng=ng, n=n: e.tensor_copy(out=Vh_[:, gi * 4:gi * 4 + ng, :], in_=ps[5][:, 0:n].rearrange("p (a d) -> p a d", d=128)),
                         x=["ps5"], w=["Vh" + tag])
                    yield

            def qprep(hd, S, qt, slot):
                qw = min(512, S["nq"])
                q0 = S["q0"] + qt * qw
                rope = S["rope"]
                qb_ = qin[slot]
                qn_, qr_ = Qn[slot], Qr[slot]
                qres = "Qh%d" % slot
                P.op("sp", lambda e: e.dma_start(out=qb_[:, :, 0:qw], in_=qnT_d.rearrange("(c p) t -> p c t", p=128)[:, :, q0:q0 + qw]),
                     r=["d:qnT"], w=["qin%d" % slot], dma="qin%d" % slot)
                yield

                def mmq(e):
                    last = None
                    for kc in range(4):
                        e.matmul(ps[4][:, 0:qw], lhsT=wuq[:, kc, hd * 192:hd * 192 + 128], rhs=qb_[:, kc, 0:qw], start=(kc == 0), stop=(kc == 3))
                    for kc in range(4):
                        last = e.matmul(ps[5][0:64, 0:qw], lhsT=wuq[:, kc, hd * 192 + 128:hd * 192 + 192], rhs=qb_[:, kc, 0:qw], start=(kc == 0), stop=(kc == 3))
                    return last
                P.op("pe", mmq, r=["wuq", "qin%d" % slot], w=["ps4", "ps5"])
                yield
                P.op("act", lambda e: e.activation(out=sqa[:, 0:qw], in_=ps[4][:, 0:qw], func=AF.Square), x=["ps4"], w=["sqa"])
                yield
                P.op("act", lambda e: e.activation(out=tr_[0:64, 0:qw], in_=ps[5][0:64, 0:qw], func=AF.Copy), x=["ps5"], w=["tr"])
                yield
                P.op("act", lambda e: e.activation(out=sqr[0:64, 0:qw], in_=tr_[0:64, 0:qw], func=AF.Square), r=["tr"], w=["sqr"])
                yield

                def mm(e):
                    e.matmul(ps[6][:, 0:qw], lhsT=onesb, rhs=sqa[:, 0:qw], start=True, stop=False)
                    return e.matmul(ps[6][:, 0:qw], lhsT=onesb[0:64, :], rhs=sqr[0:64, 0:qw], start=False, stop=True)
                P.op("pe", mm, r=["sqa", "sqr", "onesb"], w=["ps6"])
                yield
                P.op("act", lambda e: e.activation(out=rsa[:, 0:qw], in_=ps[6][:, 0:qw], func=AF.Ln, scale=1.0 / 192, bias=epsT[:, 0:1]), x=["ps6"], r=["eps"], w=["rsa"])
                yield
                P.op("act", lambda e: e.activation(out=rsa[:, 0:qw], in_=rsa[:, 0:qw], func=AF.Exp, scale=-0.5), r=["rsa"], w=["rsa"])
                yield
                P.op("dve", lambda e: e.tensor_tensor(out=tn[:, 0:qw], in0=ps[4][:, 0:qw], in1=rsa[:, 0:qw], op=ALU.mult), x=["ps4"], r=["rsa"], w=["tn"])
                yield
                P.op("dve", lambda e: e.tensor_scalar(out=qn_[:, 0:qw], in0=tn[:, 0:qw], scalar1=pvc("qhn%d" % j), scalar2=None, op0=ALU.mult), r=["tn", "pvt"], w=[qres])
                yield
                P.op("dve", lambda e: e.tensor_tensor(out=tr2[0:64, 0:qw], in0=tr_[0:64, 0:qw], in1=rsa[0:64, 0:qw], op=ALU.mult), r=["tr", "rsa"], w=["tr2"])
                yield
                if not rope:
                    P.op("dve", lambda e: e.tensor_scalar(out=qr_[0:64, 0:qw], in0=tr2[0:64, 0:qw], scalar1=pvc("qhr%d" % j, 0, 64), scalar2=None, op0=ALU.mult), r=["tr2", "pvt"], w=[qres])
                    yield
                else:
                    tc_ = slice(q0, q0 + qw)
                    P.op("dve", lambda e: e.tensor_scalar(out=tr2[0:64, 0:qw], in0=tr2[0:64, 0:qw], scalar1=pvc("qhr%d" % j, 0, 64), scalar2=None, op0=ALU.mult), r=["tr2", "pvt"], w=["tr2"])
                    yield
                    P.op("pe", lambda e: e.matmul(ps[6][0:64, 0:qw], lhsT=rotT[0:64, :], rhs=tr2[0:64, 0:qw], start=True, stop=True), r=["rotT", "tr2"], w=["ps6"])
                    yield
                    P.op("dve", lambda e: e.tensor_tensor(out=tn[0:64, 0:qw], in0=ps[6][0:64, 0:qw], in1=ropes[0:64, tc_], op=ALU.mult), x=["ps6"], r=["ropes"], w=["tn"])
                    yield
                    P.op("dve", lambda e: e.tensor_tensor(out=tr2[0:64, 0:qw], in0=tr2[0:64, 0:qw], in1=ropec[0:64, tc_], op=ALU.mult), r=["tr2", "ropec"], w=["tr2"])
                    yield
                    P.op("dve", lambda e: e.tensor_tensor(out=qr_[0:64, 0:qw], in0=tr2[0:64, 0:qw], in1=tn[0:64, 0:qw], op=ALU.add), r=["tr2", "tn"], w=[qres])
                    yield

            def drain(g):
                for _ in g:
                    pass

            def core(hd, S, qt, Khn_, Vh_, scl_, tag, slot, pending, oidx):
                kch = S["kch"]
                nk = len(kch)
                qw = min(512, S["nq"])
                q0 = S["q0"] + qt * qw
                qn_, qr_ = Qn[slot], Qr[slot]
                qres = "Qh%d" % slot
                SB = (0, 1, 7)

                def qk(a):
                    sb = SB[a % 3]
                    pt = PT[a % 4]
                    kc0 = kch[a] * 128

                    def mms_(e):
                        e.matmul(ps[sb][:, 0:qw], lhsT=Khn_[:, a * 128:(a + 1) * 128], rhs=qn_[:, 0:qw], start=True, stop=False)
                        return e.matmul(ps[sb][:, 0:qw], lhsT=Khr[0:64, kc0:kc0 + 128], rhs=qr_[0:64, 0:qw], start=False, stop=True)
                    P.op("pe", mms_, r=["Khn" + tag, "Khr", qres], w=["ps%d" % sb])
                    P.op("act", lambda e: e.activation(out=pt[:, 0:qw], in_=ps[sb][:, 0:qw], func=AF.Exp, scale=scl_[:, a:a + 1]),
                         x=["ps%d" % sb], r=["scl" + tag], w=["PT%d" % (a % 4)])

                def pv(a):
                    pt = PT[a % 4]

                    def mmo(e):
                        e.matmul(ps[2][:, 0:qw], lhsT=Vh_[:, a, :], rhs=pt[:, 0:qw], start=(a == 0), stop=(a == nk - 1))
                        return e.matmul(ps[3][:, 0:qw], lhsT=onesb, rhs=pt[:, 0:qw], start=(a == 0), stop=(a == nk - 1))
                    P.op("pe", mmo, r=["Vh" + tag, "PT%d" % (a % 4), "onesb"], w=["ps2", "ps3"])

                def drip(k):
                    for _ in range(k):
                        while pending:
                            try:
                                next(pending[0])
                                break
                            except StopIteration:
                                pending.pop(0)
                qk(0)
                if nk > 1:
                    qk(1)
                for a in range(nk):
                    if a + 2 < nk:
                        qk(a + 2)
                    pv(a)
                    drip(2 if a % 2 else 1)
                o_ = ob[oidx % 2]
                P.op("dve", lambda e: e.reciprocal(out=rec[:, 0:qw], in_=ps[3][:, 0:qw]), x=["ps3"], w=["rec"])
                P.op("dve", lambda e: e.tensor_tensor(out=o_[:, 0:qw], in0=ps[2][:, 0:qw], in1=rec[:, 0:qw], op=ALU.mult), x=["ps2"], r=["rec"], w=["ob%d" % (oidx % 2)])
                P.op("sp", lambda e: e.dma_start(out=oT_d[hd * 128:(hd + 1) * 128, q0:q0 + qw], in_=o_[:, 0:qw]), r=["ob%d" % (oidx % 2)], w=["d:oT"],
                     dma="st:aob%d" % (oidx % 2))

            seqs = [dict(q0=0, nq=4096, kch=list(range(32)) + [40, 41], rope=True)]
            for sq_ in range(4):
                seqs.append(dict(q0=4096 + 256 * sq_, nq=256, kch=[32 + 2 * sq_, 33 + 2 * sq_], rope=False))
            units = []
            for hd in range(HEADS):
                for si, S in enumerate(seqs):
                    if si == 0:
                        bufs = (KhnA[hd % 2], VhA[hd % 2], sclA[hd % 2], "A%d" % (hd % 2))
                    else:
                        bufs = (KhnB[si - 1], VhB[si - 1], sclB[si - 1], "B%d" % (si - 1))
                    units.append(dict(hd=hd, S=S, si=si, bufs=bufs))
            for u in units:
                u["kgen"] = kprep(u["hd"], u["S"], *u["bufs"])
            items = []
            for ui, u in enumerate(units):
                qw = min(512, u["S"]["nq"])
                for qt in range(u["S"]["nq"] // qw):
                    items.append(dict(ui=ui, qt=qt))
            for ii, it in enumerate(items):
                u = units[it["ui"]]
                it["qgen"] = qprep(u["hd"], u["S"], it["qt"], ii % 2)
            for ii, it in enumerate(items):
                ui = it["ui"]
                u = units[ui]
                drain(u["kgen"])
                drain(it["qgen"])
                pending = []
                if ii + 1 < len(items):
                    pending.append(items[ii + 1]["qgen"])
                if u["si"] == 0:
                    for k in range(1, 5):
                        pending.append(units[ui + k]["kgen"])
                    if ui + 5 < len(units):
                        pending.append(units[ui + 5]["kgen"])
                core(u["hd"], u["S"], it["qt"], *u["bufs"], ii % 2, pending, ii)
            phase_end()

            def in_tiles(ti, dst, key):
                P.op("sp", lambda e: e.dma_start(out=dst, in_=oT_d.rearrange("(c p) t -> p c t", p=128)[:, :, ti * 512:(ti + 1) * 512]), r=["d:oT"], w=[key], dma=key)
            out_proj(l, 0, W["mla_w_o"][j], 8, in_tiles)
            phase_end()

        for l in range(depth):
            if l % 2 == 0:
                even_mixer(l)
            else:
                mla(l)
            ffn(l)

        xt = [A.alloc([8, 512], F32) for _ in range(2)]
        yo = [A.alloc([1024], F32) for _ in range(2)]

        def ldx(ti):
            P.op("sp", lambda e: e.dma_start(out=xt[ti % 2], in_=xT_tile_ap(ti)), r=["d:xT"], w=["fx%d" % (ti % 2)], dma="fx%d" % (ti % 2))
        ldx(0)
        for ti in range(NT):
            if ti + 1 < NT:
                ldx(ti + 1)
            b = ti % 2
            for tc in range(4):
                yb = tc % 2
                for hh in range(2):
                    pb = hh

                    def tr(e, b=b, tc=tc, hh=hh, pb=pb):
                        last = None
                        for d4 in range(4):
                            dc = hh * 4 + d4
                            last = e.transpose(ps[pb][:, d4 * 128:(d4 + 1) * 128], xt[b][:, dc, tc * 128:(tc + 1) * 128], ident)
                        return last
                    P.op("pe", tr, r=["fx%d" % b, "ident"], w=["ps%d" % pb])
                    if hh == 0:
                        P.op("act", lambda e, yb=yb, pb=pb: e.activation(out=yo[yb][:, 0:512], in_=ps[pb], func=AF.Copy), x=["ps%d" % pb], w=["yo%d" % yb])
                    else:
                        P.op("dve", lambda e, yb=yb, pb=pb: e.tensor_copy(out=yo[yb][:, 512:1024], in_=ps[pb]), x=["ps%d" % pb], w=["yo%d" % yb])
                P.op("sp", lambda e, yb=yb, ti=ti, tc=tc: e.dma_start(out=y_rows(ti)[tc * 128:(tc + 1) * 128, :], in_=yo[yb]), r=["yo%d" % yb], dma="st:yo%d" % yb)
        P.barrier()
        P.emit()
        build.n_inst = P.n_inst
    return nc


_NC = {}


def make_in_maps(inp):
    C = host_consts()
    pv = pv_layout(inp).array()
    bf = ml_dtypes.bfloat16
    shared = {k: np.ascontiguousarray(inp[k], dtype=np.float32) for k in WEIGHTS}
    shared["pv"] = pv
    shared["sguT"] = np.ascontiguousarray(np.transpose(inp["sgu_w"], (0, 3, 1, 2)))
    shared["sgub"] = np.ascontiguousarray(inp["sgu_b"].reshape(2, 1, 512))
    shared["decbc"] = np.ascontiguousarray(np.broadcast_to(inp["hy_decay"].reshape(2, 1, 1024), (2, 128, 1024)))
    for k, v in C.items():
        shared["c_" + k] = v
    maps = []
    for c in range(8):
        m = dict(shared)
        m["xs"] = np.ascontiguousarray(inp["x_sample"][c])
        m["xp"] = np.ascontiguousarray(inp["x_prompt"][4 * c:4 * c + 4].reshape(1024, 1024))
        m["cckv"] = np.ascontiguousarray(inp["cache_ckv"][c])
        m["ckr"] = np.ascontiguousarray(inp["cache_krope"][c])
        cond = np.stack([inp["c"][c], inp["c_ctx"]], axis=1)
        m["condT"] = np.ascontiguousarray(cond.reshape(8, 128, 2).transpose(1, 0, 2))
        maps.append(m)
    return maps


def kernel(**inputs):
    inp = {k: np.asarray(v) for k, v in inputs.items()}
    if "nc" not in _NC:
        _NC["nc"] = build()
    nc = _NC["nc"]
    maps = make_in_maps(inp)
    res = run_bass_kernel_spmd(nc, maps, core_ids=list(range(8)))
    R = res.results
    y_sample = np.stack([np.asarray(R[c]["ys"], np.float32) for c in range(8)], 0)
    y_prompt = np.concatenate([np.asarray(R[c]["yp"], np.float32).reshape(4, 256, 1024) for c in range(8)], 0)
    nckv = np.concatenate([np.asarray(R[c]["nckv"], np.float32) for c in range(8)], 0)
    nkr = np.concatenate([np.asarray(R[c]["nkr"], np.float32) for c in range(8)], 0)
    return (y_prompt, y_sample, nckv, nkr)
```
